# Optimizing a Trainium2 kernel written in Bass

```python
import math
import jax, jax.numpy as jnp
from jax import lax
import numpy as np

D_MODEL = 2048
BATCH = 2
SEQ = 4096
DEPTH = 4
DEC_BATCH = 32
DEC_SEQ = 8
PAST_LEN = 16384
PAGE_SIZE = 128

D_MIX = D_MODEL
CHUNK = 128
W_A = D_MIX // 4
H_A = 4
C_A = W_A // H_A
W_B = D_MIX // 2
HD_B = 64
H_B = W_B // HD_B
KV_B = 4
REP_B = H_B // KV_B
WINDOW = 128
ROPE_DIM = HD_B // 4
ROPE_THETA = 500000.0
W_C = D_MIX - W_A - W_B
GC = 16
G_C = W_C // GC
P_C = 64
D_IN = 3 * W_A + 2 * W_B + 2 * KV_B * HD_B + 2 * W_C
EPS = 1e-5
NEG = -1e30

kernel_name = "hymba_chunkmlp_swa_s5_step"


def rmsnorm(x, g):
    xf = x.astype(jnp.float32)
    y = xf * lax.rsqrt(jnp.mean(xf * xf, axis=-1, keepdims=True) + EPS)
    return (y * g.astype(jnp.float32)).astype(x.dtype)


def rope(x, pos):
    half = ROPE_DIM // 2
    inv = ROPE_THETA ** (-jnp.arange(half, dtype=jnp.float32) * 2.0 / ROPE_DIM)
    ang = pos[:, None] * inv[None, :]
    cos = jnp.cos(ang)[None, :, None, :]
    sin = jnp.sin(ang)[None, :, None, :]
    xf = x.astype(jnp.float32)
    x1, x2, rest = xf[..., :half], xf[..., half:ROPE_DIM], xf[..., ROPE_DIM:]
    out = jnp.concatenate([x1 * cos - x2 * sin, x2 * cos + x1 * sin, rest], axis=-1)
    return out.astype(x.dtype)


def chunk_mlp(u, v, w_s, b_s):
    bsz, L, _ = v.shape
    n = min(L, CHUNK)
    mask = jnp.tril(jnp.ones((n, n), dtype=bool))
    ws = jnp.where(mask[None], w_s[:, :n, :n].astype(jnp.float32), 0.0)
    vc = v.astype(jnp.float32).reshape(bsz, L // n, n, H_A, C_A)
    z = jnp.einsum('hts,bnshc->bnthc', ws, vc)
    z = z + b_s[:, :n].astype(jnp.float32).T[None, None, :, :, None]
    return u.astype(jnp.float32) * z.reshape(bsz, L, W_A)


def attend_with_sinks(qb, kb, vb, mask, sinks):
    s = jnp.einsum('bnqgrd,bnkgd->bngrqk', qb.astype(jnp.float32), kb.astype(jnp.float32)) * (HD_B ** -0.5)
    s = jnp.where(mask[None, :, None, None], s, NEG)
    sink = sinks.astype(jnp.float32).reshape(KV_B, REP_B)[None, None, :, :, None, None]
    m = jnp.maximum(jnp.max(s, axis=-1, keepdims=True), sink)
    p = jnp.exp(s - m)
    w = p / (jnp.sum(p, axis=-1, keepdims=True) + jnp.exp(sink - m))
    return jnp.einsum('bngrqk,bnkgd->bnqgrd', w, vb.astype(jnp.float32))


def swa_prompt(q, k, v, sinks):
    bsz, L = q.shape[:2]
    nb = L // WINDOW
    qb = q.reshape(bsz, nb, WINDOW, KV_B, REP_B, HD_B)
    pad = ((0, 0), (WINDOW, 0), (0, 0), (0, 0))
    kp = jnp.pad(k, pad).reshape(bsz, nb + 1, WINDOW, KV_B, HD_B)
    vp = jnp.pad(v, pad).reshape(bsz, nb + 1, WINDOW, KV_B, HD_B)
    kb = jnp.concatenate([kp[:, :-1], kp[:, 1:]], axis=2)
    vb = jnp.concatenate([vp[:, :-1], vp[:, 1:]], axis=2)
    i = jnp.arange(WINDOW)[None, :, None]
    j = jnp.arange(2 * WINDOW)[None, None, :]
    blk = jnp.arange(nb)[:, None, None]
    diff = i + WINDOW - j
    mask = (diff >= 0) & (diff < WINDOW) & (blk * WINDOW + j - WINDOW >= 0)
    o = attend_with_sinks(qb, kb, vb, mask, sinks)
    return o.reshape(bsz, L, W_B)


def swa_sample(q, k, v, k_past, v_past, sinks):
    bsz, S = q.shape[:2]
    kb = jnp.concatenate([k_past.astype(k.dtype), k], axis=1)[:, None]
    vb = jnp.concatenate([v_past.astype(v.dtype), v], axis=1)[:, None]
    qb = q.reshape(bsz, 1, S, KV_B, REP_B, HD_B)
    i = jnp.arange(S)[:, None]
    j = jnp.arange(WINDOW + S)[None, :]
    diff = i + WINDOW - j
    mask = ((diff >= 0) & (diff < WINDOW))[None]
    o = attend_with_sinks(qb, kb, vb, mask, sinks)
    return o.reshape(bsz, S, W_B)


def s5(u, h0_re, h0_im, a_re, a_im, log_dt, b_re, b_im, c_re, c_im, d_skip, w_glu, b_glu):
    bsz, L, _ = u.shape
    f32 = jnp.float32
    a_re, a_im = a_re.astype(f32), a_im.astype(f32)
    dt = jnp.exp(log_dt.astype(f32))[:, None]
    mag = jnp.exp(a_re * dt)
    ab_re, ab_im = mag * jnp.cos(a_im * dt), mag * jnp.sin(a_im * dt)
    nr, ni = ab_re - 1.0, ab_im
    den = a_re * a_re + a_im * a_im
    f_re = (nr * a_re + ni * a_im) / den
    f_im = (ni * a_re - nr * a_im) / den
    br, bi = b_re.astype(f32), b_im.astype(f32)
    bb_re = f_re[..., None] * br - f_im[..., None] * bi
    bb_im = f_re[..., None] * bi + f_im[..., None] * br
    uf = u.astype(f32)
    ug = uf.reshape(bsz, L, G_C, GC)
    bu_re = jnp.einsum('gpc,blgc->blgp', bb_re, ug)
    bu_im = jnp.einsum('gpc,blgc->blgp', bb_im, ug)
    h0r, h0i = h0_re.astype(f32), h0_im.astype(f32)
    bu_re = bu_re.at[:, 0].add(ab_re * h0r - ab_im * h0i)
    bu_im = bu_im.at[:, 0].add(ab_re * h0i + ab_im * h0r)
    ar_b = jnp.broadcast_to(ab_re, bu_re.shape)
    ai_b = jnp.broadcast_to(ab_im, bu_re.shape)

    def combine(e1, e2):
        a1r, a1i, b1r, b1i = e1
        a2r, a2i, b2r, b2i = e2
        return (a2r * a1r - a2i * a1i,
                a2r * a1i + a2i * a1r,
                a2r * b1r - a2i * b1i + b2r,
                a2r * b1i + a2i * b1r + b2i)

    _, _, h_re, h_im = lax.associative_scan(combine, (ar_b, ai_b, bu_re, bu_im), axis=1)
    y = (jnp.einsum('gcp,blgp->blgc', c_re.astype(f32), h_re)
         - jnp.einsum('gcp,blgp->blgc', c_im.astype(f32), h_im))
    y = y.reshape(bsz, L, W_C) + d_skip.astype(f32) * uf
    y = jax.nn.gelu(y)
    y = y * jax.nn.sigmoid(y @ w_glu.astype(f32) + b_glu.astype(f32))
    return y, h_re[:, -1], h_im[:, -1]


def mixer_layer(x, pos, k_past, v_past, h0_re, h0_im, norm_g, w_in, w_out, w_s, b_s, sinks,
                a_re, a_im, log_dt, b_re, b_im, c_re, c_im, d_skip, w_glu, b_glu):
    bsz, L, _ = x.shape
    h = rmsnorm(x, norm_g)
    z = h @ w_in
    sizes = [W_A, W_A, W_A, W_B, KV_B * HD_B, KV_B * HD_B, W_B, W_C, W_C]
    splits = [int(s) for s in np.cumsum(sizes)[:-1]]
    u_a, v_a, g_a, q, k, v, g_b, u_c, g_c = jnp.split(z, splits, axis=-1)
    q = rope(q.reshape(bsz, L, H_B, HD_B), pos)
    k = rope(k.reshape(bsz, L, KV_B, HD_B), pos)
    v = v.reshape(bsz, L, KV_B, HD_B)
    out_a = chunk_mlp(u_a, v_a, w_s, b_s)
    if k_past is None:
        out_b = swa_prompt(q, k, v, sinks)
        k_rows, v_rows = k[:, -WINDOW:], v[:, -WINDOW:]
        h0_re = jnp.zeros((bsz, G_C, P_C), jnp.float32)
        h0_im = jnp.zeros((bsz, G_C, P_C), jnp.float32)
    else:
        out_b = swa_sample(q, k, v, k_past, v_past, sinks)
        k_rows, v_rows = k, v
    out_c, h_re, h_im = s5(u_c, h0_re, h0_im, a_re, a_im, log_dt, b_re, b_im, c_re, c_im,
                           d_skip, w_glu, b_glu)
    mixed = jnp.concatenate([
        out_a * jax.nn.silu(g_a.astype(jnp.float32)),
        out_b * jax.nn.silu(g_b.astype(jnp.float32)),
        out_c * jax.nn.silu(g_c.astype(jnp.float32))], axis=-1).astype(x.dtype)
    y = x + mixed @ w_out
    return y, k_rows, v_rows, h_re.astype(x.dtype), h_im.astype(x.dtype), v_a


def setup_inputs(seed: int = 0) -> dict:
    key = jax.random.key(seed)
    ks = jax.random.split(key, 24)
    f32 = jnp.float32
    nrm = lambda k, shape, s: jax.random.normal(k, shape, f32) * s
    n_idx = jnp.arange(P_C, dtype=f32)
    return {
        "x_prompt": nrm(ks[0], (BATCH, SEQ, D_MODEL), 1.0),
        "x_sample": nrm(ks[1], (DEC_BATCH, DEC_SEQ, D_MODEL), 1.0),
        "cache_swa_k": nrm(ks[2], (DEPTH, DEC_BATCH, WINDOW, KV_B, HD_B), 1.0),
        "cache_swa_v": nrm(ks[3], (DEPTH, DEC_BATCH, WINDOW, KV_B, HD_B), 1.0),
        "state_ssm_re": nrm(ks[4], (DEPTH, DEC_BATCH, G_C, P_C), 0.5),
        "state_ssm_im": nrm(ks[5], (DEPTH, DEC_BATCH, G_C, P_C), 0.5),
        "norm_g": 1.0 + nrm(ks[6], (DEPTH, D_MODEL), 0.02),
        "final_norm_g": 1.0 + nrm(ks[7], (D_MODEL,), 0.02),
        "w_in": nrm(ks[8], (DEPTH, D_MODEL, D_IN), D_MODEL ** -0.5),
        "w_out": nrm(ks[9], (DEPTH, D_MIX, D_MODEL), D_MIX ** -0.5),
        "chunk_w_s": nrm(ks[10], (DEPTH, H_A, CHUNK, CHUNK), CHUNK ** -0.5),
        "chunk_b_s": 1.0 + nrm(ks[11], (DEPTH, H_A, CHUNK), 0.02),
        "attn_sinks": nrm(ks[12], (DEPTH, H_B), 1.0),
        "ssm_a_re": -0.5 + nrm(ks[13], (DEPTH, G_C, P_C), 0.01),
        "ssm_a_im": math.pi * n_idx + nrm(ks[14], (DEPTH, G_C, P_C), 0.01),
        "ssm_log_dt": jax.random.uniform(ks[15], (DEPTH, G_C), f32, math.log(1e-3), math.log(1e-1)),
        "ssm_b_re": nrm(ks[16], (DEPTH, G_C, P_C, GC), (2.0 * GC) ** -0.5),
        "ssm_b_im": nrm(ks[17], (DEPTH, G_C, P_C, GC), (2.0 * GC) ** -0.5),
        "ssm_c_re": nrm(ks[18], (DEPTH, G_C, GC, P_C), (2.0 * P_C) ** -0.5),
        "ssm_c_im": nrm(ks[19], (DEPTH, G_C, GC, P_C), (2.0 * P_C) ** -0.5),
        "ssm_d": nrm(ks[20], (DEPTH, W_C), 0.5),
        "glu_w": nrm(ks[21], (DEPTH, W_C, W_C), W_C ** -0.5),
        "glu_b": nrm(ks[22], (DEPTH, W_C), 0.01),
    }


def reference(x_prompt, x_sample, cache_swa_k, cache_swa_v, state_ssm_re, state_ssm_im,
              norm_g, final_norm_g, w_in, w_out, chunk_w_s, chunk_b_s, attn_sinks,
              ssm_a_re, ssm_a_im, ssm_log_dt, ssm_b_re, ssm_b_im, ssm_c_re, ssm_c_im,
              ssm_d, glu_w, glu_b):
    pos_p = jnp.arange(SEQ, dtype=jnp.float32)
    pos_s = jnp.arange(DEC_SEQ, dtype=jnp.float32) + PAST_LEN
    xp, xs = x_prompt, x_sample
    kp_l, vp_l, hrp_l, hip_l = [], [], [], []
    ks_l, vs_l, hrs_l, his_l, va_l = [], [], [], [], []
    for l in range(DEPTH):
        wl = (norm_g[l], w_in[l], w_out[l], chunk_w_s[l], chunk_b_s[l], attn_sinks[l],
              ssm_a_re[l], ssm_a_im[l], ssm_log_dt[l], ssm_b_re[l], ssm_b_im[l],
              ssm_c_re[l], ssm_c_im[l], ssm_d[l], glu_w[l], glu_b[l])
        xp, kp, vp, hrp, hip, _ = mixer_layer(xp, pos_p, None, None, None, None, *wl)
        xs, kss, vss, hrs, his, vas = mixer_layer(xs, pos_s, cache_swa_k[l], cache_swa_v[l],
                                                  state_ssm_re[l], state_ssm_im[l], *wl)
        kp_l.append(kp); vp_l.append(vp); hrp_l.append(hrp); hip_l.append(hip)
        ks_l.append(kss); vs_l.append(vss); hrs_l.append(hrs); his_l.append(his); va_l.append(vas)
    y_prompt = rmsnorm(xp, final_norm_g)
    y_sample = rmsnorm(xs, final_norm_g)
    return (y_prompt, y_sample,
            jnp.stack(kp_l), jnp.stack(vp_l), jnp.stack(ks_l), jnp.stack(vs_l),
            jnp.stack(hrp_l), jnp.stack(hip_l), jnp.stack(hrs_l), jnp.stack(his_l),
            jnp.stack(va_l))
```

```python
import math
import numpy as np
import ml_dtypes
import concourse.bass as bass
import concourse.mybir as mybir
from concourse.bass_utils import run_bass_kernel_spmd

F32 = mybir.dt.float32
BF16 = mybir.dt.bfloat16
ALU = mybir.AluOpType
AF = mybir.ActivationFunctionType
AX = mybir.AxisListType

D = 2048
KD = 16
DIN = 5120
DEPTH = 4
SEQ = 4096
TSEG = 512
NSEG = SEQ // TSEG
NB = TSEG // 128
NS = 32
NJ = TSEG // 8
PAST = 16384
OFF_UA, OFF_VA, OFF_GA, OFF_Q, OFF_K, OFF_V, OFF_GB, OFF_UC, OFF_GC = 0, 512, 1024, 1536, 2560, 2816, 3072, 4096, 4608
NEGM = -60000.0
TW1 = 0
TW3 = 4096
TBD = 8192
TER = 10240
TEI = TER + 16 * (NJ + 4)
TRHO = TEI + 16 * (NJ + 4)
TABW = TRHO + 16


def dap(t, off, dims):
    return bass.AP(tensor=t, offset=off, ap=[[s, c] for s, c in dims])


class Prog:
    def __init__(self, nc):
        self.nc = nc
        self.ops = []
        self.last_w = {}
        self.readers = {}

    def add(self, eng, fn, r=(), w=(), dma=False):
        deps = set()
        for k in r:
            if k in self.last_w:
                deps.add(self.last_w[k])
        for k in w:
            if k in self.last_w:
                deps.add(self.last_w[k])
            for x in self.readers.get(k, ()):
                deps.add(x)
        idx = len(self.ops)
        deps.discard(idx)
        import sys as _s
        fr = _s._getframe(1)
        ln = []
        while fr is not None and len(ln) < 3:
            ln.append(fr.f_lineno); fr = fr.f_back
        self.ops.append(dict(eng=eng, fn=fn, deps=sorted(deps), dma=dma, line=ln))
        for k in w:
            self.last_w[k] = idx
            self.readers[k] = []
        for k in r:
            if k not in w:
                self.readers.setdefault(k, []).append(idx)
        return idx

    def emit(self):
        nc = self.nc
        ops = self.ops
        engs = ['pe', 'act', 'dve', 'pool', 'sp']
        RR = {'sp': 8, 'pool': 4, 'act': 2, 'pe': 1, 'dve': 1}
        for o in ops:
            o['sig'] = False
        for i, o in enumerate(ops):
            for d in o['deps']:
                p = ops[d]
                if p['dma']:
                    continue
                if p['eng'] == 'pe' and o['eng'] == 'pe' and not o['dma']:
                    continue
                p['sig'] = True
        cnt = {e: 0 for e in engs}
        dcnt = {e: 0 for e in engs}
        for o in ops:
            e = o['eng']
            if o['dma']:
                n = dcnt[e]
                o['dslot'] = n % RR[e]
                o['dval'] = 16 * (n // RR[e] + 1)
                o['dprev'] = None
                dcnt[e] += 1
            elif o['sig']:
                cnt[e] += 1
                o['val'] = cnt[e]
        lastslot = {}
        for i, o in enumerate(ops):
            if o['dma']:
                key = (o['eng'], o['dslot'])
                o['dprev'] = lastslot.get(key)
                lastslot[key] = i
        import contextlib
        with contextlib.ExitStack() as st:
            sems = {e: st.enter_context(nc.semaphore("s_" + e)) for e in engs}
            dsems = {}
            for e in ['sp', 'pool', 'act']:
                for s in range(RR[e]):
                    dsems[(e, s)] = st.enter_context(nc.semaphore("d_%s%d" % (e, s)))
            block = st.enter_context(nc.Block())
            per = {e: [o for o in ops if o['eng'] == e] for e in engs}

            def run(e, engine):
                waited = {}

                def need(sem_key, sem, val):
                    if waited.get(sem_key, 0) >= val:
                        return
                    engine.wait_ge(sem, val)
                    waited[sem_key] = val

                for o in per[e]:
                    for d in o['deps']:
                        p = ops[d]
                        if p['dma']:
                            need(('d', p['eng'], p['dslot']), dsems[(p['eng'], p['dslot'])], p['dval'])
                        else:
                            if p['eng'] == 'pe' and e == 'pe' and not o['dma']:
                                continue
                            need(('c', p['eng']), sems[p['eng']], p['val'])
                    if o['dma'] and o['dprev'] is not None:
                        p = ops[o['dprev']]
                        need(('d', p['eng'], p['dslot']), dsems[(p['eng'], p['dslot'])], p['dval'])
                    ins = o['fn'](engine)
                    if o['dma']:
                        ins.then_inc(dsems[(e, o['dslot'])], 16)
                    elif o['sig']:
                        ins.then_inc(sems[e], 1)
                if e in ('sp', 'pool', 'act'):
                    tot = {}
                    for o in per[e]:
                        if o['dma']:
                            tot[o['dslot']] = o['dval']
                    for s, v in tot.items():
                        engine.wait_ge(dsems[(e, s)], v)

            @block.tensor
            def _(en):
                run('pe', en)

            @block.scalar
            def _(en):
                run('act', en)

            @block.vector
            def _(en):
                run('dve', en)

            @block.gpsimd
            def _(en):
                run('pool', en)

            @block.sync
            def _(en):
                run('sp', en)


def build_program(depth=DEPTH, nseg=NSEG, debug=False):
    nc = bass.Bass("TRN2", target_bir_lowering=False)
    P = Prog(nc)
    seqlen = nseg * TSEG

    def din(name, shape, dt=F32):
        return nc.dram_tensor(name, list(shape), dt, kind="ExternalInput")

    def dout(name, shape, dt=F32):
        return nc.dram_tensor(name, list(shape), dt, kind="ExternalOutput")

    xp = din("xp", [seqlen, D]); xs = din("xs", [NS, D])
    ck = din("ck", [depth, 4, 128, 256]); cv = din("cv", [depth, 4, 128, 256])
    sre = din("sre", [depth, 4, 16, 128]); sim = din("sim", [depth, 4, 16, 128])
    ng = din("ng", [depth * 16 + 16, 128])
    w_in = din("w_in", [depth, D, DIN]); w_out = din("w_out", [depth, D, D])
    cws = din("cws", [depth, 4, 128, 128]); cbs = din("cbs", [1, depth * 4 * 128])
    sinks = din("sinks", [1, depth * 16])
    a_re = din("a_re", [depth, 32, 64]); a_im = din("a_im", [depth, 32, 64]); ldt = din("ldt", [depth, 32])
    b_re = din("b_re", [depth, 32, 64, 16]); b_im = din("b_im", [depth, 32, 64, 16])
    c_re = din("c_re", [depth, 512, 64]); c_im = din("c_im", [depth, 512, 64])
    dsk = din("dsk", [depth * 4, 128]); glw = din("glw", [depth, 512, 512]); glb = din("glb", [depth * 4, 128])
    c_ident = din("c_ident", [128, 128]); c_tril = din("c_tril", [128, 128])
    c_maskA = din("c_maskA", [128, 512]); c_maskF = din("c_maskF", [128, 512]); c_maskS = din("c_maskS", [32, 544])
    c_maskm = din("c_maskm", [128, 2]); c_bd = din("c_bd", [32, 32])
    c_perm = din("c_perm", [128, 128]); c_cos = din("c_cos", [128, seqlen + NS]); c_sin = din("c_sin", [128, seqlen + NS])

    y_p = dout("y_p", [seqlen, D]); y_s = dout("y_s", [NS, D])
    nk_p = dout("nk_p", [depth, 128, 256]); nv_p = dout("nv_p", [depth, 128, 256])
    nk_s = dout("nk_s", [depth, NS, 256]); nv_s = dout("nv_s", [depth, NS, 256])
    hr_p = dout("hr_p", [depth, 16, 128]); hi_p = dout("hi_p", [depth, 16, 128])
    hr_s = dout("hr_s", [depth, 4, 16, 128]); hi_s = dout("hi_s", [depth, 4, 16, 128])
    va_s = dout("va_s", [depth, NS, 512])
    s5tab = nc.dram_tensor("s5tab", [depth, 128, TABW], F32)
    wcache = nc.dram_tensor("wcache", [depth * 56, 128, KD * 128], BF16)

    NMAX = TSEG + NS
    import contextlib
    st = contextlib.ExitStack()

    def sb(name, shape, dt=F32):
        return st.enter_context(nc.sbuf_tensor(name, list(shape), dt))

    def ps(name, shape, dt=F32):
        return st.enter_context(nc.psum_tensor(name, list(shape), dt))

    xT = sb("xT", [128, KD, NMAX]); hT = sb("hT", [128, KD, NMAX], BF16); mix = sb("mix", [128, KD, NMAX], BF16)
    kT = sb("kT", [128, 2, 128 + NMAX], BF16); Vt = sb("Vt", [128, NB + 2, 256], BF16)
    vatm = sb("vatm", [128, NB + 1, 512], BF16)
    khalo = sb("khalo", [128, depth, 2, 128], BF16); vhalo = sb("vhalo", [128, depth, 256], BF16)
    cosT = sb("cosT", [128, NMAX]); sinT = sb("sinT", [128, NMAX])
    NWB = 3
    wb = [sb("wb%d" % i, [128, KD, 128], BF16) for i in range(NWB)]
    iost0_ = sb("iost0", [128, D]); iost = [iost0_, iost0_]
    ident = sb("ident", [128, 128]); identb = sb("identb", [128, 128], BF16); onesb = sb("onesb", [128, 128], BF16)
    ones1 = sb("ones1", [1, 128]); epsc = sb("epsc", [128, 1])
    maskA = sb("maskA", [128, 512], BF16); maskF = sb("maskF", [128, 512], BF16); maskS = sb("maskS", [32, 544], BF16)
    gcol = sb("gcol", [128, depth * 16 + 16]); dcol = sb("dcol", [128, depth * 4]); gbcol = sb("gbcol", [128, depth * 4])
    sinkbc = sb("sinkbc", [128, depth * 16]); bsrow = sb("bsrow", [1, 512]); bsS = sb("bsS", [1, 4, 32])
    wsT = sb("wsT", [128, depth * 4, 128], BF16); wsS = sb("wsS", [32, depth * 4, 32], BF16)
    hprev = sb("hprev", [128, depth, 16, 2]); s_in = sb("s_in", [128, depth, 4, 2, 16])
    tabA = sb("tabA", [128, TBD + 2048])
    tabE = sb("tabE", [128, TABW - TER])
    Sprev = sb("Sprev", [128, 16, 2, NJ + 4], BF16)
    sq = [sb("sq%d" % i, [128, NMAX], BF16) for i in range(2)]
    rstd = sb("rstd", [128, NMAX]); tA = [sb("tA%d" % i, [128, NMAX]) for i in range(4)]
    xsw0_ = sb("xsw0", [128, NMAX]); xsw = [xsw0_, xsw0_]
    tB0_ = sb("tB0", [128, NMAX], BF16); tB = [tB0_, tB0_]
    yb = sb("yb", [128, 4, NMAX], BF16)
    attb = sb("attb", [128, 2048], BF16); Pf = attb[:, 0:1024].rearrange("q (a b) -> q a b", b=256); Pb = attb[:, 1024:2048].rearrange("q (a b) -> q a b", b=256); glwb = attb[:].rearrange("q (a b) -> q a b", b=512); PTs = sb("PTs", [128, 1024], BF16)
    sm = sb("sm", [128, 64])
    iob = iost0_[:].bitcast(BF16)
    ckb = iob[:, 0:1024].rearrange("q (a b) -> q a b", b=256); cvb = iob[:, 1024:2048].rearrange("q (a b) -> q a b", b=256)
    ckT = iob[:, 2048:3072].rearrange("q (b k w) -> q b k w", b=4, k=2)
    PfS = Pf[0:32].rearrange("q a b -> q (a b)")[:, 0:544]; PbS = Pb[0:32].rearrange("q a b -> q (a b)")[:, 0:544]; PTS2 = sb("PTS2", [128, 5, 32], BF16)
    s5w = [sb("s5w%d" % i, [128, NJ + 4]) for i in range(10)]
    outst = sb("outst", [128, 416])
    ps0 = ps("ps0", [128, 512]); ps1 = ps("ps1", [128, 512]); ps2 = ps("ps2", [128, 512]); ps3 = ps("ps3", [128, 512])
    psS = ps("psS", [128, 1024]); ps6 = ps("ps6", [128, 512]); psT = ps("psT", [128, 1024], BF16)

    tabAb = tabA[:].bitcast(BF16)

    def E(name):
        return {'pe': 'pe', 'act': 'act', 'dve': 'dve', 'pool': 'pool', 'sp': 'sp'}[name]

    def dma(q, out, in_, r, w):
        return P.add(q, lambda e: e.dma_start(out=out, in_=in_, allow_slow_non_contiguous=True), r=r, w=w, dma=True)

    def mm(out, lhsT, rhs, start, stop, r, w, **kw):
        return P.add('pe', lambda e: e.matmul(out, lhsT=lhsT, rhs=rhs, start=start, stop=stop, **kw), r=r, w=w)

    def tr(out, in_, idn, r, w, **kw):
        return P.add('pe', lambda e: e.transpose(out, in_, idn, **kw), r=r + ['ident'], w=w)

    def act(out, in_, func, r, w, scale=1.0, bias=None, accum=None):
        def f(e):
            kw = dict(out=out, in_=in_, func=func, scale=scale)
            if bias is not None:
                kw['bias'] = bias
            if accum is not None:
                kw['accum_out'] = accum
            return e.activation(**kw)
        return P.add('act', f, r=r, w=w)

    def tt(eng, out, a, b, op, r, w):
        return P.add(eng, lambda e: e.tensor_tensor(out=out, in0=a, in1=b, op=op), r=r, w=w)

    def ts(eng, out, a, s1, op0, r, w, s2=None, op1=None):
        if op1 is None:
            return P.add(eng, lambda e: e.tensor_scalar(out=out, in0=a, scalar1=s1, scalar2=None, op0=op0), r=r, w=w)
        return P.add(eng, lambda e: e.tensor_scalar(out=out, in0=a, scalar1=s1, scalar2=s2, op0=op0, op1=op1), r=r, w=w)

    def stt(out, a, s, b, op0, op1, r, w):
        return P.add('dve', lambda e: e.scalar_tensor_tensor(out=out, in0=a, scalar=s, in1=b, op0=op0, op1=op1), r=r, w=w)

    def cp(eng, out, in_, r, w):
        if eng == 'act':
            return P.add('act', lambda e: e.copy(out=out, in_=in_), r=r, w=w)
        return P.add(eng, lambda e: e.tensor_copy(out=out, in_=in_), r=r, w=w)

    def memset(eng, ap, v, w):
        return P.add(eng, lambda e: e.memset(ap, v), r=[], w=w)

    dma('sp', ident[:], c_ident.ap(), [], ['ident'])
    dma('pool', identb[:], c_ident.ap(), [], ['ident'])
    dma('pool', maskA[:], c_maskA.ap(), [], ['masks']); dma('pool', maskF[:], c_maskF.ap(), [], ['masks'])
    dma('pool', maskS[:], c_maskS.ap(), [], ['masks'])
    memset('dve', onesb[:], 1.0, ['ident']); memset('dve', ones1[:], 1.0, ['ident']); memset('dve', epsc[:], 1e-5, ['ident'])
    xw0 = xsw[0]
    permb = sb("permb", [128, 128], BF16)
    dma('pool', permb[:], c_perm.ap(), [], ['ident'])
    memset('pool', kT[:], 0.0, ['kT']); memset('pool', Vt[:], 0.0, ['Vt'])
    memset('pool', khalo[:], 0.0, ['khalo']); memset('pool', vhalo[:], 0.0, ['vhalo']); memset('dve', hprev[:], 0.0, ['hprev'])
    dma('sp', sinkbc[:], dap(sinks, 0, [[0, 128], [1, depth * 16]]), [], ['sinkbc'])
    nrow = depth * 16 + 16
    dma('sp', iost[0][0:nrow, 0:128], ng.ap(), [], ['iost0'])
    tr(ps3[:, 0:nrow], iost[0][0:nrow, 0:128], ident[0:nrow, 0:nrow], ['iost0'], ['ps3'])
    cp('dve', gcol[:], ps3[:, 0:nrow], ['ps3'], ['gcol'])
    dma('sp', iost[0][0:depth * 4, 128:256], dsk.ap(), [], ['iost0'])
    tr(ps3[:, 0:depth * 4], iost[0][0:depth * 4, 128:256], ident[0:depth * 4, 0:depth * 4], ['iost0'], ['ps3'])
    cp('dve', dcol[:], ps3[:, 0:depth * 4], ['ps3'], ['dcol'])
    dma('sp', iost[0][0:depth * 4, 256:384], glb.ap(), [], ['iost0'])
    tr(ps3[:, 0:depth * 4], iost[0][0:depth * 4, 256:384], ident[0:depth * 4, 0:depth * 4], ['iost0'], ['ps3'])
    cp('dve', gbcol[:], ps3[:, 0:depth * 4], ['ps3'], ['gbcol'])
    dma('sp', iost[0][:, 1536:1664], c_tril.ap(), [], ['iost0'])
    for lh in range(depth * 4):
        dma('sp', iost[0][:, 512:640], dap(cws, lh * 128 * 128, [[128, 128], [1, 128]]), [], ['iost0'])
        tt('dve', iost[0][:, 640:768], iost[0][:, 512:640], iost[0][:, 1536:1664], ALU.mult, ['iost0', 'iost0'], ['iost0b'])
        tr(ps3[:, 0:128], iost[0][:, 640:768], ident[:], ['iost0b'], ['ps3'])
        cp('dve', wsT[:, lh, :], ps3[:, 0:128], ['ps3'], ['wsT'])
    dma('sp', iost[0][0:32, 1664:1696], c_bd.ap(), [], ['iost0'])
    for lh in range(depth * 4):
        for b in range(4):
            dma('sp', iost[0][8 * b:8 * b + 8, 1024:1056].rearrange("q (a s) -> q a s", s=8), dap(cws, lh * 128 * 128, [[128, 8], [0, 4], [1, 8]]), [], ['iost0'])
        tt('dve', iost[0][0:32, 1056:1088], iost[0][0:32, 1024:1056], iost[0][0:32, 1664:1696], ALU.mult, ['iost0', 'iost0'], ['iost0b'])
        tr(ps3[0:32, 0:32], iost[0][0:32, 1056:1088], ident[0:32, 0:32], ['iost0b'], ['ps3'])
        cp('dve', wsS[:, lh, :], ps3[0:32, 0:32], ['ps3'], ['wsS'])
    for l in range(depth):
        for b in range(4):
            for ri, src in enumerate((sre, sim)):
                dma('sp', iost[0][0:16, 0:128], dap(src, ((l * 4 + b) * 16) * 128, [[128, 16], [1, 128]]), [], ['iost0'])
                tr(ps3[:, 0:16], iost[0][0:16, 0:128], ident[0:16, 0:16], ['iost0'], ['ps3'])
                cp('dve', s_in[:, l, b, ri, :], ps3[:, 0:16], ['ps3'], ['s_in'])

    W = NJ + 4
    maskm = sb("maskm", [128, 2])
    dma('sp', maskm[:], c_maskm.ap(), [], ['maskm'])
    mixf = mix[:].rearrange("q a b -> q (a b)").bitcast(F32)
    xTf = xT[:].rearrange("q a b -> q (a b)")
    _o = [0]

    def carve(buf, n, lim):
        a = _o[0]; _o[0] += n
        assert _o[0] <= lim
        return buf[:, a:a + n]
    sc = [carve(mixf, 16, 4352) for i in range(72)]
    Bn = [carve(mixf, 256, 4352).rearrange("q (a b) -> q a b", b=16) for i in range(2)]
    Cn = [carve(mixf, 256, 4352).rearrange("q (a b) -> q a b", b=64) for i in range(2)]
    Cp = [carve(mixf, 512, 4352).rearrange("q (a b) -> q a b", b=32) for i in range(2)]
    Y3 = [carve(mixf, 512, 4352).rearrange("q (a b) -> q a b", b=32) for i in range(2)]
    _o[0] = 4 * 16 * W
    tmpY = [carve(xTf, 256, 8704).rearrange("q (a b) -> q a b", b=16) for i in range(4)]
    Cpad = carve(xTf, 128, 8704)
    pw_all = carve(xTf, 288, 8704).rearrange("q (k r a) -> q k r a", k=9, r=2)
    CpB_all = carve(xTf, 512, 8704).bitcast(BF16).rearrange("q (r a c) -> q r a c", r=2, a=16)
    W1st = tabAb[:, 0:8192]; W3st = tabAb[:, 8192:16384]; BDst = tabAb[:, 16384:20480]
    Est = tabE[:, 0:2 * 16 * W].rearrange("q (a b c) -> q a b c", a=2, b=16)
    Etmp = [xTf[:, i * 16 * W:(i + 1) * 16 * W].rearrange("q (a b) -> q a b", b=W) for i in range(4)]
    Y1all = hT[:].rearrange("q a b -> q (a b)")[:, 0:8192].rearrange("q (k r a c) -> q k r a c", k=8, r=2, a=16)
    rs = ['s5p', 'xT', 'hT', 'tabA', 'tabE'] + ['mixa%d' % i for i in range(4)] + ['mixb%d' % i for i in range(8)] + ['mixc%d' % i for i in range(4)]
    for l in range(depth):
        cnt_i = [0]

        def T_():
            cnt_i[0] += 1
            return sc[cnt_i[0] - 1]
        are, aim, dtl = T_(), T_(), T_()
        for m in range(2):
            dma('sp', are[64 * m:64 * m + 64, :], dap(a_re, l * 2048 + m * 64, [[1, 64], [128, 16]]), [], rs)
            dma('sp', aim[64 * m:64 * m + 64, :], dap(a_im, l * 2048 + m * 64, [[1, 64], [128, 16]]), [], rs)
            dma('sp', dtl[64 * m:64 * m + 64, :], dap(ldt, l * 32 + m, [[0, 64], [2, 16]]), [], rs)
            dma('sp', Bn[0][64 * m:64 * m + 64, :, :], dap(b_re, l * 32768 + m * 1024, [[16, 64], [2048, 16], [1, 16]]), [], rs)
            dma('sp', Bn[1][64 * m:64 * m + 64, :, :], dap(b_im, l * 32768 + m * 1024, [[16, 64], [2048, 16], [1, 16]]), [], rs)
        dma('sp', Cn[0][:], dap(c_re, l * 32768, [[64, 128], [8192, 4], [1, 64]]), [], rs)
        dma('sp', Cn[1][:], dap(c_im, l * 32768, [[64, 128], [8192, 4], [1, 64]]), [], rs)
        def poly(x, coef):
            t = T_()
            n = len(coef) - 1
            ts('dve', t[:], x, float(coef[n]), ALU.mult, rs, rs)
            for k in range(n - 1, 0, -1):
                stt(t[:], t[:], float(coef[k]), x, ALU.add, ALU.mult, rs, rs)
            ts('dve', t[:], t[:], float(coef[0]), ALU.add, rs, rs)
            return t

        def exp_poly(x, deg, nsq):
            xs_ = T_(); ts('dve', xs_[:], x, 1.0 / (2 ** nsq), ALU.mult, rs, rs)
            e = poly(xs_[:], [1.0 / math.factorial(k) for k in range(deg + 1)])
            for _ in range(nsq):
                tt('dve', e[:], e[:], e[:], ALU.mult, rs, rs)
            return e
        dt_ = exp_poly(dtl[:], 12, 3)
        ar = T_(); tt('dve', ar[:], are[:], dt_[:], ALU.mult, rs, rs)
        th = T_(); tt('dve', th[:], aim[:], dt_[:], ALU.mult, rs, rs)
        mag = exp_poly(ar[:], 7, 0)
        TWO_PI = 2.0 * math.pi
        MAGIC = 12582912.0
        u = T_(); ts('dve', u[:], th[:], 1.0 / TWO_PI, ALU.mult, rs, rs)
        n_ = T_(); ts('dve', n_[:], u[:], MAGIC, ALU.add, rs, rs)
        n2 = T_(); ts('dve', n2[:], n_[:], MAGIC, ALU.subtract, rs, rs)
        fr0 = T_(); tt('dve', fr0[:], u[:], n2[:], ALU.subtract, rs, rs)
        xq = T_(); ts('dve', xq[:], fr0[:], TWO_PI / 4.0, ALU.mult, rs, rs)
        x2 = T_(); tt('dve', x2[:], xq[:], xq[:], ALU.mult, rs, rs)
        ps_ = poly(x2[:], [(-1.0) ** k / math.factorial(2 * k + 1) for k in range(7)])
        sn = T_(); tt('dve', sn[:], ps_[:], xq[:], ALU.mult, rs, rs)
        cs = poly(x2[:], [(-1.0) ** k / math.factorial(2 * k) for k in range(8)])
        for _ in range(2):
            s2_ = T_(); tt('dve', s2_[:], sn[:], cs[:], ALU.mult, rs, rs)
            q2_ = T_(); tt('dve', q2_[:], sn[:], sn[:], ALU.mult, rs, rs)
            sn = T_(); ts('dve', sn[:], s2_[:], 2.0, ALU.mult, rs, rs)
            cs = T_(); ts('dve', cs[:], q2_[:], -2.0, ALU.mult, rs, rs, s2=1.0, op1=ALU.add)
        lr = T_(); tt('dve', lr[:], mag[:], cs[:], ALU.mult, rs, rs)
        li = T_(); tt('dve', li[:], mag[:], sn[:], ALU.mult, rs, rs)

        def cmul(ar_, ai_, br_, bi_, bc=None):
            t1, t2, t3, t4, orr, oi = T_(), T_(), T_(), T_(), T_(), T_()
            tt('dve', t1[:], ar_, br_, ALU.mult, rs, rs); tt('dve', t2[:], ai_, bi_, ALU.mult, rs, rs)
            tt('dve', orr[:], t1[:], t2[:], ALU.subtract, rs, rs)
            tt('dve', t3[:], ar_, bi_, ALU.mult, rs, rs); tt('dve', t4[:], ai_, br_, ALU.mult, rs, rs)
            tt('dve', oi[:], t3[:], t4[:], ALU.add, rs, rs)
            return orr, oi
        nr = T_(); ts('dve', nr[:], lr[:], -1.0, ALU.add, rs, rs)
        d1 = T_(); tt('dve', d1[:], are[:], are[:], ALU.mult, rs, rs)
        d2 = T_(); tt('dve', d2[:], aim[:], aim[:], ALU.mult, rs, rs)
        den = T_(); tt('dve', den[:], d1[:], d2[:], ALU.add, rs, rs)
        rden = T_(); P.add('dve', lambda e, o=rden, i=den: e.reciprocal(out=o[:], in_=i[:]), r=rs, w=rs)
        nai = T_(); ts('dve', nai[:], aim[:], -1.0, ALU.mult, rs, rs)
        f0r, f0i = cmul(nr[:], li[:], are[:], nai[:])
        fr_ = T_(); tt('dve', fr_[:], f0r[:], rden[:], ALU.mult, rs, rs)
        fi_ = T_(); tt('dve', fi_[:], f0i[:], rden[:], ALU.mult, rs, rs)
        for ri in range(2):
            for T in range(4):
                for m in range(2):
                    ts('dve', Cpad[:, 64 * m:64 * m + 64], Cn[ri][:, T, :], maskm[:, m:m + 1], ALU.mult, rs, rs)
                tr(ps3[:, 0:128], Cpad[:], ident[:], rs, ['ps3'])
                cp('dve', Cp[ri][:, 4 * T:4 * T + 4, :], ps3[:, 0:128].rearrange("q (a b) -> q a b", b=32), ['ps3'], rs)
        pw = pw_all
        memset('dve', pw[:, 0, 0, :], 1.0, rs); memset('dve', pw[:, 0, 1, :], 0.0, rs)
        cp('dve', pw[:, 1, 0, :], lr[:], rs, rs); cp('dve', pw[:, 1, 1, :], li[:], rs, rs)
        base_i = cnt_i[0]
        for k in range(2, 9):
            cnt_i[0] = base_i
            orr, oi = cmul(pw[:, k - 1, 0, :], pw[:, k - 1, 1, :], lr[:], li[:])
            cp('dve', pw[:, k, 0, :], orr[:], rs, rs); cp('dve', pw[:, k, 1, :], oi[:], rs, rs)
        W3v = W3st.rearrange("q (t a r c) -> q t a r c", t=8, a=16, r=2)
        for t in range(8):
            pr = pw[:, t + 1, 0, :].unsqueeze(2).to_broadcast([128, 16, 32])
            pi_ = pw[:, t + 1, 1, :].unsqueeze(2).to_broadcast([128, 16, 32])
            tt('dve', Y3[0][:], Cp[0][:], pr, ALU.mult, rs, rs); tt('dve', Y3[1][:], Cp[1][:], pi_, ALU.mult, rs, rs)
            tt('dve', W3v[:, t, :, 0, :], Y3[0][:], Y3[1][:], ALU.subtract, rs, rs)
            tt('dve', Y3[0][:], Cp[0][:], pi_, ALU.mult, rs, rs); tt('dve', Y3[1][:], Cp[1][:], pr, ALU.mult, rs, rs)
            tt('dve', Y3[0][:], Y3[0][:], Y3[1][:], ALU.add, rs, rs)
            ts('dve', W3v[:, t, :, 1, :], Y3[0][:], -1.0, ALU.mult, rs, rs)
        memset('dve', hT[:].rearrange("q a b -> q (a b)")[:, 0:8192], 0.0, rs)
        for k in range(8):
            cnt_i[0] = base_i
            Fr, Fi = cmul(pw[:, k, 0, :], pw[:, k, 1, :], fr_[:], fi_[:])
            Frb = Fr[:].unsqueeze(2).to_broadcast([128, 16, 16]); Fib = Fi[:].unsqueeze(2).to_broadcast([128, 16, 16])
            tt('dve', tmpY[0][:], Bn[0][:], Frb, ALU.mult, rs, rs); tt('dve', tmpY[1][:], Bn[1][:], Fib, ALU.mult, rs, rs)
            tt('dve', tmpY[2][:], Bn[1][:], Frb, ALU.mult, rs, rs); tt('dve', tmpY[3][:], Bn[0][:], Fib, ALU.mult, rs, rs)
            for m in range(2):
                sl = slice(64 * m, 64 * m + 64)
                tt('dve', Y1all[sl, k, 0, :, 16 * m:16 * m + 16], tmpY[0][sl], tmpY[1][sl], ALU.subtract, rs, rs)
                tt('dve', Y1all[sl, k, 1, :, 16 * m:16 * m + 16], tmpY[2][sl], tmpY[3][sl], ALU.add, rs, rs)
        W1v = W1st.rearrange("q (s t r c) -> q s t r c", s=8, t=4, r=2)
        for s in range(8):
            k = 7 - s
            for T in range(4):
                for ri in range(2):
                    for r4 in range(4):
                        tr(psT[32 * r4:32 * r4 + 32, 0:128], Y1all[:, k, ri, 4 * T + r4, :], identb[:], rs, ['psT'], tile_position=(0, 32 * r4))
                    cp('dve', W1v[:, s, T, ri, :], psT[:, 0:128], ['psT'], rs)
        memset('dve', BDst, 0.0, rs)
        BDv = BDst.rearrange("q (t a c) -> q t a c", t=4, a=8)
        CpB = CpB_all
        cp('dve', CpB[:, 0], Cp[0][:], rs, rs); ts('dve', CpB[:, 1], Cp[1][:], -1.0, ALU.mult, rs, rs)
        for T in range(4):
            for tau in range(8):
                for r4 in range(4):
                    pi_i = 4 * T + r4
                    for ri in range(2):
                        mm(ps3[32 * r4:32 * r4 + 32, tau * 32:tau * 32 + 32], Y1all[:, tau, ri, pi_i, :], CpB[:, ri, pi_i, :],
                           ri == 0, ri == 1, rs, ['ps3'], tile_position=(0, 32 * r4))
            for r4 in range(4):
                sl = slice(32 * r4, 32 * r4 + 32)
                cp('dve', BDv[sl, T, :, 32 * r4:32 * r4 + 32], ps3[sl, 0:256].rearrange("q (a b) -> q a b", b=32), ['ps3'], rs)
        cnt_i[0] = base_i + 6
        r2 = T_(); tt('dve', r2[:], mag[:], mag[:], ALU.mult, rs, rs)
        r4_ = T_(); tt('dve', r4_[:], r2[:], r2[:], ALU.mult, rs, rs)
        rho = T_(); tt('dve', rho[:], r4_[:], r4_[:], ALU.mult, rs, rs)
        ur, ui = cs, sn
        for _ in range(3):
            a2 = T_(); tt('dve', a2[:], ur[:], ur[:], ALU.mult, rs, rs)
            b2 = T_(); tt('dve', b2[:], ui[:], ui[:], ALU.mult, rs, rs)
            ab = T_(); tt('dve', ab[:], ur[:], ui[:], ALU.mult, rs, rs)
            ur = T_(); tt('dve', ur[:], a2[:], b2[:], ALU.subtract, rs, rs)
            ui = T_(); ts('dve', ui[:], ab[:], 2.0, ALU.mult, rs, rs)
        cp('dve', Est[:, 0, :, 0], ur[:], rs, rs)
        cp('dve', Est[:, 1, :, 0], ui[:], rs, rs)
        k = 1
        while k < NJ:
            n = min(k, NJ - k)
            br_ = Est[:, 0, :, k - 1:k].to_broadcast([128, 16, n]); bi_ = Est[:, 1, :, k - 1:k].to_broadcast([128, 16, n])
            xr = Est[:, 0, :, 0:n]; xi = Est[:, 1, :, 0:n]
            tt('dve', Etmp[0][:, :, 0:n], xr, br_, ALU.mult, rs, rs); tt('dve', Etmp[1][:, :, 0:n], xi, bi_, ALU.mult, rs, rs)
            tt('dve', Etmp[2][:, :, 0:n], xr, bi_, ALU.mult, rs, rs); tt('dve', Etmp[3][:, :, 0:n], xi, br_, ALU.mult, rs, rs)
            tt('dve', Est[:, 0, :, k:k + n], Etmp[0][:, :, 0:n], Etmp[1][:, :, 0:n], ALU.subtract, rs, rs)
            tt('dve', Est[:, 1, :, k:k + n], Etmp[2][:, :, 0:n], Etmp[3][:, :, 0:n], ALU.add, rs, rs)
            k += n
        for ri in range(2):
            cp('dve', Est[:, ri, :, NJ:NJ + 4], Est[:, ri, :, 0:1].to_broadcast([128, 16, 4]), rs, rs)
        cp('dve', tabE[:, 32 * W:32 * W + 16], rho[:], rs, rs)
        dma('sp', dap(s5tab, l * 128 * TABW, [[TABW, 128], [1, TBD + 2048]]), tabA[:], rs, ['s5tab'] + rs)
        dma('sp', dap(s5tab, l * 128 * TABW + TER, [[TABW, 128], [1, TABW - TER]]), tabE[:], rs, ['s5tab'] + rs)

    wbi = [0]
    blkc = [0]
    cur_seg = [0]

    def proj(l, wsrc, pieces_fn, N, evac, hsrc, rkeys):
        i = wbi[0] % NWB
        wbi[0] += 1
        w = wb[i]
        wk = 'wb%d' % i
        blk = l * 56 + blkc[0]
        blkc[0] += 1
        ckey = 'wc%d' % blk
        wflat = w[:].rearrange("q a b -> q (a b)")
        if cur_seg[0] == 0:
            pieces_fn(w, wk)
            if nseg > 1:
                dma('sp', dap(wcache, blk * 128 * KD * 128, [[KD * 128, 128], [1, KD * 128]]), wflat, [wk], [ckey])
        else:
            dma('sp', wflat, dap(wcache, blk * 128 * KD * 128, [[KD * 128, 128], [1, KD * 128]]), [ckey], [wk])
        pm = ps0 if (wbi[0] % 2 == 0) else ps1
        pmk = 'ps0' if (wbi[0] % 2 == 0) else 'ps1'
        for k in range(KD):
            mm(pm[:, 0:TSEG], w[:, k, :], hsrc[:, k, 0:TSEG], k == 0, k == KD - 1, [wk] + rkeys, [pmk])
            if N > TSEG:
                mm(ps2[:, 0:NS], w[:, k, :], hsrc[:, k, TSEG:N], k == 0, k == KD - 1, [wk] + rkeys, ['ps2'])
        evac(pm, pmk)

    def in_pieces(l, col_pieces):
        def f(w, wk):
            off = 0
            for c0, wd in col_pieces:
                dma('pool', w[:, :, off:off + wd], dap(w_in, l * D * DIN + c0, [[DIN, 128], [128 * DIN, KD], [1, wd]]), [], [wk])
                off += wd
        return f

    def qcols(base, tq):
        kt, i = tq // 4, tq % 4
        return [(base + 64 * (8 * kt + i), 64), (base + 64 * (8 * kt + 4 + i), 64)]

    def out_pieces(l, j):
        def f(w, wk):
            base = l * D * D + j * 128
            dma('pool', w[:, 0:4, :], dap(w_out, base, [[D, 128], [128 * D, 4], [1, 128]]), [], [wk])
            dma('pool', w[:, 12:16, :], dap(w_out, base + 1536 * D, [[D, 128], [128 * D, 4], [1, 128]]), [], [wk])
            for half in range(2):
                for kt in range(2):
                    dma('pool', w[64 * half:64 * half + 64, 4 + 4 * kt:8 + 4 * kt, :],
                        dap(w_out, base + (512 + 512 * kt + 256 * half) * D, [[D, 64], [64 * D, 4], [1, 128]]), [], [wk])
        return f

    def rmsnorm(N, gidx, out_fn):
        for k in range(KD):
            s = sq[k % 2]
            act(s[:, 0:N], xT[:, k, 0:N], AF.Square, ['xT'], ['sq%d' % (k % 2)])
            mm(ps0[:, 0:TSEG], onesb[:], s[:, 0:TSEG], k == 0, k == KD - 1, ['sq%d' % (k % 2)], ['ps0'])
            if N > TSEG:
                mm(ps2[:, 0:NS], onesb[:], s[:, TSEG:N], k == 0, k == KD - 1, ['sq%d' % (k % 2)], ['ps2'])
        act(rstd[:, 0:TSEG], ps0[:, 0:TSEG], AF.Sqrt, ['ps0'], ['rstd'], scale=1.0 / D, bias=epsc[:])
        if N > TSEG:
            act(rstd[:, TSEG:N], ps2[:, 0:NS], AF.Sqrt, ['ps2'], ['rstd'], scale=1.0 / D, bias=epsc[:])
        P.add('dve', lambda e: e.reciprocal(out=rstd[:, 0:N], in_=rstd[:, 0:N]), r=['rstd'], w=['rstd'])
        for k in range(KD):
            out_fn(k, gcol[:, gidx * 16 + k:gidx * 16 + k + 1])

    def softmax(src3, np_, nh, Wd, sinkcols, Pf_, Pb_, rk, tag):
        mx = sm[0:np_, 0:nh]; m_ = sm[0:np_, 4:4 + nh]; ng_ = sm[0:np_, 8:8 + nh]; ssum = sm[0:np_, 12:12 + nh]
        dd = sm[0:np_, 16:16 + nh]; es = sm[0:np_, 20:20 + nh]; den = sm[0:np_, 24:24 + nh]; rd = sm[0:np_, 28:28 + nh]
        P.add('dve', lambda e: e.tensor_reduce(out=mx, in_=src3, axis=AX.X, op=ALU.max), r=rk, w=['sm'])
        ts('dve', m_, mx, 0.125, ALU.mult, ['sm'], ['sm'])
        tt('dve', m_, m_, sinkcols, ALU.max, ['sm', 'sinkbc'], ['sm'])
        ts('dve', ng_, m_, -1.0, ALU.mult, ['sm'], ['sm'])
        for i in range(nh):
            act(Pf_[:, i, :], src3[:, i, :], AF.Exp, rk + ['sm'], [tag + 'Pf', 'glwb', 'sm2%d' % i], scale=0.125, bias=ng_[:, i:i + 1],
                accum=ssum[:, i:i + 1])
        tt('dve', dd, sinkcols, m_, ALU.subtract, ['sm', 'sinkbc'], ['sm'])
        act(es, dd, AF.Exp, ['sm'], ['sm'])
        tt('dve', den, ssum, es, ALU.add, ['sm'] + ['sm2%d' % i for i in range(nh)], ['sm'])
        P.add('dve', lambda e: e.reciprocal(out=rd, in_=den), r=['sm'], w=['sm'])
        tt('pool', Pb_, Pf_, rd.unsqueeze(2).to_broadcast([np_, nh, Wd]), ALU.mult, [tag + 'Pf', 'sm'], [tag + 'Pb', 'glwb'])

    evi = [0]
    for seg in range(nseg):
        N = TSEG + (NS if seg == 0 else 0)
        cur_seg[0] = seg
        dma('sp', cosT[:, 0:TSEG], dap(c_cos, seg * TSEG, [[seqlen + NS, 128], [1, TSEG]]), [], ['rope'])
        dma('sp', sinT[:, 0:TSEG], dap(c_sin, seg * TSEG, [[seqlen + NS, 128], [1, TSEG]]), [], ['rope'])
        if seg == 0:
            dma('sp', cosT[:, TSEG:N], dap(c_cos, seqlen, [[seqlen + NS, 128], [1, NS]]), [], ['rope'])
            dma('sp', sinT[:, TSEG:N], dap(c_sin, seqlen, [[seqlen + NS, 128], [1, NS]]), [], ['rope'])
        blocks = [(b, 128, dap(xp, (seg * TSEG + b * 128) * D, [[D, 128], [1, D]]), b * 128) for b in range(NB)]
        if seg == 0:
            blocks.append((NB, NS, xs.ap(), TSEG))
        for bi, (b, rows, src, c0) in enumerate(blocks):
            io = iost[bi % 2]; iok = 'iost0'
            dma('sp', io[0:rows, :], src, [], [iok])
            for k4 in range(4):
                for kk in range(4):
                    k = 4 * k4 + kk
                    tr(ps3[:, kk * 128:kk * 128 + rows], io[0:rows, k * 128:(k + 1) * 128], ident[0:rows, 0:rows], [iok], ['ps3'])
                cp('dve' if k4 % 2 == 0 else 'act', xT[:, 4 * k4:4 * k4 + 4, c0:c0 + rows],
                   ps3[:, :].rearrange("q (a b) -> q a b", b=128)[:, :, 0:rows], ['ps3'], ['xT'])

        for l in range(depth):
            blkc[0] = 0
            dma('sp', tabA[:], dap(s5tab, l * 128 * TABW, [[TABW, 128], [1, TBD + 2048]]), ['s5tab'], ['tabA'])
            dma('sp', tabE[:], dap(s5tab, l * 128 * TABW + TER, [[TABW, 128], [1, TABW - TER]]), ['s5tab'], ['tabE'])
            dma('sp', bsrow[:], dap(cbs, l * 512, [[0, 1], [1, 512]]), [], ['bsrow'])
            if seg == 0:
                for b in range(4):
                    cp('pool', bsS[:, :, 8 * b:8 * b + 8], bsrow[:].rearrange("o (g t) -> o g t", t=128)[:, :, 0:8], ['bsrow'], ['bsS'])
            dma('pool', glwb[:], dap(glw, l * 512 * 512, [[512, 128], [128 * 512, 4], [1, 512]]), [], ['glwb', 'pPf', 'pPb'])
            W1 = tabAb[:, 0:8192].rearrange("q (s t r c) -> q s t r c", s=8, t=4, r=2)
            W3 = tabAb[:, 8192:16384].rearrange("q (t a r c) -> q t a r c", t=8, a=16, r=2)
            BD = tabAb[:, 16384:20480].rearrange("q (t a c) -> q t a c", t=4, a=8)
            Er = tabE[:, 0:16 * W].rearrange("q (a b) -> q a b", b=W)
            Ei = tabE[:, 16 * W:32 * W].rearrange("q (a b) -> q a b", b=W)
            rhoc = tabE[:, 32 * W:32 * W + 16]
            NJJ = N // 8
            rmsnorm(N, l, lambda k, g: stt(hT[:, k, 0:N], xT[:, k, 0:N], g, rstd[:, 0:N], ALU.mult, ALU.mult,
                                           ['xT', 'rstd', 'gcol'], ['hT']))

            def evac_store(dst_fn, wkey):
                def f(pm, pmk):
                    evi[0] += 1
                    eng = 'act' if evi[0] % 2 == 0 else 'dve'
                    cp(eng, dst_fn(0, TSEG), pm[:, 0:TSEG], [pmk], [wkey])
                    if N > TSEG:
                        cp(eng, dst_fn(TSEG, N), ps2[:, 0:NS], ['ps2'], [wkey])
                return f

            def evac_gate(tile, wkey):
                def f(pm, pmk):
                    evi[0] += 1
                    t = tA[evi[0] % 2]; tk = 'tA%d' % (evi[0] % 2)
                    act(t[:, 0:TSEG], pm[:, 0:TSEG], AF.Silu, [pmk], [tk])
                    if N > TSEG:
                        act(t[:, TSEG:N], ps2[:, 0:NS], AF.Silu, ['ps2'], [tk])
                    tt('dve', mix[:, tile, 0:N], mix[:, tile, 0:N], t[:, 0:N], ALU.mult, [tk, wkey], [wkey])
                return f

            for T in range(4):
                proj(l, None, in_pieces(l, [(OFF_UC + 128 * T, 128)]), N,
                     evac_store(lambda a, b, T=T: mix[:, 12 + T, a:b], 'mixc%d' % T), hT, ['hT'])
            if seg == 0:
                for pi_i in range(16):
                    for ri in range(2):
                        cp('pool', Sprev[:, pi_i, ri, NJ:NJ + 4], s_in[:, l, :, ri, pi_i], ['s_in'], ['Sprev%d' % pi_i])
            for pi_i in range(16):
                T, r4 = pi_i // 4, pi_i % 4
                pd = ps3 if pi_i % 2 == 0 else ps6
                pdk = 'ps3' if pi_i % 2 == 0 else 'ps6'
                uview = mix[32 * r4:32 * r4 + 32, 12 + T, 0:N].rearrange("q (j s) -> q j s", s=8)
                for ri in range(2):
                    for s in range(8):
                        mm(pd[:, ri * 256:ri * 256 + NJJ], W1[32 * r4:32 * r4 + 32, s, T, ri, :], uview[:, :, s], s == 0, s == 7,
                           ['tabA', 'mixc%d' % T], [pdk], tile_position=(32 * r4, 0))
                Dre = pd[:, 0:NJJ]; Dim = pd[:, 256:256 + NJJ]
                er = Er[:, pi_i, 0:NJJ]; ei = Ei[:, pi_i, 0:NJJ]
                w_ = [x[:, 0:NJJ] for x in s5w]
                k5 = ['s5w']
                tt('dve', w_[0], Dre, er, ALU.mult, [pdk, 'tabE'], k5); tt('dve', w_[1], Dim, ei, ALU.mult, [pdk, 'tabE'], k5)
                tt('pool', w_[2], w_[0], w_[1], ALU.add, k5, k5)
                tt('dve', w_[0], Dim, er, ALU.mult, [pdk, 'tabE'] + k5, k5); tt('dve', w_[1], Dre, ei, ALU.mult, [pdk, 'tabE'], k5)
                tt('pool', w_[3], w_[0], w_[1], ALU.subtract, k5, k5)
                rb = rhoc[:, pi_i:pi_i + 1]
                for ri, (cc, qq) in enumerate(((w_[2], w_[4]), (w_[3], w_[5]))):
                    P.add('dve', lambda e, cc=cc, qq=qq, rbb=rb.to_broadcast([128, NJ]), ini=hprev[:, l, pi_i, ri:ri + 1]:
                          e.tensor_tensor_scan(out=qq[:, 0:NJ], data0=rbb, data1=cc[:, 0:NJ], initial=ini,
                                               op0=ALU.mult, op1=ALU.add), r=k5 + ['hprev', 'tabE'], w=k5)
                    if seg == 0:
                        stt(qq[:, NJ:NJJ], s_in[:, l, :, ri, pi_i], rb, cc[:, NJ:NJJ], ALU.mult, ALU.add, k5 + ['s_in', 'tabE'], k5)
                tt('dve', w_[6], w_[4], er, ALU.mult, k5 + ['tabE'], k5); tt('dve', w_[7], w_[5], ei, ALU.mult, k5 + ['tabE'], k5)
                tt('pool', w_[8], w_[6], w_[7], ALU.subtract, k5, k5)
                tt('dve', w_[6], w_[5], er, ALU.mult, k5 + ['tabE'], k5); tt('dve', w_[7], w_[4], ei, ALU.mult, k5 + ['tabE'], k5)
                tt('pool', w_[9], w_[6], w_[7], ALU.add, k5, k5)
                sk = 'Sprev%d' % pi_i
                for ri, S_ in enumerate((w_[8], w_[9])):
                    cp('act', Sprev[:, pi_i, ri, 0:1], hprev[:, l, pi_i, ri:ri + 1], ['hprev'], [sk])
                    cp('act', Sprev[:, pi_i, ri, 1:NJ], S_[:, 0:NJ - 1], k5, [sk])
                    cp('dve', hprev[:, l, pi_i, ri:ri + 1], S_[:, NJ - 1:NJ], k5 + [sk], ['hprev'])
                    if seg == 0:
                        cp('pool', outst[:, 0:128].rearrange("q (b r a) -> q b r a", b=4, r=2)[:, :, ri, pi_i], S_[:, NJ:NJJ],
                           k5, ['outst_s'])
            if seg == 0:
                for b in range(4):
                    for ri, dst in enumerate((hr_s, hi_s)):
                        tr(ps3[0:16, 0:128], outst[:, (b * 2 + ri) * 16:(b * 2 + ri) * 16 + 16], ident[:], ['outst_s'], ['ps3'])
                        cp('dve', outst[0:16, 128:256], ps3[0:16, 0:128], ['ps3'], ['outst_t'])
                        dma('sp', dap(dst, ((l * 4 + b) * 16) * 128, [[128, 16], [1, 128]]), outst[0:16, 128:256], ['outst_t'], ['out'])
            if seg == nseg - 1:
                for ri, dst in enumerate((hr_p, hi_p)):
                    cp('dve', outst[:, 256 + 16 * ri:272 + 16 * ri], hprev[:, l, :, ri], ['hprev'], ['outst_h'])
                    tr(ps3[0:16, 0:128], outst[:, 256 + 16 * ri:272 + 16 * ri], ident[:], ['outst_h'], ['ps3'])
                    cp('dve', outst[0:16, 288:416], ps3[0:16, 0:128], ['ps3'], ['outst_t2'])
                    dma('sp', dap(dst, l * 2048, [[128, 16], [1, 128]]), outst[0:16, 288:416], ['outst_t2'], ['out'])
            for T in range(4):
                uvw = mix[:, 12 + T, 0:N].rearrange("q (j s) -> q j s", s=8)
                for t in range(8):
                    reg = psS[:, (t // 4) * 512 + (t % 4) * W:(t // 4) * 512 + (t % 4) * W + NJJ]
                    for s in range(t + 1):
                        mm(reg, BD[:, T, t - s, :], uvw[:, :, s], s == 0, False, ['tabA', 'mixc%d' % T], ['psS'])
                    for r4 in range(4):
                        pi_i = 4 * T + r4
                        for ri in range(2):
                            o_ = psS[32 * r4:32 * r4 + 32, (t // 4) * 512 + (t % 4) * W:(t // 4) * 512 + (t % 4) * W + NJJ]
                            mm(o_, W3[:, t, pi_i, ri, :], Sprev[:, pi_i, ri, 0:NJJ], False, (ri == 1),
                               ['tabA', 'Sprev%d' % pi_i], ['psS'], tile_position=(0, 32 * r4))
                yv = tA[2][:, 0:N].rearrange("q (j s) -> q s j", s=8)
                for hb in range(2):
                    pv = psS[:, hb * 512:hb * 512 + 4 * W].rearrange("q (t j) -> q t j", j=W)[:, :, 0:NJJ]
                    uv2 = mix[:, 12 + T, 0:N].rearrange("q (j s) -> q s j", s=8)[:, 4 * hb:4 * hb + 4, :]
                    stt(yv[:, 4 * hb:4 * hb + 4, :], uv2, dcol[:, l * 4 + T:l * 4 + T + 1], pv, ALU.mult, ALU.add,
                        ['psS', 'mixc%d' % T, 'dcol'], ['tA2'])
                y_ = tA[2][:, 0:N]; g1 = tA[3][:, 0:N]
                tt('pool', g1, y_, y_, ALU.mult, ['tA2'], ['tA3'])
                ts('dve', g1, g1, 0.044715, ALU.mult, ['tA3'], ['tA3'], s2=1.0, op1=ALU.add)
                tt('pool', g1, g1, y_, ALU.mult, ['tA3', 'tA2'], ['tA3'])
                act(g1, g1, AF.Sigmoid, ['tA3'], ['tA3'], scale=2.0 * math.sqrt(2.0 / math.pi))
                tt('dve', yb[:, T, 0:N], y_, g1, ALU.mult, ['tA2', 'tA3'], ['yb'])
            for T in range(4):
                for kc in range(4):
                    mm(ps0[:, 0:TSEG], glwb[:, kc, 128 * T:128 * T + 128], yb[:, kc, 0:TSEG], kc == 0, kc == 3, ['glwb', 'yb'], ['ps0'])
                    if N > TSEG:
                        mm(ps2[:, 0:NS], glwb[:, kc, 128 * T:128 * T + 128], yb[:, kc, TSEG:N], kc == 0, kc == 3, ['glwb', 'yb'], ['ps2'])
                g2 = tA[3]
                act(g2[:, 0:TSEG], ps0[:, 0:TSEG], AF.Sigmoid, ['ps0'], ['tA3'], bias=gbcol[:, l * 4 + T:l * 4 + T + 1])
                if N > TSEG:
                    act(g2[:, TSEG:N], ps2[:, 0:NS], AF.Sigmoid, ['ps2'], ['tA3'], bias=gbcol[:, l * 4 + T:l * 4 + T + 1])
                tt('dve', mix[:, 12 + T, 0:N], yb[:, T, 0:N], g2[:, 0:N], ALU.mult, ['yb', 'tA3'], ['mixc%d' % T])
            for T in range(4):
                proj(l, None, in_pieces(l, [(OFF_GC + 128 * T, 128)]), N, evac_gate(12 + T, 'mixc%d' % T), hT, ['hT'])

            def evac_rope(dst_fn, wkey):
                def f(pm, pmk):
                    evi[0] += 1
                    i2 = evi[0] % 2
                    xf = tA[i2]; xfk = 'tA%d' % i2
                    xb_ = sq[i2]; xbk = 'sq%d' % i2
                    pw_ = ps3 if i2 == 0 else ps6
                    pwk = 'ps3' if i2 == 0 else 'ps6'
                    cp('act', xb_[:, 0:TSEG], pm[:, 0:TSEG], [pmk], [xbk])
                    if N > TSEG:
                        cp('act', xb_[:, TSEG:N], ps2[:, 0:NS], ['ps2'], [xbk])
                    mm(pw_[:, 0:TSEG], permb[:], xb_[:, 0:TSEG], True, True, [xbk, 'ident'], [pwk])
                    tt('dve', xf[:, 0:TSEG], pw_[:, 0:TSEG], sinT[:, 0:TSEG], ALU.mult, [pwk, 'rope'], [xfk])
                    if N > TSEG:
                        mm(pw_[:, 0:NS], permb[:], xb_[:, TSEG:N], True, True, [xbk, 'ident', xfk], [pwk])
                        tt('dve', xf[:, TSEG:N], pw_[:, 0:NS], sinT[:, TSEG:N], ALU.mult, [pwk, 'rope'], [xfk])
                    tt('pool', xw0[:, 0:N], xb_[:, 0:N], cosT[:, 0:N], ALU.mult, [xbk, 'rope'], ['xsw0'])
                    tt('dve', dst_fn(0, N), xf[:, 0:N], xw0[:, 0:N], ALU.add, [xfk, 'xsw0'], [wkey])
                return f

            for kt in range(2):
                cp('pool', kT[:, kt, 0:128], khalo[:, l, kt, :], ['khalo'], ['kT'])
            cp('pool', Vt[:, 0, :], vhalo[:, l, :], ['vhalo'], ['Vt'])
            for kt in range(2):
                proj(l, None, in_pieces(l, [(OFF_K + 128 * kt, 128)]), N,
                     evac_rope(lambda a, b, kt=kt: kT[:, kt, 128 + a:128 + b], 'kT'), hT, ['hT'])
            for kt in range(2):
                proj(l, None, in_pieces(l, [(OFF_V + 128 * kt, 128)]), N,
                     evac_store(lambda a, b: tB[0][:, a:b], 'tB0'), hT, ['hT'])
                nblk = NB + (1 if seg == 0 else 0)
                for b in range(nblk):
                    rows = 128 if b < NB else NS
                    tr(psT[0:rows, b * 128:b * 128 + 128], tB[0][:, b * 128:b * 128 + rows], identb[:], ['tB0'], ['psT'])
                cp('dve', Vt[:, 1:1 + NB, 128 * kt:128 * kt + 128], psT[:, 0:NB * 128].rearrange("q (a b) -> q a b", b=128), ['psT'], ['Vt'])
                if seg == 0:
                    cp('dve', Vt[0:NS, NB + 1, 128 * kt:128 * kt + 128], psT[0:NS, NB * 128:NB * 128 + 128], ['psT'], ['Vt'])
            for tq in range(8):
                proj(l, None, in_pieces(l, qcols(OFF_Q, tq)), N,
                     evac_rope(lambda a, b, tq=tq: mix[:, 4 + tq, a:b], 'mixb%d' % tq), hT, ['hT'])
            for kt in range(2):
                cp('pool', khalo[:, l, kt, :], kT[:, kt, TSEG:TSEG + 128], ['kT'], ['khalo'])
            cp('pool', vhalo[:, l, :], Vt[:, NB, :], ['Vt'], ['vhalo'])
            if seg == nseg - 1:
                for kt in range(2):
                    tr(psT[:, 0:128], kT[:, kt, TSEG:TSEG + 128], identb[:], ['kT'], ['psT'])
                    cp('dve', PTs[:, 128 * kt:128 * kt + 128], psT[:, 0:128], ['psT'], ['PTs'])
                dma('pool', dap(nk_p, l * 128 * 256, [[256, 128], [1, 256]]), PTs[:, 0:256], ['PTs'], ['out'])
                dma('pool', dap(nv_p, l * 128 * 256, [[256, 128], [1, 256]]), Vt[:, NB, :], ['Vt'], ['out'])
            if seg == 0:
                for kt in range(2):
                    tr(psT[0:NS, 0:128], kT[:, kt, 128 + TSEG:128 + N], identb[:], ['kT'], ['psT'])
                    cp('dve', PTs[0:NS, 128 * kt:128 * kt + 128], psT[0:NS, 0:128], ['psT'], ['PTs'])
                dma('pool', dap(nk_s, l * NS * 256, [[256, NS], [1, 256]]), PTs[0:NS, 0:256], ['PTs'], ['out'])
                dma('pool', dap(nv_s, l * NS * 256, [[256, NS], [1, 256]]), Vt[0:NS, NB + 1, :], ['Vt'], ['out'])
            for bq in range(NB):
                msk = maskF if (seg == 0 and bq == 0) else maskA
                for kt in range(2):
                    for half in range(2):
                        hs = slice(64 * half, 64 * half + 64)
                        for i2 in range(2):
                            for i in (2 * i2, 2 * i2 + 1):
                                tq = 4 * kt + i
                                mm(psS[:, i * 256:(i + 1) * 256], mix[hs, 4 + tq, 128 * bq:128 * bq + 128],
                                   kT[hs, kt, 128 * bq:128 * bq + 256], i % 2 == 0, False, ['mixb%d' % tq, 'kT'], ['psS'])
                            mm(psS[:, i2 * 512:(i2 + 1) * 512], identb[:], msk[:], False, True, ['masks', 'ident'], ['psS'])
                        h0 = l * 16 + 8 * kt + 4 * half
                        softmax(psS[:, :].rearrange("q (a b) -> q a b", b=256), 128, 4, 256, sinkbc[:, h0:h0 + 4], Pf, Pb, ['psS'], 'p')
                        for i in range(4):
                            for kb in range(2):
                                tr(psT[:, (2 * i + kb) * 128:(2 * i + kb) * 128 + 128], Pb[:, i, kb * 128:kb * 128 + 128], identb[:], ['pPb'], ['psT'])
                        cp('act', PTs[:], psT[:], ['psT'], ['PTs'])
                        for i in range(4):
                            for kb in range(2):
                                mm(ps6[hs, i * 128:i * 128 + 128], Vt[:, bq + kb, (2 * kt + half) * 64:(2 * kt + half) * 64 + 64],
                                   PTs[:, (2 * i + kb) * 128:(2 * i + kb) * 128 + 128], kb == 0, kb == 1, ['Vt', 'PTs'], ['ps6'],
                                   tile_position=(0, 64 * half))
                    for i in range(4):
                        cp('dve', mix[:, 4 + 4 * kt + i, 128 * bq:128 * bq + 128], ps6[:, i * 128:i * 128 + 128],
                           ['ps6'], ['mixb%d' % (4 * kt + i)])
            if seg == 0:
                dma('pool', ckb, dap(ck, l * 4 * 128 * 256, [[256, 128], [128 * 256, 4], [1, 256]]), [], ['iost0'])
                dma('pool', cvb, dap(cv, l * 4 * 128 * 256, [[256, 128], [128 * 256, 4], [1, 256]]), [], ['iost0'])
                for b in range(4):
                    for kt in range(2):
                        tr(psT[:, (2 * b + kt) * 128:(2 * b + kt) * 128 + 128], ckb[:, b, kt * 128:kt * 128 + 128], identb[:], ['iost0'], ['psT'])
                cp('dve', ckT, psT[:].rearrange("q (b k w) -> q b k w", b=4, k=2), ['psT'], ['iost0'])
                for kt in range(2):
                    for half in range(2):
                        hs = slice(64 * half, 64 * half + 64)
                        for i in range(4):
                            tq = 4 * kt + i
                            qs = mix[hs, 4 + tq, TSEG:N]
                            for b in range(4):
                                mm(psS[0:NS, b * 128:b * 128 + 128], qs, ckT[hs, b, kt, :], b == 0, False, ['mixb%d' % tq, 'iost0'], ['psS'])
                            mm(psS[0:NS, 0:512], identb[0:NS, 0:NS], maskS[:, 0:512], False, True, ['masks', 'ident'], ['psS'])
                            mm(psS[0:NS, 512:512 + NS], qs, kT[hs, kt, 128 + TSEG:128 + N], True, False, ['mixb%d' % tq, 'kT'], ['psS'])
                            mm(psS[0:NS, 512:512 + NS], identb[0:NS, 0:NS], maskS[:, 512:544], False, True, ['masks', 'ident'], ['psS'])
                            h0 = l * 16 + 8 * kt + 4 * half + i
                            softmax(psS[0:NS, 0:544].rearrange("q (a b) -> q a b", b=544), NS, 1, 544, sinkbc[0:NS, h0:h0 + 1],
                                    PfS.rearrange("q (a b) -> q a b", b=544), PbS.rearrange("q (a b) -> q a b", b=544), ['psS'], 'p')
                            for b in range(4):
                                tr(psT[:, b * 32:b * 32 + 32], PbS[:, b * 128:b * 128 + 128], identb[0:NS, 0:NS], ['pPb'], ['psT'])
                            tr(psT[0:NS, 128:160], PbS[:, 512:544], identb[0:NS, 0:NS], ['pPb'], ['psT'])
                            cp('act', PTS2[:, 0:4, :], psT[:, 0:128].rearrange("q (a b) -> q a b", b=32), ['psT'], ['PTS2'])
                            cp('act', PTS2[0:NS, 4, :], psT[0:NS, 128:160], ['psT'], ['PTS2'])
                            vc = (2 * kt + half) * 64
                            for b in range(4):
                                mm(ps6[hs, i * 32:i * 32 + 32], cvb[:, b, vc:vc + 64], PTS2[:, b, :], b == 0, False, ['iost0', 'PTS2'], ['ps6'],
                                   tile_position=(0, 64 * half))
                            mm(ps6[hs, i * 32:i * 32 + 32], Vt[0:NS, NB + 1, vc:vc + 64], PTS2[0:NS, 4, :], False, True, ['Vt', 'PTS2'], ['ps6'],
                               tile_position=(0, 64 * half))
                    for i in range(4):
                        cp('act' if i % 2 else 'dve', mix[:, 4 + 4 * kt + i, TSEG:N], ps6[:, i * 32:i * 32 + 32], ['ps6'], ['mixb%d' % (4 * kt + i)])
            for tq in range(8):
                proj(l, None, in_pieces(l, qcols(OFF_GB, tq)), N, evac_gate(4 + tq, 'mixb%d' % tq), hT, ['hT'])

            for T in range(4):
                proj(l, None, in_pieces(l, [(OFF_UA + 128 * T, 128)]), N,
                     evac_store(lambda a, b, T=T: mix[:, T, a:b], 'mixa%d' % T), hT, ['hT'])
            for T in range(4):
                proj(l, None, in_pieces(l, [(OFF_VA + 128 * T, 128)]), N,
                     evac_store(lambda a, b: tB[1][:, a:b], 'tB0'), hT, ['hT'])
                nblk = NB + (1 if seg == 0 else 0)
                for b in range(nblk):
                    rows = 128 if b < NB else NS
                    tr(psT[0:rows, b * 128:b * 128 + 128], tB[1][:, b * 128:b * 128 + rows], identb[:], ['tB0'], ['psT'])
                cp('dve', vatm[:, 0:NB, 128 * T:128 * T + 128], psT[:, 0:NB * 128].rearrange("q (a b) -> q a b", b=128), ['psT'], ['vatm'])
                if seg == 0:
                    cp('dve', vatm[0:NS, NB, 128 * T:128 * T + 128], psT[0:NS, NB * 128:NB * 128 + 128], ['psT'], ['vatm'])
            if seg == 0:
                dma('pool', dap(va_s, l * NS * 512, [[512, NS], [1, 512]]), vatm[0:NS, NB, :], ['vatm'], ['out'])
            for h in range(4):
                lh = l * 4 + h
                for b in range(NB):
                    mm(ps3[:, b * 128:b * 128 + 128], vatm[:, b, 128 * h:128 * h + 128], wsT[:, lh, :], b == 0, False, ['vatm', 'wsT'], ['ps3'])
                for b in range(NB):
                    P.add('pe', lambda e, b=b, lh=lh: e.matmul(ps3[:, b * 128:b * 128 + 128], lhsT=ones1[0:1, :],
                                                              rhs=bsrow[0:1, (lh % 4) * 128:(lh % 4) * 128 + 128], start=False, stop=(b == NB - 1)),
                          r=['bsrow', 'ident'], w=['ps3'])
                tt('dve', mix[:, h, 0:TSEG], mix[:, h, 0:TSEG], ps3[:, 0:TSEG], ALU.mult, ['ps3', 'mixa%d' % h], ['mixa%d' % h])
                if seg == 0:
                    mm(ps6[:, 0:NS], vatm[0:NS, NB, 128 * h:128 * h + 128], wsS[:, lh, :], True, False, ['vatm', 'wsS'], ['ps6'])
                    P.add('pe', lambda e, lh=lh: e.matmul(ps6[:, 0:NS], lhsT=ones1[0:1, :], rhs=bsS[0:1, lh % 4, :], start=False, stop=True),
                          r=['bsS', 'ident'], w=['ps6'])
                    tt('dve', mix[:, h, TSEG:N], mix[:, h, TSEG:N], ps6[:, 0:NS], ALU.mult, ['ps6', 'mixa%d' % h], ['mixa%d' % h])
            for T in range(4):
                proj(l, None, in_pieces(l, [(OFF_GA + 128 * T, 128)]), N, evac_gate(T, 'mixa%d' % T), hT, ['hT'])

            allmix = ['mixa%d' % i for i in range(4)] + ['mixb%d' % i for i in range(8)] + ['mixc%d' % i for i in range(4)]

            def evac_res(j):
                def f(pm, pmk):
                    tt('dve', xT[:, j, 0:TSEG], xT[:, j, 0:TSEG], pm[:, 0:TSEG], ALU.add, [pmk, 'xT'], ['xT'])
                    if N > TSEG:
                        tt('dve', xT[:, j, TSEG:N], xT[:, j, TSEG:N], ps2[:, 0:NS], ALU.add, ['ps2', 'xT'], ['xT'])
                return f
            for j in range(KD):
                proj(l, None, out_pieces(l, j), N, evac_res(j), mix, allmix)

        rmsnorm(N, depth, lambda k, g: stt(xT[:, k, 0:N], xT[:, k, 0:N], g, rstd[:, 0:N], ALU.mult, ALU.mult,
                                           ['xT', 'rstd', 'gcol'], ['xT']))
        oblocks = [(128, b * 128, dap(y_p, (seg * TSEG + b * 128) * D, [[D, 128], [1, D]])) for b in range(NB)]
        if seg == 0:
            oblocks.append((NS, TSEG, y_s.ap()))
        for bi, (rows, c0, dst) in enumerate(oblocks):
            io = iost[bi % 2]; iok = 'iost0'
            for k4 in range(4):
                for kk in range(4):
                    k = 4 * k4 + kk
                    tr(ps3[0:rows, kk * 128:kk * 128 + 128], xT[:, k, c0:c0 + rows], ident[:], ['xT'], ['ps3'])
                cp('dve' if k4 % 2 == 0 else 'act', io[0:rows, 512 * k4:512 * k4 + 512], ps3[0:rows, :], ['ps3'], [iok])
            dma('sp', dst, io[0:rows, :], [iok], ['out'])

    import os
    ks = os.environ.get('KSTOP')
    if ks:
        print('NOPS', len(P.ops)); print('LASTOPS', [(i, o['eng'], o['line']) for i, o in list(enumerate(P.ops))[max(0, int(ks) - 3):int(ks)]]); P.ops = P.ops[:int(ks)]
    P.emit()
    st.close()
    return nc


def _consts(seqlen):
    half = 8
    inv = (500000.0 ** (-np.arange(half, dtype=np.float32) * 2.0 / 16.0)).astype(np.float32)
    pos = np.concatenate([np.arange(seqlen, dtype=np.float32), np.arange(8, dtype=np.float32) + PAST] * 1)
    pos = np.concatenate([np.arange(seqlen, dtype=np.float32)] + [np.arange(8, dtype=np.float32) + np.float32(PAST)] * 4)
    ang = pos[None, :] * inv[:, None]
    cosv = np.cos(ang).astype(np.float32); sinv = np.sin(ang).astype(np.float32)
    C = np.ones((128, pos.shape[0]), np.float32); S = np.zeros((128, pos.shape[0]), np.float32)
    for h0 in (0, 64):
        C[h0:h0 + 8] = cosv; C[h0 + 8:h0 + 16] = cosv
        S[h0:h0 + 8] = -sinv; S[h0 + 8:h0 + 16] = sinv
    i = np.arange(128)[:, None]; j = np.arange(256)[None, :]
    diff = i + 128 - j
    valid = (diff >= 0) & (diff < 128)
    mA = np.where(valid, 0.0, NEGM).astype(np.float32)
    mF = mA.copy(); mF[:, 0:128] = NEGM
    mA2 = np.concatenate([mA, mA], 1); mF2 = np.concatenate([mF, mF], 1)
    mS = np.full((32, 544), NEGM, np.float32)
    for b in range(4):
        for t in range(8):
            q = 8 * b + t
            for jj in range(128):
                if jj > t:
                    mS[q, b * 128 + jj] = 0.0
            for s in range(t + 1):
                mS[q, 512 + 8 * b + s] = 0.0
    tril = np.tril(np.ones((128, 128), np.float32))
    maskm = np.zeros((128, 2), np.float32)
    for q in range(128):
        maskm[q, (q // 16) % 2] = 1.0
    bd = np.zeros((32, 32), np.float32)
    for b in range(4):
        for t in range(8):
            for s in range(t + 1):
                bd[8 * b + t, 8 * b + s] = 1.0
    perm = np.zeros((128, 128), np.float32)
    for m_ in range(128):
        mm_ = m_ % 64
        if mm_ < 8:
            perm[m_ + 8, m_] = 1.0
        elif mm_ < 16:
            perm[m_ - 8, m_] = 1.0
    return dict(c_perm=perm, c_ident=np.eye(128, dtype=np.float32), c_tril=tril, c_maskA=mA2, c_maskF=mF2, c_maskS=mS, c_maskm=maskm,
                c_bd=bd, c_cos=C, c_sin=S)


_NC_CACHE = {}


def kernel(x_prompt, x_sample, cache_swa_k, cache_swa_v, state_ssm_re, state_ssm_im,
           norm_g, final_norm_g, w_in, w_out, chunk_w_s, chunk_b_s, attn_sinks,
           ssm_a_re, ssm_a_im, ssm_log_dt, ssm_b_re, ssm_b_im, ssm_c_re, ssm_c_im,
           ssm_d, glu_w, glu_b):
    f = lambda a: np.ascontiguousarray(np.asarray(a, dtype=np.float32))
    depth = int(np.asarray(w_in).shape[0]); seqlen = int(np.asarray(x_prompt).shape[1])
    nseg = seqlen // TSEG
    key = (depth, nseg)
    if key not in _NC_CACHE:
        _NC_CACHE[key] = build_program(depth, nseg)
    nc = _NC_CACHE[key]
    cst = _consts(seqlen)
    shared = dict(
        ng=np.concatenate([f(norm_g).reshape(depth * 16, 128), f(final_norm_g).reshape(16, 128)], 0),
        w_in=f(w_in), w_out=f(w_out), cws=f(chunk_w_s), cbs=f(chunk_b_s).reshape(1, -1), sinks=f(attn_sinks).reshape(1, -1),
        a_re=f(ssm_a_re), a_im=f(ssm_a_im), ldt=f(ssm_log_dt), b_re=f(ssm_b_re), b_im=f(ssm_b_im),
        c_re=f(ssm_c_re).reshape(depth, 512, 64), c_im=f(ssm_c_im).reshape(depth, 512, 64),
        dsk=f(ssm_d).reshape(depth * 4, 128), glw=f(glu_w), glb=f(glu_b).reshape(depth * 4, 128), **cst)
    xpf, xsf = f(x_prompt), f(x_sample)
    ckf, cvf = f(cache_swa_k), f(cache_swa_v)
    srf, sif = f(state_ssm_re), f(state_ssm_im)
    nb = xpf.shape[0]
    in_maps = []
    for c in range(8):
        m = dict(shared)
        m["xp"] = xpf[c % nb]
        m["xs"] = xsf[4 * c:4 * c + 4].reshape(NS, D)
        m["ck"] = ckf[:, 4 * c:4 * c + 4].reshape(depth, 4, 128, 256)
        m["cv"] = cvf[:, 4 * c:4 * c + 4].reshape(depth, 4, 128, 256)
        m["sre"] = srf[:, 4 * c:4 * c + 4].reshape(depth, 4, 16, 128)
        m["sim"] = sif[:, 4 * c:4 * c + 4].reshape(depth, 4, 16, 128)
        in_maps.append(m)
    res = run_bass_kernel_spmd(nc, in_maps, core_ids=list(range(8))).results
    y_prompt = np.stack([res[b]["y_p"] for b in range(nb)], 0)
    y_sample = np.concatenate([res[c]["y_s"].reshape(4, 8, D) for c in range(8)], 0)
    nkp = np.stack([res[b]["nk_p"].reshape(depth, 128, 4, 64) for b in range(nb)], 1)
    nvp = np.stack([res[b]["nv_p"].reshape(depth, 128, 4, 64) for b in range(nb)], 1)
    nks = np.concatenate([res[c]["nk_s"].reshape(depth, 4, 8, 4, 64) for c in range(8)], 1)
    nvs = np.concatenate([res[c]["nv_s"].reshape(depth, 4, 8, 4, 64) for c in range(8)], 1)
    hrp = np.stack([res[b]["hr_p"].reshape(depth, 32, 64) for b in range(nb)], 1)
    hip = np.stack([res[b]["hi_p"].reshape(depth, 32, 64) for b in range(nb)], 1)
    hrs = np.concatenate([res[c]["hr_s"].reshape(depth, 4, 32, 64) for c in range(8)], 1)
    his = np.concatenate([res[c]["hi_s"].reshape(depth, 4, 32, 64) for c in range(8)], 1)
    vas = np.concatenate([res[c]["va_s"].reshape(depth, 4, 8, 512) for c in range(8)], 1)
    return tuple(np.ascontiguousarray(a, dtype=np.float32) for a in
                 (y_prompt, y_sample, nkp, nvp, nks, nvs, hrp, hip, hrs, his, vas))
```

```python
import math
import numpy as np
import ml_dtypes
import concourse.bass as bass
import concourse.mybir as mybir
from concourse.bass_utils import run_bass_kernel_spmd

F32 = mybir.dt.float32
BF16 = mybir.dt.bfloat16
ALU = mybir.AluOpType
AF = mybir.ActivationFunctionType
AX = mybir.AxisListType

D = 2048
KD = 16
DIN = 5120
DEPTH = 4
SEQ = 4096
TSEG = 512
NSEG = SEQ // TSEG
NB = TSEG // 128
NS = 32
NJ = TSEG // 8
PAST = 16384
OFF_UA, OFF_VA, OFF_GA, OFF_Q, OFF_K, OFF_V, OFF_GB, OFF_UC, OFF_GC = 0, 512, 1024, 1536, 2560, 2816, 3072, 4096, 4608
NEGM = -60000.0
TW1 = 0
TW3 = 4096
TBD = 8192
TER = 10240
TEI = TER + 16 * (NJ + 4)
TRHO = TEI + 16 * (NJ + 4)
TABW = TRHO + 16


def dap(t, off, dims):
    return bass.AP(tensor=t, offset=off, ap=[[s, c] for s, c in dims])


class Prog:
    def __init__(self, nc):
        self.nc = nc
        self.ops = []
        self.last_w = {}
        self.readers = {}

    def add(self, eng, fn, r=(), w=(), dma=False):
        deps = set()
        for k in r:
            if k in self.last_w:
                deps.add(self.last_w[k])
        for k in w:
            if k in self.last_w:
                deps.add(self.last_w[k])
            for x in self.readers.get(k, ()):
                deps.add(x)
        idx = len(self.ops)
        deps.discard(idx)
        import sys as _s
        fr = _s._getframe(1)
        ln = []
        while fr is not None and len(ln) < 3:
            ln.append(fr.f_lineno); fr = fr.f_back
        self.ops.append(dict(eng=eng, fn=fn, deps=sorted(deps), dma=dma, line=ln))
        for k in w:
            self.last_w[k] = idx
            self.readers[k] = []
        for k in r:
            if k not in w:
                self.readers.setdefault(k, []).append(idx)
        return idx

    def emit(self):
        nc = self.nc
        ops = self.ops
        engs = ['pe', 'act', 'dve', 'pool', 'sp']
        RR = {'sp': 8, 'pool': 4, 'act': 2, 'pe': 1, 'dve': 1}
        for o in ops:
            o['sig'] = False
        for i, o in enumerate(ops):
            for d in o['deps']:
                p = ops[d]
                if p['dma']:
                    continue
                if p['eng'] == 'pe' and o['eng'] == 'pe' and not o['dma']:
                    continue
                p['sig'] = True
        cnt = {e: 0 for e in engs}
        dcnt = {e: 0 for e in engs}
        for o in ops:
            e = o['eng']
            if o['dma']:
                n = dcnt[e]
                o['dslot'] = n % RR[e]
                o['dval'] = 16 * (n // RR[e] + 1)
                o['dprev'] = None
                dcnt[e] += 1
            elif o['sig']:
                cnt[e] += 1
                o['val'] = cnt[e]
        lastslot = {}
        for i, o in enumerate(ops):
            if o['dma']:
                key = (o['eng'], o['dslot'])
                o['dprev'] = lastslot.get(key)
                lastslot[key] = i
        import contextlib
        with contextlib.ExitStack() as st:
            sems = {e: st.enter_context(nc.semaphore("s_" + e)) for e in engs}
            dsems = {}
            for e in ['sp', 'pool', 'act']:
                for s in range(RR[e]):
                    dsems[(e, s)] = st.enter_context(nc.semaphore("d_%s%d" % (e, s)))
            block = st.enter_context(nc.Block())
            per = {e: [o for o in ops if o['eng'] == e] for e in engs}

            def run(e, engine):
                waited = {}

                def need(sem_key, sem, val):
                    if waited.get(sem_key, 0) >= val:
                        return
                    engine.wait_ge(sem, val)
                    waited[sem_key] = val

                for o in per[e]:
                    for d in o['deps']:
                        p = ops[d]
                        if p['dma']:
                            need(('d', p['eng'], p['dslot']), dsems[(p['eng'], p['dslot'])], p['dval'])
                        else:
                            if p['eng'] == 'pe' and e == 'pe' and not o['dma']:
                                continue
                            need(('c', p['eng']), sems[p['eng']], p['val'])
                    if o['dma'] and o['dprev'] is not None:
                        p = ops[o['dprev']]
                        need(('d', p['eng'], p['dslot']), dsems[(p['eng'], p['dslot'])], p['dval'])
                    ins = o['fn'](engine)
                    if o['dma']:
                        ins.then_inc(dsems[(e, o['dslot'])], 16)
                    elif o['sig']:
                        ins.then_inc(sems[e], 1)
                if e in ('sp', 'pool', 'act'):
                    tot = {}
                    for o in per[e]:
                        if o['dma']:
                            tot[o['dslot']] = o['dval']
                    for s, v in tot.items():
                        engine.wait_ge(dsems[(e, s)], v)

            @block.tensor
            def _(en):
                run('pe', en)

            @block.scalar
            def _(en):
                run('act', en)

            @block.vector
            def _(en):
                run('dve', en)

            @block.gpsimd
            def _(en):
                run('pool', en)

            @block.sync
            def _(en):
                run('sp', en)


def build_program(depth=DEPTH, nseg=NSEG, debug=False):
    nc = bass.Bass("TRN2", target_bir_lowering=False)
    P = Prog(nc)
    seqlen = nseg * TSEG

    def din(name, shape, dt=F32):
        return nc.dram_tensor(name, list(shape), dt, kind="ExternalInput")

    def dout(name, shape, dt=F32):
        return nc.dram_tensor(name, list(shape), dt, kind="ExternalOutput")

    xp = din("xp", [seqlen, D]); xs = din("xs", [NS, D])
    ck = din("ck", [depth, 4, 128, 256]); cv = din("cv", [depth, 4, 128, 256])
    sre = din("sre", [depth, 4, 16, 128]); sim = din("sim", [depth, 4, 16, 128])
    ng = din("ng", [depth * 16 + 16, 128])
    w_in = din("w_in", [depth, D, DIN]); w_out = din("w_out", [depth, D, D])
    cws = din("cws", [depth, 4, 128, 128]); cbs = din("cbs", [1, depth * 4 * 128])
    sinks = din("sinks", [1, depth * 16])
    a_re = din("a_re", [depth, 32, 64]); a_im = din("a_im", [depth, 32, 64]); ldt = din("ldt", [depth, 32])
    b_re = din("b_re", [depth, 32, 64, 16]); b_im = din("b_im", [depth, 32, 64, 16])
    c_re = din("c_re", [depth, 512, 64]); c_im = din("c_im", [depth, 512, 64])
    dsk = din("dsk", [depth * 4, 128]); glw = din("glw", [depth, 512, 512]); glb = din("glb", [depth * 4, 128])
    c_ident = din("c_ident", [128, 128]); c_tril = din("c_tril", [128, 128])
    c_maskA = din("c_maskA", [128, 512]); c_maskF = din("c_maskF", [128, 512]); c_maskS = din("c_maskS", [32, 544])
    c_maskm = din("c_maskm", [128, 2]); c_bd = din("c_bd", [32, 32])
    c_perm = din("c_perm", [128, 128]); c_cos = din("c_cos", [128, seqlen + NS]); c_sin = din("c_sin", [128, seqlen + NS])

    y_p = dout("y_p", [seqlen, D]); y_s = dout("y_s", [NS, D])
    nk_p = dout("nk_p", [depth, 128, 256]); nv_p = dout("nv_p", [depth, 128, 256])
    nk_s = dout("nk_s", [depth, NS, 256]); nv_s = dout("nv_s", [depth, NS, 256])
    hr_p = dout("hr_p", [depth, 16, 128]); hi_p = dout("hi_p", [depth, 16, 128])
    hr_s = dout("hr_s", [depth, 4, 16, 128]); hi_s = dout("hi_s", [depth, 4, 16, 128])
    va_s = dout("va_s", [depth, NS, 512])
    s5tab = nc.dram_tensor("s5tab", [depth, 128, TABW], F32)
    wcache = nc.dram_tensor("wcache", [depth * 56, 128, KD * 128], BF16)

    NMAX = TSEG + NS
    import contextlib
    st = contextlib.ExitStack()

    def sb(name, shape, dt=F32):
        return st.enter_context(nc.sbuf_tensor(name, list(shape), dt))

    def ps(name, shape, dt=F32):
        return st.enter_context(nc.psum_tensor(name, list(shape), dt))

    xT = sb("xT", [128, KD, NMAX]); hT = sb("hT", [128, KD, NMAX], BF16); mix = sb("mix", [128, KD, NMAX], BF16)
    kT = sb("kT", [128, 2, 128 + NMAX], BF16); Vt = sb("Vt", [128, NB + 2, 256], BF16)
    vatm = sb("vatm", [128, NB + 1, 512], BF16)
    khalo = sb("khalo", [128, depth, 2, 128], BF16); vhalo = sb("vhalo", [128, depth, 256], BF16)
    cosT = sb("cosT", [128, NMAX]); sinT = sb("sinT", [128, NMAX])
    NWB = 3
    wb = [sb("wb%d" % i, [128, KD, 128], BF16) for i in range(NWB)]
    iost0_ = sb("iost0", [128, D]); iost = [iost0_, iost0_]
    ident = sb("ident", [128, 128]); identb = sb("identb", [128, 128], BF16); onesb = sb("onesb", [128, 128], BF16)
    ones1 = sb("ones1", [1, 128]); epsc = sb("epsc", [128, 1])
    maskA = sb("maskA", [128, 512], BF16); maskF = sb("maskF", [128, 512], BF16); maskS = sb("maskS", [32, 544], BF16)
    gcol = sb("gcol", [128, depth * 16 + 16]); dcol = sb("dcol", [128, depth * 4]); gbcol = sb("gbcol", [128, depth * 4])
    sinkbc = sb("sinkbc", [128, depth * 16]); bsrow = sb("bsrow", [1, 512]); bsS = sb("bsS", [1, 4, 32])
    wsT = sb("wsT", [128, depth * 4, 128], BF16); wsS = sb("wsS", [32, depth * 4, 32], BF16)
    hprev = sb("hprev", [128, depth, 16, 2]); s_in = sb("s_in", [128, depth, 4, 2, 16])
    tabA = sb("tabA", [128, 4096 + 2048])
    tabE = sb("tabE", [128, TABW - TER])
    Sprev = sb("Sprev", [128, 16, 2, NJ + 4], BF16)
    sq = [sb("sq%d" % i, [128, NMAX], BF16) for i in range(2)]
    rstd = sb("rstd", [128, NMAX]); tA = [sb("tA%d" % i, [128, NMAX]) for i in range(4)]
    xsw0_ = sb("xsw0", [128, NMAX]); xsw = [xsw0_, xsw0_]
    tB0_ = sb("tB0", [128, NMAX], BF16); tB = [tB0_, tB0_]
    yb = sb("yb", [128, 4, NMAX], BF16)
    attb = sb("attb", [128, 4096], BF16); Pf2 = [attb[:, 2048 * i:2048 * i + 1024].rearrange("q (a b) -> q a b", b=256) for i in range(2)]; Pb2 = [attb[:, 2048 * i + 1024:2048 * i + 2048].rearrange("q (a b) -> q a b", b=256) for i in range(2)]; Pf = Pf2[0]; Pb = Pb2[0]; glwb = attb[:, 0:2048].rearrange("q (a b) -> q a b", b=512); PTs = sb("PTs", [128, 1024], BF16); PTs2 = [PTs, sb("PTsB", [128, 1024], BF16)]
    sm = sb("sm", [128, 64])
    iob = iost0_[:].bitcast(BF16)
    ckb = iob[:, 0:1024].rearrange("q (a b) -> q a b", b=256); cvb = iob[:, 1024:2048].rearrange("q (a b) -> q a b", b=256)
    ckT = iob[:, 2048:3072].rearrange("q (b k w) -> q b k w", b=4, k=2)
    PfS = Pf[0:32].rearrange("q a b -> q (a b)")[:, 0:544]; PbS = Pb[0:32].rearrange("q a b -> q (a b)")[:, 0:544]; PTS2 = sb("PTS2", [128, 5, 32], BF16)
    s5w = [sb("s5w%d" % i, [128, 4 * NJ]) for i in range(10)]
    outst = sb("outst", [128, 416])
    ps01 = ps("ps01", [128, 1024]); ps0 = ps01[:, 0:512]; ps1 = ps01[:, 512:1024]; ps2 = ps("ps2", [128, 512]); ps3 = ps("ps3", [128, 512])
    psS = ps("psS", [128, 1024]); ps6 = ps("ps6", [128, 512]); psT = ps("psT", [128, 1024], BF16)

    tabAb = tabA[:].bitcast(BF16)

    def E(name):
        return {'pe': 'pe', 'act': 'act', 'dve': 'dve', 'pool': 'pool', 'sp': 'sp'}[name]

    def dma(q, out, in_, r, w):
        return P.add(q, lambda e: e.dma_start(out=out, in_=in_, allow_slow_non_contiguous=True), r=r, w=w, dma=True)

    def mm(out, lhsT, rhs, start, stop, r, w, **kw):
        return P.add('pe', lambda e: e.matmul(out, lhsT=lhsT, rhs=rhs, start=start, stop=stop, **kw), r=r, w=w)

    def tr(out, in_, idn, r, w, **kw):
        return P.add('pe', lambda e: e.transpose(out, in_, idn, **kw), r=r + ['ident'], w=w)

    def act(out, in_, func, r, w, scale=1.0, bias=None, accum=None):
        def f(e):
            kw = dict(out=out, in_=in_, func=func, scale=scale)
            if bias is not None:
                kw['bias'] = bias
            if accum is not None:
                kw['accum_out'] = accum
            return e.activation(**kw)
        return P.add('act', f, r=r, w=w)

    def tt(eng, out, a, b, op, r, w):
        return P.add(eng, lambda e: e.tensor_tensor(out=out, in0=a, in1=b, op=op), r=r, w=w)

    def ts(eng, out, a, s1, op0, r, w, s2=None, op1=None):
        if op1 is None:
            return P.add(eng, lambda e: e.tensor_scalar(out=out, in0=a, scalar1=s1, scalar2=None, op0=op0), r=r, w=w)
        return P.add(eng, lambda e: e.tensor_scalar(out=out, in0=a, scalar1=s1, scalar2=s2, op0=op0, op1=op1), r=r, w=w)

    def stt(out, a, s, b, op0, op1, r, w):
        return P.add('dve', lambda e: e.scalar_tensor_tensor(out=out, in0=a, scalar=s, in1=b, op0=op0, op1=op1), r=r, w=w)

    def cp(eng, out, in_, r, w):
        if eng == 'act':
            return P.add('act', lambda e: e.copy(out=out, in_=in_), r=r, w=w)
        return P.add(eng, lambda e: e.tensor_copy(out=out, in_=in_), r=r, w=w)

    def memset(eng, ap, v, w):
        return P.add(eng, lambda e: e.memset(ap, v), r=[], w=w)

    dma('sp', ident[:], c_ident.ap(), [], ['ident'])
    dma('pool', identb[:], c_ident.ap(), [], ['ident'])
    dma('pool', maskA[:], c_maskA.ap(), [], ['masks']); dma('pool', maskF[:], c_maskF.ap(), [], ['masks'])
    dma('pool', maskS[:], c_maskS.ap(), [], ['masks'])
    memset('dve', onesb[:], 1.0, ['ident']); memset('dve', ones1[:], 1.0, ['ident']); memset('dve', epsc[:], 1e-5, ['ident'])
    xw0 = xsw[0]
    permb = sb("permb", [128, 128], BF16)
    dma('pool', permb[:], c_perm.ap(), [], ['ident'])
    memset('pool', kT[:], 0.0, ['kT']); memset('pool', Vt[:], 0.0, ['Vt'])
    memset('pool', khalo[:], 0.0, ['khalo']); memset('pool', vhalo[:], 0.0, ['vhalo']); memset('dve', hprev[:], 0.0, ['hprev'])
    dma('sp', sinkbc[:], dap(sinks, 0, [[0, 128], [1, depth * 16]]), [], ['sinkbc'])
    nrow = depth * 16 + 16
    dma('sp', iost[0][0:nrow, 0:128], ng.ap(), [], ['iost0'])
    tr(ps3[:, 0:nrow], iost[0][0:nrow, 0:128], ident[0:nrow, 0:nrow], ['iost0'], ['ps3'])
    cp('dve', gcol[:], ps3[:, 0:nrow], ['ps3'], ['gcol'])
    dma('sp', iost[0][0:depth * 4, 128:256], dsk.ap(), [], ['iost0'])
    tr(ps3[:, 0:depth * 4], iost[0][0:depth * 4, 128:256], ident[0:depth * 4, 0:depth * 4], ['iost0'], ['ps3'])
    cp('dve', dcol[:], ps3[:, 0:depth * 4], ['ps3'], ['dcol'])
    dma('sp', iost[0][0:depth * 4, 256:384], glb.ap(), [], ['iost0'])
    tr(ps3[:, 0:depth * 4], iost[0][0:depth * 4, 256:384], ident[0:depth * 4, 0:depth * 4], ['iost0'], ['ps3'])
    cp('dve', gbcol[:], ps3[:, 0:depth * 4], ['ps3'], ['gbcol'])
    dma('sp', iost[0][:, 1536:1664], c_tril.ap(), [], ['iost0'])
    for lh in range(depth * 4):
        dma('sp', iost[0][:, 512:640], dap(cws, lh * 128 * 128, [[128, 128], [1, 128]]), [], ['iost0'])
        tt('dve', iost[0][:, 640:768], iost[0][:, 512:640], iost[0][:, 1536:1664], ALU.mult, ['iost0', 'iost0'], ['iost0b'])
        tr(ps3[:, 0:128], iost[0][:, 640:768], ident[:], ['iost0b'], ['ps3'])
        cp('dve', wsT[:, lh, :], ps3[:, 0:128], ['ps3'], ['wsT'])
    dma('sp', iost[0][0:32, 1664:1696], c_bd.ap(), [], ['iost0'])
    for lh in range(depth * 4):
        for b in range(4):
            dma('sp', iost[0][8 * b:8 * b + 8, 1024:1056].rearrange("q (a s) -> q a s", s=8), dap(cws, lh * 128 * 128, [[128, 8], [0, 4], [1, 8]]), [], ['iost0'])
        tt('dve', iost[0][0:32, 1056:1088], iost[0][0:32, 1024:1056], iost[0][0:32, 1664:1696], ALU.mult, ['iost0', 'iost0'], ['iost0b'])
        tr(ps3[0:32, 0:32], iost[0][0:32, 1056:1088], ident[0:32, 0:32], ['iost0b'], ['ps3'])
        cp('dve', wsS[:, lh, :], ps3[0:32, 0:32], ['ps3'], ['wsS'])
    for l in range(depth):
        for b in range(4):
            for ri, src in enumerate((sre, sim)):
                dma('sp', iost[0][0:16, 0:128], dap(src, ((l * 4 + b) * 16) * 128, [[128, 16], [1, 128]]), [], ['iost0'])
                tr(ps3[:, 0:16], iost[0][0:16, 0:128], ident[0:16, 0:16], ['iost0'], ['ps3'])
                cp('dve', s_in[:, l, b, ri, :], ps3[:, 0:16], ['ps3'], ['s_in'])

    W = NJ + 4
    maskm = sb("maskm", [128, 2])
    dma('sp', maskm[:], c_maskm.ap(), [], ['maskm'])
    mixf = mix[:].rearrange("q a b -> q (a b)").bitcast(F32)
    xTf = xT[:].rearrange("q a b -> q (a b)")
    _o = [0]

    def carve(buf, n, lim):
        a = _o[0]; _o[0] += n
        assert _o[0] <= lim
        return buf[:, a:a + n]
    sc = [carve(mixf, 16, 4352) for i in range(72)]
    Bn = [carve(mixf, 256, 4352).rearrange("q (a b) -> q a b", b=16) for i in range(2)]
    Cn = [carve(mixf, 256, 4352).rearrange("q (a b) -> q a b", b=64) for i in range(2)]
    Cp = [carve(mixf, 512, 4352).rearrange("q (a b) -> q a b", b=32) for i in range(2)]
    Y3 = [carve(mixf, 512, 4352).rearrange("q (a b) -> q a b", b=32) for i in range(2)]
    _o[0] = 4 * 16 * W
    tmpY = [carve(xTf, 256, 8704).rearrange("q (a b) -> q a b", b=16) for i in range(4)]
    Cpad = carve(xTf, 128, 8704)
    pw_all = carve(xTf, 288, 8704).rearrange("q (k r a) -> q k r a", k=9, r=2)
    CpB_all = carve(xTf, 512, 8704).bitcast(BF16).rearrange("q (r a c) -> q r a c", r=2, a=16)
    W1st = tabAb[:, 0:8192]; BDst = tabAb[:, 8192:12288]; W3half = iost0_[:].bitcast(BF16)
    Est = tabE[:, 0:2 * 16 * W].rearrange("q (a b c) -> q a b c", a=2, b=16)
    Etmp = [xTf[:, i * 16 * W:(i + 1) * 16 * W].rearrange("q (a b) -> q a b", b=W) for i in range(4)]
    Y1all = hT[:].rearrange("q a b -> q (a b)")[:, 0:8192].rearrange("q (k r a c) -> q k r a c", k=8, r=2, a=16)
    rs = ['s5p', 'xT', 'hT', 'tabA', 'tabW', 'tabE', 'iost0'] + ['mixa%d' % i for i in range(4)] + ['mixb%d' % i for i in range(8)] + ['mixc%d' % i for i in range(4)]
    for l in range(depth):
        cnt_i = [0]

        def T_():
            cnt_i[0] += 1
            return sc[cnt_i[0] - 1]
        are, aim, dtl = T_(), T_(), T_()
        for m in range(2):
            dma('sp', are[64 * m:64 * m + 64, :], dap(a_re, l * 2048 + m * 64, [[1, 64], [128, 16]]), [], rs)
            dma('sp', aim[64 * m:64 * m + 64, :], dap(a_im, l * 2048 + m * 64, [[1, 64], [128, 16]]), [], rs)
            dma('sp', dtl[64 * m:64 * m + 64, :], dap(ldt, l * 32 + m, [[0, 64], [2, 16]]), [], rs)
            dma('sp', Bn[0][64 * m:64 * m + 64, :, :], dap(b_re, l * 32768 + m * 1024, [[16, 64], [2048, 16], [1, 16]]), [], rs)
            dma('sp', Bn[1][64 * m:64 * m + 64, :, :], dap(b_im, l * 32768 + m * 1024, [[16, 64], [2048, 16], [1, 16]]), [], rs)
        dma('sp', Cn[0][:], dap(c_re, l * 32768, [[64, 128], [8192, 4], [1, 64]]), [], rs)
        dma('sp', Cn[1][:], dap(c_im, l * 32768, [[64, 128], [8192, 4], [1, 64]]), [], rs)
        def poly(x, coef):
            t = T_()
            n = len(coef) - 1
            ts('dve', t[:], x, float(coef[n]), ALU.mult, rs, rs)
            for k in range(n - 1, 0, -1):
                stt(t[:], t[:], float(coef[k]), x, ALU.add, ALU.mult, rs, rs)
            ts('dve', t[:], t[:], float(coef[0]), ALU.add, rs, rs)
            return t

        def exp_poly(x, deg, nsq):
            xs_ = T_(); ts('dve', xs_[:], x, 1.0 / (2 ** nsq), ALU.mult, rs, rs)
            e = poly(xs_[:], [1.0 / math.factorial(k) for k in range(deg + 1)])
            for _ in range(nsq):
                tt('dve', e[:], e[:], e[:], ALU.mult, rs, rs)
            return e
        dt_ = exp_poly(dtl[:], 12, 3)
        ar = T_(); tt('dve', ar[:], are[:], dt_[:], ALU.mult, rs, rs)
        th = T_(); tt('dve', th[:], aim[:], dt_[:], ALU.mult, rs, rs)
        mag = exp_poly(ar[:], 7, 0)
        TWO_PI = 2.0 * math.pi
        MAGIC = 12582912.0
        u = T_(); ts('dve', u[:], th[:], 1.0 / TWO_PI, ALU.mult, rs, rs)
        n_ = T_(); ts('dve', n_[:], u[:], MAGIC, ALU.add, rs, rs)
        n2 = T_(); ts('dve', n2[:], n_[:], MAGIC, ALU.subtract, rs, rs)
        fr0 = T_(); tt('dve', fr0[:], u[:], n2[:], ALU.subtract, rs, rs)
        xq = T_(); ts('dve', xq[:], fr0[:], TWO_PI / 4.0, ALU.mult, rs, rs)
        x2 = T_(); tt('dve', x2[:], xq[:], xq[:], ALU.mult, rs, rs)
        ps_ = poly(x2[:], [(-1.0) ** k / math.factorial(2 * k + 1) for k in range(7)])
        sn = T_(); tt('dve', sn[:], ps_[:], xq[:], ALU.mult, rs, rs)
        cs = poly(x2[:], [(-1.0) ** k / math.factorial(2 * k) for k in range(8)])
        for _ in range(2):
            s2_ = T_(); tt('dve', s2_[:], sn[:], cs[:], ALU.mult, rs, rs)
            q2_ = T_(); tt('dve', q2_[:], sn[:], sn[:], ALU.mult, rs, rs)
            sn = T_(); ts('dve', sn[:], s2_[:], 2.0, ALU.mult, rs, rs)
            cs = T_(); ts('dve', cs[:], q2_[:], -2.0, ALU.mult, rs, rs, s2=1.0, op1=ALU.add)
        lr = T_(); tt('dve', lr[:], mag[:], cs[:], ALU.mult, rs, rs)
        li = T_(); tt('dve', li[:], mag[:], sn[:], ALU.mult, rs, rs)

        def cmul(ar_, ai_, br_, bi_, bc=None):
            t1, t2, t3, t4, orr, oi = T_(), T_(), T_(), T_(), T_(), T_()
            tt('dve', t1[:], ar_, br_, ALU.mult, rs, rs); tt('dve', t2[:], ai_, bi_, ALU.mult, rs, rs)
            tt('dve', orr[:], t1[:], t2[:], ALU.subtract, rs, rs)
            tt('dve', t3[:], ar_, bi_, ALU.mult, rs, rs); tt('dve', t4[:], ai_, br_, ALU.mult, rs, rs)
            tt('dve', oi[:], t3[:], t4[:], ALU.add, rs, rs)
            return orr, oi
        nr = T_(); ts('dve', nr[:], lr[:], -1.0, ALU.add, rs, rs)
        d1 = T_(); tt('dve', d1[:], are[:], are[:], ALU.mult, rs, rs)
        d2 = T_(); tt('dve', d2[:], aim[:], aim[:], ALU.mult, rs, rs)
        den = T_(); tt('dve', den[:], d1[:], d2[:], ALU.add, rs, rs)
        rden = T_(); P.add('dve', lambda e, o=rden, i=den: e.reciprocal(out=o[:], in_=i[:]), r=rs, w=rs)
        nai = T_(); ts('dve', nai[:], aim[:], -1.0, ALU.mult, rs, rs)
        f0r, f0i = cmul(nr[:], li[:], are[:], nai[:])
        fr_ = T_(); tt('dve', fr_[:], f0r[:], rden[:], ALU.mult, rs, rs)
        fi_ = T_(); tt('dve', fi_[:], f0i[:], rden[:], ALU.mult, rs, rs)
        for ri in range(2):
            for T in range(4):
                for m in range(2):
                    ts('dve', Cpad[:, 64 * m:64 * m + 64], Cn[ri][:, T, :], maskm[:, m:m + 1], ALU.mult, rs, rs)
                tr(ps3[:, 0:128], Cpad[:], ident[:], rs, ['ps3'])
                cp('dve', Cp[ri][:, 4 * T:4 * T + 4, :], ps3[:, 0:128].rearrange("q (a b) -> q a b", b=32), ['ps3'], rs)
        pw = pw_all
        memset('dve', pw[:, 0, 0, :], 1.0, rs); memset('dve', pw[:, 0, 1, :], 0.0, rs)
        cp('dve', pw[:, 1, 0, :], lr[:], rs, rs); cp('dve', pw[:, 1, 1, :], li[:], rs, rs)
        base_i = cnt_i[0]
        for k in range(2, 9):
            cnt_i[0] = base_i
            orr, oi = cmul(pw[:, k - 1, 0, :], pw[:, k - 1, 1, :], lr[:], li[:])
            cp('dve', pw[:, k, 0, :], orr[:], rs, rs); cp('dve', pw[:, k, 1, :], oi[:], rs, rs)
        W3h = W3half.rearrange("q (t a r c) -> q t a r c", t=4, a=16, r=2)
        for t in range(8):
            pr = pw[:, t + 1, 0, :].unsqueeze(2).to_broadcast([128, 16, 32])
            pi_ = pw[:, t + 1, 1, :].unsqueeze(2).to_broadcast([128, 16, 32])
            tt('dve', Y3[0][:], Cp[0][:], pr, ALU.mult, rs, rs); tt('dve', Y3[1][:], Cp[1][:], pi_, ALU.mult, rs, rs)
            tt('dve', W3h[:, t % 4, :, 0, :], Y3[0][:], Y3[1][:], ALU.subtract, rs, rs)
            tt('dve', Y3[0][:], Cp[0][:], pi_, ALU.mult, rs, rs); tt('dve', Y3[1][:], Cp[1][:], pr, ALU.mult, rs, rs)
            tt('dve', Y3[0][:], Y3[0][:], Y3[1][:], ALU.add, rs, rs)
            ts('dve', W3h[:, t % 4, :, 1, :], Y3[0][:], -1.0, ALU.mult, rs, rs)
            if t % 4 == 3:
                dma('sp', dap(s5tab, l * 128 * TABW + TW3 + (t // 4) * 2048, [[TABW, 128], [1, 2048]]), iost0_[:], rs, ['s5tab'] + rs)
        memset('dve', hT[:].rearrange("q a b -> q (a b)")[:, 0:8192], 0.0, rs)
        for k in range(8):
            cnt_i[0] = base_i
            Fr, Fi = cmul(pw[:, k, 0, :], pw[:, k, 1, :], fr_[:], fi_[:])
            Frb = Fr[:].unsqueeze(2).to_broadcast([128, 16, 16]); Fib = Fi[:].unsqueeze(2).to_broadcast([128, 16, 16])
            tt('dve', tmpY[0][:], Bn[0][:], Frb, ALU.mult, rs, rs); tt('dve', tmpY[1][:], Bn[1][:], Fib, ALU.mult, rs, rs)
            tt('dve', tmpY[2][:], Bn[1][:], Frb, ALU.mult, rs, rs); tt('dve', tmpY[3][:], Bn[0][:], Fib, ALU.mult, rs, rs)
            for m in range(2):
                sl = slice(64 * m, 64 * m + 64)
                tt('dve', Y1all[sl, k, 0, :, 16 * m:16 * m + 16], tmpY[0][sl], tmpY[1][sl], ALU.subtract, rs, rs)
                tt('dve', Y1all[sl, k, 1, :, 16 * m:16 * m + 16], tmpY[2][sl], tmpY[3][sl], ALU.add, rs, rs)
        W1v = W1st.rearrange("q (s t r c) -> q s t r c", s=8, t=4, r=2)
        for s in range(8):
            k = 7 - s
            for T in range(4):
                for ri in range(2):
                    for r4 in range(4):
                        tr(psT[32 * r4:32 * r4 + 32, 0:128], Y1all[:, k, ri, 4 * T + r4, :], identb[:], rs, ['psT'], tile_position=(0, 32 * r4))
                    cp('dve', W1v[:, s, T, ri, :], psT[:, 0:128], ['psT'], rs)
        memset('dve', BDst, 0.0, rs)
        BDv = BDst.rearrange("q (t a c) -> q t a c", t=4, a=8)
        CpB = CpB_all
        cp('dve', CpB[:, 0], Cp[0][:], rs, rs); ts('dve', CpB[:, 1], Cp[1][:], -1.0, ALU.mult, rs, rs)
        for T in range(4):
            for tau in range(8):
                for r4 in range(4):
                    pi_i = 4 * T + r4
                    for ri in range(2):
                        mm(ps3[32 * r4:32 * r4 + 32, tau * 32:tau * 32 + 32], Y1all[:, tau, ri, pi_i, :], CpB[:, ri, pi_i, :],
                           ri == 0, ri == 1, rs, ['ps3'], tile_position=(0, 32 * r4))
            for r4 in range(4):
                sl = slice(32 * r4, 32 * r4 + 32)
                cp('dve', BDv[sl, T, :, 32 * r4:32 * r4 + 32], ps3[sl, 0:256].rearrange("q (a b) -> q a b", b=32), ['ps3'], rs)
        cnt_i[0] = base_i + 6
        r2 = T_(); tt('dve', r2[:], mag[:], mag[:], ALU.mult, rs, rs)
        r4_ = T_(); tt('dve', r4_[:], r2[:], r2[:], ALU.mult, rs, rs)
        rho = T_(); tt('dve', rho[:], r4_[:], r4_[:], ALU.mult, rs, rs)
        ur, ui = cs, sn
        for _ in range(3):
            a2 = T_(); tt('dve', a2[:], ur[:], ur[:], ALU.mult, rs, rs)
            b2 = T_(); tt('dve', b2[:], ui[:], ui[:], ALU.mult, rs, rs)
            ab = T_(); tt('dve', ab[:], ur[:], ui[:], ALU.mult, rs, rs)
            ur = T_(); tt('dve', ur[:], a2[:], b2[:], ALU.subtract, rs, rs)
            ui = T_(); ts('dve', ui[:], ab[:], 2.0, ALU.mult, rs, rs)
        cp('dve', Est[:, 0, :, 0], ur[:], rs, rs)
        cp('dve', Est[:, 1, :, 0], ui[:], rs, rs)
        k = 1
        while k < NJ:
            n = min(k, NJ - k)
            br_ = Est[:, 0, :, k - 1:k].to_broadcast([128, 16, n]); bi_ = Est[:, 1, :, k - 1:k].to_broadcast([128, 16, n])
            xr = Est[:, 0, :, 0:n]; xi = Est[:, 1, :, 0:n]
            tt('dve', Etmp[0][:, :, 0:n], xr, br_, ALU.mult, rs, rs); tt('dve', Etmp[1][:, :, 0:n], xi, bi_, ALU.mult, rs, rs)
            tt('dve', Etmp[2][:, :, 0:n], xr, bi_, ALU.mult, rs, rs); tt('dve', Etmp[3][:, :, 0:n], xi, br_, ALU.mult, rs, rs)
            tt('dve', Est[:, 0, :, k:k + n], Etmp[0][:, :, 0:n], Etmp[1][:, :, 0:n], ALU.subtract, rs, rs)
            tt('dve', Est[:, 1, :, k:k + n], Etmp[2][:, :, 0:n], Etmp[3][:, :, 0:n], ALU.add, rs, rs)
            k += n
        for ri in range(2):
            cp('dve', Est[:, ri, :, NJ:NJ + 4], Est[:, ri, :, 0:1].to_broadcast([128, 16, 4]), rs, rs)
        cp('dve', tabE[:, 32 * W:32 * W + 16], rho[:], rs, rs)
        dma('sp', dap(s5tab, l * 128 * TABW + TW1, [[TABW, 128], [1, 4096]]), tabA[:, 0:4096], rs, ['s5tab'] + rs)
        dma('sp', dap(s5tab, l * 128 * TABW + TBD, [[TABW, 128], [1, 2048]]), tabA[:, 4096:6144], rs, ['s5tab'] + rs)
        dma('sp', dap(s5tab, l * 128 * TABW + TER, [[TABW, 128], [1, TABW - TER]]), tabE[:], rs, ['s5tab'] + rs)

    wbi = [0]
    blkc = [0]
    cur_seg = [0]

    def proj(l, wsrc, pieces_fn, N, evac, hsrc, rkeys):
        i = wbi[0] % NWB
        wbi[0] += 1
        w = wb[i]
        wk = 'wb%d' % i
        blk = l * 56 + blkc[0]
        blkc[0] += 1
        ckey = 'wc%d' % blk
        wflat = w[:].rearrange("q a b -> q (a b)")
        if cur_seg[0] == 0:
            pieces_fn(w, wk)
            if nseg > 1:
                dma('sp', dap(wcache, blk * 128 * KD * 128, [[KD * 128, 128], [1, KD * 128]]), wflat, [wk], [ckey])
        else:
            dma('sp', wflat, dap(wcache, blk * 128 * KD * 128, [[KD * 128, 128], [1, KD * 128]]), [ckey], [wk])
        pm = ps0 if (wbi[0] % 2 == 0) else ps1
        pmk = 'ps0' if (wbi[0] % 2 == 0) else 'ps1'
        for k in range(KD):
            mm(pm[:, 0:TSEG], w[:, k, :], hsrc[:, k, 0:TSEG], k == 0, k == KD - 1, [wk] + rkeys, [pmk])
            if N > TSEG:
                mm(ps2[:, 0:NS], w[:, k, :], hsrc[:, k, TSEG:N], k == 0, k == KD - 1, [wk] + rkeys, ['ps2'])
        evac(pm, pmk)

    def in_pieces(l, col_pieces):
        def f(w, wk):
            off = 0
            for c0, wd in col_pieces:
                dma('pool', w[:, :, off:off + wd], dap(w_in, l * D * DIN + c0, [[DIN, 128], [128 * DIN, KD], [1, wd]]), [], [wk])
                off += wd
        return f

    def qcols(base, tq):
        kt, i = tq // 4, tq % 4
        return [(base + 64 * (8 * kt + i), 64), (base + 64 * (8 * kt + 4 + i), 64)]

    def out_pieces(l, j):
        def f(w, wk):
            base = l * D * D + j * 128
            dma('pool', w[:, 0:4, :], dap(w_out, base, [[D, 128], [128 * D, 4], [1, 128]]), [], [wk])
            dma('pool', w[:, 12:16, :], dap(w_out, base + 1536 * D, [[D, 128], [128 * D, 4], [1, 128]]), [], [wk])
            for half in range(2):
                for kt in range(2):
                    dma('pool', w[64 * half:64 * half + 64, 4 + 4 * kt:8 + 4 * kt, :],
                        dap(w_out, base + (512 + 512 * kt + 256 * half) * D, [[D, 64], [64 * D, 4], [1, 128]]), [], [wk])
        return f

    def rmsnorm(N, gidx, out_fn):
        for k in range(KD):
            s = sq[k % 2]
            act(s[:, 0:N], xT[:, k, 0:N], AF.Square, ['xT'], ['sq%d' % (k % 2)])
            mm(ps0[:, 0:TSEG], onesb[:], s[:, 0:TSEG], k == 0, k == KD - 1, ['sq%d' % (k % 2)], ['ps0'])
            if N > TSEG:
                mm(ps2[:, 0:NS], onesb[:], s[:, TSEG:N], k == 0, k == KD - 1, ['sq%d' % (k % 2)], ['ps2'])
        act(rstd[:, 0:TSEG], ps0[:, 0:TSEG], AF.Sqrt, ['ps0'], ['rstd'], scale=1.0 / D, bias=epsc[:])
        if N > TSEG:
            act(rstd[:, TSEG:N], ps2[:, 0:NS], AF.Sqrt, ['ps2'], ['rstd'], scale=1.0 / D, bias=epsc[:])
        P.add('dve', lambda e: e.reciprocal(out=rstd[:, 0:N], in_=rstd[:, 0:N]), r=['rstd'], w=['rstd'])
        for k in range(KD):
            out_fn(k, gcol[:, gidx * 16 + k:gidx * 16 + k + 1])

    def softmax(src3, np_, nh, Wd, sinkcols, Pf_, Pb_, rk, tag, so=0):
        smk = 'sm%d' % so
        mx = sm[0:np_, so + 0:so + nh]; m_ = sm[0:np_, so + 4:so + 4 + nh]; ng_ = sm[0:np_, so + 8:so + 8 + nh]; ssum = sm[0:np_, so + 12:so + 12 + nh]
        dd = sm[0:np_, so + 16:so + 16 + nh]; es = sm[0:np_, so + 20:so + 20 + nh]; den = sm[0:np_, so + 24:so + 24 + nh]; rd = sm[0:np_, so + 28:so + 28 + nh]
        P.add('dve', lambda e: e.tensor_reduce(out=mx, in_=src3, axis=AX.X, op=ALU.max), r=rk, w=[smk])
        ts('dve', m_, mx, 0.125, ALU.mult, [smk], [smk])
        tt('dve', m_, m_, sinkcols, ALU.max, [smk, 'sinkbc'], [smk])
        ts('dve', ng_, m_, -1.0, ALU.mult, [smk], [smk])
        for i in range(nh):
            act(Pf_[:, i, :], src3[:, i, :], AF.Exp, rk + [smk], [tag + 'Pf', 'glwb', smk + '2%d' % i], scale=0.125, bias=ng_[:, i:i + 1],
                accum=ssum[:, i:i + 1])
        tt('dve', dd, sinkcols, m_, ALU.subtract, [smk, 'sinkbc'], [smk])
        act(es, dd, AF.Exp, [smk], [smk])
        tt('dve', den, ssum, es, ALU.add, [smk] + [smk + '2%d' % i for i in range(nh)], [smk])
        P.add('dve', lambda e: e.reciprocal(out=rd, in_=den), r=[smk], w=[smk])
        tt('pool', Pb_, Pf_, rd.unsqueeze(2).to_broadcast([np_, nh, Wd]), ALU.mult, [tag + 'Pf', smk], [tag + 'Pb', 'glwb'])

    evi = [0]
    for seg in range(nseg):
        N = TSEG + (NS if seg == 0 else 0)
        cur_seg[0] = seg
        dma('sp', cosT[:, 0:TSEG], dap(c_cos, seg * TSEG, [[seqlen + NS, 128], [1, TSEG]]), [], ['rope'])
        dma('sp', sinT[:, 0:TSEG], dap(c_sin, seg * TSEG, [[seqlen + NS, 128], [1, TSEG]]), [], ['rope'])
        if seg == 0:
            dma('sp', cosT[:, TSEG:N], dap(c_cos, seqlen, [[seqlen + NS, 128], [1, NS]]), [], ['rope'])
            dma('sp', sinT[:, TSEG:N], dap(c_sin, seqlen, [[seqlen + NS, 128], [1, NS]]), [], ['rope'])
        blocks = [(b, 128, dap(xp, (seg * TSEG + b * 128) * D, [[D, 128], [1, D]]), b * 128) for b in range(NB)]
        if seg == 0:
            blocks.append((NB, NS, xs.ap(), TSEG))
        for bi, (b, rows, src, c0) in enumerate(blocks):
            io = iost[bi % 2]; iok = 'iost0'
            dma('sp', io[0:rows, :], src, [], [iok])
            for k4 in range(4):
                for kk in range(4):
                    k = 4 * k4 + kk
                    tr(ps3[:, kk * 128:kk * 128 + rows], io[0:rows, k * 128:(k + 1) * 128], ident[0:rows, 0:rows], [iok], ['ps3'])
                cp('dve' if k4 % 2 == 0 else 'act', xT[:, 4 * k4:4 * k4 + 4, c0:c0 + rows],
                   ps3[:, :].rearrange("q (a b) -> q a b", b=128)[:, :, 0:rows], ['ps3'], ['xT'])

        for l in range(depth):
            blkc[0] = 0
            dma('sp', tabA[:, 0:4096], dap(s5tab, l * 128 * TABW + TW1, [[TABW, 128], [1, 4096]]), ['s5tab'], ['tabW'])
            dma('sp', tabA[:, 4096:6144], dap(s5tab, l * 128 * TABW + TBD, [[TABW, 128], [1, 2048]]), ['s5tab'], ['tabA'])
            dma('sp', tabE[:], dap(s5tab, l * 128 * TABW + TER, [[TABW, 128], [1, TABW - TER]]), ['s5tab'], ['tabE'])
            dma('sp', bsrow[:], dap(cbs, l * 512, [[0, 1], [1, 512]]), [], ['bsrow'])
            if seg == 0:
                for b in range(4):
                    cp('pool', bsS[:, :, 8 * b:8 * b + 8], bsrow[:].rearrange("o (g t) -> o g t", t=128)[:, :, 0:8], ['bsrow'], ['bsS'])
            dma('pool', glwb[:], dap(glw, l * 512 * 512, [[512, 128], [128 * 512, 4], [1, 512]]), [], ['glwb', 'p0Pf', 'p0Pb'])
            W1 = tabAb[:, 0:8192].rearrange("q (s t r c) -> q s t r c", s=8, t=4, r=2)
            W3 = tabAb[:, 0:8192].rearrange("q (t a r c) -> q t a r c", t=8, a=16, r=2)
            BD = tabAb[:, 8192:12288].rearrange("q (t a c) -> q t a c", t=4, a=8)
            Er = tabE[:, 0:16 * W].rearrange("q (a b) -> q a b", b=W)
            Ei = tabE[:, 16 * W:32 * W].rearrange("q (a b) -> q a b", b=W)
            rhoc = tabE[:, 32 * W:32 * W + 16]
            NJJ = N // 8
            rmsnorm(N, l, lambda k, g: stt(hT[:, k, 0:N], xT[:, k, 0:N], g, rstd[:, 0:N], ALU.mult, ALU.mult,
                                           ['xT', 'rstd', 'gcol'], ['hT']))

            def evac_store(dst_fn, wkey):
                def f(pm, pmk):
                    evi[0] += 1
                    eng = 'act' if evi[0] % 2 == 0 else 'dve'
                    cp(eng, dst_fn(0, TSEG), pm[:, 0:TSEG], [pmk], [wkey])
                    if N > TSEG:
                        cp(eng, dst_fn(TSEG, N), ps2[:, 0:NS], ['ps2'], [wkey])
                return f

            def evac_gate(tile, wkey):
                def f(pm, pmk):
                    evi[0] += 1
                    t = tA[evi[0] % 2]; tk = 'tA%d' % (evi[0] % 2)
                    act(t[:, 0:TSEG], pm[:, 0:TSEG], AF.Silu, [pmk], [tk])
                    if N > TSEG:
                        act(t[:, TSEG:N], ps2[:, 0:NS], AF.Silu, ['ps2'], [tk])
                    tt('dve', mix[:, tile, 0:N], mix[:, tile, 0:N], t[:, 0:N], ALU.mult, [tk, wkey], [wkey])
                return f

            for T in range(4):
                proj(l, None, in_pieces(l, [(OFF_UC + 128 * T, 128)]), N,
                     evac_store(lambda a, b, T=T: mix[:, 12 + T, a:b], 'mixc%d' % T), hT, ['hT'])
            if seg == 0:
                for pi_i in range(16):
                    for ri in range(2):
                        cp('pool', Sprev[:, pi_i, ri, NJ:NJ + 4], s_in[:, l, :, ri, pi_i], ['s_in'], ['Sprev%d' % pi_i])
            G = 4 if NJJ * 8 <= 512 else 2
            WP = 512 // (2 * G)
            Er4 = Er.rearrange("q (t r) j -> q t r j", r=4); Ei4 = Ei.rearrange("q (t r) j -> q t r j", r=4)
            Sp4 = Sprev[:].rearrange("q (t r) i j -> q t r i j", r=4)
            hp4 = hprev[:].rearrange("q l (t r) i -> q l t r i", r=4)
            gi = 0
            for r4 in range(4):
              for T0 in range(0, 4, G):
                gi += 1
                pd = ps3 if gi % 2 == 0 else ps6
                pdk = 'ps3' if gi % 2 == 0 else 'ps6'
                pis = [4 * (T0 + g) + r4 for g in range(G)]
                sks = ['Sprev%d' % p_ for p_ in pis]
                for g in range(G):
                    T = T0 + g
                    uview = mix[32 * r4:32 * r4 + 32, 12 + T, 0:N].rearrange("q (j s) -> q j s", s=8)
                    for ri in range(2):
                        c0 = (g * 2 + ri) * WP
                        for s_ in range(8):
                            mm(pd[:, c0:c0 + NJJ], W1[32 * r4:32 * r4 + 32, s_, T, ri, :], uview[:, :, s_], s_ == 0, s_ == 7,
                               ['tabW', 'mixc%d' % T], [pdk], tile_position=(32 * r4, 0))
                Dv = pd[:, 0:512].rearrange("q (g r j) -> q g r j", g=G, r=2)
                Dre = Dv[:, :, 0, 0:NJJ]; Dim = Dv[:, :, 1, 0:NJJ]
                er = Er4[:, T0:T0 + G, r4, 0:NJJ]; ei = Ei4[:, T0:T0 + G, r4, 0:NJJ]
                w_ = [x[:, 0:G * NJJ].rearrange("q (g j) -> q g j", g=G) for x in s5w]
                k5 = ['s5w']
                tt('dve', w_[0], Dre, er, ALU.mult, [pdk, 'tabE'], k5); tt('dve', w_[1], Dim, ei, ALU.mult, [pdk, 'tabE'], k5)
                tt('pool', w_[2], w_[0], w_[1], ALU.add, k5, k5)
                tt('dve', w_[0], Dim, er, ALU.mult, [pdk, 'tabE'] + k5, k5); tt('dve', w_[1], Dre, ei, ALU.mult, [pdk, 'tabE'], k5)
                tt('pool', w_[3], w_[0], w_[1], ALU.subtract, k5, k5)
                for g in range(G):
                    pi_i = pis[g]
                    rb = rhoc[:, pi_i:pi_i + 1]
                    for ri, (cc, qq) in enumerate(((w_[2], w_[4]), (w_[3], w_[5]))):
                        P.add('dve', lambda e, cc=cc[:, g, 0:NJ], qq=qq[:, g, 0:NJ], rbb=rb.to_broadcast([128, NJ]),
                              ini=hprev[:, l, pi_i, ri:ri + 1]:
                              e.tensor_tensor_scan(out=qq, data0=rbb, data1=cc, initial=ini,
                                                   op0=ALU.mult, op1=ALU.add), r=k5 + ['hprev', 'tabE'], w=k5)
                        if seg == 0:
                            stt(qq[:, g, NJ:NJJ], s_in[:, l, :, ri, pi_i], rb, cc[:, g, NJ:NJJ], ALU.mult, ALU.add, k5 + ['s_in', 'tabE'], k5)
                tt('dve', w_[6], w_[4], er, ALU.mult, k5 + ['tabE'], k5); tt('dve', w_[7], w_[5], ei, ALU.mult, k5 + ['tabE'], k5)
                tt('pool', w_[8], w_[6], w_[7], ALU.subtract, k5, k5)
                tt('dve', w_[6], w_[5], er, ALU.mult, k5 + ['tabE'], k5); tt('dve', w_[7], w_[4], ei, ALU.mult, k5 + ['tabE'], k5)
                tt('pool', w_[9], w_[6], w_[7], ALU.add, k5, k5)
                for ri, S_ in enumerate((w_[8], w_[9])):
                    cp('act', Sp4[:, T0:T0 + G, r4, ri, 0:1], hp4[:, l, T0:T0 + G, r4, ri:ri + 1], ['hprev'], sks)
                    cp('act', Sp4[:, T0:T0 + G, r4, ri, 1:NJ], S_[:, :, 0:NJ - 1], k5, sks)
                    cp('dve', hp4[:, l, T0:T0 + G, r4, ri:ri + 1], S_[:, :, NJ - 1:NJ], k5 + sks, ['hprev'])
                    if seg == 0:
                        for g in range(G):
                            cp('pool', outst[:, 0:128].rearrange("q (b r a) -> q b r a", b=4, r=2)[:, :, ri, pis[g]], S_[:, g, NJ:NJJ],
                               k5, ['outst_s'])
            if seg == 0:
                for b in range(4):
                    for ri, dst in enumerate((hr_s, hi_s)):
                        tr(ps3[0:16, 0:128], outst[:, (b * 2 + ri) * 16:(b * 2 + ri) * 16 + 16], ident[:], ['outst_s'], ['ps3'])
                        cp('dve', outst[0:16, 128:256], ps3[0:16, 0:128], ['ps3'], ['outst_t'])
                        dma('sp', dap(dst, ((l * 4 + b) * 16) * 128, [[128, 16], [1, 128]]), outst[0:16, 128:256], ['outst_t'], ['out'])
            if seg == nseg - 1:
                for ri, dst in enumerate((hr_p, hi_p)):
                    cp('dve', outst[:, 256 + 16 * ri:272 + 16 * ri], hprev[:, l, :, ri], ['hprev'], ['outst_h'])
                    tr(ps3[0:16, 0:128], outst[:, 256 + 16 * ri:272 + 16 * ri], ident[:], ['outst_h'], ['ps3'])
                    cp('dve', outst[0:16, 288:416], ps3[0:16, 0:128], ['ps3'], ['outst_t2'])
                    dma('sp', dap(dst, l * 2048, [[128, 16], [1, 128]]), outst[0:16, 288:416], ['outst_t2'], ['out'])
            dma('sp', tabA[:, 0:4096], dap(s5tab, l * 128 * TABW + TW3, [[TABW, 128], [1, 4096]]), ['s5tab'], ['tabW'])
            def evac_rope(dst_fn, wkey):
                def f(pm, pmk):
                    evi[0] += 1
                    i2 = evi[0] % 2
                    xf = tA[i2]; xfk = 'tA%d' % i2
                    xb_ = sq[i2]; xbk = 'sq%d' % i2
                    pw_ = ps3 if i2 == 0 else ps6
                    pwk = 'ps3' if i2 == 0 else 'ps6'
                    cp('act', xb_[:, 0:TSEG], pm[:, 0:TSEG], [pmk], [xbk])
                    if N > TSEG:
                        cp('act', xb_[:, TSEG:N], ps2[:, 0:NS], ['ps2'], [xbk])
                    mm(pw_[:, 0:TSEG], permb[:], xb_[:, 0:TSEG], True, True, [xbk, 'ident'], [pwk])
                    tt('dve', xf[:, 0:TSEG], pw_[:, 0:TSEG], sinT[:, 0:TSEG], ALU.mult, [pwk, 'rope'], [xfk])
                    if N > TSEG:
                        mm(pw_[:, 0:NS], permb[:], xb_[:, TSEG:N], True, True, [xbk, 'ident', xfk], [pwk])
                        tt('dve', xf[:, TSEG:N], pw_[:, 0:NS], sinT[:, TSEG:N], ALU.mult, [pwk, 'rope'], [xfk])
                    tt('pool', xw0[:, 0:N], xb_[:, 0:N], cosT[:, 0:N], ALU.mult, [xbk, 'rope'], ['xsw0'])
                    tt('dve', dst_fn(0, N), xf[:, 0:N], xw0[:, 0:N], ALU.add, [xfk, 'xsw0'], [wkey])
                return f

            for kt in range(2):
                cp('pool', kT[:, kt, 0:128], khalo[:, l, kt, :], ['khalo'], ['kT'])
            cp('pool', Vt[:, 0, :], vhalo[:, l, :], ['vhalo'], ['Vt'])
            for kt in range(2):
                proj(l, None, in_pieces(l, [(OFF_K + 128 * kt, 128)]), N,
                     evac_rope(lambda a, b, kt=kt: kT[:, kt, 128 + a:128 + b], 'kT'), hT, ['hT'])
            for kt in range(2):
                proj(l, None, in_pieces(l, [(OFF_V + 128 * kt, 128)]), N,
                     evac_store(lambda a, b: tB[0][:, a:b], 'tB0'), hT, ['hT'])
                nblk = NB + (1 if seg == 0 else 0)
                for b in range(nblk):
                    rows = 128 if b < NB else NS
                    tr(psT[0:rows, b * 128:b * 128 + 128], tB[0][:, b * 128:b * 128 + rows], identb[:], ['tB0'], ['psT'])
                cp('dve', Vt[:, 1:1 + NB, 128 * kt:128 * kt + 128], psT[:, 0:NB * 128].rearrange("q (a b) -> q a b", b=128), ['psT'], ['Vt'])
                if seg == 0:
                    cp('dve', Vt[0:NS, NB + 1, 128 * kt:128 * kt + 128], psT[0:NS, NB * 128:NB * 128 + 128], ['psT'], ['Vt'])
            for tq in range(8):
                proj(l, None, in_pieces(l, qcols(OFF_Q, tq)), N,
                     evac_rope(lambda a, b, tq=tq: mix[:, 4 + tq, a:b], 'mixb%d' % tq), hT, ['hT'])
            for kt in range(2):
                cp('pool', khalo[:, l, kt, :], kT[:, kt, TSEG:TSEG + 128], ['kT'], ['khalo'])
            cp('pool', vhalo[:, l, :], Vt[:, NB, :], ['Vt'], ['vhalo'])
            if seg == nseg - 1:
                for kt in range(2):
                    tr(psT[:, 0:128], kT[:, kt, TSEG:TSEG + 128], identb[:], ['kT'], ['psT'])
                    cp('dve', PTs[:, 128 * kt:128 * kt + 128], psT[:, 0:128], ['psT'], ['PTs0'])
                dma('pool', dap(nk_p, l * 128 * 256, [[256, 128], [1, 256]]), PTs[:, 0:256], ['PTs0'], ['out'])
                dma('pool', dap(nv_p, l * 128 * 256, [[256, 128], [1, 256]]), Vt[:, NB, :], ['Vt'], ['out'])
            if seg == 0:
                for kt in range(2):
                    tr(psT[0:NS, 0:128], kT[:, kt, 128 + TSEG:128 + N], identb[:], ['kT'], ['psT'])
                    cp('dve', PTs[0:NS, 128 * kt:128 * kt + 128], psT[0:NS, 0:128], ['psT'], ['PTs0'])
                dma('pool', dap(nk_s, l * NS * 256, [[256, NS], [1, 256]]), PTs[0:NS, 0:256], ['PTs0'], ['out'])
                dma('pool', dap(nv_s, l * NS * 256, [[256, NS], [1, 256]]), Vt[0:NS, NB + 1, :], ['Vt'], ['out'])
            for T in range(4):
                uvw = mix[:, 12 + T, 0:N].rearrange("q (j s) -> q j s", s=8)
                for t in range(8):
                    reg = psS[:, (t // 4) * 512 + (t % 4) * W:(t // 4) * 512 + (t % 4) * W + NJJ]
                    for s in range(t + 1):
                        mm(reg, BD[:, T, t - s, :], uvw[:, :, s], s == 0, False, ['tabA', 'mixc%d' % T], ['psS'])
                    for r4 in range(4):
                        pi_i = 4 * T + r4
                        for ri in range(2):
                            o_ = psS[32 * r4:32 * r4 + 32, (t // 4) * 512 + (t % 4) * W:(t // 4) * 512 + (t % 4) * W + NJJ]
                            mm(o_, W3[:, t, pi_i, ri, :], Sprev[:, pi_i, ri, 0:NJJ], False, (ri == 1),
                               ['tabW', 'Sprev%d' % pi_i], ['psS'], tile_position=(0, 32 * r4))
                yv = tA[2][:, 0:N].rearrange("q (j s) -> q s j", s=8)
                for hb in range(2):
                    pv = psS[:, hb * 512:hb * 512 + 4 * W].rearrange("q (t j) -> q t j", j=W)[:, :, 0:NJJ]
                    uv2 = mix[:, 12 + T, 0:N].rearrange("q (j s) -> q s j", s=8)[:, 4 * hb:4 * hb + 4, :]
                    stt(yv[:, 4 * hb:4 * hb + 4, :], uv2, dcol[:, l * 4 + T:l * 4 + T + 1], pv, ALU.mult, ALU.add,
                        ['psS', 'mixc%d' % T, 'dcol'], ['tA2'])
                y_ = tA[2][:, 0:N]; g1 = tA[3][:, 0:N]
                tt('pool', g1, y_, y_, ALU.mult, ['tA2'], ['tA3'])
                ts('dve', g1, g1, 0.044715, ALU.mult, ['tA3'], ['tA3'], s2=1.0, op1=ALU.add)
                tt('pool', g1, g1, y_, ALU.mult, ['tA3', 'tA2'], ['tA3'])
                act(g1, g1, AF.Sigmoid, ['tA3'], ['tA3'], scale=2.0 * math.sqrt(2.0 / math.pi))
                tt('dve', yb[:, T, 0:N], y_, g1, ALU.mult, ['tA2', 'tA3'], ['yb'])
            for T in range(4):
                for kc in range(4):
                    mm(ps0[:, 0:TSEG], glwb[:, kc, 128 * T:128 * T + 128], yb[:, kc, 0:TSEG], kc == 0, kc == 3, ['glwb', 'yb'], ['ps0'])
                    if N > TSEG:
                        mm(ps2[:, 0:NS], glwb[:, kc, 128 * T:128 * T + 128], yb[:, kc, TSEG:N], kc == 0, kc == 3, ['glwb', 'yb'], ['ps2'])
                g2 = tA[3]
                act(g2[:, 0:TSEG], ps0[:, 0:TSEG], AF.Sigmoid, ['ps0'], ['tA3'], bias=gbcol[:, l * 4 + T:l * 4 + T + 1])
                if N > TSEG:
                    act(g2[:, TSEG:N], ps2[:, 0:NS], AF.Sigmoid, ['ps2'], ['tA3'], bias=gbcol[:, l * 4 + T:l * 4 + T + 1])
                tt('dve', mix[:, 12 + T, 0:N], yb[:, T, 0:N], g2[:, 0:N], ALU.mult, ['yb', 'tA3'], ['mixc%d' % T])
            for T in range(4):
                proj(l, None, in_pieces(l, [(OFF_GC + 128 * T, 128)]), N, evac_gate(12 + T, 'mixc%d' % T), hT, ['hT'])

            units = [(bq, kt, half) for bq in range(NB) for kt in range(2) for half in range(2)]
            import os as _os
            if _os.environ.get('V1'):
                psT2 = [psT, psT]; psTk = ['psT', 'psT']
            else:
                psT2 = [psT, ps3[:].bitcast(BF16)]; psTk = ['psT', 'ps3']
            Obuf = [ps6, ps2]; Obk = ['ps6', 'ps2']

            def stage1(u, bq, kt, half):
                par = u % 2
                Sb = psS if par == 0 else ps01
                Sk = ['psS'] if par == 0 else ['ps0', 'ps1']
                msk = maskF if (seg == 0 and bq == 0) else maskA
                hs = slice(64 * half, 64 * half + 64)
                for i2 in range(2):
                    for i in (2 * i2, 2 * i2 + 1):
                        tq = 4 * kt + i
                        mm(Sb[:, i * 256:(i + 1) * 256], mix[hs, 4 + tq, 128 * bq:128 * bq + 128],
                           kT[hs, kt, 128 * bq:128 * bq + 256], i % 2 == 0, False, ['mixb%d' % tq, 'kT'], Sk)
                    mm(Sb[:, i2 * 512:(i2 + 1) * 512], identb[:], msk[:], False, True, ['masks', 'ident'], Sk)

            def stage1b(u, bq, kt, half):
                par = u % 2
                Sb = psS if par == 0 else ps01
                Sk = ['psS'] if par == 0 else ['ps0', 'ps1']
                h0 = l * 16 + 8 * kt + 4 * half
                softmax(Sb[:, :].rearrange("q (a b) -> q a b", b=256), 128, 4, 256, sinkbc[:, h0:h0 + 4], Pf2[par], Pb2[par], Sk,
                        'p%d' % par, so=32 * par)

            def stage2(u, bq, kt, half):
                par = u % 2
                hs = slice(64 * half, 64 * half + 64)
                pT = psT2[par]; pTk = psTk[par]; PT_ = PTs2[par]; PTk = 'PTs%d' % par
                Ob = Obuf[kt]; Ok = Obk[kt]
                for i in range(4):
                    for kb in range(2):
                        tr(pT[:, (2 * i + kb) * 128:(2 * i + kb) * 128 + 128], Pb2[par][:, i, kb * 128:kb * 128 + 128], identb[:],
                           ['p%dPb' % par], [pTk])
                cp('act', PT_[:], pT[:], [pTk], [PTk])
                for i in range(4):
                    for kb in range(2):
                        mm(Ob[hs, i * 128:i * 128 + 128], Vt[:, bq + kb, (2 * kt + half) * 64:(2 * kt + half) * 64 + 64],
                           PT_[:, (2 * i + kb) * 128:(2 * i + kb) * 128 + 128], kb == 0, kb == 1, ['Vt', PTk], [Ok],
                           tile_position=(0, 64 * half))
                if half == 1:
                    for i in range(4):
                        cp('dve', mix[:, 4 + 4 * kt + i, 128 * bq:128 * bq + 128], Ob[:, i * 128:i * 128 + 128],
                           [Ok], ['mixb%d' % (4 * kt + i)])
            for u, (bq, kt, half) in enumerate(units):
                stage1(u, bq, kt, half)
                if u > 0:
                    stage2(u - 1, *units[u - 1])
                stage1b(u, bq, kt, half)
            stage2(len(units) - 1, *units[-1])
            if seg == 0:
                dma('pool', ckb, dap(ck, l * 4 * 128 * 256, [[256, 128], [128 * 256, 4], [1, 256]]), [], ['iost0'])
                dma('pool', cvb, dap(cv, l * 4 * 128 * 256, [[256, 128], [128 * 256, 4], [1, 256]]), [], ['iost0'])
                for b in range(4):
                    for kt in range(2):
                        tr(psT[:, (2 * b + kt) * 128:(2 * b + kt) * 128 + 128], ckb[:, b, kt * 128:kt * 128 + 128], identb[:], ['iost0'], ['psT'])
                cp('dve', ckT, psT[:].rearrange("q (b k w) -> q b k w", b=4, k=2), ['psT'], ['iost0'])
                for kt in range(2):
                    for half in range(2):
                        hs = slice(64 * half, 64 * half + 64)
                        for i in range(4):
                            tq = 4 * kt + i
                            qs = mix[hs, 4 + tq, TSEG:N]
                            for b in range(4):
                                mm(psS[0:NS, b * 128:b * 128 + 128], qs, ckT[hs, b, kt, :], b == 0, False, ['mixb%d' % tq, 'iost0'], ['psS'])
                            mm(psS[0:NS, 0:512], identb[0:NS, 0:NS], maskS[:, 0:512], False, True, ['masks', 'ident'], ['psS'])
                            mm(psS[0:NS, 512:512 + NS], qs, kT[hs, kt, 128 + TSEG:128 + N], True, False, ['mixb%d' % tq, 'kT'], ['psS'])
                            mm(psS[0:NS, 512:512 + NS], identb[0:NS, 0:NS], maskS[:, 512:544], False, True, ['masks', 'ident'], ['psS'])
                            h0 = l * 16 + 8 * kt + 4 * half + i
                            softmax(psS[0:NS, 0:544].rearrange("q (a b) -> q a b", b=544), NS, 1, 544, sinkbc[0:NS, h0:h0 + 1],
                                    PfS.rearrange("q (a b) -> q a b", b=544), PbS.rearrange("q (a b) -> q a b", b=544), ['psS'], 'p0')
                            for b in range(4):
                                tr(psT[:, b * 32:b * 32 + 32], PbS[:, b * 128:b * 128 + 128], identb[0:NS, 0:NS], ['p0Pb'], ['psT'])
                            tr(psT[0:NS, 128:160], PbS[:, 512:544], identb[0:NS, 0:NS], ['p0Pb'], ['psT'])
                            cp('act', PTS2[:, 0:4, :], psT[:, 0:128].rearrange("q (a b) -> q a b", b=32), ['psT'], ['PTS2'])
                            cp('act', PTS2[0:NS, 4, :], psT[0:NS, 128:160], ['psT'], ['PTS2'])
                            vc = (2 * kt + half) * 64
                            for b in range(4):
                                mm(ps6[hs, i * 32:i * 32 + 32], cvb[:, b, vc:vc + 64], PTS2[:, b, :], b == 0, False, ['iost0', 'PTS2'], ['ps6'],
                                   tile_position=(0, 64 * half))
                            mm(ps6[hs, i * 32:i * 32 + 32], Vt[0:NS, NB + 1, vc:vc + 64], PTS2[0:NS, 4, :], False, True, ['Vt', 'PTS2'], ['ps6'],
                               tile_position=(0, 64 * half))
                    for i in range(4):
                        cp('act' if i % 2 else 'dve', mix[:, 4 + 4 * kt + i, TSEG:N], ps6[:, i * 32:i * 32 + 32], ['ps6'], ['mixb%d' % (4 * kt + i)])
            for tq in range(8):
                proj(l, None, in_pieces(l, qcols(OFF_GB, tq)), N, evac_gate(4 + tq, 'mixb%d' % tq), hT, ['hT'])

            for T in range(4):
                proj(l, None, in_pieces(l, [(OFF_UA + 128 * T, 128)]), N,
                     evac_store(lambda a, b, T=T: mix[:, T, a:b], 'mixa%d' % T), hT, ['hT'])
            for T in range(4):
                proj(l, None, in_pieces(l, [(OFF_VA + 128 * T, 128)]), N,
                     evac_store(lambda a, b: tB[1][:, a:b], 'tB0'), hT, ['hT'])
                nblk = NB + (1 if seg == 0 else 0)
                for b in range(nblk):
                    rows = 128 if b < NB else NS
                    tr(psT[0:rows, b * 128:b * 128 + 128], tB[1][:, b * 128:b * 128 + rows], identb[:], ['tB0'], ['psT'])
                cp('dve', vatm[:, 0:NB, 128 * T:128 * T + 128], psT[:, 0:NB * 128].rearrange("q (a b) -> q a b", b=128), ['psT'], ['vatm'])
                if seg == 0:
                    cp('dve', vatm[0:NS, NB, 128 * T:128 * T + 128], psT[0:NS, NB * 128:NB * 128 + 128], ['psT'], ['vatm'])
            if seg == 0:
                dma('pool', dap(va_s, l * NS * 512, [[512, NS], [1, 512]]), vatm[0:NS, NB, :], ['vatm'], ['out'])
            for h in range(4):
                lh = l * 4 + h
                for b in range(NB):
                    mm(ps3[:, b * 128:b * 128 + 128], vatm[:, b, 128 * h:128 * h + 128], wsT[:, lh, :], b == 0, False, ['vatm', 'wsT'], ['ps3'])
                for b in range(NB):
                    P.add('pe', lambda e, b=b, lh=lh: e.matmul(ps3[:, b * 128:b * 128 + 128], lhsT=ones1[0:1, :],
                                                              rhs=bsrow[0:1, (lh % 4) * 128:(lh % 4) * 128 + 128], start=False, stop=(b == NB - 1)),
                          r=['bsrow', 'ident'], w=['ps3'])
                tt('dve', mix[:, h, 0:TSEG], mix[:, h, 0:TSEG], ps3[:, 0:TSEG], ALU.mult, ['ps3', 'mixa%d' % h], ['mixa%d' % h])
                if seg == 0:
                    mm(ps6[:, 0:NS], vatm[0:NS, NB, 128 * h:128 * h + 128], wsS[:, lh, :], True, False, ['vatm', 'wsS'], ['ps6'])
                    P.add('pe', lambda e, lh=lh: e.matmul(ps6[:, 0:NS], lhsT=ones1[0:1, :], rhs=bsS[0:1, lh % 4, :], start=False, stop=True),
                          r=['bsS', 'ident'], w=['ps6'])
                    tt('dve', mix[:, h, TSEG:N], mix[:, h, TSEG:N], ps6[:, 0:NS], ALU.mult, ['ps6', 'mixa%d' % h], ['mixa%d' % h])
            for T in range(4):
                proj(l, None, in_pieces(l, [(OFF_GA + 128 * T, 128)]), N, evac_gate(T, 'mixa%d' % T), hT, ['hT'])

            allmix = ['mixa%d' % i for i in range(4)] + ['mixb%d' % i for i in range(8)] + ['mixc%d' % i for i in range(4)]

            def evac_res(j):
                def f(pm, pmk):
                    tt('dve', xT[:, j, 0:TSEG], xT[:, j, 0:TSEG], pm[:, 0:TSEG], ALU.add, [pmk, 'xT'], ['xT'])
                    if N > TSEG:
                        tt('dve', xT[:, j, TSEG:N], xT[:, j, TSEG:N], ps2[:, 0:NS], ALU.add, ['ps2', 'xT'], ['xT'])
                return f
            for j in range(KD):
                proj(l, None, out_pieces(l, j), N, evac_res(j), mix, allmix)

        rmsnorm(N, depth, lambda k, g: stt(xT[:, k, 0:N], xT[:, k, 0:N], g, rstd[:, 0:N], ALU.mult, ALU.mult,
                                           ['xT', 'rstd', 'gcol'], ['xT']))
        oblocks = [(128, b * 128, dap(y_p, (seg * TSEG + b * 128) * D, [[D, 128], [1, D]])) for b in range(NB)]
        if seg == 0:
            oblocks.append((NS, TSEG, y_s.ap()))
        for bi, (rows, c0, dst) in enumerate(oblocks):
            io = iost[bi % 2]; iok = 'iost0'
            for k4 in range(4):
                for kk in range(4):
                    k = 4 * k4 + kk
                    tr(ps3[0:rows, kk * 128:kk * 128 + 128], xT[:, k, c0:c0 + rows], ident[:], ['xT'], ['ps3'])
                cp('dve' if k4 % 2 == 0 else 'act', io[0:rows, 512 * k4:512 * k4 + 512], ps3[0:rows, :], ['ps3'], [iok])
            dma('sp', dst, io[0:rows, :], [iok], ['out'])

    import os
    ks = os.environ.get('KSTOP')
    if ks:
        print('NOPS', len(P.ops)); print('LASTOPS', [(i, o['eng'], o['line']) for i, o in list(enumerate(P.ops))[max(0, int(ks) - 3):int(ks)]]); P.ops = P.ops[:int(ks)]
    P.emit()
    st.close()
    return nc


def _consts(seqlen):
    half = 8
    inv = (500000.0 ** (-np.arange(half, dtype=np.float32) * 2.0 / 16.0)).astype(np.float32)
    pos = np.concatenate([np.arange(seqlen, dtype=np.float32), np.arange(8, dtype=np.float32) + PAST] * 1)
    pos = np.concatenate([np.arange(seqlen, dtype=np.float32)] + [np.arange(8, dtype=np.float32) + np.float32(PAST)] * 4)
    ang = pos[None, :] * inv[:, None]
    cosv = np.cos(ang).astype(np.float32); sinv = np.sin(ang).astype(np.float32)
    C = np.ones((128, pos.shape[0]), np.float32); S = np.zeros((128, pos.shape[0]), np.float32)
    for h0 in (0, 64):
        C[h0:h0 + 8] = cosv; C[h0 + 8:h0 + 16] = cosv
        S[h0:h0 + 8] = -sinv; S[h0 + 8:h0 + 16] = sinv
    i = np.arange(128)[:, None]; j = np.arange(256)[None, :]
    diff = i + 128 - j
    valid = (diff >= 0) & (diff < 128)
    mA = np.where(valid, 0.0, NEGM).astype(np.float32)
    mF = mA.copy(); mF[:, 0:128] = NEGM
    mA2 = np.concatenate([mA, mA], 1); mF2 = np.concatenate([mF, mF], 1)
    mS = np.full((32, 544), NEGM, np.float32)
    for b in range(4):
        for t in range(8):
            q = 8 * b + t
            for jj in range(128):
                if jj > t:
                    mS[q, b * 128 + jj] = 0.0
            for s in range(t + 1):
                mS[q, 512 + 8 * b + s] = 0.0
    tril = np.tril(np.ones((128, 128), np.float32))
    maskm = np.zeros((128, 2), np.float32)
    for q in range(128):
        maskm[q, (q // 16) % 2] = 1.0
    bd = np.zeros((32, 32), np.float32)
    for b in range(4):
        for t in range(8):
            for s in range(t + 1):
                bd[8 * b + t, 8 * b + s] = 1.0
    perm = np.zeros((128, 128), np.float32)
    for m_ in range(128):
        mm_ = m_ % 64
        if mm_ < 8:
            perm[m_ + 8, m_] = 1.0
        elif mm_ < 16:
            perm[m_ - 8, m_] = 1.0
    return dict(c_perm=perm, c_ident=np.eye(128, dtype=np.float32), c_tril=tril, c_maskA=mA2, c_maskF=mF2, c_maskS=mS, c_maskm=maskm,
                c_bd=bd, c_cos=C, c_sin=S)


_NC_CACHE = {}


def kernel(x_prompt, x_sample, cache_swa_k, cache_swa_v, state_ssm_re, state_ssm_im,
           norm_g, final_norm_g, w_in, w_out, chunk_w_s, chunk_b_s, attn_sinks,
           ssm_a_re, ssm_a_im, ssm_log_dt, ssm_b_re, ssm_b_im, ssm_c_re, ssm_c_im,
           ssm_d, glu_w, glu_b):
    f = lambda a: np.ascontiguousarray(np.asarray(a, dtype=np.float32))
    depth = int(np.asarray(w_in).shape[0]); seqlen = int(np.asarray(x_prompt).shape[1])
    nseg = seqlen // TSEG
    key = (depth, nseg)
    if key not in _NC_CACHE:
        _NC_CACHE[key] = build_program(depth, nseg)
    nc = _NC_CACHE[key]
    cst = _consts(seqlen)
    shared = dict(
        ng=np.concatenate([f(norm_g).reshape(depth * 16, 128), f(final_norm_g).reshape(16, 128)], 0),
        w_in=f(w_in), w_out=f(w_out), cws=f(chunk_w_s), cbs=f(chunk_b_s).reshape(1, -1), sinks=f(attn_sinks).reshape(1, -1),
        a_re=f(ssm_a_re), a_im=f(ssm_a_im), ldt=f(ssm_log_dt), b_re=f(ssm_b_re), b_im=f(ssm_b_im),
        c_re=f(ssm_c_re).reshape(depth, 512, 64), c_im=f(ssm_c_im).reshape(depth, 512, 64),
        dsk=f(ssm_d).reshape(depth * 4, 128), glw=f(glu_w), glb=f(glu_b).reshape(depth * 4, 128), **cst)
    xpf, xsf = f(x_prompt), f(x_sample)
    ckf, cvf = f(cache_swa_k), f(cache_swa_v)
    srf, sif = f(state_ssm_re), f(state_ssm_im)
    nb = xpf.shape[0]
    in_maps = []
    for c in range(8):
        m = dict(shared)
        m["xp"] = xpf[c % nb]
        m["xs"] = xsf[4 * c:4 * c + 4].reshape(NS, D)
        m["ck"] = ckf[:, 4 * c:4 * c + 4].reshape(depth, 4, 128, 256)
        m["cv"] = cvf[:, 4 * c:4 * c + 4].reshape(depth, 4, 128, 256)
        m["sre"] = srf[:, 4 * c:4 * c + 4].reshape(depth, 4, 16, 128)
        m["sim"] = sif[:, 4 * c:4 * c + 4].reshape(depth, 4, 16, 128)
        in_maps.append(m)
    res = run_bass_kernel_spmd(nc, in_maps, core_ids=list(range(8))).results
    y_prompt = np.stack([res[b]["y_p"] for b in range(nb)], 0)
    y_sample = np.concatenate([res[c]["y_s"].reshape(4, 8, D) for c in range(8)], 0)
    nkp = np.stack([res[b]["nk_p"].reshape(depth, 128, 4, 64) for b in range(nb)], 1)
    nvp = np.stack([res[b]["nv_p"].reshape(depth, 128, 4, 64) for b in range(nb)], 1)
    nks = np.concatenate([res[c]["nk_s"].reshape(depth, 4, 8, 4, 64) for c in range(8)], 1)
    nvs = np.concatenate([res[c]["nv_s"].reshape(depth, 4, 8, 4, 64) for c in range(8)], 1)
    hrp = np.stack([res[b]["hr_p"].reshape(depth, 32, 64) for b in range(nb)], 1)
    hip = np.stack([res[b]["hi_p"].reshape(depth, 32, 64) for b in range(nb)], 1)
    hrs = np.concatenate([res[c]["hr_s"].reshape(depth, 4, 32, 64) for c in range(8)], 1)
    his = np.concatenate([res[c]["hi_s"].reshape(depth, 4, 32, 64) for c in range(8)], 1)
    vas = np.concatenate([res[c]["va_s"].reshape(depth, 4, 8, 512) for c in range(8)], 1)
    return tuple(np.ascontiguousarray(a, dtype=np.float32) for a in
                 (y_prompt, y_sample, nkp, nvp, nks, nvs, hrp, hip, hrs, his, vas))
```

```python
import math
import numpy as np
import ml_dtypes
import concourse.bass as bass
import concourse.mybir as mybir
from concourse.bass_utils import run_bass_kernel_spmd

F32 = mybir.dt.float32
BF16 = mybir.dt.bfloat16
ALU = mybir.AluOpType
AF = mybir.ActivationFunctionType
AX = mybir.AxisListType

D = 2048
KD = 16
DIN = 5120
DEPTH = 4
SEQ = 4096
TSEG = 512
NSEG = SEQ // TSEG
NB = TSEG // 128
NS = 32
NJ = TSEG // 8
PAST = 16384
OFF_UA, OFF_VA, OFF_GA, OFF_Q, OFF_K, OFF_V, OFF_GB, OFF_UC, OFF_GC = 0, 512, 1024, 1536, 2560, 2816, 3072, 4096, 4608
NEGM = -60000.0
TW1 = 0
TW3 = 4096
TBD = 8192
TER = 10240
TEI = TER + 16 * (NJ + 4)
TRHO = TEI + 16 * (NJ + 4)
TABW = TRHO + 16


def dap(t, off, dims):
    return bass.AP(tensor=t, offset=off, ap=[[s, c] for s, c in dims])


class Prog:
    def __init__(self, nc):
        self.nc = nc
        self.ops = []
        self.last_w = {}
        self.readers = {}

    def add(self, eng, fn, r=(), w=(), dma=False):
        deps = set()
        for k in r:
            if k in self.last_w:
                deps.add(self.last_w[k])
        for k in w:
            if k in self.last_w:
                deps.add(self.last_w[k])
            for x in self.readers.get(k, ()):
                deps.add(x)
        idx = len(self.ops)
        deps.discard(idx)
        import sys as _s
        fr = _s._getframe(1)
        ln = []
        while fr is not None and len(ln) < 3:
            ln.append(fr.f_lineno); fr = fr.f_back
        self.ops.append(dict(eng=eng, fn=fn, deps=sorted(deps), dma=dma, line=ln))
        for k in w:
            self.last_w[k] = idx
            self.readers[k] = []
        for k in r:
            if k not in w:
                self.readers.setdefault(k, []).append(idx)
        return idx

    def emit(self):
        nc = self.nc
        ops = self.ops
        engs = ['pe', 'act', 'dve', 'pool', 'sp']
        RR = {'sp': 8, 'pool': 4, 'act': 2, 'pe': 1, 'dve': 1}
        for o in ops:
            o['sig'] = False
        for i, o in enumerate(ops):
            for d in o['deps']:
                p = ops[d]
                if p['dma']:
                    continue
                if p['eng'] == 'pe' and o['eng'] == 'pe' and not o['dma']:
                    continue
                p['sig'] = True
        cnt = {e: 0 for e in engs}
        dcnt = {e: 0 for e in engs}
        for o in ops:
            e = o['eng']
            if o['dma']:
                n = dcnt[e]
                o['dslot'] = n % RR[e]
                o['dval'] = 16 * (n // RR[e] + 1)
                o['dprev'] = None
                dcnt[e] += 1
            elif o['sig']:
                cnt[e] += 1
                o['val'] = cnt[e]
        lastslot = {}
        for i, o in enumerate(ops):
            if o['dma']:
                key = (o['eng'], o['dslot'])
                o['dprev'] = lastslot.get(key)
                lastslot[key] = i
        import contextlib
        with contextlib.ExitStack() as st:
            sems = {e: st.enter_context(nc.semaphore("s_" + e)) for e in engs}
            dsems = {}
            for e in ['sp', 'pool', 'act']:
                for s in range(RR[e]):
                    dsems[(e, s)] = st.enter_context(nc.semaphore("d_%s%d" % (e, s)))
            block = st.enter_context(nc.Block())
            per = {e: [o for o in ops if o['eng'] == e] for e in engs}

            def run(e, engine):
                waited = {}

                def need(sem_key, sem, val):
                    if waited.get(sem_key, 0) >= val:
                        return
                    engine.wait_ge(sem, val)
                    waited[sem_key] = val

                for o in per[e]:
                    for d in o['deps']:
                        p = ops[d]
                        if p['dma']:
                            need(('d', p['eng'], p['dslot']), dsems[(p['eng'], p['dslot'])], p['dval'])
                        else:
                            if p['eng'] == 'pe' and e == 'pe' and not o['dma']:
                                continue
                            need(('c', p['eng']), sems[p['eng']], p['val'])
                    if o['dma'] and o['dprev'] is not None:
                        p = ops[o['dprev']]
                        need(('d', p['eng'], p['dslot']), dsems[(p['eng'], p['dslot'])], p['dval'])
                    ins = o['fn'](engine)
                    if o['dma']:
                        ins.then_inc(dsems[(e, o['dslot'])], 16)
                    elif o['sig']:
                        ins.then_inc(sems[e], 1)
                if e in ('sp', 'pool', 'act'):
                    tot = {}
                    for o in per[e]:
                        if o['dma']:
                            tot[o['dslot']] = o['dval']
                    for s, v in tot.items():
                        engine.wait_ge(dsems[(e, s)], v)

            @block.tensor
            def _(en):
                run('pe', en)

            @block.scalar
            def _(en):
                run('act', en)

            @block.vector
            def _(en):
                run('dve', en)

            @block.gpsimd
            def _(en):
                run('pool', en)

            @block.sync
            def _(en):
                run('sp', en)


def build_program(depth=DEPTH, nseg=NSEG, debug=False):
    nc = bass.Bass("TRN2", target_bir_lowering=False)
    P = Prog(nc)
    seqlen = nseg * TSEG

    def din(name, shape, dt=F32):
        return nc.dram_tensor(name, list(shape), dt, kind="ExternalInput")

    def dout(name, shape, dt=F32):
        return nc.dram_tensor(name, list(shape), dt, kind="ExternalOutput")

    xp = din("xp", [seqlen, D]); xs = din("xs", [NS, D])
    ck = din("ck", [depth, 4, 128, 256]); cv = din("cv", [depth, 4, 128, 256])
    sre = din("sre", [depth, 4, 16, 128]); sim = din("sim", [depth, 4, 16, 128])
    ng = din("ng", [depth * 16 + 16, 128])
    w_in = din("w_in", [depth, D, DIN]); w_out = din("w_out", [depth, D, D])
    cws = din("cws", [depth, 4, 128, 128]); cbs = din("cbs", [1, depth * 4 * 128])
    sinks = din("sinks", [1, depth * 16])
    a_re = din("a_re", [depth, 32, 64]); a_im = din("a_im", [depth, 32, 64]); ldt = din("ldt", [depth, 32])
    b_re = din("b_re", [depth, 32, 64, 16]); b_im = din("b_im", [depth, 32, 64, 16])
    c_re = din("c_re", [depth, 512, 64]); c_im = din("c_im", [depth, 512, 64])
    dsk = din("dsk", [depth * 4, 128]); glw = din("glw", [depth, 512, 512]); glb = din("glb", [depth * 4, 128])
    c_ident = din("c_ident", [128, 128]); c_tril = din("c_tril", [128, 128])
    c_maskA = din("c_maskA", [128, 512]); c_maskF = din("c_maskF", [128, 512]); c_maskS = din("c_maskS", [32, 544])
    c_maskm = din("c_maskm", [128, 2]); c_bd = din("c_bd", [32, 32])
    c_perm = din("c_perm", [128, 128]); c_cos = din("c_cos", [128, seqlen + NS]); c_sin = din("c_sin", [128, seqlen + NS])

    y_p = dout("y_p", [seqlen, D]); y_s = dout("y_s", [NS, D])
    nk_p = dout("nk_p", [depth, 128, 256]); nv_p = dout("nv_p", [depth, 128, 256])
    nk_s = dout("nk_s", [depth, NS, 256]); nv_s = dout("nv_s", [depth, NS, 256])
    hr_p = dout("hr_p", [depth, 16, 128]); hi_p = dout("hi_p", [depth, 16, 128])
    hr_s = dout("hr_s", [depth, 4, 16, 128]); hi_s = dout("hi_s", [depth, 4, 16, 128])
    va_s = dout("va_s", [depth, NS, 512])
    s5tab = nc.dram_tensor("s5tab", [depth, 128, TABW], F32)
    wcache = nc.dram_tensor("wcache", [depth * 56, 128, KD * 128], BF16)

    NMAX = TSEG + NS
    import contextlib
    st = contextlib.ExitStack()

    def sb(name, shape, dt=F32):
        return st.enter_context(nc.sbuf_tensor(name, list(shape), dt))

    def ps(name, shape, dt=F32):
        return st.enter_context(nc.psum_tensor(name, list(shape), dt))

    xT = sb("xT", [128, KD, NMAX]); hT = sb("hT", [128, KD, NMAX], BF16); mix = sb("mix", [128, KD, NMAX], BF16)
    kT = sb("kT", [128, 2, 128 + NMAX], BF16); Vt = sb("Vt", [128, NB + 2, 256], BF16)
    vatm = sb("vatm", [128, NB + 1, 512], BF16)
    khalo = sb("khalo", [128, depth, 2, 128], BF16); vhalo = sb("vhalo", [128, depth, 256], BF16)
    cosT = sb("cosT", [128, NMAX]); sinT = sb("sinT", [128, NMAX])
    NWB = 3
    wb = [sb("wb%d" % i, [128, KD, 128], BF16) for i in range(NWB)]
    iost0_ = sb("iost0", [128, D]); iost = [iost0_, iost0_]
    ident = sb("ident", [128, 128]); identb = sb("identb", [128, 128], BF16); onesb = sb("onesb", [128, 128], BF16)
    ones1 = sb("ones1", [1, 128]); epsc = sb("epsc", [128, 1])
    maskA = sb("maskA", [128, 512], BF16); maskF = sb("maskF", [128, 512], BF16); maskS = sb("maskS", [32, 544], BF16)
    gcol = sb("gcol", [128, depth * 16 + 16]); dcol = sb("dcol", [128, depth * 4]); gbcol = sb("gbcol", [128, depth * 4])
    sinkbc = sb("sinkbc", [128, depth * 16]); bsrow = sb("bsrow", [1, 512]); bsS = sb("bsS", [1, 4, 32])
    wsT = sb("wsT", [128, depth * 4, 128], BF16); wsS = sb("wsS", [32, depth * 4, 32], BF16)
    hprev = sb("hprev", [128, depth, 16, 2]); s_in = sb("s_in", [128, depth, 4, 2, 16])
    tabA = sb("tabA", [128, 4096 + 2048])
    tabE = sb("tabE", [128, TABW - TER])
    Sprev = sb("Sprev", [128, 16, 2, NJ + 4], BF16)
    sq = [sb("sq%d" % i, [128, NMAX], BF16) for i in range(2)]
    rstd = sb("rstd", [128, NMAX]); tA = [sb("tA%d" % i, [128, NMAX]) for i in range(4)]
    xsw0_ = sb("xsw0", [128, NMAX]); xsw = [xsw0_, xsw0_]
    tB0_ = sb("tB0", [128, NMAX], BF16); tB = [tB0_, tB0_]
    yb = sb("yb", [128, 4, NMAX], BF16)
    attb = sb("attb", [128, 4096], BF16); Pf2 = [attb[:, 2048 * i:2048 * i + 1024].rearrange("q (a b) -> q a b", b=256) for i in range(2)]; Pb2 = [attb[:, 2048 * i + 1024:2048 * i + 2048].rearrange("q (a b) -> q a b", b=256) for i in range(2)]; Pf = Pf2[0]; Pb = Pb2[0]; glwb = attb[:, 0:2048].rearrange("q (a b) -> q a b", b=512); PTs = sb("PTs", [128, 1024], BF16); PTs2 = [PTs, sb("PTsB", [128, 1024], BF16)]
    sm = sb("sm", [128, 64])
    iob = iost0_[:].bitcast(BF16)
    ckb = iob[:, 0:1024].rearrange("q (a b) -> q a b", b=256); cvb = iob[:, 1024:2048].rearrange("q (a b) -> q a b", b=256)
    ckT = iob[:, 2048:3072].rearrange("q (b k w) -> q b k w", b=4, k=2)
    PfS = Pf[0:32].rearrange("q a b -> q (a b)")[:, 0:544]; PbS = Pb[0:32].rearrange("q a b -> q (a b)")[:, 0:544]; PTS2 = sb("PTS2", [128, 5, 32], BF16)
    s5w = [sb("s5w%d" % i, [128, 4 * NJ]) for i in range(10)]
    outst = sb("outst", [128, 416])
    ps01 = ps("ps01", [128, 1024]); ps0 = ps01[:, 0:512]; ps1 = ps01[:, 512:1024]; ps2 = ps("ps2", [128, 512]); ps3 = ps("ps3", [128, 512])
    psS = ps("psS", [128, 1024]); ps6 = ps("ps6", [128, 512]); psT = ps("psT", [128, 1024], BF16)

    tabAb = tabA[:].bitcast(BF16)

    def E(name):
        return {'pe': 'pe', 'act': 'act', 'dve': 'dve', 'pool': 'pool', 'sp': 'sp'}[name]

    def dma(q, out, in_, r, w):
        return P.add(q, lambda e: e.dma_start(out=out, in_=in_, allow_slow_non_contiguous=True), r=r, w=w, dma=True)

    def mm(out, lhsT, rhs, start, stop, r, w, **kw):
        return P.add('pe', lambda e: e.matmul(out, lhsT=lhsT, rhs=rhs, start=start, stop=stop, **kw), r=r, w=w)

    def tr(out, in_, idn, r, w, **kw):
        return P.add('pe', lambda e: e.transpose(out, in_, idn, **kw), r=r + ['ident'], w=w)

    def act(out, in_, func, r, w, scale=1.0, bias=None, accum=None):
        def f(e):
            kw = dict(out=out, in_=in_, func=func, scale=scale)
            if bias is not None:
                kw['bias'] = bias
            if accum is not None:
                kw['accum_out'] = accum
            return e.activation(**kw)
        return P.add('act', f, r=r, w=w)

    def tt(eng, out, a, b, op, r, w):
        return P.add(eng, lambda e: e.tensor_tensor(out=out, in0=a, in1=b, op=op), r=r, w=w)

    def ts(eng, out, a, s1, op0, r, w, s2=None, op1=None):
        if op1 is None:
            return P.add(eng, lambda e: e.tensor_scalar(out=out, in0=a, scalar1=s1, scalar2=None, op0=op0), r=r, w=w)
        return P.add(eng, lambda e: e.tensor_scalar(out=out, in0=a, scalar1=s1, scalar2=s2, op0=op0, op1=op1), r=r, w=w)

    def stt(out, a, s, b, op0, op1, r, w):
        return P.add('dve', lambda e: e.scalar_tensor_tensor(out=out, in0=a, scalar=s, in1=b, op0=op0, op1=op1), r=r, w=w)

    def cp(eng, out, in_, r, w):
        if eng == 'act':
            return P.add('act', lambda e: e.copy(out=out, in_=in_), r=r, w=w)
        return P.add(eng, lambda e: e.tensor_copy(out=out, in_=in_), r=r, w=w)

    def memset(eng, ap, v, w):
        return P.add(eng, lambda e: e.memset(ap, v), r=[], w=w)

    dma('sp', ident[:], c_ident.ap(), [], ['ident'])
    dma('pool', identb[:], c_ident.ap(), [], ['ident'])
    dma('pool', maskA[:], c_maskA.ap(), [], ['masks']); dma('pool', maskF[:], c_maskF.ap(), [], ['masks'])
    dma('pool', maskS[:], c_maskS.ap(), [], ['masks'])
    memset('dve', onesb[:], 1.0, ['ident']); memset('dve', ones1[:], 1.0, ['ident']); memset('dve', epsc[:], 1e-5, ['ident'])
    xw0 = xsw[0]
    permb = sb("permb", [128, 128], BF16)
    dma('pool', permb[:], c_perm.ap(), [], ['ident'])
    memset('pool', kT[:], 0.0, ['kT']); memset('pool', Vt[:], 0.0, ['Vt'])
    memset('pool', khalo[:], 0.0, ['khalo']); memset('pool', vhalo[:], 0.0, ['vhalo']); memset('dve', hprev[:], 0.0, ['hprev'])
    dma('sp', sinkbc[:], dap(sinks, 0, [[0, 128], [1, depth * 16]]), [], ['sinkbc'])
    nrow = depth * 16 + 16
    dma('sp', iost[0][0:nrow, 0:128], ng.ap(), [], ['iost0'])
    tr(ps3[:, 0:nrow], iost[0][0:nrow, 0:128], ident[0:nrow, 0:nrow], ['iost0'], ['ps3'])
    cp('dve', gcol[:], ps3[:, 0:nrow], ['ps3'], ['gcol'])
    dma('sp', iost[0][0:depth * 4, 128:256], dsk.ap(), [], ['iost0'])
    tr(ps3[:, 0:depth * 4], iost[0][0:depth * 4, 128:256], ident[0:depth * 4, 0:depth * 4], ['iost0'], ['ps3'])
    cp('dve', dcol[:], ps3[:, 0:depth * 4], ['ps3'], ['dcol'])
    dma('sp', iost[0][0:depth * 4, 256:384], glb.ap(), [], ['iost0'])
    tr(ps3[:, 0:depth * 4], iost[0][0:depth * 4, 256:384], ident[0:depth * 4, 0:depth * 4], ['iost0'], ['ps3'])
    cp('dve', gbcol[:], ps3[:, 0:depth * 4], ['ps3'], ['gbcol'])
    dma('sp', iost[0][:, 1536:1664], c_tril.ap(), [], ['iost0'])
    for lh in range(depth * 4):
        dma('sp', iost[0][:, 512:640], dap(cws, lh * 128 * 128, [[128, 128], [1, 128]]), [], ['iost0'])
        tt('dve', iost[0][:, 640:768], iost[0][:, 512:640], iost[0][:, 1536:1664], ALU.mult, ['iost0', 'iost0'], ['iost0b'])
        tr(ps3[:, 0:128], iost[0][:, 640:768], ident[:], ['iost0b'], ['ps3'])
        cp('dve', wsT[:, lh, :], ps3[:, 0:128], ['ps3'], ['wsT'])
    dma('sp', iost[0][0:32, 1664:1696], c_bd.ap(), [], ['iost0'])
    for lh in range(depth * 4):
        for b in range(4):
            dma('sp', iost[0][8 * b:8 * b + 8, 1024:1056].rearrange("q (a s) -> q a s", s=8), dap(cws, lh * 128 * 128, [[128, 8], [0, 4], [1, 8]]), [], ['iost0'])
        tt('dve', iost[0][0:32, 1056:1088], iost[0][0:32, 1024:1056], iost[0][0:32, 1664:1696], ALU.mult, ['iost0', 'iost0'], ['iost0b'])
        tr(ps3[0:32, 0:32], iost[0][0:32, 1056:1088], ident[0:32, 0:32], ['iost0b'], ['ps3'])
        cp('dve', wsS[:, lh, :], ps3[0:32, 0:32], ['ps3'], ['wsS'])
    for l in range(depth):
        for b in range(4):
            for ri, src in enumerate((sre, sim)):
                dma('sp', iost[0][0:16, 0:128], dap(src, ((l * 4 + b) * 16) * 128, [[128, 16], [1, 128]]), [], ['iost0'])
                tr(ps3[:, 0:16], iost[0][0:16, 0:128], ident[0:16, 0:16], ['iost0'], ['ps3'])
                cp('dve', s_in[:, l, b, ri, :], ps3[:, 0:16], ['ps3'], ['s_in'])

    W = NJ + 4
    maskm = sb("maskm", [128, 2])
    dma('sp', maskm[:], c_maskm.ap(), [], ['maskm'])
    mixf = mix[:].rearrange("q a b -> q (a b)").bitcast(F32)
    xTf = xT[:].rearrange("q a b -> q (a b)")
    _o = [0]

    def carve(buf, n, lim):
        a = _o[0]; _o[0] += n
        assert _o[0] <= lim
        return buf[:, a:a + n]
    sc = [carve(mixf, 16, 4352) for i in range(72)]
    Bn = [carve(mixf, 256, 4352).rearrange("q (a b) -> q a b", b=16) for i in range(2)]
    Cn = [carve(mixf, 256, 4352).rearrange("q (a b) -> q a b", b=64) for i in range(2)]
    Cp = [carve(mixf, 512, 4352).rearrange("q (a b) -> q a b", b=32) for i in range(2)]
    Y3 = [carve(mixf, 512, 4352).rearrange("q (a b) -> q a b", b=32) for i in range(2)]
    _o[0] = 4 * 16 * W
    tmpY = [carve(xTf, 256, 8704).rearrange("q (a b) -> q a b", b=16) for i in range(4)]
    Cpad = carve(xTf, 128, 8704)
    pw_all = carve(xTf, 288, 8704).rearrange("q (k r a) -> q k r a", k=9, r=2)
    CpB_all = carve(xTf, 512, 8704).bitcast(BF16).rearrange("q (r a c) -> q r a c", r=2, a=16)
    W1st = tabAb[:, 0:8192]; BDst = tabAb[:, 8192:12288]; W3half = iost0_[:].bitcast(BF16)
    Est = tabE[:, 0:2 * 16 * W].rearrange("q (a b c) -> q a b c", a=2, b=16)
    Etmp = [xTf[:, i * 16 * W:(i + 1) * 16 * W].rearrange("q (a b) -> q a b", b=W) for i in range(4)]
    Y1all = hT[:].rearrange("q a b -> q (a b)")[:, 0:8192].rearrange("q (k r a c) -> q k r a c", k=8, r=2, a=16)
    rs = ['s5p', 'xT', 'hT', 'tabA', 'tabW', 'tabE', 'iost0'] + ['mixa%d' % i for i in range(4)] + ['mixb%d' % i for i in range(8)] + ['mixc%d' % i for i in range(4)]
    for l in range(depth):
        cnt_i = [0]

        def T_():
            cnt_i[0] += 1
            return sc[cnt_i[0] - 1]
        are, aim, dtl = T_(), T_(), T_()
        for m in range(2):
            dma('sp', are[64 * m:64 * m + 64, :], dap(a_re, l * 2048 + m * 64, [[1, 64], [128, 16]]), [], rs)
            dma('sp', aim[64 * m:64 * m + 64, :], dap(a_im, l * 2048 + m * 64, [[1, 64], [128, 16]]), [], rs)
            dma('sp', dtl[64 * m:64 * m + 64, :], dap(ldt, l * 32 + m, [[0, 64], [2, 16]]), [], rs)
            dma('sp', Bn[0][64 * m:64 * m + 64, :, :], dap(b_re, l * 32768 + m * 1024, [[16, 64], [2048, 16], [1, 16]]), [], rs)
            dma('sp', Bn[1][64 * m:64 * m + 64, :, :], dap(b_im, l * 32768 + m * 1024, [[16, 64], [2048, 16], [1, 16]]), [], rs)
        dma('sp', Cn[0][:], dap(c_re, l * 32768, [[64, 128], [8192, 4], [1, 64]]), [], rs)
        dma('sp', Cn[1][:], dap(c_im, l * 32768, [[64, 128], [8192, 4], [1, 64]]), [], rs)
        def poly(x, coef):
            t = T_()
            n = len(coef) - 1
            ts('dve', t[:], x, float(coef[n]), ALU.mult, rs, rs)
            for k in range(n - 1, 0, -1):
                stt(t[:], t[:], float(coef[k]), x, ALU.add, ALU.mult, rs, rs)
            ts('dve', t[:], t[:], float(coef[0]), ALU.add, rs, rs)
            return t

        def exp_poly(x, deg, nsq):
            xs_ = T_(); ts('dve', xs_[:], x, 1.0 / (2 ** nsq), ALU.mult, rs, rs)
            e = poly(xs_[:], [1.0 / math.factorial(k) for k in range(deg + 1)])
            for _ in range(nsq):
                tt('dve', e[:], e[:], e[:], ALU.mult, rs, rs)
            return e
        dt_ = exp_poly(dtl[:], 12, 3)
        ar = T_(); tt('dve', ar[:], are[:], dt_[:], ALU.mult, rs, rs)
        th = T_(); tt('dve', th[:], aim[:], dt_[:], ALU.mult, rs, rs)
        mag = exp_poly(ar[:], 7, 0)
        TWO_PI = 2.0 * math.pi
        MAGIC = 12582912.0
        u = T_(); ts('dve', u[:], th[:], 1.0 / TWO_PI, ALU.mult, rs, rs)
        n_ = T_(); ts('dve', n_[:], u[:], MAGIC, ALU.add, rs, rs)
        n2 = T_(); ts('dve', n2[:], n_[:], MAGIC, ALU.subtract, rs, rs)
        fr0 = T_(); tt('dve', fr0[:], u[:], n2[:], ALU.subtract, rs, rs)
        xq = T_(); ts('dve', xq[:], fr0[:], TWO_PI / 4.0, ALU.mult, rs, rs)
        x2 = T_(); tt('dve', x2[:], xq[:], xq[:], ALU.mult, rs, rs)
        ps_ = poly(x2[:], [(-1.0) ** k / math.factorial(2 * k + 1) for k in range(7)])
        sn = T_(); tt('dve', sn[:], ps_[:], xq[:], ALU.mult, rs, rs)
        cs = poly(x2[:], [(-1.0) ** k / math.factorial(2 * k) for k in range(8)])
        for _ in range(2):
            s2_ = T_(); tt('dve', s2_[:], sn[:], cs[:], ALU.mult, rs, rs)
            q2_ = T_(); tt('dve', q2_[:], sn[:], sn[:], ALU.mult, rs, rs)
            sn = T_(); ts('dve', sn[:], s2_[:], 2.0, ALU.mult, rs, rs)
            cs = T_(); ts('dve', cs[:], q2_[:], -2.0, ALU.mult, rs, rs, s2=1.0, op1=ALU.add)
        lr = T_(); tt('dve', lr[:], mag[:], cs[:], ALU.mult, rs, rs)
        li = T_(); tt('dve', li[:], mag[:], sn[:], ALU.mult, rs, rs)

        def cmul(ar_, ai_, br_, bi_, bc=None):
            t1, t2, t3, t4, orr, oi = T_(), T_(), T_(), T_(), T_(), T_()
            tt('dve', t1[:], ar_, br_, ALU.mult, rs, rs); tt('dve', t2[:], ai_, bi_, ALU.mult, rs, rs)
            tt('dve', orr[:], t1[:], t2[:], ALU.subtract, rs, rs)
            tt('dve', t3[:], ar_, bi_, ALU.mult, rs, rs); tt('dve', t4[:], ai_, br_, ALU.mult, rs, rs)
            tt('dve', oi[:], t3[:], t4[:], ALU.add, rs, rs)
            return orr, oi
        nr = T_(); ts('dve', nr[:], lr[:], -1.0, ALU.add, rs, rs)
        d1 = T_(); tt('dve', d1[:], are[:], are[:], ALU.mult, rs, rs)
        d2 = T_(); tt('dve', d2[:], aim[:], aim[:], ALU.mult, rs, rs)
        den = T_(); tt('dve', den[:], d1[:], d2[:], ALU.add, rs, rs)
        rden = T_(); P.add('dve', lambda e, o=rden, i=den: e.reciprocal(out=o[:], in_=i[:]), r=rs, w=rs)
        nai = T_(); ts('dve', nai[:], aim[:], -1.0, ALU.mult, rs, rs)
        f0r, f0i = cmul(nr[:], li[:], are[:], nai[:])
        fr_ = T_(); tt('dve', fr_[:], f0r[:], rden[:], ALU.mult, rs, rs)
        fi_ = T_(); tt('dve', fi_[:], f0i[:], rden[:], ALU.mult, rs, rs)
        for ri in range(2):
            for T in range(4):
                for m in range(2):
                    ts('dve', Cpad[:, 64 * m:64 * m + 64], Cn[ri][:, T, :], maskm[:, m:m + 1], ALU.mult, rs, rs)
                tr(ps3[:, 0:128], Cpad[:], ident[:], rs, ['ps3'])
                cp('dve', Cp[ri][:, 4 * T:4 * T + 4, :], ps3[:, 0:128].rearrange("q (a b) -> q a b", b=32), ['ps3'], rs)
        pw = pw_all
        memset('dve', pw[:, 0, 0, :], 1.0, rs); memset('dve', pw[:, 0, 1, :], 0.0, rs)
        cp('dve', pw[:, 1, 0, :], lr[:], rs, rs); cp('dve', pw[:, 1, 1, :], li[:], rs, rs)
        base_i = cnt_i[0]
        for k in range(2, 9):
            cnt_i[0] = base_i
            orr, oi = cmul(pw[:, k - 1, 0, :], pw[:, k - 1, 1, :], lr[:], li[:])
            cp('dve', pw[:, k, 0, :], orr[:], rs, rs); cp('dve', pw[:, k, 1, :], oi[:], rs, rs)
        W3h = W3half.rearrange("q (t a r c) -> q t a r c", t=4, a=16, r=2)
        for t in range(8):
            pr = pw[:, t + 1, 0, :].unsqueeze(2).to_broadcast([128, 16, 32])
            pi_ = pw[:, t + 1, 1, :].unsqueeze(2).to_broadcast([128, 16, 32])
            tt('dve', Y3[0][:], Cp[0][:], pr, ALU.mult, rs, rs); tt('dve', Y3[1][:], Cp[1][:], pi_, ALU.mult, rs, rs)
            tt('dve', W3h[:, t % 4, :, 0, :], Y3[0][:], Y3[1][:], ALU.subtract, rs, rs)
            tt('dve', Y3[0][:], Cp[0][:], pi_, ALU.mult, rs, rs); tt('dve', Y3[1][:], Cp[1][:], pr, ALU.mult, rs, rs)
            tt('dve', Y3[0][:], Y3[0][:], Y3[1][:], ALU.add, rs, rs)
            ts('dve', W3h[:, t % 4, :, 1, :], Y3[0][:], -1.0, ALU.mult, rs, rs)
            if t % 4 == 3:
                dma('sp', dap(s5tab, l * 128 * TABW + TW3 + (t // 4) * 2048, [[TABW, 128], [1, 2048]]), iost0_[:], rs, ['s5tab'] + rs)
        memset('dve', hT[:].rearrange("q a b -> q (a b)")[:, 0:8192], 0.0, rs)
        for k in range(8):
            cnt_i[0] = base_i
            Fr, Fi = cmul(pw[:, k, 0, :], pw[:, k, 1, :], fr_[:], fi_[:])
            Frb = Fr[:].unsqueeze(2).to_broadcast([128, 16, 16]); Fib = Fi[:].unsqueeze(2).to_broadcast([128, 16, 16])
            tt('dve', tmpY[0][:], Bn[0][:], Frb, ALU.mult, rs, rs); tt('dve', tmpY[1][:], Bn[1][:], Fib, ALU.mult, rs, rs)
            tt('dve', tmpY[2][:], Bn[1][:], Frb, ALU.mult, rs, rs); tt('dve', tmpY[3][:], Bn[0][:], Fib, ALU.mult, rs, rs)
            for m in range(2):
                sl = slice(64 * m, 64 * m + 64)
                tt('dve', Y1all[sl, k, 0, :, 16 * m:16 * m + 16], tmpY[0][sl], tmpY[1][sl], ALU.subtract, rs, rs)
                tt('dve', Y1all[sl, k, 1, :, 16 * m:16 * m + 16], tmpY[2][sl], tmpY[3][sl], ALU.add, rs, rs)
        W1v = W1st.rearrange("q (s t r c) -> q s t r c", s=8, t=4, r=2)
        for s in range(8):
            k = 7 - s
            for T in range(4):
                for ri in range(2):
                    for r4 in range(4):
                        tr(psT[32 * r4:32 * r4 + 32, 0:128], Y1all[:, k, ri, 4 * T + r4, :], identb[:], rs, ['psT'], tile_position=(0, 32 * r4))
                    cp('dve', W1v[:, s, T, ri, :], psT[:, 0:128], ['psT'], rs)
        memset('dve', BDst, 0.0, rs)
        BDv = BDst.rearrange("q (t a c) -> q t a c", t=4, a=8)
        CpB = CpB_all
        cp('dve', CpB[:, 0], Cp[0][:], rs, rs); ts('dve', CpB[:, 1], Cp[1][:], -1.0, ALU.mult, rs, rs)
        for T in range(4):
            for tau in range(8):
                for r4 in range(4):
                    pi_i = 4 * T + r4
                    for ri in range(2):
                        mm(ps3[32 * r4:32 * r4 + 32, tau * 32:tau * 32 + 32], Y1all[:, tau, ri, pi_i, :], CpB[:, ri, pi_i, :],
                           ri == 0, ri == 1, rs, ['ps3'], tile_position=(0, 32 * r4))
            for r4 in range(4):
                sl = slice(32 * r4, 32 * r4 + 32)
                cp('dve', BDv[sl, T, :, 32 * r4:32 * r4 + 32], ps3[sl, 0:256].rearrange("q (a b) -> q a b", b=32), ['ps3'], rs)
        cnt_i[0] = base_i + 6
        r2 = T_(); tt('dve', r2[:], mag[:], mag[:], ALU.mult, rs, rs)
        r4_ = T_(); tt('dve', r4_[:], r2[:], r2[:], ALU.mult, rs, rs)
        rho = T_(); tt('dve', rho[:], r4_[:], r4_[:], ALU.mult, rs, rs)
        ur, ui = cs, sn
        for _ in range(3):
            a2 = T_(); tt('dve', a2[:], ur[:], ur[:], ALU.mult, rs, rs)
            b2 = T_(); tt('dve', b2[:], ui[:], ui[:], ALU.mult, rs, rs)
            ab = T_(); tt('dve', ab[:], ur[:], ui[:], ALU.mult, rs, rs)
            ur = T_(); tt('dve', ur[:], a2[:], b2[:], ALU.subtract, rs, rs)
            ui = T_(); ts('dve', ui[:], ab[:], 2.0, ALU.mult, rs, rs)
        cp('dve', Est[:, 0, :, 0], ur[:], rs, rs)
        cp('dve', Est[:, 1, :, 0], ui[:], rs, rs)
        k = 1
        while k < NJ:
            n = min(k, NJ - k)
            br_ = Est[:, 0, :, k - 1:k].to_broadcast([128, 16, n]); bi_ = Est[:, 1, :, k - 1:k].to_broadcast([128, 16, n])
            xr = Est[:, 0, :, 0:n]; xi = Est[:, 1, :, 0:n]
            tt('dve', Etmp[0][:, :, 0:n], xr, br_, ALU.mult, rs, rs); tt('dve', Etmp[1][:, :, 0:n], xi, bi_, ALU.mult, rs, rs)
            tt('dve', Etmp[2][:, :, 0:n], xr, bi_, ALU.mult, rs, rs); tt('dve', Etmp[3][:, :, 0:n], xi, br_, ALU.mult, rs, rs)
            tt('dve', Est[:, 0, :, k:k + n], Etmp[0][:, :, 0:n], Etmp[1][:, :, 0:n], ALU.subtract, rs, rs)
            tt('dve', Est[:, 1, :, k:k + n], Etmp[2][:, :, 0:n], Etmp[3][:, :, 0:n], ALU.add, rs, rs)
            k += n
        for ri in range(2):
            cp('dve', Est[:, ri, :, NJ:NJ + 4], Est[:, ri, :, 0:1].to_broadcast([128, 16, 4]), rs, rs)
        cp('dve', tabE[:, 32 * W:32 * W + 16], rho[:], rs, rs)
        dma('sp', dap(s5tab, l * 128 * TABW + TW1, [[TABW, 128], [1, 4096]]), tabA[:, 0:4096], rs, ['s5tab'] + rs)
        dma('sp', dap(s5tab, l * 128 * TABW + TBD, [[TABW, 128], [1, 2048]]), tabA[:, 4096:6144], rs, ['s5tab'] + rs)
        dma('sp', dap(s5tab, l * 128 * TABW + TER, [[TABW, 128], [1, TABW - TER]]), tabE[:], rs, ['s5tab'] + rs)

    wbi = [0]
    blkc = [0]
    cur_seg = [0]

    def proj(l, wsrc, pieces_fn, N, evac, hsrc, rkeys):
        i = wbi[0] % NWB
        wbi[0] += 1
        w = wb[i]
        wk = 'wb%d' % i
        blk = l * 56 + blkc[0]
        blkc[0] += 1
        ckey = 'wc%d' % blk
        wflat = w[:].rearrange("q a b -> q (a b)")
        if cur_seg[0] == 0:
            pieces_fn(w, wk)
            if nseg > 1:
                dma('sp', dap(wcache, blk * 128 * KD * 128, [[KD * 128, 128], [1, KD * 128]]), wflat, [wk], [ckey])
        else:
            dma('sp', wflat, dap(wcache, blk * 128 * KD * 128, [[KD * 128, 128], [1, KD * 128]]), [ckey], [wk])
        pm = ps0 if (wbi[0] % 2 == 0) else ps1
        pmk = 'ps0' if (wbi[0] % 2 == 0) else 'ps1'
        for k in range(KD):
            mm(pm[:, 0:TSEG], w[:, k, :], hsrc[:, k, 0:TSEG], k == 0, k == KD - 1, [wk] + rkeys, [pmk])
            if N > TSEG:
                mm(ps2[:, 0:NS], w[:, k, :], hsrc[:, k, TSEG:N], k == 0, k == KD - 1, [wk] + rkeys, ['ps2'])
        evac(pm, pmk)

    def in_pieces(l, col_pieces):
        def f(w, wk):
            off = 0
            for c0, wd in col_pieces:
                dma('pool', w[:, :, off:off + wd], dap(w_in, l * D * DIN + c0, [[DIN, 128], [128 * DIN, KD], [1, wd]]), [], [wk])
                off += wd
        return f

    def qcols(base, tq):
        kt, i = tq // 4, tq % 4
        return [(base + 64 * (8 * kt + i), 64), (base + 64 * (8 * kt + 4 + i), 64)]

    def out_pieces(l, j):
        def f(w, wk):
            base = l * D * D + j * 128
            dma('pool', w[:, 0:4, :], dap(w_out, base, [[D, 128], [128 * D, 4], [1, 128]]), [], [wk])
            dma('pool', w[:, 12:16, :], dap(w_out, base + 1536 * D, [[D, 128], [128 * D, 4], [1, 128]]), [], [wk])
            for half in range(2):
                for kt in range(2):
                    dma('pool', w[64 * half:64 * half + 64, 4 + 4 * kt:8 + 4 * kt, :],
                        dap(w_out, base + (512 + 512 * kt + 256 * half) * D, [[D, 64], [64 * D, 4], [1, 128]]), [], [wk])
        return f

    def rmsnorm(N, gidx, out_fn):
        for k in range(KD):
            s = sq[k % 2]
            act(s[:, 0:N], xT[:, k, 0:N], AF.Square, ['xT'], ['sq%d' % (k % 2)])
            mm(ps0[:, 0:TSEG], onesb[:], s[:, 0:TSEG], k == 0, k == KD - 1, ['sq%d' % (k % 2)], ['ps0'])
            if N > TSEG:
                mm(ps2[:, 0:NS], onesb[:], s[:, TSEG:N], k == 0, k == KD - 1, ['sq%d' % (k % 2)], ['ps2'])
        act(rstd[:, 0:TSEG], ps0[:, 0:TSEG], AF.Sqrt, ['ps0'], ['rstd'], scale=1.0 / D, bias=epsc[:])
        if N > TSEG:
            act(rstd[:, TSEG:N], ps2[:, 0:NS], AF.Sqrt, ['ps2'], ['rstd'], scale=1.0 / D, bias=epsc[:])
        P.add('dve', lambda e: e.reciprocal(out=rstd[:, 0:N], in_=rstd[:, 0:N]), r=['rstd'], w=['rstd'])
        for k in range(KD):
            out_fn(k, gcol[:, gidx * 16 + k:gidx * 16 + k + 1])

    def softmax(src3, np_, nh, Wd, sinkcols, Pf_, Pb_, rk, tag, so=0):
        smk = 'sm%d' % so
        mx = sm[0:np_, so + 0:so + nh]; m_ = sm[0:np_, so + 4:so + 4 + nh]; ng_ = sm[0:np_, so + 8:so + 8 + nh]; ssum = sm[0:np_, so + 12:so + 12 + nh]
        dd = sm[0:np_, so + 16:so + 16 + nh]; es = sm[0:np_, so + 20:so + 20 + nh]; den = sm[0:np_, so + 24:so + 24 + nh]; rd = sm[0:np_, so + 28:so + 28 + nh]
        P.add('dve', lambda e: e.tensor_reduce(out=mx, in_=src3, axis=AX.X, op=ALU.max), r=rk, w=[smk])
        ts('dve', m_, mx, 0.125, ALU.mult, [smk], [smk])
        tt('dve', m_, m_, sinkcols, ALU.max, [smk, 'sinkbc'], [smk])
        ts('dve', ng_, m_, -1.0, ALU.mult, [smk], [smk])
        for i in range(nh):
            act(Pf_[:, i, :], src3[:, i, :], AF.Exp, rk + [smk], [tag + 'Pf', 'glwb', smk + '2%d' % i], scale=0.125, bias=ng_[:, i:i + 1],
                accum=ssum[:, i:i + 1])
        tt('dve', dd, sinkcols, m_, ALU.subtract, [smk, 'sinkbc'], [smk])
        act(es, dd, AF.Exp, [smk], [smk])
        tt('dve', den, ssum, es, ALU.add, [smk] + [smk + '2%d' % i for i in range(nh)], [smk])
        P.add('dve', lambda e: e.reciprocal(out=rd, in_=den), r=[smk], w=[smk])
        tt('pool', Pb_, Pf_, rd.unsqueeze(2).to_broadcast([np_, nh, Wd]), ALU.mult, [tag + 'Pf', smk], [tag + 'Pb', 'glwb'])

    evi = [0]
    for seg in range(nseg):
        N = TSEG + (NS if seg == 0 else 0)
        cur_seg[0] = seg
        dma('sp', cosT[:, 0:TSEG], dap(c_cos, seg * TSEG, [[seqlen + NS, 128], [1, TSEG]]), [], ['rope'])
        dma('sp', sinT[:, 0:TSEG], dap(c_sin, seg * TSEG, [[seqlen + NS, 128], [1, TSEG]]), [], ['rope'])
        if seg == 0:
            dma('sp', cosT[:, TSEG:N], dap(c_cos, seqlen, [[seqlen + NS, 128], [1, NS]]), [], ['rope'])
            dma('sp', sinT[:, TSEG:N], dap(c_sin, seqlen, [[seqlen + NS, 128], [1, NS]]), [], ['rope'])
        blocks = [(b, 128, dap(xp, (seg * TSEG + b * 128) * D, [[D, 128], [1, D]]), b * 128) for b in range(NB)]
        if seg == 0:
            blocks.append((NB, NS, xs.ap(), TSEG))
        for bi, (b, rows, src, c0) in enumerate(blocks):
            io = iost[bi % 2]; iok = 'iost0'
            dma('sp', io[0:rows, :], src, [], [iok])
            for k4 in range(4):
                for kk in range(4):
                    k = 4 * k4 + kk
                    tr(ps3[:, kk * 128:kk * 128 + rows], io[0:rows, k * 128:(k + 1) * 128], ident[0:rows, 0:rows], [iok], ['ps3'])
                cp('dve' if k4 % 2 == 0 else 'act', xT[:, 4 * k4:4 * k4 + 4, c0:c0 + rows],
                   ps3[:, :].rearrange("q (a b) -> q a b", b=128)[:, :, 0:rows], ['ps3'], ['xT'])

        for l in range(depth):
            blkc[0] = 0
            dma('sp', tabA[:, 0:4096], dap(s5tab, l * 128 * TABW + TW1, [[TABW, 128], [1, 4096]]), ['s5tab'], ['tabW'])
            dma('sp', tabA[:, 4096:6144], dap(s5tab, l * 128 * TABW + TBD, [[TABW, 128], [1, 2048]]), ['s5tab'], ['tabA'])
            dma('sp', tabE[:], dap(s5tab, l * 128 * TABW + TER, [[TABW, 128], [1, TABW - TER]]), ['s5tab'], ['tabE'])
            dma('sp', bsrow[:], dap(cbs, l * 512, [[0, 1], [1, 512]]), [], ['bsrow'])
            if seg == 0:
                for b in range(4):
                    cp('pool', bsS[:, :, 8 * b:8 * b + 8], bsrow[:].rearrange("o (g t) -> o g t", t=128)[:, :, 0:8], ['bsrow'], ['bsS'])
            dma('pool', glwb[:], dap(glw, l * 512 * 512, [[512, 128], [128 * 512, 4], [1, 512]]), [], ['glwb', 'p0Pf', 'p0Pb'])
            W1 = tabAb[:, 0:8192].rearrange("q (s t r c) -> q s t r c", s=8, t=4, r=2)
            W3 = tabAb[:, 0:8192].rearrange("q (t a r c) -> q t a r c", t=8, a=16, r=2)
            BD = tabAb[:, 8192:12288].rearrange("q (t a c) -> q t a c", t=4, a=8)
            Er = tabE[:, 0:16 * W].rearrange("q (a b) -> q a b", b=W)
            Ei = tabE[:, 16 * W:32 * W].rearrange("q (a b) -> q a b", b=W)
            rhoc = tabE[:, 32 * W:32 * W + 16]
            NJJ = N // 8
            rmsnorm(N, l, lambda k, g: stt(hT[:, k, 0:N], xT[:, k, 0:N], g, rstd[:, 0:N], ALU.mult, ALU.mult,
                                           ['xT', 'rstd', 'gcol'], ['hT']))

            def evac_store(dst_fn, wkey):
                def f(pm, pmk):
                    evi[0] += 1
                    eng = 'act' if evi[0] % 2 == 0 else 'dve'
                    cp(eng, dst_fn(0, TSEG), pm[:, 0:TSEG], [pmk], [wkey])
                    if N > TSEG:
                        cp(eng, dst_fn(TSEG, N), ps2[:, 0:NS], ['ps2'], [wkey])
                return f

            def evac_gate(tile, wkey):
                def f(pm, pmk):
                    evi[0] += 1
                    t = tA[evi[0] % 2]; tk = 'tA%d' % (evi[0] % 2)
                    act(t[:, 0:TSEG], pm[:, 0:TSEG], AF.Silu, [pmk], [tk])
                    if N > TSEG:
                        act(t[:, TSEG:N], ps2[:, 0:NS], AF.Silu, ['ps2'], [tk])
                    tt('dve', mix[:, tile, 0:N], mix[:, tile, 0:N], t[:, 0:N], ALU.mult, [tk, wkey], [wkey])
                return f

            for T in range(4):
                proj(l, None, in_pieces(l, [(OFF_UC + 128 * T, 128)]), N,
                     evac_store(lambda a, b, T=T: mix[:, 12 + T, a:b], 'mixc%d' % T), hT, ['hT'])
            if seg == 0:
                for pi_i in range(16):
                    for ri in range(2):
                        cp('pool', Sprev[:, pi_i, ri, NJ:NJ + 4], s_in[:, l, :, ri, pi_i], ['s_in'], ['Sprev%d' % pi_i])
            G = 4 if NJJ * 8 <= 512 else 2
            WP = 512 // (2 * G)
            Er4 = Er.rearrange("q (t r) j -> q t r j", r=4); Ei4 = Ei.rearrange("q (t r) j -> q t r j", r=4)
            Sp4 = Sprev[:].rearrange("q (t r) i j -> q t r i j", r=4)
            hp4 = hprev[:].rearrange("q l (t r) i -> q l t r i", r=4)
            gi = 0
            for r4 in range(4):
              for T0 in range(0, 4, G):
                gi += 1
                pd = ps3 if gi % 2 == 0 else ps6
                pdk = 'ps3' if gi % 2 == 0 else 'ps6'
                pis = [4 * (T0 + g) + r4 for g in range(G)]
                sks = ['Sprev%d' % p_ for p_ in pis]
                for g in range(G):
                    T = T0 + g
                    uview = mix[32 * r4:32 * r4 + 32, 12 + T, 0:N].rearrange("q (j s) -> q j s", s=8)
                    for ri in range(2):
                        c0 = (g * 2 + ri) * WP
                        for s_ in range(8):
                            mm(pd[:, c0:c0 + NJJ], W1[32 * r4:32 * r4 + 32, s_, T, ri, :], uview[:, :, s_], s_ == 0, s_ == 7,
                               ['tabW', 'mixc%d' % T], [pdk], tile_position=(32 * r4, 0))
                Dv = pd[:, 0:512].rearrange("q (g r j) -> q g r j", g=G, r=2)
                Dre = Dv[:, :, 0, 0:NJJ]; Dim = Dv[:, :, 1, 0:NJJ]
                er = Er4[:, T0:T0 + G, r4, 0:NJJ]; ei = Ei4[:, T0:T0 + G, r4, 0:NJJ]
                w_ = [x[:, 0:G * NJJ].rearrange("q (g j) -> q g j", g=G) for x in s5w]
                k5 = ['s5w']
                tt('dve', w_[0], Dre, er, ALU.mult, [pdk, 'tabE'], k5); tt('dve', w_[1], Dim, ei, ALU.mult, [pdk, 'tabE'], k5)
                tt('pool', w_[2], w_[0], w_[1], ALU.add, k5, k5)
                tt('dve', w_[0], Dim, er, ALU.mult, [pdk, 'tabE'] + k5, k5); tt('dve', w_[1], Dre, ei, ALU.mult, [pdk, 'tabE'], k5)
                tt('pool', w_[3], w_[0], w_[1], ALU.subtract, k5, k5)
                for g in range(G):
                    pi_i = pis[g]
                    rb = rhoc[:, pi_i:pi_i + 1]
                    for ri, (cc, qq) in enumerate(((w_[2], w_[4]), (w_[3], w_[5]))):
                        P.add('dve', lambda e, cc=cc[:, g, 0:NJ], qq=qq[:, g, 0:NJ], rbb=rb.to_broadcast([128, NJ]),
                              ini=hprev[:, l, pi_i, ri:ri + 1]:
                              e.tensor_tensor_scan(out=qq, data0=rbb, data1=cc, initial=ini,
                                                   op0=ALU.mult, op1=ALU.add), r=k5 + ['hprev', 'tabE'], w=k5)
                        if seg == 0:
                            stt(qq[:, g, NJ:NJJ], s_in[:, l, :, ri, pi_i], rb, cc[:, g, NJ:NJJ], ALU.mult, ALU.add, k5 + ['s_in', 'tabE'], k5)
                tt('dve', w_[6], w_[4], er, ALU.mult, k5 + ['tabE'], k5); tt('dve', w_[7], w_[5], ei, ALU.mult, k5 + ['tabE'], k5)
                tt('pool', w_[8], w_[6], w_[7], ALU.subtract, k5, k5)
                tt('dve', w_[6], w_[5], er, ALU.mult, k5 + ['tabE'], k5); tt('dve', w_[7], w_[4], ei, ALU.mult, k5 + ['tabE'], k5)
                tt('pool', w_[9], w_[6], w_[7], ALU.add, k5, k5)
                for ri, S_ in enumerate((w_[8], w_[9])):
                    cp('act', Sp4[:, T0:T0 + G, r4, ri, 0:1], hp4[:, l, T0:T0 + G, r4, ri:ri + 1], ['hprev'], sks)
                    cp('act', Sp4[:, T0:T0 + G, r4, ri, 1:NJ], S_[:, :, 0:NJ - 1], k5, sks)
                    cp('dve', hp4[:, l, T0:T0 + G, r4, ri:ri + 1], S_[:, :, NJ - 1:NJ], k5 + sks, ['hprev'])
                    if seg == 0:
                        for g in range(G):
                            cp('pool', outst[:, 0:128].rearrange("q (b r a) -> q b r a", b=4, r=2)[:, :, ri, pis[g]], S_[:, g, NJ:NJJ],
                               k5, ['outst_s'])
            if seg == 0:
                for b in range(4):
                    for ri, dst in enumerate((hr_s, hi_s)):
                        tr(ps3[0:16, 0:128], outst[:, (b * 2 + ri) * 16:(b * 2 + ri) * 16 + 16], ident[:], ['outst_s'], ['ps3'])
                        cp('dve', outst[0:16, 128:256], ps3[0:16, 0:128], ['ps3'], ['outst_t'])
                        dma('sp', dap(dst, ((l * 4 + b) * 16) * 128, [[128, 16], [1, 128]]), outst[0:16, 128:256], ['outst_t'], ['out'])
            if seg == nseg - 1:
                for ri, dst in enumerate((hr_p, hi_p)):
                    cp('dve', outst[:, 256 + 16 * ri:272 + 16 * ri], hprev[:, l, :, ri], ['hprev'], ['outst_h'])
                    tr(ps3[0:16, 0:128], outst[:, 256 + 16 * ri:272 + 16 * ri], ident[:], ['outst_h'], ['ps3'])
                    cp('dve', outst[0:16, 288:416], ps3[0:16, 0:128], ['ps3'], ['outst_t2'])
                    dma('sp', dap(dst, l * 2048, [[128, 16], [1, 128]]), outst[0:16, 288:416], ['outst_t2'], ['out'])
            dma('sp', tabA[:, 0:4096], dap(s5tab, l * 128 * TABW + TW3, [[TABW, 128], [1, 4096]]), ['s5tab'], ['tabW'])
            def evac_rope(dst_fn, wkey):
                def f(pm, pmk):
                    evi[0] += 1
                    i2 = evi[0] % 2
                    xf = tA[i2]; xfk = 'tA%d' % i2
                    xb_ = sq[i2]; xbk = 'sq%d' % i2
                    pw_ = ps3 if i2 == 0 else ps6
                    pwk = 'ps3' if i2 == 0 else 'ps6'
                    cp('act', xb_[:, 0:TSEG], pm[:, 0:TSEG], [pmk], [xbk])
                    if N > TSEG:
                        cp('act', xb_[:, TSEG:N], ps2[:, 0:NS], ['ps2'], [xbk])
                    mm(pw_[:, 0:TSEG], permb[:], xb_[:, 0:TSEG], True, True, [xbk, 'ident'], [pwk])
                    tt('dve', xf[:, 0:TSEG], pw_[:, 0:TSEG], sinT[:, 0:TSEG], ALU.mult, [pwk, 'rope'], [xfk])
                    if N > TSEG:
                        mm(pw_[:, 0:NS], permb[:], xb_[:, TSEG:N], True, True, [xbk, 'ident', xfk], [pwk])
                        tt('dve', xf[:, TSEG:N], pw_[:, 0:NS], sinT[:, TSEG:N], ALU.mult, [pwk, 'rope'], [xfk])
                    tt('pool', xw0[:, 0:N], xb_[:, 0:N], cosT[:, 0:N], ALU.mult, [xbk, 'rope'], ['xsw0'])
                    tt('dve', dst_fn(0, N), xf[:, 0:N], xw0[:, 0:N], ALU.add, [xfk, 'xsw0'], [wkey])
                return f

            for kt in range(2):
                cp('pool', kT[:, kt, 0:128], khalo[:, l, kt, :], ['khalo'], ['kT'])
            cp('pool', Vt[:, 0, :], vhalo[:, l, :], ['vhalo'], ['Vt'])
            for kt in range(2):
                proj(l, None, in_pieces(l, [(OFF_K + 128 * kt, 128)]), N,
                     evac_rope(lambda a, b, kt=kt: kT[:, kt, 128 + a:128 + b], 'kT'), hT, ['hT'])
            for kt in range(2):
                proj(l, None, in_pieces(l, [(OFF_V + 128 * kt, 128)]), N,
                     evac_store(lambda a, b: tB[0][:, a:b], 'tB0'), hT, ['hT'])
                nblk = NB + (1 if seg == 0 else 0)
                for b in range(nblk):
                    rows = 128 if b < NB else NS
                    tr(psT[0:rows, b * 128:b * 128 + 128], tB[0][:, b * 128:b * 128 + rows], identb[:], ['tB0'], ['psT'])
                cp('dve', Vt[:, 1:1 + NB, 128 * kt:128 * kt + 128], psT[:, 0:NB * 128].rearrange("q (a b) -> q a b", b=128), ['psT'], ['Vt'])
                if seg == 0:
                    cp('dve', Vt[0:NS, NB + 1, 128 * kt:128 * kt + 128], psT[0:NS, NB * 128:NB * 128 + 128], ['psT'], ['Vt'])
            for tq in range(8):
                proj(l, None, in_pieces(l, qcols(OFF_Q, tq)), N,
                     evac_rope(lambda a, b, tq=tq: mix[:, 4 + tq, a:b], 'mixb%d' % tq), hT, ['hT'])
            for kt in range(2):
                cp('pool', khalo[:, l, kt, :], kT[:, kt, TSEG:TSEG + 128], ['kT'], ['khalo'])
            cp('pool', vhalo[:, l, :], Vt[:, NB, :], ['Vt'], ['vhalo'])
            if seg == nseg - 1:
                for kt in range(2):
                    tr(psT[:, 0:128], kT[:, kt, TSEG:TSEG + 128], identb[:], ['kT'], ['psT'])
                    cp('dve', PTs[:, 128 * kt:128 * kt + 128], psT[:, 0:128], ['psT'], ['PTs0'])
                dma('pool', dap(nk_p, l * 128 * 256, [[256, 128], [1, 256]]), PTs[:, 0:256], ['PTs0'], ['out'])
                dma('pool', dap(nv_p, l * 128 * 256, [[256, 128], [1, 256]]), Vt[:, NB, :], ['Vt'], ['out'])
            if seg == 0:
                for kt in range(2):
                    tr(psT[0:NS, 0:128], kT[:, kt, 128 + TSEG:128 + N], identb[:], ['kT'], ['psT'])
                    cp('dve', PTs[0:NS, 128 * kt:128 * kt + 128], psT[0:NS, 0:128], ['psT'], ['PTs0'])
                dma('pool', dap(nk_s, l * NS * 256, [[256, NS], [1, 256]]), PTs[0:NS, 0:256], ['PTs0'], ['out'])
                dma('pool', dap(nv_s, l * NS * 256, [[256, NS], [1, 256]]), Vt[0:NS, NB + 1, :], ['Vt'], ['out'])
            for T in range(4):
                uvw = mix[:, 12 + T, 0:N].rearrange("q (j s) -> q j s", s=8)
                for t in range(8):
                    reg = psS[:, (t // 4) * 512 + (t % 4) * W:(t // 4) * 512 + (t % 4) * W + NJJ]
                    for s in range(t + 1):
                        mm(reg, BD[:, T, t - s, :], uvw[:, :, s], s == 0, False, ['tabA', 'mixc%d' % T], ['psS'])
                    for r4 in range(4):
                        pi_i = 4 * T + r4
                        for ri in range(2):
                            o_ = psS[32 * r4:32 * r4 + 32, (t // 4) * 512 + (t % 4) * W:(t // 4) * 512 + (t % 4) * W + NJJ]
                            mm(o_, W3[:, t, pi_i, ri, :], Sprev[:, pi_i, ri, 0:NJJ], False, (ri == 1),
                               ['tabW', 'Sprev%d' % pi_i], ['psS'], tile_position=(0, 32 * r4))
                yv = tA[2][:, 0:N].rearrange("q (j s) -> q s j", s=8)
                for hb in range(2):
                    pv = psS[:, hb * 512:hb * 512 + 4 * W].rearrange("q (t j) -> q t j", j=W)[:, :, 0:NJJ]
                    uv2 = mix[:, 12 + T, 0:N].rearrange("q (j s) -> q s j", s=8)[:, 4 * hb:4 * hb + 4, :]
                    stt(yv[:, 4 * hb:4 * hb + 4, :], uv2, dcol[:, l * 4 + T:l * 4 + T + 1], pv, ALU.mult, ALU.add,
                        ['psS', 'mixc%d' % T, 'dcol'], ['tA2'])
                y_ = tA[2][:, 0:N]; g1 = tA[3][:, 0:N]
                tt('pool', g1, y_, y_, ALU.mult, ['tA2'], ['tA3'])
                ts('dve', g1, g1, 0.044715, ALU.mult, ['tA3'], ['tA3'], s2=1.0, op1=ALU.add)
                tt('pool', g1, g1, y_, ALU.mult, ['tA3', 'tA2'], ['tA3'])
                act(g1, g1, AF.Sigmoid, ['tA3'], ['tA3'], scale=2.0 * math.sqrt(2.0 / math.pi))
                tt('dve', yb[:, T, 0:N], y_, g1, ALU.mult, ['tA2', 'tA3'], ['yb'])
            for T in range(4):
                for kc in range(4):
                    mm(ps0[:, 0:TSEG], glwb[:, kc, 128 * T:128 * T + 128], yb[:, kc, 0:TSEG], kc == 0, kc == 3, ['glwb', 'yb'], ['ps0'])
                    if N > TSEG:
                        mm(ps2[:, 0:NS], glwb[:, kc, 128 * T:128 * T + 128], yb[:, kc, TSEG:N], kc == 0, kc == 3, ['glwb', 'yb'], ['ps2'])
                g2 = tA[3]
                act(g2[:, 0:TSEG], ps0[:, 0:TSEG], AF.Sigmoid, ['ps0'], ['tA3'], bias=gbcol[:, l * 4 + T:l * 4 + T + 1])
                if N > TSEG:
                    act(g2[:, TSEG:N], ps2[:, 0:NS], AF.Sigmoid, ['ps2'], ['tA3'], bias=gbcol[:, l * 4 + T:l * 4 + T + 1])
                tt('dve', mix[:, 12 + T, 0:N], yb[:, T, 0:N], g2[:, 0:N], ALU.mult, ['yb', 'tA3'], ['mixc%d' % T])
            for T in range(4):
                proj(l, None, in_pieces(l, [(OFF_GC + 128 * T, 128)]), N, evac_gate(12 + T, 'mixc%d' % T), hT, ['hT'])

            units = [(bq, kt, half) for bq in range(NB) for kt in range(2) for half in range(2)]
            import os as _os
            if _os.environ.get('V1'):
                psT2 = [psT, psT]; psTk = ['psT', 'psT']
            else:
                psT2 = [psT, ps3[:].bitcast(BF16)]; psTk = ['psT', 'ps3']
            Obuf = [ps6, ps2]; Obk = ['ps6', 'ps2']

            def stage1(u, bq, kt, half):
                par = u % 2
                Sb = psS if par == 0 else ps01
                Sk = ['psS'] if par == 0 else ['ps0', 'ps1']
                msk = maskF if (seg == 0 and bq == 0) else maskA
                hs = slice(64 * half, 64 * half + 64)
                for i2 in range(2):
                    for i in (2 * i2, 2 * i2 + 1):
                        tq = 4 * kt + i
                        mm(Sb[:, i * 256:(i + 1) * 256], mix[hs, 4 + tq, 128 * bq:128 * bq + 128],
                           kT[hs, kt, 128 * bq:128 * bq + 256], i % 2 == 0, False, ['mixb%d' % tq, 'kT'], Sk)
                    mm(Sb[:, i2 * 512:(i2 + 1) * 512], identb[:], msk[:], False, True, ['masks', 'ident'], Sk)

            def smv(par, np_=128, nh=4):
                so = 32 * par
                return dict(mx=sm[0:np_, so + 0:so + nh], m=sm[0:np_, so + 4:so + 4 + nh], ng=sm[0:np_, so + 8:so + 8 + nh],
                            ssum=sm[0:np_, so + 12:so + 12 + nh], dd=sm[0:np_, so + 16:so + 16 + nh], es=sm[0:np_, so + 20:so + 20 + nh],
                            den=sm[0:np_, so + 24:so + 24 + nh], rd=sm[0:np_, so + 28:so + 28 + nh])

            def stageF(u, bq, kt, half):
                par = u % 2
                Sb = psS if par == 0 else ps01
                Sk = ['psS'] if par == 0 else ['ps0', 'ps1']
                v = smv(par); kA = 'smA%d' % par
                h0 = l * 16 + 8 * kt + 4 * half
                sinkcols = sinkbc[:, h0:h0 + 4]
                src3 = Sb[:, :].rearrange("q (a b) -> q a b", b=256)
                P.add('dve', lambda e, o=v['mx'], i=src3: e.tensor_reduce(out=o, in_=i, axis=AX.X, op=ALU.max), r=Sk, w=[kA])
                ts('dve', v['m'], v['mx'], 0.125, ALU.mult, [kA], [kA])
                tt('dve', v['m'], v['m'], sinkcols, ALU.max, [kA, 'sinkbc'], [kA])
                ts('dve', v['ng'], v['m'], -1.0, ALU.mult, [kA], [kA])
                tt('dve', v['dd'], sinkcols, v['m'], ALU.subtract, [kA, 'sinkbc'], [kA])

            def stageX(u, bq, kt, half):
                par = u % 2
                Sb = psS if par == 0 else ps01
                Sk = ['psS'] if par == 0 else ['ps0', 'ps1']
                v = smv(par); kA = 'smA%d' % par; kS = 'smS%d' % par
                src3 = Sb[:, :].rearrange("q (a b) -> q a b", b=256)
                for i in range(4):
                    act(Pf2[par][:, i, :], src3[:, i, :], AF.Exp, Sk + [kA], ['p%dPf' % par, 'glwb', kS], scale=0.125,
                        bias=v['ng'][:, i:i + 1], accum=v['ssum'][:, i:i + 1])
                act(v['es'], v['dd'], AF.Exp, [kA], [kS])

            def stageB(u, bq, kt, half):
                par = u % 2
                v = smv(par); kS = 'smS%d' % par; kR = 'smR%d' % par
                tt('dve', v['den'], v['ssum'], v['es'], ALU.add, [kS], [kR])
                P.add('dve', lambda e, o=v['rd'], i=v['den']: e.reciprocal(out=o, in_=i), r=[kR], w=[kR])
                tt('pool', Pb2[par], Pf2[par], v['rd'].unsqueeze(2).to_broadcast([128, 4, 256]), ALU.mult, ['p%dPf' % par, kR],
                   ['p%dPb' % par, 'glwb'])

            def stage2(u, bq, kt, half):
                par = u % 2
                hs = slice(64 * half, 64 * half + 64)
                pT = psT2[par]; pTk = psTk[par]; PT_ = PTs2[par]; PTk = 'PTs%d' % par
                Ob = Obuf[kt]; Ok = Obk[kt]
                for i in range(4):
                    for kb in range(2):
                        tr(pT[:, (2 * i + kb) * 128:(2 * i + kb) * 128 + 128], Pb2[par][:, i, kb * 128:kb * 128 + 128], identb[:],
                           ['p%dPb' % par], [pTk])
                cp('act', PT_[:], pT[:], [pTk], [PTk])
                for i in range(4):
                    for kb in range(2):
                        mm(Ob[hs, i * 128:i * 128 + 128], Vt[:, bq + kb, (2 * kt + half) * 64:(2 * kt + half) * 64 + 64],
                           PT_[:, (2 * i + kb) * 128:(2 * i + kb) * 128 + 128], kb == 0, kb == 1, ['Vt', PTk], [Ok],
                           tile_position=(0, 64 * half))
                if half == 1:
                    for i in range(4):
                        cp('dve', mix[:, 4 + 4 * kt + i, 128 * bq:128 * bq + 128], Ob[:, i * 128:i * 128 + 128],
                           [Ok], ['mixb%d' % (4 * kt + i)])
            nu = len(units)
            for k_ in range(nu + 2):
                if k_ < nu:
                    stage1(k_, *units[k_])
                if k_ >= 2:
                    stageB(k_ - 2, *units[k_ - 2])
                if k_ < nu:
                    stageF(k_, *units[k_])
                    stageX(k_, *units[k_])
                if k_ >= 2:
                    stage2(k_ - 2, *units[k_ - 2])
            if seg == 0:
                dma('pool', ckb, dap(ck, l * 4 * 128 * 256, [[256, 128], [128 * 256, 4], [1, 256]]), [], ['iost0'])
                dma('pool', cvb, dap(cv, l * 4 * 128 * 256, [[256, 128], [128 * 256, 4], [1, 256]]), [], ['iost0'])
                for b in range(4):
                    for kt in range(2):
                        tr(psT[:, (2 * b + kt) * 128:(2 * b + kt) * 128 + 128], ckb[:, b, kt * 128:kt * 128 + 128], identb[:], ['iost0'], ['psT'])
                cp('dve', ckT, psT[:].rearrange("q (b k w) -> q b k w", b=4, k=2), ['psT'], ['iost0'])
                for kt in range(2):
                    for half in range(2):
                        hs = slice(64 * half, 64 * half + 64)
                        for i in range(4):
                            tq = 4 * kt + i
                            qs = mix[hs, 4 + tq, TSEG:N]
                            for b in range(4):
                                mm(psS[0:NS, b * 128:b * 128 + 128], qs, ckT[hs, b, kt, :], b == 0, False, ['mixb%d' % tq, 'iost0'], ['psS'])
                            mm(psS[0:NS, 0:512], identb[0:NS, 0:NS], maskS[:, 0:512], False, True, ['masks', 'ident'], ['psS'])
                            mm(psS[0:NS, 512:512 + NS], qs, kT[hs, kt, 128 + TSEG:128 + N], True, False, ['mixb%d' % tq, 'kT'], ['psS'])
                            mm(psS[0:NS, 512:512 + NS], identb[0:NS, 0:NS], maskS[:, 512:544], False, True, ['masks', 'ident'], ['psS'])
                            h0 = l * 16 + 8 * kt + 4 * half + i
                            softmax(psS[0:NS, 0:544].rearrange("q (a b) -> q a b", b=544), NS, 1, 544, sinkbc[0:NS, h0:h0 + 1],
                                    PfS.rearrange("q (a b) -> q a b", b=544), PbS.rearrange("q (a b) -> q a b", b=544), ['psS'], 'p0')
                            for b in range(4):
                                tr(psT[:, b * 32:b * 32 + 32], PbS[:, b * 128:b * 128 + 128], identb[0:NS, 0:NS], ['p0Pb'], ['psT'])
                            tr(psT[0:NS, 128:160], PbS[:, 512:544], identb[0:NS, 0:NS], ['p0Pb'], ['psT'])
                            cp('act', PTS2[:, 0:4, :], psT[:, 0:128].rearrange("q (a b) -> q a b", b=32), ['psT'], ['PTS2'])
                            cp('act', PTS2[0:NS, 4, :], psT[0:NS, 128:160], ['psT'], ['PTS2'])
                            vc = (2 * kt + half) * 64
                            for b in range(4):
                                mm(ps6[hs, i * 32:i * 32 + 32], cvb[:, b, vc:vc + 64], PTS2[:, b, :], b == 0, False, ['iost0', 'PTS2'], ['ps6'],
                                   tile_position=(0, 64 * half))
                            mm(ps6[hs, i * 32:i * 32 + 32], Vt[0:NS, NB + 1, vc:vc + 64], PTS2[0:NS, 4, :], False, True, ['Vt', 'PTS2'], ['ps6'],
                               tile_position=(0, 64 * half))
                    for i in range(4):
                        cp('act' if i % 2 else 'dve', mix[:, 4 + 4 * kt + i, TSEG:N], ps6[:, i * 32:i * 32 + 32], ['ps6'], ['mixb%d' % (4 * kt + i)])
            for tq in range(8):
                proj(l, None, in_pieces(l, qcols(OFF_GB, tq)), N, evac_gate(4 + tq, 'mixb%d' % tq), hT, ['hT'])

            for T in range(4):
                proj(l, None, in_pieces(l, [(OFF_UA + 128 * T, 128)]), N,
                     evac_store(lambda a, b, T=T: mix[:, T, a:b], 'mixa%d' % T), hT, ['hT'])
            for T in range(4):
                proj(l, None, in_pieces(l, [(OFF_VA + 128 * T, 128)]), N,
                     evac_store(lambda a, b: tB[1][:, a:b], 'tB0'), hT, ['hT'])
                nblk = NB + (1 if seg == 0 else 0)
                for b in range(nblk):
                    rows = 128 if b < NB else NS
                    tr(psT[0:rows, b * 128:b * 128 + 128], tB[1][:, b * 128:b * 128 + rows], identb[:], ['tB0'], ['psT'])
                cp('dve', vatm[:, 0:NB, 128 * T:128 * T + 128], psT[:, 0:NB * 128].rearrange("q (a b) -> q a b", b=128), ['psT'], ['vatm'])
                if seg == 0:
                    cp('dve', vatm[0:NS, NB, 128 * T:128 * T + 128], psT[0:NS, NB * 128:NB * 128 + 128], ['psT'], ['vatm'])
            if seg == 0:
                dma('pool', dap(va_s, l * NS * 512, [[512, NS], [1, 512]]), vatm[0:NS, NB, :], ['vatm'], ['out'])
            for h in range(4):
                lh = l * 4 + h
                for b in range(NB):
                    mm(ps3[:, b * 128:b * 128 + 128], vatm[:, b, 128 * h:128 * h + 128], wsT[:, lh, :], b == 0, False, ['vatm', 'wsT'], ['ps3'])
                for b in range(NB):
                    P.add('pe', lambda e, b=b, lh=lh: e.matmul(ps3[:, b * 128:b * 128 + 128], lhsT=ones1[0:1, :],
                                                              rhs=bsrow[0:1, (lh % 4) * 128:(lh % 4) * 128 + 128], start=False, stop=(b == NB - 1)),
                          r=['bsrow', 'ident'], w=['ps3'])
                tt('dve', mix[:, h, 0:TSEG], mix[:, h, 0:TSEG], ps3[:, 0:TSEG], ALU.mult, ['ps3', 'mixa%d' % h], ['mixa%d' % h])
                if seg == 0:
                    mm(ps6[:, 0:NS], vatm[0:NS, NB, 128 * h:128 * h + 128], wsS[:, lh, :], True, False, ['vatm', 'wsS'], ['ps6'])
                    P.add('pe', lambda e, lh=lh: e.matmul(ps6[:, 0:NS], lhsT=ones1[0:1, :], rhs=bsS[0:1, lh % 4, :], start=False, stop=True),
                          r=['bsS', 'ident'], w=['ps6'])
                    tt('dve', mix[:, h, TSEG:N], mix[:, h, TSEG:N], ps6[:, 0:NS], ALU.mult, ['ps6', 'mixa%d' % h], ['mixa%d' % h])
            for T in range(4):
                proj(l, None, in_pieces(l, [(OFF_GA + 128 * T, 128)]), N, evac_gate(T, 'mixa%d' % T), hT, ['hT'])

            allmix = ['mixa%d' % i for i in range(4)] + ['mixb%d' % i for i in range(8)] + ['mixc%d' % i for i in range(4)]

            def evac_res(j):
                def f(pm, pmk):
                    tt('dve', xT[:, j, 0:TSEG], xT[:, j, 0:TSEG], pm[:, 0:TSEG], ALU.add, [pmk, 'xT'], ['xT'])
                    if N > TSEG:
                        tt('dve', xT[:, j, TSEG:N], xT[:, j, TSEG:N], ps2[:, 0:NS], ALU.add, ['ps2', 'xT'], ['xT'])
                return f
            for j in range(KD):
                proj(l, None, out_pieces(l, j), N, evac_res(j), mix, allmix)

        rmsnorm(N, depth, lambda k, g: stt(xT[:, k, 0:N], xT[:, k, 0:N], g, rstd[:, 0:N], ALU.mult, ALU.mult,
                                           ['xT', 'rstd', 'gcol'], ['xT']))
        oblocks = [(128, b * 128, dap(y_p, (seg * TSEG + b * 128) * D, [[D, 128], [1, D]])) for b in range(NB)]
        if seg == 0:
            oblocks.append((NS, TSEG, y_s.ap()))
        for bi, (rows, c0, dst) in enumerate(oblocks):
            io = iost[bi % 2]; iok = 'iost0'
            for k4 in range(4):
                for kk in range(4):
                    k = 4 * k4 + kk
                    tr(ps3[0:rows, kk * 128:kk * 128 + 128], xT[:, k, c0:c0 + rows], ident[:], ['xT'], ['ps3'])
                cp('dve' if k4 % 2 == 0 else 'act', io[0:rows, 512 * k4:512 * k4 + 512], ps3[0:rows, :], ['ps3'], [iok])
            dma('sp', dst, io[0:rows, :], [iok], ['out'])

    import os
    ks = os.environ.get('KSTOP')
    if ks:
        print('NOPS', len(P.ops)); print('LASTOPS', [(i, o['eng'], o['line']) for i, o in list(enumerate(P.ops))[max(0, int(ks) - 3):int(ks)]]); P.ops = P.ops[:int(ks)]
    P.emit()
    st.close()
    return nc


def _consts(seqlen):
    half = 8
    inv = (500000.0 ** (-np.arange(half, dtype=np.float32) * 2.0 / 16.0)).astype(np.float32)
    pos = np.concatenate([np.arange(seqlen, dtype=np.float32), np.arange(8, dtype=np.float32) + PAST] * 1)
    pos = np.concatenate([np.arange(seqlen, dtype=np.float32)] + [np.arange(8, dtype=np.float32) + np.float32(PAST)] * 4)
    ang = pos[None, :] * inv[:, None]
    cosv = np.cos(ang).astype(np.float32); sinv = np.sin(ang).astype(np.float32)
    C = np.ones((128, pos.shape[0]), np.float32); S = np.zeros((128, pos.shape[0]), np.float32)
    for h0 in (0, 64):
        C[h0:h0 + 8] = cosv; C[h0 + 8:h0 + 16] = cosv
        S[h0:h0 + 8] = -sinv; S[h0 + 8:h0 + 16] = sinv
    i = np.arange(128)[:, None]; j = np.arange(256)[None, :]
    diff = i + 128 - j
    valid = (diff >= 0) & (diff < 128)
    mA = np.where(valid, 0.0, NEGM).astype(np.float32)
    mF = mA.copy(); mF[:, 0:128] = NEGM
    mA2 = np.concatenate([mA, mA], 1); mF2 = np.concatenate([mF, mF], 1)
    mS = np.full((32, 544), NEGM, np.float32)
    for b in range(4):
        for t in range(8):
            q = 8 * b + t
            for jj in range(128):
                if jj > t:
                    mS[q, b * 128 + jj] = 0.0
            for s in range(t + 1):
                mS[q, 512 + 8 * b + s] = 0.0
    tril = np.tril(np.ones((128, 128), np.float32))
    maskm = np.zeros((128, 2), np.float32)
    for q in range(128):
        maskm[q, (q // 16) % 2] = 1.0
    bd = np.zeros((32, 32), np.float32)
    for b in range(4):
        for t in range(8):
            for s in range(t + 1):
                bd[8 * b + t, 8 * b + s] = 1.0
    perm = np.zeros((128, 128), np.float32)
    for m_ in range(128):
        mm_ = m_ % 64
        if mm_ < 8:
            perm[m_ + 8, m_] = 1.0
        elif mm_ < 16:
            perm[m_ - 8, m_] = 1.0
    return dict(c_perm=perm, c_ident=np.eye(128, dtype=np.float32), c_tril=tril, c_maskA=mA2, c_maskF=mF2, c_maskS=mS, c_maskm=maskm,
                c_bd=bd, c_cos=C, c_sin=S)


_NC_CACHE = {}


def kernel(x_prompt, x_sample, cache_swa_k, cache_swa_v, state_ssm_re, state_ssm_im,
           norm_g, final_norm_g, w_in, w_out, chunk_w_s, chunk_b_s, attn_sinks,
           ssm_a_re, ssm_a_im, ssm_log_dt, ssm_b_re, ssm_b_im, ssm_c_re, ssm_c_im,
           ssm_d, glu_w, glu_b):
    f = lambda a: np.ascontiguousarray(np.asarray(a, dtype=np.float32))
    depth = int(np.asarray(w_in).shape[0]); seqlen = int(np.asarray(x_prompt).shape[1])
    nseg = seqlen // TSEG
    key = (depth, nseg)
    if key not in _NC_CACHE:
        _NC_CACHE[key] = build_program(depth, nseg)
    nc = _NC_CACHE[key]
    cst = _consts(seqlen)
    shared = dict(
        ng=np.concatenate([f(norm_g).reshape(depth * 16, 128), f(final_norm_g).reshape(16, 128)], 0),
        w_in=f(w_in), w_out=f(w_out), cws=f(chunk_w_s), cbs=f(chunk_b_s).reshape(1, -1), sinks=f(attn_sinks).reshape(1, -1),
        a_re=f(ssm_a_re), a_im=f(ssm_a_im), ldt=f(ssm_log_dt), b_re=f(ssm_b_re), b_im=f(ssm_b_im),
        c_re=f(ssm_c_re).reshape(depth, 512, 64), c_im=f(ssm_c_im).reshape(depth, 512, 64),
        dsk=f(ssm_d).reshape(depth * 4, 128), glw=f(glu_w), glb=f(glu_b).reshape(depth * 4, 128), **cst)
    xpf, xsf = f(x_prompt), f(x_sample)
    ckf, cvf = f(cache_swa_k), f(cache_swa_v)
    srf, sif = f(state_ssm_re), f(state_ssm_im)
    nb = xpf.shape[0]
    in_maps = []
    for c in range(8):
        m = dict(shared)
        m["xp"] = xpf[c % nb]
        m["xs"] = xsf[4 * c:4 * c + 4].reshape(NS, D)
        m["ck"] = ckf[:, 4 * c:4 * c + 4].reshape(depth, 4, 128, 256)
        m["cv"] = cvf[:, 4 * c:4 * c + 4].reshape(depth, 4, 128, 256)
        m["sre"] = srf[:, 4 * c:4 * c + 4].reshape(depth, 4, 16, 128)
        m["sim"] = sif[:, 4 * c:4 * c + 4].reshape(depth, 4, 16, 128)
        in_maps.append(m)
    res = run_bass_kernel_spmd(nc, in_maps, core_ids=list(range(8))).results
    y_prompt = np.stack([res[b]["y_p"] for b in range(nb)], 0)
    y_sample = np.concatenate([res[c]["y_s"].reshape(4, 8, D) for c in range(8)], 0)
    nkp = np.stack([res[b]["nk_p"].reshape(depth, 128, 4, 64) for b in range(nb)], 1)
    nvp = np.stack([res[b]["nv_p"].reshape(depth, 128, 4, 64) for b in range(nb)], 1)
    nks = np.concatenate([res[c]["nk_s"].reshape(depth, 4, 8, 4, 64) for c in range(8)], 1)
    nvs = np.concatenate([res[c]["nv_s"].reshape(depth, 4, 8, 4, 64) for c in range(8)], 1)
    hrp = np.stack([res[b]["hr_p"].reshape(depth, 32, 64) for b in range(nb)], 1)
    hip = np.stack([res[b]["hi_p"].reshape(depth, 32, 64) for b in range(nb)], 1)
    hrs = np.concatenate([res[c]["hr_s"].reshape(depth, 4, 32, 64) for c in range(8)], 1)
    his = np.concatenate([res[c]["hi_s"].reshape(depth, 4, 32, 64) for c in range(8)], 1)
    vas = np.concatenate([res[c]["va_s"].reshape(depth, 4, 8, 512) for c in range(8)], 1)
    return tuple(np.ascontiguousarray(a, dtype=np.float32) for a in
                 (y_prompt, y_sample, nkp, nvp, nks, nvs, hrp, hip, hrs, his, vas))
```

```python
import math
import numpy as np
import ml_dtypes
import concourse.bass as bass
import concourse.mybir as mybir
from concourse.bass_utils import run_bass_kernel_spmd

F32 = mybir.dt.float32
BF16 = mybir.dt.bfloat16
ALU = mybir.AluOpType
AF = mybir.ActivationFunctionType
AX = mybir.AxisListType

D = 2048
KD = 16
DIN = 5120
DEPTH = 4
SEQ = 4096
TSEG = 512
NSEG = SEQ // TSEG
NB = TSEG // 128
NS = 32
NJ = TSEG // 8
PAST = 16384
OFF_UA, OFF_VA, OFF_GA, OFF_Q, OFF_K, OFF_V, OFF_GB, OFF_UC, OFF_GC = 0, 512, 1024, 1536, 2560, 2816, 3072, 4096, 4608
NEGM = -60000.0
TW1 = 0
TW3 = 4096
TBD = 8192
TER = 10240
TEI = TER + 16 * (NJ + 4)
TRHO = TEI + 16 * (NJ + 4)
TABW = TRHO + 16


def dap(t, off, dims):
    return bass.AP(tensor=t, offset=off, ap=[[s, c] for s, c in dims])


class Prog:
    def __init__(self, nc):
        self.nc = nc
        self.ops = []
        self.last_w = {}
        self.readers = {}

    def add(self, eng, fn, r=(), w=(), dma=False):
        deps = set()
        for k in r:
            if k in self.last_w:
                deps.add(self.last_w[k])
        for k in w:
            if k in self.last_w:
                deps.add(self.last_w[k])
            for x in self.readers.get(k, ()):
                deps.add(x)
        idx = len(self.ops)
        deps.discard(idx)
        import sys as _s
        fr = _s._getframe(1)
        ln = []
        while fr is not None and len(ln) < 3:
            ln.append(fr.f_lineno); fr = fr.f_back
        self.ops.append(dict(eng=eng, fn=fn, deps=sorted(deps), dma=dma, line=ln))
        for k in w:
            self.last_w[k] = idx
            self.readers[k] = []
        for k in r:
            if k not in w:
                self.readers.setdefault(k, []).append(idx)
        return idx

    def emit(self):
        nc = self.nc
        ops = self.ops
        engs = ['pe', 'act', 'dve', 'pool', 'sp']
        RR = {'sp': 8, 'pool': 4, 'act': 2, 'pe': 1, 'dve': 1}
        for o in ops:
            o['sig'] = False
        for i, o in enumerate(ops):
            for d in o['deps']:
                p = ops[d]
                if p['dma']:
                    continue
                if p['eng'] == 'pe' and o['eng'] == 'pe' and not o['dma']:
                    continue
                p['sig'] = True
        cnt = {e: 0 for e in engs}
        dcnt = {e: 0 for e in engs}
        for o in ops:
            e = o['eng']
            if o['dma']:
                n = dcnt[e]
                o['dslot'] = n % RR[e]
                o['dval'] = 16 * (n // RR[e] + 1)
                o['dprev'] = None
                dcnt[e] += 1
            elif o['sig']:
                cnt[e] += 1
                o['val'] = cnt[e]
        lastslot = {}
        for i, o in enumerate(ops):
            if o['dma']:
                key = (o['eng'], o['dslot'])
                o['dprev'] = lastslot.get(key)
                lastslot[key] = i
        import contextlib
        with contextlib.ExitStack() as st:
            sems = {e: st.enter_context(nc.semaphore("s_" + e)) for e in engs}
            dsems = {}
            for e in ['sp', 'pool', 'act']:
                for s in range(RR[e]):
                    dsems[(e, s)] = st.enter_context(nc.semaphore("d_%s%d" % (e, s)))
            block = st.enter_context(nc.Block())
            per = {e: [o for o in ops if o['eng'] == e] for e in engs}

            def run(e, engine):
                waited = {}

                def need(sem_key, sem, val):
                    if waited.get(sem_key, 0) >= val:
                        return
                    engine.wait_ge(sem, val)
                    waited[sem_key] = val

                for o in per[e]:
                    for d in o['deps']:
                        p = ops[d]
                        if p['dma']:
                            need(('d', p['eng'], p['dslot']), dsems[(p['eng'], p['dslot'])], p['dval'])
                        else:
                            if p['eng'] == 'pe' and e == 'pe' and not o['dma']:
                                continue
                            need(('c', p['eng']), sems[p['eng']], p['val'])
                    if o['dma'] and o['dprev'] is not None:
                        p = ops[o['dprev']]
                        need(('d', p['eng'], p['dslot']), dsems[(p['eng'], p['dslot'])], p['dval'])
                    ins = o['fn'](engine)
                    if o['dma']:
                        ins.then_inc(dsems[(e, o['dslot'])], 16)
                    elif o['sig']:
                        ins.then_inc(sems[e], 1)
                if e in ('sp', 'pool', 'act'):
                    tot = {}
                    for o in per[e]:
                        if o['dma']:
                            tot[o['dslot']] = o['dval']
                    for s, v in tot.items():
                        engine.wait_ge(dsems[(e, s)], v)

            @block.tensor
            def _(en):
                run('pe', en)

            @block.scalar
            def _(en):
                run('act', en)

            @block.vector
            def _(en):
                run('dve', en)

            @block.gpsimd
            def _(en):
                run('pool', en)

            @block.sync
            def _(en):
                run('sp', en)


def build_program(depth=DEPTH, nseg=NSEG, debug=False):
    nc = bass.Bass("TRN2", target_bir_lowering=False)
    P = Prog(nc)
    seqlen = nseg * TSEG

    def din(name, shape, dt=F32):
        return nc.dram_tensor(name, list(shape), dt, kind="ExternalInput")

    def dout(name, shape, dt=F32):
        return nc.dram_tensor(name, list(shape), dt, kind="ExternalOutput")

    xp = din("xp", [seqlen, D]); xs = din("xs", [NS, D])
    ck = din("ck", [depth, 4, 128, 256]); cv = din("cv", [depth, 4, 128, 256])
    sre = din("sre", [depth, 4, 16, 128]); sim = din("sim", [depth, 4, 16, 128])
    ng = din("ng", [depth * 16 + 16, 128])
    w_in = din("w_in", [depth, D, DIN]); w_out = din("w_out", [depth, D, D])
    cws = din("cws", [depth, 4, 128, 128]); cbs = din("cbs", [1, depth * 4 * 128])
    sinks = din("sinks", [1, depth * 16])
    a_re = din("a_re", [depth, 32, 64]); a_im = din("a_im", [depth, 32, 64]); ldt = din("ldt", [depth, 32])
    b_re = din("b_re", [depth, 32, 64, 16]); b_im = din("b_im", [depth, 32, 64, 16])
    c_re = din("c_re", [depth, 512, 64]); c_im = din("c_im", [depth, 512, 64])
    dsk = din("dsk", [depth * 4, 128]); glw = din("glw", [depth, 512, 512]); glb = din("glb", [depth * 4, 128])
    c_ident = din("c_ident", [128, 128]); c_tril = din("c_tril", [128, 128])
    c_maskA = din("c_maskA", [128, 512]); c_maskF = din("c_maskF", [128, 512]); c_maskS = din("c_maskS", [32, 544])
    c_maskm = din("c_maskm", [128, 2]); c_bd = din("c_bd", [32, 32])
    c_perm = din("c_perm", [128, 128]); c_cos = din("c_cos", [128, seqlen + NS]); c_sin = din("c_sin", [128, seqlen + NS])

    y_p = dout("y_p", [seqlen, D]); y_s = dout("y_s", [NS, D])
    nk_p = dout("nk_p", [depth, 128, 256]); nv_p = dout("nv_p", [depth, 128, 256])
    nk_s = dout("nk_s", [depth, NS, 256]); nv_s = dout("nv_s", [depth, NS, 256])
    hr_p = dout("hr_p", [depth, 16, 128]); hi_p = dout("hi_p", [depth, 16, 128])
    hr_s = dout("hr_s", [depth, 4, 16, 128]); hi_s = dout("hi_s", [depth, 4, 16, 128])
    va_s = dout("va_s", [depth, NS, 512])
    s5tab = nc.dram_tensor("s5tab", [depth, 128, TABW], F32)
    wcache = nc.dram_tensor("wcache", [depth * 56, 128, KD * 128], BF16)

    NMAX = TSEG + NS
    import contextlib
    st = contextlib.ExitStack()

    def sb(name, shape, dt=F32):
        return st.enter_context(nc.sbuf_tensor(name, list(shape), dt))

    def ps(name, shape, dt=F32):
        return st.enter_context(nc.psum_tensor(name, list(shape), dt))

    xT = sb("xT", [128, KD, NMAX]); hT = sb("hT", [128, KD, NMAX], BF16); mix = sb("mix", [128, KD, NMAX], BF16)
    kT = sb("kT", [128, 2, 128 + NMAX], BF16); Vt = sb("Vt", [128, NB + 2, 256], BF16)
    vatm = sb("vatm", [128, NB + 1, 512], BF16)
    khalo = sb("khalo", [128, depth, 2, 128], BF16); vhalo = sb("vhalo", [128, depth, 256], BF16)
    cosT = sb("cosT", [128, NMAX]); sinT = sb("sinT", [128, NMAX])
    NWB = 3
    wb = [sb("wb%d" % i, [128, KD, 128], BF16) for i in range(NWB)]
    iost0_ = sb("iost0", [128, D]); iost = [iost0_, iost0_]
    ident = sb("ident", [128, 128]); identb = sb("identb", [128, 128], BF16); onesb = sb("onesb", [128, 128], BF16)
    ones1 = sb("ones1", [1, 128]); epsc = sb("epsc", [128, 1])
    maskA = sb("maskA", [128, 512], BF16); maskF = sb("maskF", [128, 512], BF16); maskS = sb("maskS", [32, 544], BF16)
    gcol = sb("gcol", [128, depth * 16 + 16]); dcol = sb("dcol", [128, depth * 4]); gbcol = sb("gbcol", [128, depth * 4])
    sinkbc = sb("sinkbc", [128, depth * 16]); bsrow = sb("bsrow", [1, 512]); bsS = sb("bsS", [1, 4, 32])
    wsT = sb("wsT", [128, depth * 4, 128], BF16); wsS = sb("wsS", [32, depth * 4, 32], BF16)
    hprev = sb("hprev", [128, depth, 16, 2]); s_in = sb("s_in", [128, depth, 4, 2, 16])
    tabA = sb("tabA", [128, 4096 + 2048])
    tabE = sb("tabE", [128, TABW - TER])
    Sprev = sb("Sprev", [128, 16, 2, NJ + 4], BF16)
    sq = [sb("sq%d" % i, [128, NMAX], BF16) for i in range(2)]
    rstd = sb("rstd", [128, NMAX]); tA = [sb("tA%d" % i, [128, NMAX]) for i in range(4)]
    xsw0_ = sb("xsw0", [128, NMAX]); xsw = [xsw0_, xsw0_]
    tB0_ = sb("tB0", [128, NMAX], BF16); tB = [tB0_, tB0_]
    yb = sb("yb", [128, 4, NMAX], BF16)
    attb = sb("attb", [128, 4096], BF16); Pf2 = [attb[:, 2048 * i:2048 * i + 1024].rearrange("q (a b) -> q a b", b=256) for i in range(2)]; Pb2 = [attb[:, 2048 * i + 1024:2048 * i + 2048].rearrange("q (a b) -> q a b", b=256) for i in range(2)]; Pf = Pf2[0]; Pb = Pb2[0]; glwb = attb[:, 0:2048].rearrange("q (a b) -> q a b", b=512); PTs = sb("PTs", [128, 1024], BF16); PTs2 = [PTs, sb("PTsB", [128, 1024], BF16)]
    sm = sb("sm", [128, 64])
    iob = iost0_[:].bitcast(BF16)
    ckb = iob[:, 0:1024].rearrange("q (a b) -> q a b", b=256); cvb = iob[:, 1024:2048].rearrange("q (a b) -> q a b", b=256)
    ckT = iob[:, 2048:3072].rearrange("q (b k w) -> q b k w", b=4, k=2)
    PfS = Pf[0:32].rearrange("q a b -> q (a b)")[:, 0:544]; PbS = Pb[0:32].rearrange("q a b -> q (a b)")[:, 0:544]; PTS2 = sb("PTS2", [128, 5, 32], BF16)
    s5w = [sb("s5w%d" % i, [128, 4 * NJ]) for i in range(10)]
    outst = sb("outst", [128, 416])
    ps01 = ps("ps01", [128, 1024]); ps0 = ps01[:, 0:512]; ps1 = ps01[:, 512:1024]; ps2 = ps("ps2", [128, 512]); ps3 = ps("ps3", [128, 512])
    psS = ps("psS", [128, 1024]); ps6 = ps("ps6", [128, 512]); psT = ps("psT", [128, 1024], BF16)

    tabAb = tabA[:].bitcast(BF16)

    def E(name):
        return {'pe': 'pe', 'act': 'act', 'dve': 'dve', 'pool': 'pool', 'sp': 'sp'}[name]

    def dma(q, out, in_, r, w):
        return P.add(q, lambda e: e.dma_start(out=out, in_=in_, allow_slow_non_contiguous=True), r=r, w=w, dma=True)

    def mm(out, lhsT, rhs, start, stop, r, w, **kw):
        return P.add('pe', lambda e: e.matmul(out, lhsT=lhsT, rhs=rhs, start=start, stop=stop, **kw), r=r, w=w)

    def tr(out, in_, idn, r, w, **kw):
        return P.add('pe', lambda e: e.transpose(out, in_, idn, **kw), r=r + ['ident'], w=w)

    def act(out, in_, func, r, w, scale=1.0, bias=None, accum=None):
        def f(e):
            kw = dict(out=out, in_=in_, func=func, scale=scale)
            if bias is not None:
                kw['bias'] = bias
            if accum is not None:
                kw['accum_out'] = accum
            return e.activation(**kw)
        return P.add('act', f, r=r, w=w)

    def tt(eng, out, a, b, op, r, w):
        return P.add(eng, lambda e: e.tensor_tensor(out=out, in0=a, in1=b, op=op), r=r, w=w)

    def ts(eng, out, a, s1, op0, r, w, s2=None, op1=None):
        if op1 is None:
            return P.add(eng, lambda e: e.tensor_scalar(out=out, in0=a, scalar1=s1, scalar2=None, op0=op0), r=r, w=w)
        return P.add(eng, lambda e: e.tensor_scalar(out=out, in0=a, scalar1=s1, scalar2=s2, op0=op0, op1=op1), r=r, w=w)

    def stt(out, a, s, b, op0, op1, r, w):
        return P.add('dve', lambda e: e.scalar_tensor_tensor(out=out, in0=a, scalar=s, in1=b, op0=op0, op1=op1), r=r, w=w)

    def cp(eng, out, in_, r, w):
        if eng == 'act':
            return P.add('act', lambda e: e.copy(out=out, in_=in_), r=r, w=w)
        return P.add(eng, lambda e: e.tensor_copy(out=out, in_=in_), r=r, w=w)

    def memset(eng, ap, v, w):
        return P.add(eng, lambda e: e.memset(ap, v), r=[], w=w)

    dma('sp', ident[:], c_ident.ap(), [], ['ident'])
    dma('pool', identb[:], c_ident.ap(), [], ['ident'])
    dma('pool', maskA[:], c_maskA.ap(), [], ['masks']); dma('pool', maskF[:], c_maskF.ap(), [], ['masks'])
    dma('pool', maskS[:], c_maskS.ap(), [], ['masks'])
    memset('dve', onesb[:], 1.0, ['ident']); memset('dve', ones1[:], 1.0, ['ident']); memset('dve', epsc[:], 1e-5, ['ident'])
    xw0 = xsw[0]
    permb = sb("permb", [128, 128], BF16)
    dma('pool', permb[:], c_perm.ap(), [], ['ident'])
    memset('pool', kT[:], 0.0, ['kT']); memset('pool', Vt[:], 0.0, ['Vt'])
    memset('pool', khalo[:], 0.0, ['khalo']); memset('pool', vhalo[:], 0.0, ['vhalo']); memset('dve', hprev[:], 0.0, ['hprev'])
    dma('sp', sinkbc[:], dap(sinks, 0, [[0, 128], [1, depth * 16]]), [], ['sinkbc'])
    nrow = depth * 16 + 16
    dma('sp', iost[0][0:nrow, 0:128], ng.ap(), [], ['iost0'])
    tr(ps3[:, 0:nrow], iost[0][0:nrow, 0:128], ident[0:nrow, 0:nrow], ['iost0'], ['ps3'])
    cp('dve', gcol[:], ps3[:, 0:nrow], ['ps3'], ['gcol'])
    dma('sp', iost[0][0:depth * 4, 128:256], dsk.ap(), [], ['iost0'])
    tr(ps3[:, 0:depth * 4], iost[0][0:depth * 4, 128:256], ident[0:depth * 4, 0:depth * 4], ['iost0'], ['ps3'])
    cp('dve', dcol[:], ps3[:, 0:depth * 4], ['ps3'], ['dcol'])
    dma('sp', iost[0][0:depth * 4, 256:384], glb.ap(), [], ['iost0'])
    tr(ps3[:, 0:depth * 4], iost[0][0:depth * 4, 256:384], ident[0:depth * 4, 0:depth * 4], ['iost0'], ['ps3'])
    cp('dve', gbcol[:], ps3[:, 0:depth * 4], ['ps3'], ['gbcol'])
    dma('sp', iost[0][:, 1536:1664], c_tril.ap(), [], ['iost0'])
    for lh in range(depth * 4):
        dma('sp', iost[0][:, 512:640], dap(cws, lh * 128 * 128, [[128, 128], [1, 128]]), [], ['iost0'])
        tt('dve', iost[0][:, 640:768], iost[0][:, 512:640], iost[0][:, 1536:1664], ALU.mult, ['iost0', 'iost0'], ['iost0b'])
        tr(ps3[:, 0:128], iost[0][:, 640:768], ident[:], ['iost0b'], ['ps3'])
        cp('dve', wsT[:, lh, :], ps3[:, 0:128], ['ps3'], ['wsT'])
    dma('sp', iost[0][0:32, 1664:1696], c_bd.ap(), [], ['iost0'])
    for lh in range(depth * 4):
        for b in range(4):
            dma('sp', iost[0][8 * b:8 * b + 8, 1024:1056].rearrange("q (a s) -> q a s", s=8), dap(cws, lh * 128 * 128, [[128, 8], [0, 4], [1, 8]]), [], ['iost0'])
        tt('dve', iost[0][0:32, 1056:1088], iost[0][0:32, 1024:1056], iost[0][0:32, 1664:1696], ALU.mult, ['iost0', 'iost0'], ['iost0b'])
        tr(ps3[0:32, 0:32], iost[0][0:32, 1056:1088], ident[0:32, 0:32], ['iost0b'], ['ps3'])
        cp('dve', wsS[:, lh, :], ps3[0:32, 0:32], ['ps3'], ['wsS'])
    for l in range(depth):
        for b in range(4):
            for ri, src in enumerate((sre, sim)):
                dma('sp', iost[0][0:16, 0:128], dap(src, ((l * 4 + b) * 16) * 128, [[128, 16], [1, 128]]), [], ['iost0'])
                tr(ps3[:, 0:16], iost[0][0:16, 0:128], ident[0:16, 0:16], ['iost0'], ['ps3'])
                cp('dve', s_in[:, l, b, ri, :], ps3[:, 0:16], ['ps3'], ['s_in'])

    W = NJ + 4
    maskm = sb("maskm", [128, 2])
    dma('sp', maskm[:], c_maskm.ap(), [], ['maskm'])
    mixf = mix[:].rearrange("q a b -> q (a b)").bitcast(F32)
    xTf = xT[:].rearrange("q a b -> q (a b)")
    _o = [0]

    def carve(buf, n, lim):
        a = _o[0]; _o[0] += n
        assert _o[0] <= lim
        return buf[:, a:a + n]
    sc = [carve(mixf, 16, 4352) for i in range(72)]
    Bn = [carve(mixf, 256, 4352).rearrange("q (a b) -> q a b", b=16) for i in range(2)]
    Cn = [carve(mixf, 256, 4352).rearrange("q (a b) -> q a b", b=64) for i in range(2)]
    Cp = [carve(mixf, 512, 4352).rearrange("q (a b) -> q a b", b=32) for i in range(2)]
    Y3 = [carve(mixf, 512, 4352).rearrange("q (a b) -> q a b", b=32) for i in range(2)]
    _o[0] = 4 * 16 * W
    tmpY = [carve(xTf, 256, 8704).rearrange("q (a b) -> q a b", b=16) for i in range(4)]
    Cpad = carve(xTf, 128, 8704)
    pw_all = carve(xTf, 288, 8704).rearrange("q (k r a) -> q k r a", k=9, r=2)
    CpB_all = carve(xTf, 512, 8704).bitcast(BF16).rearrange("q (r a c) -> q r a c", r=2, a=16)
    W1st = tabAb[:, 0:8192]; BDst = tabAb[:, 8192:12288]; W3half = iost0_[:].bitcast(BF16)
    Est = tabE[:, 0:2 * 16 * W].rearrange("q (a b c) -> q a b c", a=2, b=16)
    Etmp = [xTf[:, i * 16 * W:(i + 1) * 16 * W].rearrange("q (a b) -> q a b", b=W) for i in range(4)]
    Y1all = hT[:].rearrange("q a b -> q (a b)")[:, 0:8192].rearrange("q (k r a c) -> q k r a c", k=8, r=2, a=16)
    rs = ['s5p', 'xT', 'hT', 'tabA', 'tabW', 'tabE', 'iost0'] + ['mixa%d' % i for i in range(4)] + ['mixb%d' % i for i in range(8)] + ['mixc%d' % i for i in range(4)]
    for l in range(depth):
        cnt_i = [0]

        def T_():
            cnt_i[0] += 1
            return sc[cnt_i[0] - 1]
        are, aim, dtl = T_(), T_(), T_()
        for m in range(2):
            dma('sp', are[64 * m:64 * m + 64, :], dap(a_re, l * 2048 + m * 64, [[1, 64], [128, 16]]), [], rs)
            dma('sp', aim[64 * m:64 * m + 64, :], dap(a_im, l * 2048 + m * 64, [[1, 64], [128, 16]]), [], rs)
            dma('sp', dtl[64 * m:64 * m + 64, :], dap(ldt, l * 32 + m, [[0, 64], [2, 16]]), [], rs)
            dma('sp', Bn[0][64 * m:64 * m + 64, :, :], dap(b_re, l * 32768 + m * 1024, [[16, 64], [2048, 16], [1, 16]]), [], rs)
            dma('sp', Bn[1][64 * m:64 * m + 64, :, :], dap(b_im, l * 32768 + m * 1024, [[16, 64], [2048, 16], [1, 16]]), [], rs)
        dma('sp', Cn[0][:], dap(c_re, l * 32768, [[64, 128], [8192, 4], [1, 64]]), [], rs)
        dma('sp', Cn[1][:], dap(c_im, l * 32768, [[64, 128], [8192, 4], [1, 64]]), [], rs)
        def poly(x, coef):
            t = T_()
            n = len(coef) - 1
            ts('dve', t[:], x, float(coef[n]), ALU.mult, rs, rs)
            for k in range(n - 1, 0, -1):
                stt(t[:], t[:], float(coef[k]), x, ALU.add, ALU.mult, rs, rs)
            ts('dve', t[:], t[:], float(coef[0]), ALU.add, rs, rs)
            return t

        def exp_poly(x, deg, nsq):
            xs_ = T_(); ts('dve', xs_[:], x, 1.0 / (2 ** nsq), ALU.mult, rs, rs)
            e = poly(xs_[:], [1.0 / math.factorial(k) for k in range(deg + 1)])
            for _ in range(nsq):
                tt('dve', e[:], e[:], e[:], ALU.mult, rs, rs)
            return e
        dt_ = exp_poly(dtl[:], 12, 3)
        ar = T_(); tt('dve', ar[:], are[:], dt_[:], ALU.mult, rs, rs)
        th = T_(); tt('dve', th[:], aim[:], dt_[:], ALU.mult, rs, rs)
        mag = exp_poly(ar[:], 7, 0)
        TWO_PI = 2.0 * math.pi
        MAGIC = 12582912.0
        u = T_(); ts('dve', u[:], th[:], 1.0 / TWO_PI, ALU.mult, rs, rs)
        n_ = T_(); ts('dve', n_[:], u[:], MAGIC, ALU.add, rs, rs)
        n2 = T_(); ts('dve', n2[:], n_[:], MAGIC, ALU.subtract, rs, rs)
        fr0 = T_(); tt('dve', fr0[:], u[:], n2[:], ALU.subtract, rs, rs)
        xq = T_(); ts('dve', xq[:], fr0[:], TWO_PI / 4.0, ALU.mult, rs, rs)
        x2 = T_(); tt('dve', x2[:], xq[:], xq[:], ALU.mult, rs, rs)
        ps_ = poly(x2[:], [(-1.0) ** k / math.factorial(2 * k + 1) for k in range(7)])
        sn = T_(); tt('dve', sn[:], ps_[:], xq[:], ALU.mult, rs, rs)
        cs = poly(x2[:], [(-1.0) ** k / math.factorial(2 * k) for k in range(8)])
        for _ in range(2):
            s2_ = T_(); tt('dve', s2_[:], sn[:], cs[:], ALU.mult, rs, rs)
            q2_ = T_(); tt('dve', q2_[:], sn[:], sn[:], ALU.mult, rs, rs)
            sn = T_(); ts('dve', sn[:], s2_[:], 2.0, ALU.mult, rs, rs)
            cs = T_(); ts('dve', cs[:], q2_[:], -2.0, ALU.mult, rs, rs, s2=1.0, op1=ALU.add)
        lr = T_(); tt('dve', lr[:], mag[:], cs[:], ALU.mult, rs, rs)
        li = T_(); tt('dve', li[:], mag[:], sn[:], ALU.mult, rs, rs)

        def cmul(ar_, ai_, br_, bi_, bc=None):
            t1, t2, t3, t4, orr, oi = T_(), T_(), T_(), T_(), T_(), T_()
            tt('dve', t1[:], ar_, br_, ALU.mult, rs, rs); tt('dve', t2[:], ai_, bi_, ALU.mult, rs, rs)
            tt('dve', orr[:], t1[:], t2[:], ALU.subtract, rs, rs)
            tt('dve', t3[:], ar_, bi_, ALU.mult, rs, rs); tt('dve', t4[:], ai_, br_, ALU.mult, rs, rs)
            tt('dve', oi[:], t3[:], t4[:], ALU.add, rs, rs)
            return orr, oi
        nr = T_(); ts('dve', nr[:], lr[:], -1.0, ALU.add, rs, rs)
        d1 = T_(); tt('dve', d1[:], are[:], are[:], ALU.mult, rs, rs)
        d2 = T_(); tt('dve', d2[:], aim[:], aim[:], ALU.mult, rs, rs)
        den = T_(); tt('dve', den[:], d1[:], d2[:], ALU.add, rs, rs)
        rden = T_(); P.add('dve', lambda e, o=rden, i=den: e.reciprocal(out=o[:], in_=i[:]), r=rs, w=rs)
        nai = T_(); ts('dve', nai[:], aim[:], -1.0, ALU.mult, rs, rs)
        f0r, f0i = cmul(nr[:], li[:], are[:], nai[:])
        fr_ = T_(); tt('dve', fr_[:], f0r[:], rden[:], ALU.mult, rs, rs)
        fi_ = T_(); tt('dve', fi_[:], f0i[:], rden[:], ALU.mult, rs, rs)
        for ri in range(2):
            for T in range(4):
                for m in range(2):
                    ts('dve', Cpad[:, 64 * m:64 * m + 64], Cn[ri][:, T, :], maskm[:, m:m + 1], ALU.mult, rs, rs)
                tr(ps3[:, 0:128], Cpad[:], ident[:], rs, ['ps3'])
                cp('dve', Cp[ri][:, 4 * T:4 * T + 4, :], ps3[:, 0:128].rearrange("q (a b) -> q a b", b=32), ['ps3'], rs)
        pw = pw_all
        memset('dve', pw[:, 0, 0, :], 1.0, rs); memset('dve', pw[:, 0, 1, :], 0.0, rs)
        cp('dve', pw[:, 1, 0, :], lr[:], rs, rs); cp('dve', pw[:, 1, 1, :], li[:], rs, rs)
        base_i = cnt_i[0]
        for k in range(2, 9):
            cnt_i[0] = base_i
            orr, oi = cmul(pw[:, k - 1, 0, :], pw[:, k - 1, 1, :], lr[:], li[:])
            cp('dve', pw[:, k, 0, :], orr[:], rs, rs); cp('dve', pw[:, k, 1, :], oi[:], rs, rs)
        W3h = W3half.rearrange("q (t a r c) -> q t a r c", t=4, a=16, r=2)
        for t in range(8):
            pr = pw[:, t + 1, 0, :].unsqueeze(2).to_broadcast([128, 16, 32])
            pi_ = pw[:, t + 1, 1, :].unsqueeze(2).to_broadcast([128, 16, 32])
            tt('dve', Y3[0][:], Cp[0][:], pr, ALU.mult, rs, rs); tt('dve', Y3[1][:], Cp[1][:], pi_, ALU.mult, rs, rs)
            tt('dve', W3h[:, t % 4, :, 0, :], Y3[0][:], Y3[1][:], ALU.subtract, rs, rs)
            tt('dve', Y3[0][:], Cp[0][:], pi_, ALU.mult, rs, rs); tt('dve', Y3[1][:], Cp[1][:], pr, ALU.mult, rs, rs)
            tt('dve', Y3[0][:], Y3[0][:], Y3[1][:], ALU.add, rs, rs)
            ts('dve', W3h[:, t % 4, :, 1, :], Y3[0][:], -1.0, ALU.mult, rs, rs)
            if t % 4 == 3:
                dma('sp', dap(s5tab, l * 128 * TABW + TW3 + (t // 4) * 2048, [[TABW, 128], [1, 2048]]), iost0_[:], rs, ['s5tab'] + rs)
        memset('dve', hT[:].rearrange("q a b -> q (a b)")[:, 0:8192], 0.0, rs)
        for k in range(8):
            cnt_i[0] = base_i
            Fr, Fi = cmul(pw[:, k, 0, :], pw[:, k, 1, :], fr_[:], fi_[:])
            Frb = Fr[:].unsqueeze(2).to_broadcast([128, 16, 16]); Fib = Fi[:].unsqueeze(2).to_broadcast([128, 16, 16])
            tt('dve', tmpY[0][:], Bn[0][:], Frb, ALU.mult, rs, rs); tt('dve', tmpY[1][:], Bn[1][:], Fib, ALU.mult, rs, rs)
            tt('dve', tmpY[2][:], Bn[1][:], Frb, ALU.mult, rs, rs); tt('dve', tmpY[3][:], Bn[0][:], Fib, ALU.mult, rs, rs)
            for m in range(2):
                sl = slice(64 * m, 64 * m + 64)
                tt('dve', Y1all[sl, k, 0, :, 16 * m:16 * m + 16], tmpY[0][sl], tmpY[1][sl], ALU.subtract, rs, rs)
                tt('dve', Y1all[sl, k, 1, :, 16 * m:16 * m + 16], tmpY[2][sl], tmpY[3][sl], ALU.add, rs, rs)
        W1v = W1st.rearrange("q (s t r c) -> q s t r c", s=8, t=4, r=2)
        for s in range(8):
            k = 7 - s
            for T in range(4):
                for ri in range(2):
                    for r4 in range(4):
                        tr(psT[32 * r4:32 * r4 + 32, 0:128], Y1all[:, k, ri, 4 * T + r4, :], identb[:], rs, ['psT'], tile_position=(0, 32 * r4))
                    cp('dve', W1v[:, s, T, ri, :], psT[:, 0:128], ['psT'], rs)
        memset('dve', BDst, 0.0, rs)
        BDv = BDst.rearrange("q (t a c) -> q t a c", t=4, a=8)
        CpB = CpB_all
        cp('dve', CpB[:, 0], Cp[0][:], rs, rs); ts('dve', CpB[:, 1], Cp[1][:], -1.0, ALU.mult, rs, rs)
        for T in range(4):
            for tau in range(8):
                for r4 in range(4):
                    pi_i = 4 * T + r4
                    for ri in range(2):
                        mm(ps3[32 * r4:32 * r4 + 32, tau * 32:tau * 32 + 32], Y1all[:, tau, ri, pi_i, :], CpB[:, ri, pi_i, :],
                           ri == 0, ri == 1, rs, ['ps3'], tile_position=(0, 32 * r4))
            for r4 in range(4):
                sl = slice(32 * r4, 32 * r4 + 32)
                cp('dve', BDv[sl, T, :, 32 * r4:32 * r4 + 32], ps3[sl, 0:256].rearrange("q (a b) -> q a b", b=32), ['ps3'], rs)
        cnt_i[0] = base_i + 6
        r2 = T_(); tt('dve', r2[:], mag[:], mag[:], ALU.mult, rs, rs)
        r4_ = T_(); tt('dve', r4_[:], r2[:], r2[:], ALU.mult, rs, rs)
        rho = T_(); tt('dve', rho[:], r4_[:], r4_[:], ALU.mult, rs, rs)
        ur, ui = cs, sn
        for _ in range(3):
            a2 = T_(); tt('dve', a2[:], ur[:], ur[:], ALU.mult, rs, rs)
            b2 = T_(); tt('dve', b2[:], ui[:], ui[:], ALU.mult, rs, rs)
            ab = T_(); tt('dve', ab[:], ur[:], ui[:], ALU.mult, rs, rs)
            ur = T_(); tt('dve', ur[:], a2[:], b2[:], ALU.subtract, rs, rs)
            ui = T_(); ts('dve', ui[:], ab[:], 2.0, ALU.mult, rs, rs)
        cp('dve', Est[:, 0, :, 0], ur[:], rs, rs)
        cp('dve', Est[:, 1, :, 0], ui[:], rs, rs)
        k = 1
        while k < NJ:
            n = min(k, NJ - k)
            br_ = Est[:, 0, :, k - 1:k].to_broadcast([128, 16, n]); bi_ = Est[:, 1, :, k - 1:k].to_broadcast([128, 16, n])
            xr = Est[:, 0, :, 0:n]; xi = Est[:, 1, :, 0:n]
            tt('dve', Etmp[0][:, :, 0:n], xr, br_, ALU.mult, rs, rs); tt('dve', Etmp[1][:, :, 0:n], xi, bi_, ALU.mult, rs, rs)
            tt('dve', Etmp[2][:, :, 0:n], xr, bi_, ALU.mult, rs, rs); tt('dve', Etmp[3][:, :, 0:n], xi, br_, ALU.mult, rs, rs)
            tt('dve', Est[:, 0, :, k:k + n], Etmp[0][:, :, 0:n], Etmp[1][:, :, 0:n], ALU.subtract, rs, rs)
            tt('dve', Est[:, 1, :, k:k + n], Etmp[2][:, :, 0:n], Etmp[3][:, :, 0:n], ALU.add, rs, rs)
            k += n
        for ri in range(2):
            cp('dve', Est[:, ri, :, NJ:NJ + 4], Est[:, ri, :, 0:1].to_broadcast([128, 16, 4]), rs, rs)
        cp('dve', tabE[:, 32 * W:32 * W + 16], rho[:], rs, rs)
        dma('sp', dap(s5tab, l * 128 * TABW + TW1, [[TABW, 128], [1, 4096]]), tabA[:, 0:4096], rs, ['s5tab'] + rs)
        dma('sp', dap(s5tab, l * 128 * TABW + TBD, [[TABW, 128], [1, 2048]]), tabA[:, 4096:6144], rs, ['s5tab'] + rs)
        dma('sp', dap(s5tab, l * 128 * TABW + TER, [[TABW, 128], [1, TABW - TER]]), tabE[:], rs, ['s5tab'] + rs)

    wbi = [0]
    blkc = [0]
    cur_seg = [0]

    def proj(l, wsrc, pieces_fn, N, evac, hsrc, rkeys):
        i = wbi[0] % NWB
        wbi[0] += 1
        w = wb[i]
        wk = 'wb%d' % i
        blk = l * 56 + blkc[0]
        blkc[0] += 1
        ckey = 'wc%d' % blk
        wflat = w[:].rearrange("q a b -> q (a b)")
        if cur_seg[0] == 0:
            pieces_fn(w, wk)
            if nseg > 1:
                dma('sp', dap(wcache, blk * 128 * KD * 128, [[KD * 128, 128], [1, KD * 128]]), wflat, [wk], [ckey])
        else:
            dma('sp', wflat, dap(wcache, blk * 128 * KD * 128, [[KD * 128, 128], [1, KD * 128]]), [ckey], [wk])
        pm = ps0 if (wbi[0] % 2 == 0) else ps1
        pmk = 'ps0' if (wbi[0] % 2 == 0) else 'ps1'
        for k in range(KD):
            mm(pm[:, 0:TSEG], w[:, k, :], hsrc[:, k, 0:TSEG], k == 0, k == KD - 1, [wk] + rkeys, [pmk])
            if N > TSEG:
                mm(ps2[:, 0:NS], w[:, k, :], hsrc[:, k, TSEG:N], k == 0, k == KD - 1, [wk] + rkeys, ['ps2'])
        evac(pm, pmk)

    def in_pieces(l, col_pieces):
        def f(w, wk):
            off = 0
            for c0, wd in col_pieces:
                dma('pool', w[:, :, off:off + wd], dap(w_in, l * D * DIN + c0, [[DIN, 128], [128 * DIN, KD], [1, wd]]), [], [wk])
                off += wd
        return f

    def qcols(base, tq):
        kt, i = tq // 4, tq % 4
        return [(base + 64 * (8 * kt + i), 64), (base + 64 * (8 * kt + 4 + i), 64)]

    def out_pieces(l, j):
        def f(w, wk):
            base = l * D * D + j * 128
            dma('pool', w[:, 0:4, :], dap(w_out, base, [[D, 128], [128 * D, 4], [1, 128]]), [], [wk])
            dma('pool', w[:, 12:16, :], dap(w_out, base + 1536 * D, [[D, 128], [128 * D, 4], [1, 128]]), [], [wk])
            for half in range(2):
                for kt in range(2):
                    dma('pool', w[64 * half:64 * half + 64, 4 + 4 * kt:8 + 4 * kt, :],
                        dap(w_out, base + (512 + 512 * kt + 256 * half) * D, [[D, 64], [64 * D, 4], [1, 128]]), [], [wk])
        return f

    def rmsnorm(N, gidx, out_fn):
        for k in range(KD):
            s = sq[k % 2]
            act(s[:, 0:N], xT[:, k, 0:N], AF.Square, ['xT'], ['sq%d' % (k % 2)])
            mm(ps0[:, 0:TSEG], onesb[:], s[:, 0:TSEG], k == 0, k == KD - 1, ['sq%d' % (k % 2)], ['ps0'])
            if N > TSEG:
                mm(ps2[:, 0:NS], onesb[:], s[:, TSEG:N], k == 0, k == KD - 1, ['sq%d' % (k % 2)], ['ps2'])
        act(rstd[:, 0:TSEG], ps0[:, 0:TSEG], AF.Sqrt, ['ps0'], ['rstd'], scale=1.0 / D, bias=epsc[:])
        if N > TSEG:
            act(rstd[:, TSEG:N], ps2[:, 0:NS], AF.Sqrt, ['ps2'], ['rstd'], scale=1.0 / D, bias=epsc[:])
        P.add('dve', lambda e: e.reciprocal(out=rstd[:, 0:N], in_=rstd[:, 0:N]), r=['rstd'], w=['rstd'])
        for k in range(KD):
            out_fn(k, gcol[:, gidx * 16 + k:gidx * 16 + k + 1])

    def softmax(src3, np_, nh, Wd, sinkcols, Pf_, Pb_, rk, tag, so=0):
        smk = 'sm%d' % so
        mx = sm[0:np_, so + 0:so + nh]; m_ = sm[0:np_, so + 4:so + 4 + nh]; ng_ = sm[0:np_, so + 8:so + 8 + nh]; ssum = sm[0:np_, so + 12:so + 12 + nh]
        dd = sm[0:np_, so + 16:so + 16 + nh]; es = sm[0:np_, so + 20:so + 20 + nh]; den = sm[0:np_, so + 24:so + 24 + nh]; rd = sm[0:np_, so + 28:so + 28 + nh]
        P.add('dve', lambda e: e.tensor_reduce(out=mx, in_=src3, axis=AX.X, op=ALU.max), r=rk, w=[smk])
        ts('dve', m_, mx, 0.125, ALU.mult, [smk], [smk])
        tt('dve', m_, m_, sinkcols, ALU.max, [smk, 'sinkbc'], [smk])
        ts('dve', ng_, m_, -1.0, ALU.mult, [smk], [smk])
        for i in range(nh):
            act(Pf_[:, i, :], src3[:, i, :], AF.Exp, rk + [smk], [tag + 'Pf', 'glwb', smk + '2%d' % i], scale=0.125, bias=ng_[:, i:i + 1],
                accum=ssum[:, i:i + 1])
        tt('dve', dd, sinkcols, m_, ALU.subtract, [smk, 'sinkbc'], [smk])
        act(es, dd, AF.Exp, [smk], [smk])
        tt('dve', den, ssum, es, ALU.add, [smk] + [smk + '2%d' % i for i in range(nh)], [smk])
        P.add('dve', lambda e: e.reciprocal(out=rd, in_=den), r=[smk], w=[smk])
        tt('pool', Pb_, Pf_, rd.unsqueeze(2).to_broadcast([np_, nh, Wd]), ALU.mult, [tag + 'Pf', smk], [tag + 'Pb', 'glwb'])

    evi = [0]
    for seg in range(nseg):
        N = TSEG + (NS if seg == 0 else 0)
        cur_seg[0] = seg
        dma('sp', cosT[:, 0:TSEG], dap(c_cos, seg * TSEG, [[seqlen + NS, 128], [1, TSEG]]), [], ['rope'])
        dma('sp', sinT[:, 0:TSEG], dap(c_sin, seg * TSEG, [[seqlen + NS, 128], [1, TSEG]]), [], ['rope'])
        if seg == 0:
            dma('sp', cosT[:, TSEG:N], dap(c_cos, seqlen, [[seqlen + NS, 128], [1, NS]]), [], ['rope'])
            dma('sp', sinT[:, TSEG:N], dap(c_sin, seqlen, [[seqlen + NS, 128], [1, NS]]), [], ['rope'])
        blocks = [(b, 128, dap(xp, (seg * TSEG + b * 128) * D, [[D, 128], [1, D]]), b * 128) for b in range(NB)]
        if seg == 0:
            blocks.append((NB, NS, xs.ap(), TSEG))
        for bi, (b, rows, src, c0) in enumerate(blocks):
            io = iost[bi % 2]; iok = 'iost0'
            dma('sp', io[0:rows, :], src, [], [iok])
            for k4 in range(4):
                for kk in range(4):
                    k = 4 * k4 + kk
                    tr(ps3[:, kk * 128:kk * 128 + rows], io[0:rows, k * 128:(k + 1) * 128], ident[0:rows, 0:rows], [iok], ['ps3'])
                cp('dve' if k4 % 2 == 0 else 'act', xT[:, 4 * k4:4 * k4 + 4, c0:c0 + rows],
                   ps3[:, :].rearrange("q (a b) -> q a b", b=128)[:, :, 0:rows], ['ps3'], ['xT'])

        for l in range(depth):
            blkc[0] = 0
            dma('sp', tabA[:, 0:4096], dap(s5tab, l * 128 * TABW + TW1, [[TABW, 128], [1, 4096]]), ['s5tab'], ['tabW'])
            dma('sp', tabA[:, 4096:6144], dap(s5tab, l * 128 * TABW + TBD, [[TABW, 128], [1, 2048]]), ['s5tab'], ['tabA'])
            dma('sp', tabE[:], dap(s5tab, l * 128 * TABW + TER, [[TABW, 128], [1, TABW - TER]]), ['s5tab'], ['tabE'])
            dma('sp', bsrow[:], dap(cbs, l * 512, [[0, 1], [1, 512]]), [], ['bsrow'])
            if seg == 0:
                for b in range(4):
                    cp('pool', bsS[:, :, 8 * b:8 * b + 8], bsrow[:].rearrange("o (g t) -> o g t", t=128)[:, :, 0:8], ['bsrow'], ['bsS'])
            dma('pool', glwb[:], dap(glw, l * 512 * 512, [[512, 128], [128 * 512, 4], [1, 512]]), [], ['glwb', 'p0Pf', 'p0Pb'])
            W1 = tabAb[:, 0:8192].rearrange("q (s t r c) -> q s t r c", s=8, t=4, r=2)
            W3 = tabAb[:, 0:8192].rearrange("q (t a r c) -> q t a r c", t=8, a=16, r=2)
            BD = tabAb[:, 8192:12288].rearrange("q (t a c) -> q t a c", t=4, a=8)
            Er = tabE[:, 0:16 * W].rearrange("q (a b) -> q a b", b=W)
            Ei = tabE[:, 16 * W:32 * W].rearrange("q (a b) -> q a b", b=W)
            rhoc = tabE[:, 32 * W:32 * W + 16]
            NJJ = N // 8
            rmsnorm(N, l, lambda k, g: stt(hT[:, k, 0:N], xT[:, k, 0:N], g, rstd[:, 0:N], ALU.mult, ALU.mult,
                                           ['xT', 'rstd', 'gcol'], ['hT']))

            def evac_store(dst_fn, wkey):
                def f(pm, pmk):
                    evi[0] += 1
                    eng = 'act' if evi[0] % 2 == 0 else 'dve'
                    cp(eng, dst_fn(0, TSEG), pm[:, 0:TSEG], [pmk], [wkey])
                    if N > TSEG:
                        cp(eng, dst_fn(TSEG, N), ps2[:, 0:NS], ['ps2'], [wkey])
                return f

            def evac_gate(tile, wkey):
                def f(pm, pmk):
                    evi[0] += 1
                    t = tA[evi[0] % 2]; tk = 'tA%d' % (evi[0] % 2)
                    act(t[:, 0:TSEG], pm[:, 0:TSEG], AF.Silu, [pmk], [tk])
                    if N > TSEG:
                        act(t[:, TSEG:N], ps2[:, 0:NS], AF.Silu, ['ps2'], [tk])
                    tt('dve', mix[:, tile, 0:N], mix[:, tile, 0:N], t[:, 0:N], ALU.mult, [tk, wkey], [wkey])
                return f

            for T in range(4):
                proj(l, None, in_pieces(l, [(OFF_UC + 128 * T, 128)]), N,
                     evac_store(lambda a, b, T=T: mix[:, 12 + T, a:b], 'mixc%d' % T), hT, ['hT'])
            if seg == 0:
                for pi_i in range(16):
                    for ri in range(2):
                        cp('pool', Sprev[:, pi_i, ri, NJ:NJ + 4], s_in[:, l, :, ri, pi_i], ['s_in'], ['Sprev%d' % pi_i])
            G = 4 if NJJ * 8 <= 512 else 2
            WP = 512 // (2 * G)
            Er4 = Er.rearrange("q (t r) j -> q t r j", r=4); Ei4 = Ei.rearrange("q (t r) j -> q t r j", r=4)
            Sp4 = Sprev[:].rearrange("q (t r) i j -> q t r i j", r=4)
            hp4 = hprev[:].rearrange("q l (t r) i -> q l t r i", r=4)
            gi = 0
            for r4 in range(4):
              for T0 in range(0, 4, G):
                gi += 1
                pd = ps3 if gi % 2 == 0 else ps6
                pdk = 'ps3' if gi % 2 == 0 else 'ps6'
                pis = [4 * (T0 + g) + r4 for g in range(G)]
                sks = ['Sprev%d' % p_ for p_ in pis]
                for g in range(G):
                    T = T0 + g
                    uview = mix[32 * r4:32 * r4 + 32, 12 + T, 0:N].rearrange("q (j s) -> q j s", s=8)
                    for ri in range(2):
                        c0 = (g * 2 + ri) * WP
                        for s_ in range(8):
                            mm(pd[:, c0:c0 + NJJ], W1[32 * r4:32 * r4 + 32, s_, T, ri, :], uview[:, :, s_], s_ == 0, s_ == 7,
                               ['tabW', 'mixc%d' % T], [pdk], tile_position=(32 * r4, 0))
                Dv = pd[:, 0:512].rearrange("q (g r j) -> q g r j", g=G, r=2)
                Dre = Dv[:, :, 0, 0:NJJ]; Dim = Dv[:, :, 1, 0:NJJ]
                er = Er4[:, T0:T0 + G, r4, 0:NJJ]; ei = Ei4[:, T0:T0 + G, r4, 0:NJJ]
                w_ = [x[:, 0:G * NJJ].rearrange("q (g j) -> q g j", g=G) for x in s5w]
                k5 = ['s5w']
                tt('dve', w_[0], Dre, er, ALU.mult, [pdk, 'tabE'], k5); tt('dve', w_[1], Dim, ei, ALU.mult, [pdk, 'tabE'], k5)
                tt('pool', w_[2], w_[0], w_[1], ALU.add, k5, k5)
                tt('dve', w_[0], Dim, er, ALU.mult, [pdk, 'tabE'] + k5, k5); tt('dve', w_[1], Dre, ei, ALU.mult, [pdk, 'tabE'], k5)
                tt('pool', w_[3], w_[0], w_[1], ALU.subtract, k5, k5)
                for g in range(G):
                    pi_i = pis[g]
                    rb = rhoc[:, pi_i:pi_i + 1]
                    for ri, (cc, qq) in enumerate(((w_[2], w_[4]), (w_[3], w_[5]))):
                        P.add('dve', lambda e, cc=cc[:, g, 0:NJ], qq=qq[:, g, 0:NJ], rbb=rb.to_broadcast([128, NJ]),
                              ini=hprev[:, l, pi_i, ri:ri + 1]:
                              e.tensor_tensor_scan(out=qq, data0=rbb, data1=cc, initial=ini,
                                                   op0=ALU.mult, op1=ALU.add), r=k5 + ['hprev', 'tabE'], w=k5)
                        if seg == 0:
                            stt(qq[:, g, NJ:NJJ], s_in[:, l, :, ri, pi_i], rb, cc[:, g, NJ:NJJ], ALU.mult, ALU.add, k5 + ['s_in', 'tabE'], k5)
                tt('dve', w_[6], w_[4], er, ALU.mult, k5 + ['tabE'], k5); tt('dve', w_[7], w_[5], ei, ALU.mult, k5 + ['tabE'], k5)
                tt('pool', w_[8], w_[6], w_[7], ALU.subtract, k5, k5)
                tt('dve', w_[6], w_[5], er, ALU.mult, k5 + ['tabE'], k5); tt('dve', w_[7], w_[4], ei, ALU.mult, k5 + ['tabE'], k5)
                tt('pool', w_[9], w_[6], w_[7], ALU.add, k5, k5)
                for ri, S_ in enumerate((w_[8], w_[9])):
                    cp('act', Sp4[:, T0:T0 + G, r4, ri, 0:1], hp4[:, l, T0:T0 + G, r4, ri:ri + 1], ['hprev'], sks)
                    cp('act', Sp4[:, T0:T0 + G, r4, ri, 1:NJ], S_[:, :, 0:NJ - 1], k5, sks)
                    cp('dve', hp4[:, l, T0:T0 + G, r4, ri:ri + 1], S_[:, :, NJ - 1:NJ], k5 + sks, ['hprev'])
                    if seg == 0:
                        for g in range(G):
                            cp('pool', outst[:, 0:128].rearrange("q (b r a) -> q b r a", b=4, r=2)[:, :, ri, pis[g]], S_[:, g, NJ:NJJ],
                               k5, ['outst_s'])
            if seg == 0:
                for b in range(4):
                    for ri, dst in enumerate((hr_s, hi_s)):
                        tr(ps3[0:16, 0:128], outst[:, (b * 2 + ri) * 16:(b * 2 + ri) * 16 + 16], ident[:], ['outst_s'], ['ps3'])
                        cp('dve', outst[0:16, 128:256], ps3[0:16, 0:128], ['ps3'], ['outst_t'])
                        dma('sp', dap(dst, ((l * 4 + b) * 16) * 128, [[128, 16], [1, 128]]), outst[0:16, 128:256], ['outst_t'], ['out'])
            if seg == nseg - 1:
                for ri, dst in enumerate((hr_p, hi_p)):
                    cp('dve', outst[:, 256 + 16 * ri:272 + 16 * ri], hprev[:, l, :, ri], ['hprev'], ['outst_h'])
                    tr(ps3[0:16, 0:128], outst[:, 256 + 16 * ri:272 + 16 * ri], ident[:], ['outst_h'], ['ps3'])
                    cp('dve', outst[0:16, 288:416], ps3[0:16, 0:128], ['ps3'], ['outst_t2'])
                    dma('sp', dap(dst, l * 2048, [[128, 16], [1, 128]]), outst[0:16, 288:416], ['outst_t2'], ['out'])
            dma('sp', tabA[:, 0:4096], dap(s5tab, l * 128 * TABW + TW3, [[TABW, 128], [1, 4096]]), ['s5tab'], ['tabW'])
            def evac_rope(dst_fn, wkey):
                def f(pm, pmk):
                    evi[0] += 1
                    i2 = evi[0] % 2
                    xf = tA[i2]; xfk = 'tA%d' % i2
                    xb_ = sq[i2]; xbk = 'sq%d' % i2
                    pw_ = ps3 if i2 == 0 else ps6
                    pwk = 'ps3' if i2 == 0 else 'ps6'
                    cp('act', xb_[:, 0:TSEG], pm[:, 0:TSEG], [pmk], [xbk])
                    if N > TSEG:
                        cp('act', xb_[:, TSEG:N], ps2[:, 0:NS], ['ps2'], [xbk])
                    mm(pw_[:, 0:TSEG], permb[:], xb_[:, 0:TSEG], True, True, [xbk, 'ident'], [pwk])
                    tt('dve', xf[:, 0:TSEG], pw_[:, 0:TSEG], sinT[:, 0:TSEG], ALU.mult, [pwk, 'rope'], [xfk])
                    if N > TSEG:
                        mm(pw_[:, 0:NS], permb[:], xb_[:, TSEG:N], True, True, [xbk, 'ident', xfk], [pwk])
                        tt('dve', xf[:, TSEG:N], pw_[:, 0:NS], sinT[:, TSEG:N], ALU.mult, [pwk, 'rope'], [xfk])
                    tt('pool', xw0[:, 0:N], xb_[:, 0:N], cosT[:, 0:N], ALU.mult, [xbk, 'rope'], ['xsw0'])
                    tt('dve', dst_fn(0, N), xf[:, 0:N], xw0[:, 0:N], ALU.add, [xfk, 'xsw0'], [wkey])
                return f

            for kt in range(2):
                cp('pool', kT[:, kt, 0:128], khalo[:, l, kt, :], ['khalo'], ['kT'])
            cp('pool', Vt[:, 0, :], vhalo[:, l, :], ['vhalo'], ['Vt'])
            for kt in range(2):
                proj(l, None, in_pieces(l, [(OFF_K + 128 * kt, 128)]), N,
                     evac_rope(lambda a, b, kt=kt: kT[:, kt, 128 + a:128 + b], 'kT'), hT, ['hT'])
            for kt in range(2):
                proj(l, None, in_pieces(l, [(OFF_V + 128 * kt, 128)]), N,
                     evac_store(lambda a, b: tB[0][:, a:b], 'tB0'), hT, ['hT'])
                nblk = NB + (1 if seg == 0 else 0)
                for b in range(nblk):
                    rows = 128 if b < NB else NS
                    tr(psT[0:rows, b * 128:b * 128 + 128], tB[0][:, b * 128:b * 128 + rows], identb[:], ['tB0'], ['psT'])
                cp('dve', Vt[:, 1:1 + NB, 128 * kt:128 * kt + 128], psT[:, 0:NB * 128].rearrange("q (a b) -> q a b", b=128), ['psT'], ['Vt'])
                if seg == 0:
                    cp('dve', Vt[0:NS, NB + 1, 128 * kt:128 * kt + 128], psT[0:NS, NB * 128:NB * 128 + 128], ['psT'], ['Vt'])
            for tq in range(8):
                proj(l, None, in_pieces(l, qcols(OFF_Q, tq)), N,
                     evac_rope(lambda a, b, tq=tq: mix[:, 4 + tq, a:b], 'mixb%d' % tq), hT, ['hT'])
            for kt in range(2):
                cp('pool', khalo[:, l, kt, :], kT[:, kt, TSEG:TSEG + 128], ['kT'], ['khalo'])
            cp('pool', vhalo[:, l, :], Vt[:, NB, :], ['Vt'], ['vhalo'])
            if seg == nseg - 1:
                for kt in range(2):
                    tr(psT[:, 0:128], kT[:, kt, TSEG:TSEG + 128], identb[:], ['kT'], ['psT'])
                    cp('dve', PTs[:, 128 * kt:128 * kt + 128], psT[:, 0:128], ['psT'], ['PTs0'])
                dma('pool', dap(nk_p, l * 128 * 256, [[256, 128], [1, 256]]), PTs[:, 0:256], ['PTs0'], ['out'])
                dma('pool', dap(nv_p, l * 128 * 256, [[256, 128], [1, 256]]), Vt[:, NB, :], ['Vt'], ['out'])
            if seg == 0:
                for kt in range(2):
                    tr(psT[0:NS, 0:128], kT[:, kt, 128 + TSEG:128 + N], identb[:], ['kT'], ['psT'])
                    cp('dve', PTs[0:NS, 128 * kt:128 * kt + 128], psT[0:NS, 0:128], ['psT'], ['PTs0'])
                dma('pool', dap(nk_s, l * NS * 256, [[256, NS], [1, 256]]), PTs[0:NS, 0:256], ['PTs0'], ['out'])
                dma('pool', dap(nv_s, l * NS * 256, [[256, NS], [1, 256]]), Vt[0:NS, NB + 1, :], ['Vt'], ['out'])
            for T in range(4):
                uvw = mix[:, 12 + T, 0:N].rearrange("q (j s) -> q j s", s=8)
                for t in range(8):
                    reg = psS[:, (t // 4) * 512 + (t % 4) * W:(t // 4) * 512 + (t % 4) * W + NJJ]
                    for s in range(t + 1):
                        mm(reg, BD[:, T, t - s, :], uvw[:, :, s], s == 0, False, ['tabA', 'mixc%d' % T], ['psS'])
                    for r4 in range(4):
                        pi_i = 4 * T + r4
                        for ri in range(2):
                            o_ = psS[32 * r4:32 * r4 + 32, (t // 4) * 512 + (t % 4) * W:(t // 4) * 512 + (t % 4) * W + NJJ]
                            mm(o_, W3[:, t, pi_i, ri, :], Sprev[:, pi_i, ri, 0:NJJ], False, (ri == 1),
                               ['tabW', 'Sprev%d' % pi_i], ['psS'], tile_position=(0, 32 * r4))
                yv = tA[2][:, 0:N].rearrange("q (j s) -> q s j", s=8)
                for hb in range(2):
                    pv = psS[:, hb * 512:hb * 512 + 4 * W].rearrange("q (t j) -> q t j", j=W)[:, :, 0:NJJ]
                    uv2 = mix[:, 12 + T, 0:N].rearrange("q (j s) -> q s j", s=8)[:, 4 * hb:4 * hb + 4, :]
                    stt(yv[:, 4 * hb:4 * hb + 4, :], uv2, dcol[:, l * 4 + T:l * 4 + T + 1], pv, ALU.mult, ALU.add,
                        ['psS', 'mixc%d' % T, 'dcol'], ['tA2'])
                y_ = tA[2][:, 0:N]; g1 = tA[3][:, 0:N]
                tt('pool', g1, y_, y_, ALU.mult, ['tA2'], ['tA3'])
                ts('dve', g1, g1, 0.044715, ALU.mult, ['tA3'], ['tA3'], s2=1.0, op1=ALU.add)
                tt('pool', g1, g1, y_, ALU.mult, ['tA3', 'tA2'], ['tA3'])
                act(g1, g1, AF.Sigmoid, ['tA3'], ['tA3'], scale=2.0 * math.sqrt(2.0 / math.pi))
                tt('dve', yb[:, T, 0:N], y_, g1, ALU.mult, ['tA2', 'tA3'], ['yb'])
            for T in range(4):
                for kc in range(4):
                    mm(ps0[:, 0:TSEG], glwb[:, kc, 128 * T:128 * T + 128], yb[:, kc, 0:TSEG], kc == 0, kc == 3, ['glwb', 'yb'], ['ps0'])
                    if N > TSEG:
                        mm(ps2[:, 0:NS], glwb[:, kc, 128 * T:128 * T + 128], yb[:, kc, TSEG:N], kc == 0, kc == 3, ['glwb', 'yb'], ['ps2'])
                g2 = tA[3]
                act(g2[:, 0:TSEG], ps0[:, 0:TSEG], AF.Sigmoid, ['ps0'], ['tA3'], bias=gbcol[:, l * 4 + T:l * 4 + T + 1])
                if N > TSEG:
                    act(g2[:, TSEG:N], ps2[:, 0:NS], AF.Sigmoid, ['ps2'], ['tA3'], bias=gbcol[:, l * 4 + T:l * 4 + T + 1])
                tt('dve', mix[:, 12 + T, 0:N], yb[:, T, 0:N], g2[:, 0:N], ALU.mult, ['yb', 'tA3'], ['mixc%d' % T])
            for T in range(4):
                proj(l, None, in_pieces(l, [(OFF_GC + 128 * T, 128)]), N, evac_gate(12 + T, 'mixc%d' % T), hT, ['hT'])

            units = [(bq, kt, half) for bq in range(NB) for kt in range(2) for half in range(2)]
            import os as _os
            if _os.environ.get('V1'):
                psT2 = [psT, psT]; psTk = ['psT', 'psT']
            else:
                psT2 = [psT, ps3[:].bitcast(BF16)]; psTk = ['psT', 'ps3']
            Obuf = [ps6, ps2]; Obk = ['ps6', 'ps2']

            def stage1(u, bq, kt, half):
                par = u % 2
                Sb = psS if par == 0 else ps01
                Sk = ['psS'] if par == 0 else ['ps0', 'ps1']
                msk = maskF if (seg == 0 and bq == 0) else maskA
                hs = slice(64 * half, 64 * half + 64)
                for i2 in range(2):
                    for i in (2 * i2, 2 * i2 + 1):
                        tq = 4 * kt + i
                        mm(Sb[:, i * 256:(i + 1) * 256], mix[hs, 4 + tq, 128 * bq:128 * bq + 128],
                           kT[hs, kt, 128 * bq:128 * bq + 256], i % 2 == 0, False, ['mixb%d' % tq, 'kT'], Sk)
                    mm(Sb[:, i2 * 512:(i2 + 1) * 512], identb[:], msk[:], False, True, ['masks', 'ident'], Sk)

            def smv(par, np_=128, nh=4):
                so = 32 * par
                return dict(mx=sm[0:np_, so + 0:so + nh], m=sm[0:np_, so + 4:so + 4 + nh], ng=sm[0:np_, so + 8:so + 8 + nh],
                            ssum=sm[0:np_, so + 12:so + 12 + nh], dd=sm[0:np_, so + 16:so + 16 + nh], es=sm[0:np_, so + 20:so + 20 + nh],
                            den=sm[0:np_, so + 24:so + 24 + nh], rd=sm[0:np_, so + 28:so + 28 + nh])

            def stageF(u, bq, kt, half):
                par = u % 2
                Sb = psS if par == 0 else ps01
                Sk = ['psS'] if par == 0 else ['ps0', 'ps1']
                v = smv(par); kA = 'smA%d' % par
                h0 = l * 16 + 8 * kt + 4 * half
                sinkcols = sinkbc[:, h0:h0 + 4]
                src3 = Sb[:, :].rearrange("q (a b) -> q a b", b=256)
                P.add('dve', lambda e, o=v['mx'], i=src3: e.tensor_reduce(out=o, in_=i, axis=AX.X, op=ALU.max), r=Sk, w=[kA])
                ts('dve', v['m'], v['mx'], 0.125, ALU.mult, [kA], [kA])
                tt('dve', v['m'], v['m'], sinkcols, ALU.max, [kA, 'sinkbc'], [kA])
                ts('dve', v['ng'], v['m'], -1.0, ALU.mult, [kA], [kA])
                tt('dve', v['dd'], sinkcols, v['m'], ALU.subtract, [kA, 'sinkbc'], [kA])

            def stageX(u, bq, kt, half):
                par = u % 2
                Sb = psS if par == 0 else ps01
                Sk = ['psS'] if par == 0 else ['ps0', 'ps1']
                v = smv(par); kA = 'smA%d' % par; kS = 'smS%d' % par
                src3 = Sb[:, :].rearrange("q (a b) -> q a b", b=256)
                for i in range(4):
                    act(Pf2[par][:, i, :], src3[:, i, :], AF.Exp, Sk + [kA], ['p%dPf' % par, 'glwb', kS], scale=0.125,
                        bias=v['ng'][:, i:i + 1], accum=v['ssum'][:, i:i + 1])
                act(v['es'], v['dd'], AF.Exp, [kA], [kS])

            def stageB(u, bq, kt, half):
                par = u % 2
                v = smv(par); kS = 'smS%d' % par; kR = 'smR%d' % par
                tt('dve', v['den'], v['ssum'], v['es'], ALU.add, [kS], [kR])
                P.add('dve', lambda e, o=v['rd'], i=v['den']: e.reciprocal(out=o, in_=i), r=[kR], w=[kR])
                tt('pool', Pb2[par], Pf2[par], v['rd'].unsqueeze(2).to_broadcast([128, 4, 256]), ALU.mult, ['p%dPf' % par, kR],
                   ['p%dPb' % par, 'glwb'])

            def stage2(u, bq, kt, half):
                par = u % 2
                hs = slice(64 * half, 64 * half + 64)
                pT = psT2[par]; pTk = psTk[par]; PT_ = PTs2[par]; PTk = 'PTs%d' % par
                Ob = Obuf[kt]; Ok = Obk[kt]
                for i in range(4):
                    for kb in range(2):
                        tr(pT[:, (2 * i + kb) * 128:(2 * i + kb) * 128 + 128], Pb2[par][:, i, kb * 128:kb * 128 + 128], identb[:],
                           ['p%dPb' % par], [pTk])
                cp('act', PT_[:], pT[:], [pTk], [PTk])
                for i in range(4):
                    for kb in range(2):
                        mm(Ob[hs, i * 128:i * 128 + 128], Vt[:, bq + kb, (2 * kt + half) * 64:(2 * kt + half) * 64 + 64],
                           PT_[:, (2 * i + kb) * 128:(2 * i + kb) * 128 + 128], kb == 0, kb == 1, ['Vt', PTk], [Ok],
                           tile_position=(0, 64 * half))
                if half == 1:
                    for i in range(4):
                        cp('dve', mix[:, 4 + 4 * kt + i, 128 * bq:128 * bq + 128], Ob[:, i * 128:i * 128 + 128],
                           [Ok], ['mixb%d' % (4 * kt + i)])
            nu = len(units)
            stage1(0, *units[0])
            for k_ in range(nu + 2):
                if k_ + 1 < nu:
                    stage1(k_ + 1, *units[k_ + 1])
                if k_ >= 2:
                    stageB(k_ - 2, *units[k_ - 2])
                if k_ < nu:
                    stageF(k_, *units[k_])
                    stageX(k_, *units[k_])
                if k_ >= 2:
                    stage2(k_ - 2, *units[k_ - 2])
            if seg == 0:
                dma('pool', ckb, dap(ck, l * 4 * 128 * 256, [[256, 128], [128 * 256, 4], [1, 256]]), [], ['iost0'])
                dma('pool', cvb, dap(cv, l * 4 * 128 * 256, [[256, 128], [128 * 256, 4], [1, 256]]), [], ['iost0'])
                for b in range(4):
                    for kt in range(2):
                        tr(psT[:, (2 * b + kt) * 128:(2 * b + kt) * 128 + 128], ckb[:, b, kt * 128:kt * 128 + 128], identb[:], ['iost0'], ['psT'])
                cp('dve', ckT, psT[:].rearrange("q (b k w) -> q b k w", b=4, k=2), ['psT'], ['iost0'])
                for kt in range(2):
                    for half in range(2):
                        hs = slice(64 * half, 64 * half + 64)
                        for i in range(4):
                            tq = 4 * kt + i
                            qs = mix[hs, 4 + tq, TSEG:N]
                            for b in range(4):
                                mm(psS[0:NS, b * 128:b * 128 + 128], qs, ckT[hs, b, kt, :], b == 0, False, ['mixb%d' % tq, 'iost0'], ['psS'])
                            mm(psS[0:NS, 0:512], identb[0:NS, 0:NS], maskS[:, 0:512], False, True, ['masks', 'ident'], ['psS'])
                            mm(psS[0:NS, 512:512 + NS], qs, kT[hs, kt, 128 + TSEG:128 + N], True, False, ['mixb%d' % tq, 'kT'], ['psS'])
                            mm(psS[0:NS, 512:512 + NS], identb[0:NS, 0:NS], maskS[:, 512:544], False, True, ['masks', 'ident'], ['psS'])
                            h0 = l * 16 + 8 * kt + 4 * half + i
                            softmax(psS[0:NS, 0:544].rearrange("q (a b) -> q a b", b=544), NS, 1, 544, sinkbc[0:NS, h0:h0 + 1],
                                    PfS.rearrange("q (a b) -> q a b", b=544), PbS.rearrange("q (a b) -> q a b", b=544), ['psS'], 'p0')
                            for b in range(4):
                                tr(psT[:, b * 32:b * 32 + 32], PbS[:, b * 128:b * 128 + 128], identb[0:NS, 0:NS], ['p0Pb'], ['psT'])
                            tr(psT[0:NS, 128:160], PbS[:, 512:544], identb[0:NS, 0:NS], ['p0Pb'], ['psT'])
                            cp('act', PTS2[:, 0:4, :], psT[:, 0:128].rearrange("q (a b) -> q a b", b=32), ['psT'], ['PTS2'])
                            cp('act', PTS2[0:NS, 4, :], psT[0:NS, 128:160], ['psT'], ['PTS2'])
                            vc = (2 * kt + half) * 64
                            for b in range(4):
                                mm(ps6[hs, i * 32:i * 32 + 32], cvb[:, b, vc:vc + 64], PTS2[:, b, :], b == 0, False, ['iost0', 'PTS2'], ['ps6'],
                                   tile_position=(0, 64 * half))
                            mm(ps6[hs, i * 32:i * 32 + 32], Vt[0:NS, NB + 1, vc:vc + 64], PTS2[0:NS, 4, :], False, True, ['Vt', 'PTS2'], ['ps6'],
                               tile_position=(0, 64 * half))
                    for i in range(4):
                        cp('act' if i % 2 else 'dve', mix[:, 4 + 4 * kt + i, TSEG:N], ps6[:, i * 32:i * 32 + 32], ['ps6'], ['mixb%d' % (4 * kt + i)])
            for tq in range(8):
                proj(l, None, in_pieces(l, qcols(OFF_GB, tq)), N, evac_gate(4 + tq, 'mixb%d' % tq), hT, ['hT'])

            for T in range(4):
                proj(l, None, in_pieces(l, [(OFF_UA + 128 * T, 128)]), N,
                     evac_store(lambda a, b, T=T: mix[:, T, a:b], 'mixa%d' % T), hT, ['hT'])
            for T in range(4):
                proj(l, None, in_pieces(l, [(OFF_VA + 128 * T, 128)]), N,
                     evac_store(lambda a, b: tB[1][:, a:b], 'tB0'), hT, ['hT'])
                nblk = NB + (1 if seg == 0 else 0)
                for b in range(nblk):
                    rows = 128 if b < NB else NS
                    tr(psT[0:rows, b * 128:b * 128 + 128], tB[1][:, b * 128:b * 128 + rows], identb[:], ['tB0'], ['psT'])
                cp('dve', vatm[:, 0:NB, 128 * T:128 * T + 128], psT[:, 0:NB * 128].rearrange("q (a b) -> q a b", b=128), ['psT'], ['vatm'])
                if seg == 0:
                    cp('dve', vatm[0:NS, NB, 128 * T:128 * T + 128], psT[0:NS, NB * 128:NB * 128 + 128], ['psT'], ['vatm'])
            if seg == 0:
                dma('pool', dap(va_s, l * NS * 512, [[512, NS], [1, 512]]), vatm[0:NS, NB, :], ['vatm'], ['out'])
            for h in range(4):
                lh = l * 4 + h
                for b in range(NB):
                    mm(ps3[:, b * 128:b * 128 + 128], vatm[:, b, 128 * h:128 * h + 128], wsT[:, lh, :], b == 0, False, ['vatm', 'wsT'], ['ps3'])
                for b in range(NB):
                    P.add('pe', lambda e, b=b, lh=lh: e.matmul(ps3[:, b * 128:b * 128 + 128], lhsT=ones1[0:1, :],
                                                              rhs=bsrow[0:1, (lh % 4) * 128:(lh % 4) * 128 + 128], start=False, stop=(b == NB - 1)),
                          r=['bsrow', 'ident'], w=['ps3'])
                tt('dve', mix[:, h, 0:TSEG], mix[:, h, 0:TSEG], ps3[:, 0:TSEG], ALU.mult, ['ps3', 'mixa%d' % h], ['mixa%d' % h])
                if seg == 0:
                    mm(ps6[:, 0:NS], vatm[0:NS, NB, 128 * h:128 * h + 128], wsS[:, lh, :], True, False, ['vatm', 'wsS'], ['ps6'])
                    P.add('pe', lambda e, lh=lh: e.matmul(ps6[:, 0:NS], lhsT=ones1[0:1, :], rhs=bsS[0:1, lh % 4, :], start=False, stop=True),
                          r=['bsS', 'ident'], w=['ps6'])
                    tt('dve', mix[:, h, TSEG:N], mix[:, h, TSEG:N], ps6[:, 0:NS], ALU.mult, ['ps6', 'mixa%d' % h], ['mixa%d' % h])
            for T in range(4):
                proj(l, None, in_pieces(l, [(OFF_GA + 128 * T, 128)]), N, evac_gate(T, 'mixa%d' % T), hT, ['hT'])

            allmix = ['mixa%d' % i for i in range(4)] + ['mixb%d' % i for i in range(8)] + ['mixc%d' % i for i in range(4)]

            def evac_res(j):
                def f(pm, pmk):
                    tt('dve', xT[:, j, 0:TSEG], xT[:, j, 0:TSEG], pm[:, 0:TSEG], ALU.add, [pmk, 'xT'], ['xT'])
                    if N > TSEG:
                        tt('dve', xT[:, j, TSEG:N], xT[:, j, TSEG:N], ps2[:, 0:NS], ALU.add, ['ps2', 'xT'], ['xT'])
                return f
            for j in range(KD):
                proj(l, None, out_pieces(l, j), N, evac_res(j), mix, allmix)

        rmsnorm(N, depth, lambda k, g: stt(xT[:, k, 0:N], xT[:, k, 0:N], g, rstd[:, 0:N], ALU.mult, ALU.mult,
                                           ['xT', 'rstd', 'gcol'], ['xT']))
        oblocks = [(128, b * 128, dap(y_p, (seg * TSEG + b * 128) * D, [[D, 128], [1, D]])) for b in range(NB)]
        if seg == 0:
            oblocks.append((NS, TSEG, y_s.ap()))
        for bi, (rows, c0, dst) in enumerate(oblocks):
            io = iost[bi % 2]; iok = 'iost0'
            for k4 in range(4):
                for kk in range(4):
                    k = 4 * k4 + kk
                    tr(ps3[0:rows, kk * 128:kk * 128 + 128], xT[:, k, c0:c0 + rows], ident[:], ['xT'], ['ps3'])
                cp('dve' if k4 % 2 == 0 else 'act', io[0:rows, 512 * k4:512 * k4 + 512], ps3[0:rows, :], ['ps3'], [iok])
            dma('sp', dst, io[0:rows, :], [iok], ['out'])

    import os
    ks = os.environ.get('KSTOP')
    if ks:
        print('NOPS', len(P.ops)); print('LASTOPS', [(i, o['eng'], o['line']) for i, o in list(enumerate(P.ops))[max(0, int(ks) - 3):int(ks)]]); P.ops = P.ops[:int(ks)]
    P.emit()
    st.close()
    return nc


def _consts(seqlen):
    half = 8
    inv = (500000.0 ** (-np.arange(half, dtype=np.float32) * 2.0 / 16.0)).astype(np.float32)
    pos = np.concatenate([np.arange(seqlen, dtype=np.float32), np.arange(8, dtype=np.float32) + PAST] * 1)
    pos = np.concatenate([np.arange(seqlen, dtype=np.float32)] + [np.arange(8, dtype=np.float32) + np.float32(PAST)] * 4)
    ang = pos[None, :] * inv[:, None]
    cosv = np.cos(ang).astype(np.float32); sinv = np.sin(ang).astype(np.float32)
    C = np.ones((128, pos.shape[0]), np.float32); S = np.zeros((128, pos.shape[0]), np.float32)
    for h0 in (0, 64):
        C[h0:h0 + 8] = cosv; C[h0 + 8:h0 + 16] = cosv
        S[h0:h0 + 8] = -sinv; S[h0 + 8:h0 + 16] = sinv
    i = np.arange(128)[:, None]; j = np.arange(256)[None, :]
    diff = i + 128 - j
    valid = (diff >= 0) & (diff < 128)
    mA = np.where(valid, 0.0, NEGM).astype(np.float32)
    mF = mA.copy(); mF[:, 0:128] = NEGM
    mA2 = np.concatenate([mA, mA], 1); mF2 = np.concatenate([mF, mF], 1)
    mS = np.full((32, 544), NEGM, np.float32)
    for b in range(4):
        for t in range(8):
            q = 8 * b + t
            for jj in range(128):
                if jj > t:
                    mS[q, b * 128 + jj] = 0.0
            for s in range(t + 1):
                mS[q, 512 + 8 * b + s] = 0.0
    tril = np.tril(np.ones((128, 128), np.float32))
    maskm = np.zeros((128, 2), np.float32)
    for q in range(128):
        maskm[q, (q // 16) % 2] = 1.0
    bd = np.zeros((32, 32), np.float32)
    for b in range(4):
        for t in range(8):
            for s in range(t + 1):
                bd[8 * b + t, 8 * b + s] = 1.0
    perm = np.zeros((128, 128), np.float32)
    for m_ in range(128):
        mm_ = m_ % 64
        if mm_ < 8:
            perm[m_ + 8, m_] = 1.0
        elif mm_ < 16:
            perm[m_ - 8, m_] = 1.0
    return dict(c_perm=perm, c_ident=np.eye(128, dtype=np.float32), c_tril=tril, c_maskA=mA2, c_maskF=mF2, c_maskS=mS, c_maskm=maskm,
                c_bd=bd, c_cos=C, c_sin=S)


_NC_CACHE = {}


def kernel(x_prompt, x_sample, cache_swa_k, cache_swa_v, state_ssm_re, state_ssm_im,
           norm_g, final_norm_g, w_in, w_out, chunk_w_s, chunk_b_s, attn_sinks,
           ssm_a_re, ssm_a_im, ssm_log_dt, ssm_b_re, ssm_b_im, ssm_c_re, ssm_c_im,
           ssm_d, glu_w, glu_b):
    f = lambda a: np.ascontiguousarray(np.asarray(a, dtype=np.float32))
    depth = int(np.asarray(w_in).shape[0]); seqlen = int(np.asarray(x_prompt).shape[1])
    nseg = seqlen // TSEG
    key = (depth, nseg)
    if key not in _NC_CACHE:
        _NC_CACHE[key] = build_program(depth, nseg)
    nc = _NC_CACHE[key]
    cst = _consts(seqlen)
    shared = dict(
        ng=np.concatenate([f(norm_g).reshape(depth * 16, 128), f(final_norm_g).reshape(16, 128)], 0),
        w_in=f(w_in), w_out=f(w_out), cws=f(chunk_w_s), cbs=f(chunk_b_s).reshape(1, -1), sinks=f(attn_sinks).reshape(1, -1),
        a_re=f(ssm_a_re), a_im=f(ssm_a_im), ldt=f(ssm_log_dt), b_re=f(ssm_b_re), b_im=f(ssm_b_im),
        c_re=f(ssm_c_re).reshape(depth, 512, 64), c_im=f(ssm_c_im).reshape(depth, 512, 64),
        dsk=f(ssm_d).reshape(depth * 4, 128), glw=f(glu_w), glb=f(glu_b).reshape(depth * 4, 128), **cst)
    xpf, xsf = f(x_prompt), f(x_sample)
    ckf, cvf = f(cache_swa_k), f(cache_swa_v)
    srf, sif = f(state_ssm_re), f(state_ssm_im)
    nb = xpf.shape[0]
    in_maps = []
    for c in range(8):
        m = dict(shared)
        m["xp"] = xpf[c % nb]
        m["xs"] = xsf[4 * c:4 * c + 4].reshape(NS, D)
        m["ck"] = ckf[:, 4 * c:4 * c + 4].reshape(depth, 4, 128, 256)
        m["cv"] = cvf[:, 4 * c:4 * c + 4].reshape(depth, 4, 128, 256)
        m["sre"] = srf[:, 4 * c:4 * c + 4].reshape(depth, 4, 16, 128)
        m["sim"] = sif[:, 4 * c:4 * c + 4].reshape(depth, 4, 16, 128)
        in_maps.append(m)
    res = run_bass_kernel_spmd(nc, in_maps, core_ids=list(range(8))).results
    y_prompt = np.stack([res[b]["y_p"] for b in range(nb)], 0)
    y_sample = np.concatenate([res[c]["y_s"].reshape(4, 8, D) for c in range(8)], 0)
    nkp = np.stack([res[b]["nk_p"].reshape(depth, 128, 4, 64) for b in range(nb)], 1)
    nvp = np.stack([res[b]["nv_p"].reshape(depth, 128, 4, 64) for b in range(nb)], 1)
    nks = np.concatenate([res[c]["nk_s"].reshape(depth, 4, 8, 4, 64) for c in range(8)], 1)
    nvs = np.concatenate([res[c]["nv_s"].reshape(depth, 4, 8, 4, 64) for c in range(8)], 1)
    hrp = np.stack([res[b]["hr_p"].reshape(depth, 32, 64) for b in range(nb)], 1)
    hip = np.stack([res[b]["hi_p"].reshape(depth, 32, 64) for b in range(nb)], 1)
    hrs = np.concatenate([res[c]["hr_s"].reshape(depth, 4, 32, 64) for c in range(8)], 1)
    his = np.concatenate([res[c]["hi_s"].reshape(depth, 4, 32, 64) for c in range(8)], 1)
    vas = np.concatenate([res[c]["va_s"].reshape(depth, 4, 8, 512) for c in range(8)], 1)
    return tuple(np.ascontiguousarray(a, dtype=np.float32) for a in
                 (y_prompt, y_sample, nkp, nvp, nks, nvs, hrp, hip, hrs, his, vas))
```

```python
import math
import numpy as np
import ml_dtypes
import concourse.bass as bass
import concourse.mybir as mybir
from concourse.bass_utils import run_bass_kernel_spmd

F32 = mybir.dt.float32
BF16 = mybir.dt.bfloat16
ALU = mybir.AluOpType
AF = mybir.ActivationFunctionType
AX = mybir.AxisListType

D = 2048
KD = 16
DIN = 5120
DEPTH = 4
SEQ = 4096
TSEG = 512
NSEG = SEQ // TSEG
NB = TSEG // 128
NS = 32
NJ = TSEG // 8
PAST = 16384
OFF_UA, OFF_VA, OFF_GA, OFF_Q, OFF_K, OFF_V, OFF_GB, OFF_UC, OFF_GC = 0, 512, 1024, 1536, 2560, 2816, 3072, 4096, 4608
NEGM = -60000.0
TW1 = 0
TW3 = 4096
TBD = 8192
TER = 10240
TEI = TER + 16 * (NJ + 4)
TRHO = TEI + 16 * (NJ + 4)
TABW = TRHO + 16


def dap(t, off, dims):
    return bass.AP(tensor=t, offset=off, ap=[[s, c] for s, c in dims])


class Prog:
    def __init__(self, nc):
        self.nc = nc
        self.ops = []
        self.last_w = {}
        self.readers = {}

    def add(self, eng, fn, r=(), w=(), dma=False):
        deps = set()
        for k in r:
            if k in self.last_w:
                deps.add(self.last_w[k])
        for k in w:
            if k in self.last_w:
                deps.add(self.last_w[k])
            for x in self.readers.get(k, ()):
                deps.add(x)
        idx = len(self.ops)
        deps.discard(idx)
        import sys as _s
        fr = _s._getframe(1)
        ln = []
        while fr is not None and len(ln) < 3:
            ln.append(fr.f_lineno); fr = fr.f_back
        self.ops.append(dict(eng=eng, fn=fn, deps=sorted(deps), dma=dma, line=ln))
        for k in w:
            self.last_w[k] = idx
            self.readers[k] = []
        for k in r:
            if k not in w:
                self.readers.setdefault(k, []).append(idx)
        return idx

    def emit(self):
        nc = self.nc
        ops = self.ops
        engs = ['pe', 'act', 'dve', 'pool', 'sp']
        RR = {'sp': 8, 'pool': 4, 'act': 2, 'pe': 1, 'dve': 1}
        for o in ops:
            o['sig'] = False
        for i, o in enumerate(ops):
            for d in o['deps']:
                p = ops[d]
                if p['dma']:
                    continue
                if p['eng'] == 'pe' and o['eng'] == 'pe' and not o['dma']:
                    continue
                p['sig'] = True
        cnt = {e: 0 for e in engs}
        dcnt = {e: 0 for e in engs}
        for o in ops:
            e = o['eng']
            if o['dma']:
                n = dcnt[e]
                o['dslot'] = n % RR[e]
                o['dval'] = 16 * (n // RR[e] + 1)
                o['dprev'] = None
                dcnt[e] += 1
            elif o['sig']:
                cnt[e] += 1
                o['val'] = cnt[e]
        lastslot = {}
        for i, o in enumerate(ops):
            if o['dma']:
                key = (o['eng'], o['dslot'])
                o['dprev'] = lastslot.get(key)
                lastslot[key] = i
        import contextlib
        with contextlib.ExitStack() as st:
            sems = {e: st.enter_context(nc.semaphore("s_" + e)) for e in engs}
            dsems = {}
            for e in ['sp', 'pool', 'act']:
                for s in range(RR[e]):
                    dsems[(e, s)] = st.enter_context(nc.semaphore("d_%s%d" % (e, s)))
            block = st.enter_context(nc.Block())
            per = {e: [o for o in ops if o['eng'] == e] for e in engs}

            def run(e, engine):
                waited = {}

                def need(sem_key, sem, val):
                    if waited.get(sem_key, 0) >= val:
                        return
                    engine.wait_ge(sem, val)
                    waited[sem_key] = val

                for o in per[e]:
                    for d in o['deps']:
                        p = ops[d]
                        if p['dma']:
                            need(('d', p['eng'], p['dslot']), dsems[(p['eng'], p['dslot'])], p['dval'])
                        else:
                            if p['eng'] == 'pe' and e == 'pe' and not o['dma']:
                                continue
                            need(('c', p['eng']), sems[p['eng']], p['val'])
                    if o['dma'] and o['dprev'] is not None:
                        p = ops[o['dprev']]
                        need(('d', p['eng'], p['dslot']), dsems[(p['eng'], p['dslot'])], p['dval'])
                    ins = o['fn'](engine)
                    if o['dma']:
                        ins.then_inc(dsems[(e, o['dslot'])], 16)
                    elif o['sig']:
                        ins.then_inc(sems[e], 1)
                if e in ('sp', 'pool', 'act'):
                    tot = {}
                    for o in per[e]:
                        if o['dma']:
                            tot[o['dslot']] = o['dval']
                    for s, v in tot.items():
                        engine.wait_ge(dsems[(e, s)], v)

            @block.tensor
            def _(en):
                run('pe', en)

            @block.scalar
            def _(en):
                run('act', en)

            @block.vector
            def _(en):
                run('dve', en)

            @block.gpsimd
            def _(en):
                run('pool', en)

            @block.sync
            def _(en):
                run('sp', en)


def build_program(depth=DEPTH, nseg=NSEG, debug=False):
    nc = bass.Bass("TRN2", target_bir_lowering=False)
    P = Prog(nc)
    seqlen = nseg * TSEG

    def din(name, shape, dt=F32):
        return nc.dram_tensor(name, list(shape), dt, kind="ExternalInput")

    def dout(name, shape, dt=F32):
        return nc.dram_tensor(name, list(shape), dt, kind="ExternalOutput")

    xp = din("xp", [seqlen, D]); xs = din("xs", [NS, D])
    ck = din("ck", [depth, 4, 128, 256]); cv = din("cv", [depth, 4, 128, 256])
    sre = din("sre", [depth, 4, 16, 128]); sim = din("sim", [depth, 4, 16, 128])
    ng = din("ng", [depth * 16 + 16, 128])
    w_in = din("w_in", [depth, D, DIN]); w_out = din("w_out", [depth, D, D])
    cws = din("cws", [depth, 4, 128, 128]); cbs = din("cbs", [1, depth * 4 * 128])
    sinks = din("sinks", [1, depth * 16])
    a_re = din("a_re", [depth, 32, 64]); a_im = din("a_im", [depth, 32, 64]); ldt = din("ldt", [depth, 32])
    b_re = din("b_re", [depth, 32, 64, 16]); b_im = din("b_im", [depth, 32, 64, 16])
    c_re = din("c_re", [depth, 512, 64]); c_im = din("c_im", [depth, 512, 64])
    dsk = din("dsk", [depth * 4, 128]); glw = din("glw", [depth, 512, 512]); glb = din("glb", [depth * 4, 128])
    c_ident = din("c_ident", [128, 128]); c_tril = din("c_tril", [128, 128])
    c_maskA = din("c_maskA", [128, 512]); c_maskF = din("c_maskF", [128, 512]); c_maskS = din("c_maskS", [32, 544])
    c_maskm = din("c_maskm", [128, 2]); c_bd = din("c_bd", [32, 32])
    c_perm = din("c_perm", [128, 128]); c_cos = din("c_cos", [128, seqlen + NS]); c_sin = din("c_sin", [128, seqlen + NS])

    y_p = dout("y_p", [seqlen, D]); y_s = dout("y_s", [NS, D])
    nk_p = dout("nk_p", [depth, 128, 256]); nv_p = dout("nv_p", [depth, 128, 256])
    nk_s = dout("nk_s", [depth, NS, 256]); nv_s = dout("nv_s", [depth, NS, 256])
    hr_p = dout("hr_p", [depth, 16, 128]); hi_p = dout("hi_p", [depth, 16, 128])
    hr_s = dout("hr_s", [depth, 4, 16, 128]); hi_s = dout("hi_s", [depth, 4, 16, 128])
    va_s = dout("va_s", [depth, NS, 512])
    s5tab = nc.dram_tensor("s5tab", [depth, 128, TABW], F32)
    wcache = nc.dram_tensor("wcache", [depth * 56, 128, KD * 128], BF16)

    NMAX = TSEG + NS
    import contextlib
    st = contextlib.ExitStack()

    def sb(name, shape, dt=F32):
        return st.enter_context(nc.sbuf_tensor(name, list(shape), dt))

    def ps(name, shape, dt=F32):
        return st.enter_context(nc.psum_tensor(name, list(shape), dt))

    xT = sb("xT", [128, KD, NMAX]); hT = sb("hT", [128, KD, NMAX], BF16); mix = sb("mix", [128, KD, NMAX], BF16)
    kT = sb("kT", [128, 2, 128 + NMAX], BF16); Vt = sb("Vt", [128, NB + 2, 256], BF16)
    vatm = sb("vatm", [128, NB + 1, 512], BF16)
    khalo = sb("khalo", [128, depth, 2, 128], BF16); vhalo = sb("vhalo", [128, depth, 256], BF16)
    cosT = sb("cosT", [128, NMAX]); sinT = sb("sinT", [128, NMAX])
    NWB = 3
    wb = [sb("wb%d" % i, [128, KD, 128], BF16) for i in range(NWB)]
    iost0_ = sb("iost0", [128, D]); iost = [iost0_, iost0_]
    ident = sb("ident", [128, 128]); identb = sb("identb", [128, 128], BF16); onesb = sb("onesb", [128, 128], BF16)
    ones1 = sb("ones1", [1, 128]); epsc = sb("epsc", [128, 1])
    maskA = sb("maskA", [128, 512], BF16); maskF = sb("maskF", [128, 512], BF16); maskS = sb("maskS", [32, 544], BF16)
    gcol = sb("gcol", [128, depth * 16 + 16]); dcol = sb("dcol", [128, depth * 4]); gbcol = sb("gbcol", [128, depth * 4])
    sinkbc = sb("sinkbc", [128, depth * 16]); bsrow = sb("bsrow", [1, 512]); bsS = sb("bsS", [1, 4, 32])
    wsT = sb("wsT", [128, depth * 4, 128], BF16); wsS = sb("wsS", [32, depth * 4, 32], BF16)
    hprev = sb("hprev", [128, depth, 16, 2]); s_in = sb("s_in", [128, depth, 4, 2, 16])
    tabA = sb("tabA", [128, 4096 + 2048])
    tabE = sb("tabE", [128, TABW - TER])
    Sprev = sb("Sprev", [128, 16, 2, NJ + 4], BF16)
    sq = [sb("sq%d" % i, [128, NMAX], BF16) for i in range(2)]
    rstd = sb("rstd", [128, NMAX]); tA = [sb("tA%d" % i, [128, NMAX]) for i in range(4)]
    xsw0_ = sb("xsw0", [128, NMAX]); xsw = [xsw0_, xsw0_]
    tB0_ = sb("tB0", [128, NMAX], BF16); tB = [tB0_, tB0_]
    yb = sb("yb", [128, 4, NMAX], BF16)
    attb = sb("attb", [128, 4096], BF16); Pf2 = [attb[:, 2048 * i:2048 * i + 1024].rearrange("q (a b) -> q a b", b=256) for i in range(2)]; Pb2 = [attb[:, 2048 * i + 1024:2048 * i + 2048].rearrange("q (a b) -> q a b", b=256) for i in range(2)]; Pf = Pf2[0]; Pb = Pb2[0]; glwb = attb[:, 0:2048].rearrange("q (a b) -> q a b", b=512); PTs = sb("PTs", [128, 1024], BF16); PTs2 = [PTs, sb("PTsB", [128, 1024], BF16)]
    sm = sb("sm", [128, 64])
    iob = iost0_[:].bitcast(BF16)
    ckb = iob[:, 0:1024].rearrange("q (a b) -> q a b", b=256); cvb = iob[:, 1024:2048].rearrange("q (a b) -> q a b", b=256)
    ckT = iob[:, 2048:3072].rearrange("q (b k w) -> q b k w", b=4, k=2)
    PfS = Pf[0:32].rearrange("q a b -> q (a b)")[:, 0:544]; PbS = Pb[0:32].rearrange("q a b -> q (a b)")[:, 0:544]; PTS2 = sb("PTS2", [128, 5, 32], BF16)
    s5w = [sb("s5w%d" % i, [128, 4 * NJ]) for i in range(10)]
    outst = sb("outst", [128, 416])
    ps01 = ps("ps01", [128, 1024]); ps0 = ps01[:, 0:512]; ps1 = ps01[:, 512:1024]; ps2 = ps("ps2", [128, 512]); ps3 = ps("ps3", [128, 512])
    psS = ps("psS", [128, 1024]); ps6 = ps("ps6", [128, 512]); psT = ps("psT", [128, 1024], BF16)

    tabAb = tabA[:].bitcast(BF16)

    def E(name):
        return {'pe': 'pe', 'act': 'act', 'dve': 'dve', 'pool': 'pool', 'sp': 'sp'}[name]

    def dma(q, out, in_, r, w):
        return P.add(q, lambda e: e.dma_start(out=out, in_=in_, allow_slow_non_contiguous=True), r=r, w=w, dma=True)

    def mm(out, lhsT, rhs, start, stop, r, w, **kw):
        return P.add('pe', lambda e: e.matmul(out, lhsT=lhsT, rhs=rhs, start=start, stop=stop, **kw), r=r, w=w)

    def tr(out, in_, idn, r, w, **kw):
        return P.add('pe', lambda e: e.transpose(out, in_, idn, **kw), r=r + ['ident'], w=w)

    def act(out, in_, func, r, w, scale=1.0, bias=None, accum=None):
        def f(e):
            kw = dict(out=out, in_=in_, func=func, scale=scale)
            if bias is not None:
                kw['bias'] = bias
            if accum is not None:
                kw['accum_out'] = accum
            return e.activation(**kw)
        return P.add('act', f, r=r, w=w)

    def tt(eng, out, a, b, op, r, w):
        return P.add(eng, lambda e: e.tensor_tensor(out=out, in0=a, in1=b, op=op), r=r, w=w)

    def ts(eng, out, a, s1, op0, r, w, s2=None, op1=None):
        if op1 is None:
            return P.add(eng, lambda e: e.tensor_scalar(out=out, in0=a, scalar1=s1, scalar2=None, op0=op0), r=r, w=w)
        return P.add(eng, lambda e: e.tensor_scalar(out=out, in0=a, scalar1=s1, scalar2=s2, op0=op0, op1=op1), r=r, w=w)

    def stt(out, a, s, b, op0, op1, r, w):
        return P.add('dve', lambda e: e.scalar_tensor_tensor(out=out, in0=a, scalar=s, in1=b, op0=op0, op1=op1), r=r, w=w)

    def cp(eng, out, in_, r, w):
        if eng == 'act':
            return P.add('act', lambda e: e.copy(out=out, in_=in_), r=r, w=w)
        return P.add(eng, lambda e: e.tensor_copy(out=out, in_=in_), r=r, w=w)

    def memset(eng, ap, v, w):
        return P.add(eng, lambda e: e.memset(ap, v), r=[], w=w)

    dma('sp', ident[:], c_ident.ap(), [], ['ident'])
    dma('pool', identb[:], c_ident.ap(), [], ['ident'])
    dma('pool', maskA[:], c_maskA.ap(), [], ['masks']); dma('pool', maskF[:], c_maskF.ap(), [], ['masks'])
    dma('pool', maskS[:], c_maskS.ap(), [], ['masks'])
    memset('dve', onesb[:], 1.0, ['ident']); memset('dve', ones1[:], 1.0, ['ident']); memset('dve', epsc[:], 1e-5, ['ident'])
    xw0 = xsw[0]
    permb = sb("permb", [128, 128], BF16)
    dma('pool', permb[:], c_perm.ap(), [], ['ident'])
    memset('pool', kT[:], 0.0, ['kT']); memset('pool', Vt[:], 0.0, ['Vt'])
    memset('pool', khalo[:], 0.0, ['khalo']); memset('pool', vhalo[:], 0.0, ['vhalo']); memset('dve', hprev[:], 0.0, ['hprev'])
    dma('sp', sinkbc[:], dap(sinks, 0, [[0, 128], [1, depth * 16]]), [], ['sinkbc'])
    nrow = depth * 16 + 16
    dma('sp', iost[0][0:nrow, 0:128], ng.ap(), [], ['iost0'])
    tr(ps3[:, 0:nrow], iost[0][0:nrow, 0:128], ident[0:nrow, 0:nrow], ['iost0'], ['ps3'])
    cp('dve', gcol[:], ps3[:, 0:nrow], ['ps3'], ['gcol'])
    dma('sp', iost[0][0:depth * 4, 128:256], dsk.ap(), [], ['iost0'])
    tr(ps3[:, 0:depth * 4], iost[0][0:depth * 4, 128:256], ident[0:depth * 4, 0:depth * 4], ['iost0'], ['ps3'])
    cp('dve', dcol[:], ps3[:, 0:depth * 4], ['ps3'], ['dcol'])
    dma('sp', iost[0][0:depth * 4, 256:384], glb.ap(), [], ['iost0'])
    tr(ps3[:, 0:depth * 4], iost[0][0:depth * 4, 256:384], ident[0:depth * 4, 0:depth * 4], ['iost0'], ['ps3'])
    cp('dve', gbcol[:], ps3[:, 0:depth * 4], ['ps3'], ['gbcol'])
    dma('sp', iost[0][:, 1536:1664], c_tril.ap(), [], ['iost0'])
    for lh in range(depth * 4):
        dma('sp', iost[0][:, 512:640], dap(cws, lh * 128 * 128, [[128, 128], [1, 128]]), [], ['iost0'])
        tt('dve', iost[0][:, 640:768], iost[0][:, 512:640], iost[0][:, 1536:1664], ALU.mult, ['iost0', 'iost0'], ['iost0b'])
        tr(ps3[:, 0:128], iost[0][:, 640:768], ident[:], ['iost0b'], ['ps3'])
        cp('dve', wsT[:, lh, :], ps3[:, 0:128], ['ps3'], ['wsT'])
    dma('sp', iost[0][0:32, 1664:1696], c_bd.ap(), [], ['iost0'])
    for lh in range(depth * 4):
        for b in range(4):
            dma('sp', iost[0][8 * b:8 * b + 8, 1024:1056].rearrange("q (a s) -> q a s", s=8), dap(cws, lh * 128 * 128, [[128, 8], [0, 4], [1, 8]]), [], ['iost0'])
        tt('dve', iost[0][0:32, 1056:1088], iost[0][0:32, 1024:1056], iost[0][0:32, 1664:1696], ALU.mult, ['iost0', 'iost0'], ['iost0b'])
        tr(ps3[0:32, 0:32], iost[0][0:32, 1056:1088], ident[0:32, 0:32], ['iost0b'], ['ps3'])
        cp('dve', wsS[:, lh, :], ps3[0:32, 0:32], ['ps3'], ['wsS'])
    for l in range(depth):
        for b in range(4):
            for ri, src in enumerate((sre, sim)):
                dma('sp', iost[0][0:16, 0:128], dap(src, ((l * 4 + b) * 16) * 128, [[128, 16], [1, 128]]), [], ['iost0'])
                tr(ps3[:, 0:16], iost[0][0:16, 0:128], ident[0:16, 0:16], ['iost0'], ['ps3'])
                cp('dve', s_in[:, l, b, ri, :], ps3[:, 0:16], ['ps3'], ['s_in'])

    W = NJ + 4
    maskm = sb("maskm", [128, 2])
    dma('sp', maskm[:], c_maskm.ap(), [], ['maskm'])
    mixf = mix[:].rearrange("q a b -> q (a b)").bitcast(F32)
    xTf = xT[:].rearrange("q a b -> q (a b)")
    _o = [0]

    def carve(buf, n, lim):
        a = _o[0]; _o[0] += n
        assert _o[0] <= lim
        return buf[:, a:a + n]
    sc = [carve(mixf, 16, 4352) for i in range(72)]
    Bn = [carve(mixf, 256, 4352).rearrange("q (a b) -> q a b", b=16) for i in range(2)]
    Cn = [carve(mixf, 256, 4352).rearrange("q (a b) -> q a b", b=64) for i in range(2)]
    Cp = [carve(mixf, 512, 4352).rearrange("q (a b) -> q a b", b=32) for i in range(2)]
    Y3 = [carve(mixf, 512, 4352).rearrange("q (a b) -> q a b", b=32) for i in range(2)]
    _o[0] = 4 * 16 * W
    tmpY = [carve(xTf, 256, 8704).rearrange("q (a b) -> q a b", b=16) for i in range(4)]
    Cpad = carve(xTf, 128, 8704)
    pw_all = carve(xTf, 288, 8704).rearrange("q (k r a) -> q k r a", k=9, r=2)
    CpB_all = carve(xTf, 512, 8704).bitcast(BF16).rearrange("q (r a c) -> q r a c", r=2, a=16)
    W1st = tabAb[:, 0:8192]; BDst = tabAb[:, 8192:12288]; W3half = iost0_[:].bitcast(BF16)
    Est = tabE[:, 0:2 * 16 * W].rearrange("q (a b c) -> q a b c", a=2, b=16)
    Etmp = [xTf[:, i * 16 * W:(i + 1) * 16 * W].rearrange("q (a b) -> q a b", b=W) for i in range(4)]
    Y1all = hT[:].rearrange("q a b -> q (a b)")[:, 0:8192].rearrange("q (k r a c) -> q k r a c", k=8, r=2, a=16)
    rs = ['s5p', 'xT', 'hT', 'tabA', 'tabW', 'tabE', 'iost0'] + ['mixa%d' % i for i in range(4)] + ['mixb%d' % i for i in range(8)] + ['mixc%d' % i for i in range(4)]
    for l in range(depth):
        cnt_i = [0]

        def T_():
            cnt_i[0] += 1
            return sc[cnt_i[0] - 1]
        are, aim, dtl = T_(), T_(), T_()
        for m in range(2):
            dma('sp', are[64 * m:64 * m + 64, :], dap(a_re, l * 2048 + m * 64, [[1, 64], [128, 16]]), [], rs)
            dma('sp', aim[64 * m:64 * m + 64, :], dap(a_im, l * 2048 + m * 64, [[1, 64], [128, 16]]), [], rs)
            dma('sp', dtl[64 * m:64 * m + 64, :], dap(ldt, l * 32 + m, [[0, 64], [2, 16]]), [], rs)
            dma('sp', Bn[0][64 * m:64 * m + 64, :, :], dap(b_re, l * 32768 + m * 1024, [[16, 64], [2048, 16], [1, 16]]), [], rs)
            dma('sp', Bn[1][64 * m:64 * m + 64, :, :], dap(b_im, l * 32768 + m * 1024, [[16, 64], [2048, 16], [1, 16]]), [], rs)
        dma('sp', Cn[0][:], dap(c_re, l * 32768, [[64, 128], [8192, 4], [1, 64]]), [], rs)
        dma('sp', Cn[1][:], dap(c_im, l * 32768, [[64, 128], [8192, 4], [1, 64]]), [], rs)
        def poly(x, coef):
            t = T_()
            n = len(coef) - 1
            ts('dve', t[:], x, float(coef[n]), ALU.mult, rs, rs)
            for k in range(n - 1, 0, -1):
                stt(t[:], t[:], float(coef[k]), x, ALU.add, ALU.mult, rs, rs)
            ts('dve', t[:], t[:], float(coef[0]), ALU.add, rs, rs)
            return t

        def exp_poly(x, deg, nsq):
            xs_ = T_(); ts('dve', xs_[:], x, 1.0 / (2 ** nsq), ALU.mult, rs, rs)
            e = poly(xs_[:], [1.0 / math.factorial(k) for k in range(deg + 1)])
            for _ in range(nsq):
                tt('dve', e[:], e[:], e[:], ALU.mult, rs, rs)
            return e
        dt_ = exp_poly(dtl[:], 12, 3)
        ar = T_(); tt('dve', ar[:], are[:], dt_[:], ALU.mult, rs, rs)
        th = T_(); tt('dve', th[:], aim[:], dt_[:], ALU.mult, rs, rs)
        mag = exp_poly(ar[:], 7, 0)
        TWO_PI = 2.0 * math.pi
        MAGIC = 12582912.0
        u = T_(); ts('dve', u[:], th[:], 1.0 / TWO_PI, ALU.mult, rs, rs)
        n_ = T_(); ts('dve', n_[:], u[:], MAGIC, ALU.add, rs, rs)
        n2 = T_(); ts('dve', n2[:], n_[:], MAGIC, ALU.subtract, rs, rs)
        fr0 = T_(); tt('dve', fr0[:], u[:], n2[:], ALU.subtract, rs, rs)
        xq = T_(); ts('dve', xq[:], fr0[:], TWO_PI / 4.0, ALU.mult, rs, rs)
        x2 = T_(); tt('dve', x2[:], xq[:], xq[:], ALU.mult, rs, rs)
        ps_ = poly(x2[:], [(-1.0) ** k / math.factorial(2 * k + 1) for k in range(7)])
        sn = T_(); tt('dve', sn[:], ps_[:], xq[:], ALU.mult, rs, rs)
        cs = poly(x2[:], [(-1.0) ** k / math.factorial(2 * k) for k in range(8)])
        for _ in range(2):
            s2_ = T_(); tt('dve', s2_[:], sn[:], cs[:], ALU.mult, rs, rs)
            q2_ = T_(); tt('dve', q2_[:], sn[:], sn[:], ALU.mult, rs, rs)
            sn = T_(); ts('dve', sn[:], s2_[:], 2.0, ALU.mult, rs, rs)
            cs = T_(); ts('dve', cs[:], q2_[:], -2.0, ALU.mult, rs, rs, s2=1.0, op1=ALU.add)
        lr = T_(); tt('dve', lr[:], mag[:], cs[:], ALU.mult, rs, rs)
        li = T_(); tt('dve', li[:], mag[:], sn[:], ALU.mult, rs, rs)

        def cmul(ar_, ai_, br_, bi_, bc=None):
            t1, t2, t3, t4, orr, oi = T_(), T_(), T_(), T_(), T_(), T_()
            tt('dve', t1[:], ar_, br_, ALU.mult, rs, rs); tt('dve', t2[:], ai_, bi_, ALU.mult, rs, rs)
            tt('dve', orr[:], t1[:], t2[:], ALU.subtract, rs, rs)
            tt('dve', t3[:], ar_, bi_, ALU.mult, rs, rs); tt('dve', t4[:], ai_, br_, ALU.mult, rs, rs)
            tt('dve', oi[:], t3[:], t4[:], ALU.add, rs, rs)
            return orr, oi
        nr = T_(); ts('dve', nr[:], lr[:], -1.0, ALU.add, rs, rs)
        d1 = T_(); tt('dve', d1[:], are[:], are[:], ALU.mult, rs, rs)
        d2 = T_(); tt('dve', d2[:], aim[:], aim[:], ALU.mult, rs, rs)
        den = T_(); tt('dve', den[:], d1[:], d2[:], ALU.add, rs, rs)
        rden = T_(); P.add('dve', lambda e, o=rden, i=den: e.reciprocal(out=o[:], in_=i[:]), r=rs, w=rs)
        nai = T_(); ts('dve', nai[:], aim[:], -1.0, ALU.mult, rs, rs)
        f0r, f0i = cmul(nr[:], li[:], are[:], nai[:])
        fr_ = T_(); tt('dve', fr_[:], f0r[:], rden[:], ALU.mult, rs, rs)
        fi_ = T_(); tt('dve', fi_[:], f0i[:], rden[:], ALU.mult, rs, rs)
        for ri in range(2):
            for T in range(4):
                for m in range(2):
                    ts('dve', Cpad[:, 64 * m:64 * m + 64], Cn[ri][:, T, :], maskm[:, m:m + 1], ALU.mult, rs, rs)
                tr(ps3[:, 0:128], Cpad[:], ident[:], rs, ['ps3'])
                cp('dve', Cp[ri][:, 4 * T:4 * T + 4, :], ps3[:, 0:128].rearrange("q (a b) -> q a b", b=32), ['ps3'], rs)
        pw = pw_all
        memset('dve', pw[:, 0, 0, :], 1.0, rs); memset('dve', pw[:, 0, 1, :], 0.0, rs)
        cp('dve', pw[:, 1, 0, :], lr[:], rs, rs); cp('dve', pw[:, 1, 1, :], li[:], rs, rs)
        base_i = cnt_i[0]
        for k in range(2, 9):
            cnt_i[0] = base_i
            orr, oi = cmul(pw[:, k - 1, 0, :], pw[:, k - 1, 1, :], lr[:], li[:])
            cp('dve', pw[:, k, 0, :], orr[:], rs, rs); cp('dve', pw[:, k, 1, :], oi[:], rs, rs)
        W3h = W3half.rearrange("q (t a r c) -> q t a r c", t=4, a=16, r=2)
        for t in range(8):
            pr = pw[:, t + 1, 0, :].unsqueeze(2).to_broadcast([128, 16, 32])
            pi_ = pw[:, t + 1, 1, :].unsqueeze(2).to_broadcast([128, 16, 32])
            tt('dve', Y3[0][:], Cp[0][:], pr, ALU.mult, rs, rs); tt('dve', Y3[1][:], Cp[1][:], pi_, ALU.mult, rs, rs)
            tt('dve', W3h[:, t % 4, :, 0, :], Y3[0][:], Y3[1][:], ALU.subtract, rs, rs)
            tt('dve', Y3[0][:], Cp[0][:], pi_, ALU.mult, rs, rs); tt('dve', Y3[1][:], Cp[1][:], pr, ALU.mult, rs, rs)
            tt('dve', Y3[0][:], Y3[0][:], Y3[1][:], ALU.add, rs, rs)
            ts('dve', W3h[:, t % 4, :, 1, :], Y3[0][:], -1.0, ALU.mult, rs, rs)
            if t % 4 == 3:
                dma('sp', dap(s5tab, l * 128 * TABW + TW3 + (t // 4) * 2048, [[TABW, 128], [1, 2048]]), iost0_[:], rs, ['s5tab'] + rs)
        memset('dve', hT[:].rearrange("q a b -> q (a b)")[:, 0:8192], 0.0, rs)
        for k in range(8):
            cnt_i[0] = base_i
            Fr, Fi = cmul(pw[:, k, 0, :], pw[:, k, 1, :], fr_[:], fi_[:])
            Frb = Fr[:].unsqueeze(2).to_broadcast([128, 16, 16]); Fib = Fi[:].unsqueeze(2).to_broadcast([128, 16, 16])
            tt('dve', tmpY[0][:], Bn[0][:], Frb, ALU.mult, rs, rs); tt('dve', tmpY[1][:], Bn[1][:], Fib, ALU.mult, rs, rs)
            tt('dve', tmpY[2][:], Bn[1][:], Frb, ALU.mult, rs, rs); tt('dve', tmpY[3][:], Bn[0][:], Fib, ALU.mult, rs, rs)
            for m in range(2):
                sl = slice(64 * m, 64 * m + 64)
                tt('dve', Y1all[sl, k, 0, :, 16 * m:16 * m + 16], tmpY[0][sl], tmpY[1][sl], ALU.subtract, rs, rs)
                tt('dve', Y1all[sl, k, 1, :, 16 * m:16 * m + 16], tmpY[2][sl], tmpY[3][sl], ALU.add, rs, rs)
        W1v = W1st.rearrange("q (s t r c) -> q s t r c", s=8, t=4, r=2)
        for s in range(8):
            k = 7 - s
            for T in range(4):
                for ri in range(2):
                    for r4 in range(4):
                        tr(psT[32 * r4:32 * r4 + 32, 0:128], Y1all[:, k, ri, 4 * T + r4, :], identb[:], rs, ['psT'], tile_position=(0, 32 * r4))
                    cp('dve', W1v[:, s, T, ri, :], psT[:, 0:128], ['psT'], rs)
        memset('dve', BDst, 0.0, rs)
        BDv = BDst.rearrange("q (t a c) -> q t a c", t=4, a=8)
        CpB = CpB_all
        cp('dve', CpB[:, 0], Cp[0][:], rs, rs); ts('dve', CpB[:, 1], Cp[1][:], -1.0, ALU.mult, rs, rs)
        for T in range(4):
            for tau in range(8):
                for r4 in range(4):
                    pi_i = 4 * T + r4
                    for ri in range(2):
                        mm(ps3[32 * r4:32 * r4 + 32, tau * 32:tau * 32 + 32], Y1all[:, tau, ri, pi_i, :], CpB[:, ri, pi_i, :],
                           ri == 0, ri == 1, rs, ['ps3'], tile_position=(0, 32 * r4))
            for r4 in range(4):
                sl = slice(32 * r4, 32 * r4 + 32)
                cp('dve', BDv[sl, T, :, 32 * r4:32 * r4 + 32], ps3[sl, 0:256].rearrange("q (a b) -> q a b", b=32), ['ps3'], rs)
        cnt_i[0] = base_i + 6
        r2 = T_(); tt('dve', r2[:], mag[:], mag[:], ALU.mult, rs, rs)
        r4_ = T_(); tt('dve', r4_[:], r2[:], r2[:], ALU.mult, rs, rs)
        rho = T_(); tt('dve', rho[:], r4_[:], r4_[:], ALU.mult, rs, rs)
        ur, ui = cs, sn
        for _ in range(3):
            a2 = T_(); tt('dve', a2[:], ur[:], ur[:], ALU.mult, rs, rs)
            b2 = T_(); tt('dve', b2[:], ui[:], ui[:], ALU.mult, rs, rs)
            ab = T_(); tt('dve', ab[:], ur[:], ui[:], ALU.mult, rs, rs)
            ur = T_(); tt('dve', ur[:], a2[:], b2[:], ALU.subtract, rs, rs)
            ui = T_(); ts('dve', ui[:], ab[:], 2.0, ALU.mult, rs, rs)
        cp('dve', Est[:, 0, :, 0], ur[:], rs, rs)
        cp('dve', Est[:, 1, :, 0], ui[:], rs, rs)
        k = 1
        while k < NJ:
            n = min(k, NJ - k)
            br_ = Est[:, 0, :, k - 1:k].to_broadcast([128, 16, n]); bi_ = Est[:, 1, :, k - 1:k].to_broadcast([128, 16, n])
            xr = Est[:, 0, :, 0:n]; xi = Est[:, 1, :, 0:n]
            tt('dve', Etmp[0][:, :, 0:n], xr, br_, ALU.mult, rs, rs); tt('dve', Etmp[1][:, :, 0:n], xi, bi_, ALU.mult, rs, rs)
            tt('dve', Etmp[2][:, :, 0:n], xr, bi_, ALU.mult, rs, rs); tt('dve', Etmp[3][:, :, 0:n], xi, br_, ALU.mult, rs, rs)
            tt('dve', Est[:, 0, :, k:k + n], Etmp[0][:, :, 0:n], Etmp[1][:, :, 0:n], ALU.subtract, rs, rs)
            tt('dve', Est[:, 1, :, k:k + n], Etmp[2][:, :, 0:n], Etmp[3][:, :, 0:n], ALU.add, rs, rs)
            k += n
        for ri in range(2):
            cp('dve', Est[:, ri, :, NJ:NJ + 4], Est[:, ri, :, 0:1].to_broadcast([128, 16, 4]), rs, rs)
        cp('dve', tabE[:, 32 * W:32 * W + 16], rho[:], rs, rs)
        dma('sp', dap(s5tab, l * 128 * TABW + TW1, [[TABW, 128], [1, 4096]]), tabA[:, 0:4096], rs, ['s5tab'] + rs)
        dma('sp', dap(s5tab, l * 128 * TABW + TBD, [[TABW, 128], [1, 2048]]), tabA[:, 4096:6144], rs, ['s5tab'] + rs)
        dma('sp', dap(s5tab, l * 128 * TABW + TER, [[TABW, 128], [1, TABW - TER]]), tabE[:], rs, ['s5tab'] + rs)

    wbi = [0]
    blkc = [0]
    cur_seg = [0]

    def proj(l, wsrc, pieces_fn, N, evac, hsrc, rkeys):
        i = wbi[0] % NWB
        wbi[0] += 1
        w = wb[i]
        wk = 'wb%d' % i
        blk = l * 56 + blkc[0]
        blkc[0] += 1
        ckey = 'wc%d' % blk
        wflat = w[:].rearrange("q a b -> q (a b)")
        if cur_seg[0] == 0:
            pieces_fn(w, wk)
            if nseg > 1:
                dma('sp', dap(wcache, blk * 128 * KD * 128, [[KD * 128, 128], [1, KD * 128]]), wflat, [wk], [ckey])
        else:
            dma('sp', wflat, dap(wcache, blk * 128 * KD * 128, [[KD * 128, 128], [1, KD * 128]]), [ckey], [wk])
        pm = ps0 if (wbi[0] % 2 == 0) else ps1
        pmk = 'ps0' if (wbi[0] % 2 == 0) else 'ps1'
        for k in range(KD):
            mm(pm[:, 0:TSEG], w[:, k, :], hsrc[:, k, 0:TSEG], k == 0, k == KD - 1, [wk] + rkeys, [pmk])
            if N > TSEG:
                mm(ps2[:, 0:NS], w[:, k, :], hsrc[:, k, TSEG:N], k == 0, k == KD - 1, [wk] + rkeys, ['ps2'])
        evac(pm, pmk)

    def in_pieces(l, col_pieces):
        def f(w, wk):
            off = 0
            for c0, wd in col_pieces:
                dma('pool', w[:, :, off:off + wd], dap(w_in, l * D * DIN + c0, [[DIN, 128], [128 * DIN, KD], [1, wd]]), [], [wk])
                off += wd
        return f

    def qcols(base, tq):
        kt, i = tq // 4, tq % 4
        return [(base + 64 * (8 * kt + i), 64), (base + 64 * (8 * kt + 4 + i), 64)]

    def out_pieces(l, j):
        def f(w, wk):
            base = l * D * D + j * 128
            dma('pool', w[:, 0:4, :], dap(w_out, base, [[D, 128], [128 * D, 4], [1, 128]]), [], [wk])
            dma('pool', w[:, 12:16, :], dap(w_out, base + 1536 * D, [[D, 128], [128 * D, 4], [1, 128]]), [], [wk])
            for half in range(2):
                for kt in range(2):
                    dma('pool', w[64 * half:64 * half + 64, 4 + 4 * kt:8 + 4 * kt, :],
                        dap(w_out, base + (512 + 512 * kt + 256 * half) * D, [[D, 64], [64 * D, 4], [1, 128]]), [], [wk])
        return f

    def rmsnorm(N, gidx, out_fn):
        for k in range(KD):
            s = sq[k % 2]
            act(s[:, 0:N], xT[:, k, 0:N], AF.Square, ['xT'], ['sq%d' % (k % 2)])
            mm(ps0[:, 0:TSEG], onesb[:], s[:, 0:TSEG], k == 0, k == KD - 1, ['sq%d' % (k % 2)], ['ps0'])
            if N > TSEG:
                mm(ps2[:, 0:NS], onesb[:], s[:, TSEG:N], k == 0, k == KD - 1, ['sq%d' % (k % 2)], ['ps2'])
        act(rstd[:, 0:TSEG], ps0[:, 0:TSEG], AF.Sqrt, ['ps0'], ['rstd'], scale=1.0 / D, bias=epsc[:])
        if N > TSEG:
            act(rstd[:, TSEG:N], ps2[:, 0:NS], AF.Sqrt, ['ps2'], ['rstd'], scale=1.0 / D, bias=epsc[:])
        P.add('dve', lambda e: e.reciprocal(out=rstd[:, 0:N], in_=rstd[:, 0:N]), r=['rstd'], w=['rstd'])
        for k in range(KD):
            out_fn(k, gcol[:, gidx * 16 + k:gidx * 16 + k + 1])

    def softmax(src3, np_, nh, Wd, sinkcols, Pf_, Pb_, rk, tag, so=0):
        smk = 'sm%d' % so
        mx = sm[0:np_, so + 0:so + nh]; m_ = sm[0:np_, so + 4:so + 4 + nh]; ng_ = sm[0:np_, so + 8:so + 8 + nh]; ssum = sm[0:np_, so + 12:so + 12 + nh]
        dd = sm[0:np_, so + 16:so + 16 + nh]; es = sm[0:np_, so + 20:so + 20 + nh]; den = sm[0:np_, so + 24:so + 24 + nh]; rd = sm[0:np_, so + 28:so + 28 + nh]
        P.add('dve', lambda e: e.tensor_reduce(out=mx, in_=src3, axis=AX.X, op=ALU.max), r=rk, w=[smk])
        ts('dve', m_, mx, 0.125, ALU.mult, [smk], [smk])
        tt('dve', m_, m_, sinkcols, ALU.max, [smk, 'sinkbc'], [smk])
        ts('dve', ng_, m_, -1.0, ALU.mult, [smk], [smk])
        for i in range(nh):
            act(Pf_[:, i, :], src3[:, i, :], AF.Exp, rk + [smk], [tag + 'Pf', 'glwb', smk + '2%d' % i], scale=0.125, bias=ng_[:, i:i + 1],
                accum=ssum[:, i:i + 1])
        tt('dve', dd, sinkcols, m_, ALU.subtract, [smk, 'sinkbc'], [smk])
        act(es, dd, AF.Exp, [smk], [smk])
        tt('dve', den, ssum, es, ALU.add, [smk] + [smk + '2%d' % i for i in range(nh)], [smk])
        P.add('dve', lambda e: e.reciprocal(out=rd, in_=den), r=[smk], w=[smk])
        tt('pool', Pb_, Pf_, rd.unsqueeze(2).to_broadcast([np_, nh, Wd]), ALU.mult, [tag + 'Pf', smk], [tag + 'Pb', 'glwb'])

    evi = [0]
    for seg in range(nseg):
        N = TSEG + (NS if seg == 0 else 0)
        cur_seg[0] = seg
        dma('sp', cosT[:, 0:TSEG], dap(c_cos, seg * TSEG, [[seqlen + NS, 128], [1, TSEG]]), [], ['rope'])
        dma('sp', sinT[:, 0:TSEG], dap(c_sin, seg * TSEG, [[seqlen + NS, 128], [1, TSEG]]), [], ['rope'])
        if seg == 0:
            dma('sp', cosT[:, TSEG:N], dap(c_cos, seqlen, [[seqlen + NS, 128], [1, NS]]), [], ['rope'])
            dma('sp', sinT[:, TSEG:N], dap(c_sin, seqlen, [[seqlen + NS, 128], [1, NS]]), [], ['rope'])
        blocks = [(b, 128, dap(xp, (seg * TSEG + b * 128) * D, [[D, 128], [1, D]]), b * 128) for b in range(NB)]
        if seg == 0:
            blocks.append((NB, NS, xs.ap(), TSEG))
        for bi, (b, rows, src, c0) in enumerate(blocks):
            io = iost[bi % 2]; iok = 'iost0'
            dma('sp', io[0:rows, :], src, [], [iok])
            for k4 in range(4):
                for kk in range(4):
                    k = 4 * k4 + kk
                    tr(ps3[:, kk * 128:kk * 128 + rows], io[0:rows, k * 128:(k + 1) * 128], ident[0:rows, 0:rows], [iok], ['ps3'])
                cp('dve' if k4 % 2 == 0 else 'act', xT[:, 4 * k4:4 * k4 + 4, c0:c0 + rows],
                   ps3[:, :].rearrange("q (a b) -> q a b", b=128)[:, :, 0:rows], ['ps3'], ['xT'])

        for l in range(depth):
            blkc[0] = 0
            dma('sp', tabA[:, 0:4096], dap(s5tab, l * 128 * TABW + TW1, [[TABW, 128], [1, 4096]]), ['s5tab'], ['tabW'])
            dma('sp', tabA[:, 4096:6144], dap(s5tab, l * 128 * TABW + TBD, [[TABW, 128], [1, 2048]]), ['s5tab'], ['tabA'])
            dma('sp', tabE[:], dap(s5tab, l * 128 * TABW + TER, [[TABW, 128], [1, TABW - TER]]), ['s5tab'], ['tabE'])
            dma('sp', bsrow[:], dap(cbs, l * 512, [[0, 1], [1, 512]]), [], ['bsrow'])
            if seg == 0:
                for b in range(4):
                    cp('pool', bsS[:, :, 8 * b:8 * b + 8], bsrow[:].rearrange("o (g t) -> o g t", t=128)[:, :, 0:8], ['bsrow'], ['bsS'])
            dma('pool', glwb[:], dap(glw, l * 512 * 512, [[512, 128], [128 * 512, 4], [1, 512]]), [], ['glwb', 'p0Pf', 'p0Pb'])
            W1 = tabAb[:, 0:8192].rearrange("q (s t r c) -> q s t r c", s=8, t=4, r=2)
            W3 = tabAb[:, 0:8192].rearrange("q (t a r c) -> q t a r c", t=8, a=16, r=2)
            BD = tabAb[:, 8192:12288].rearrange("q (t a c) -> q t a c", t=4, a=8)
            Er = tabE[:, 0:16 * W].rearrange("q (a b) -> q a b", b=W)
            Ei = tabE[:, 16 * W:32 * W].rearrange("q (a b) -> q a b", b=W)
            rhoc = tabE[:, 32 * W:32 * W + 16]
            NJJ = N // 8
            rmsnorm(N, l, lambda k, g: stt(hT[:, k, 0:N], xT[:, k, 0:N], g, rstd[:, 0:N], ALU.mult, ALU.mult,
                                           ['xT', 'rstd', 'gcol'], ['hT']))

            def evac_store(dst_fn, wkey):
                def f(pm, pmk):
                    evi[0] += 1
                    eng = 'act' if evi[0] % 2 == 0 else 'dve'
                    cp(eng, dst_fn(0, TSEG), pm[:, 0:TSEG], [pmk], [wkey])
                    if N > TSEG:
                        cp(eng, dst_fn(TSEG, N), ps2[:, 0:NS], ['ps2'], [wkey])
                return f

            def evac_gate(tile, wkey):
                def f(pm, pmk):
                    evi[0] += 1
                    t = tA[evi[0] % 2]; tk = 'tA%d' % (evi[0] % 2)
                    act(t[:, 0:TSEG], pm[:, 0:TSEG], AF.Silu, [pmk], [tk])
                    if N > TSEG:
                        act(t[:, TSEG:N], ps2[:, 0:NS], AF.Silu, ['ps2'], [tk])
                    tt('dve', mix[:, tile, 0:N], mix[:, tile, 0:N], t[:, 0:N], ALU.mult, [tk, wkey], [wkey])
                return f

            for T in range(4):
                proj(l, None, in_pieces(l, [(OFF_UC + 128 * T, 128)]), N,
                     evac_store(lambda a, b, T=T: mix[:, 12 + T, a:b], 'mixc%d' % T), hT, ['hT'])
            if seg == 0:
                for pi_i in range(16):
                    for ri in range(2):
                        cp('pool', Sprev[:, pi_i, ri, NJ:NJ + 4], s_in[:, l, :, ri, pi_i], ['s_in'], ['Sprev%d' % pi_i])
            G = 4 if NJJ * 8 <= 512 else 2
            WP = 512 // (2 * G)
            Er4 = Er.rearrange("q (t r) j -> q t r j", r=4); Ei4 = Ei.rearrange("q (t r) j -> q t r j", r=4)
            Sp4 = Sprev[:].rearrange("q (t r) i j -> q t r i j", r=4)
            hp4 = hprev[:].rearrange("q l (t r) i -> q l t r i", r=4)
            gi = 0
            for r4 in range(4):
              for T0 in range(0, 4, G):
                gi += 1
                pd = ps3 if gi % 2 == 0 else ps6
                pdk = 'ps3' if gi % 2 == 0 else 'ps6'
                pis = [4 * (T0 + g) + r4 for g in range(G)]
                sks = ['Sprev%d' % p_ for p_ in pis]
                for g in range(G):
                    T = T0 + g
                    uview = mix[32 * r4:32 * r4 + 32, 12 + T, 0:N].rearrange("q (j s) -> q j s", s=8)
                    for ri in range(2):
                        c0 = (g * 2 + ri) * WP
                        for s_ in range(8):
                            mm(pd[:, c0:c0 + NJJ], W1[32 * r4:32 * r4 + 32, s_, T, ri, :], uview[:, :, s_], s_ == 0, s_ == 7,
                               ['tabW', 'mixc%d' % T], [pdk], tile_position=(32 * r4, 0))
                Dv = pd[:, 0:512].rearrange("q (g r j) -> q g r j", g=G, r=2)
                Dre = Dv[:, :, 0, 0:NJJ]; Dim = Dv[:, :, 1, 0:NJJ]
                er = Er4[:, T0:T0 + G, r4, 0:NJJ]; ei = Ei4[:, T0:T0 + G, r4, 0:NJJ]
                w_ = [x[:, 0:G * NJJ].rearrange("q (g j) -> q g j", g=G) for x in s5w]
                k5 = ['s5w']
                tt('dve', w_[0], Dre, er, ALU.mult, [pdk, 'tabE'], k5); tt('dve', w_[1], Dim, ei, ALU.mult, [pdk, 'tabE'], k5)
                tt('pool', w_[2], w_[0], w_[1], ALU.add, k5, k5)
                tt('dve', w_[0], Dim, er, ALU.mult, [pdk, 'tabE'] + k5, k5); tt('dve', w_[1], Dre, ei, ALU.mult, [pdk, 'tabE'], k5)
                tt('pool', w_[3], w_[0], w_[1], ALU.subtract, k5, k5)
                for g in range(G):
                    pi_i = pis[g]
                    rb = rhoc[:, pi_i:pi_i + 1]
                    for ri, (cc, qq) in enumerate(((w_[2], w_[4]), (w_[3], w_[5]))):
                        P.add('dve', lambda e, cc=cc[:, g, 0:NJ], qq=qq[:, g, 0:NJ], rbb=rb.to_broadcast([128, NJ]),
                              ini=hprev[:, l, pi_i, ri:ri + 1]:
                              e.tensor_tensor_scan(out=qq, data0=rbb, data1=cc, initial=ini,
                                                   op0=ALU.mult, op1=ALU.add), r=k5 + ['hprev', 'tabE'], w=k5)
                        if seg == 0:
                            stt(qq[:, g, NJ:NJJ], s_in[:, l, :, ri, pi_i], rb, cc[:, g, NJ:NJJ], ALU.mult, ALU.add, k5 + ['s_in', 'tabE'], k5)
                tt('dve', w_[6], w_[4], er, ALU.mult, k5 + ['tabE'], k5); tt('dve', w_[7], w_[5], ei, ALU.mult, k5 + ['tabE'], k5)
                tt('pool', w_[8], w_[6], w_[7], ALU.subtract, k5, k5)
                tt('dve', w_[6], w_[5], er, ALU.mult, k5 + ['tabE'], k5); tt('dve', w_[7], w_[4], ei, ALU.mult, k5 + ['tabE'], k5)
                tt('pool', w_[9], w_[6], w_[7], ALU.add, k5, k5)
                for ri, S_ in enumerate((w_[8], w_[9])):
                    cp('act', Sp4[:, T0:T0 + G, r4, ri, 0:1], hp4[:, l, T0:T0 + G, r4, ri:ri + 1], ['hprev'], sks)
                    cp('act', Sp4[:, T0:T0 + G, r4, ri, 1:NJ], S_[:, :, 0:NJ - 1], k5, sks)
                    cp('dve', hp4[:, l, T0:T0 + G, r4, ri:ri + 1], S_[:, :, NJ - 1:NJ], k5 + sks, ['hprev'])
                    if seg == 0:
                        for g in range(G):
                            cp('pool', outst[:, 0:128].rearrange("q (b r a) -> q b r a", b=4, r=2)[:, :, ri, pis[g]], S_[:, g, NJ:NJJ],
                               k5, ['outst_s'])
            if seg == 0:
                for b in range(4):
                    for ri, dst in enumerate((hr_s, hi_s)):
                        tr(ps3[0:16, 0:128], outst[:, (b * 2 + ri) * 16:(b * 2 + ri) * 16 + 16], ident[:], ['outst_s'], ['ps3'])
                        cp('dve', outst[0:16, 128:256], ps3[0:16, 0:128], ['ps3'], ['outst_t'])
                        dma('sp', dap(dst, ((l * 4 + b) * 16) * 128, [[128, 16], [1, 128]]), outst[0:16, 128:256], ['outst_t'], ['out'])
            if seg == nseg - 1:
                for ri, dst in enumerate((hr_p, hi_p)):
                    cp('dve', outst[:, 256 + 16 * ri:272 + 16 * ri], hprev[:, l, :, ri], ['hprev'], ['outst_h'])
                    tr(ps3[0:16, 0:128], outst[:, 256 + 16 * ri:272 + 16 * ri], ident[:], ['outst_h'], ['ps3'])
                    cp('dve', outst[0:16, 288:416], ps3[0:16, 0:128], ['ps3'], ['outst_t2'])
                    dma('sp', dap(dst, l * 2048, [[128, 16], [1, 128]]), outst[0:16, 288:416], ['outst_t2'], ['out'])
            dma('sp', tabA[:, 0:4096], dap(s5tab, l * 128 * TABW + TW3, [[TABW, 128], [1, 4096]]), ['s5tab'], ['tabW'])
            def evac_rope(dst_fn, wkey):
                def f(pm, pmk):
                    evi[0] += 1
                    i2 = evi[0] % 2
                    xf = tA[i2]; xfk = 'tA%d' % i2
                    xb_ = sq[i2]; xbk = 'sq%d' % i2
                    pw_ = ps3 if i2 == 0 else ps6
                    pwk = 'ps3' if i2 == 0 else 'ps6'
                    cp('act', xb_[:, 0:TSEG], pm[:, 0:TSEG], [pmk], [xbk])
                    if N > TSEG:
                        cp('act', xb_[:, TSEG:N], ps2[:, 0:NS], ['ps2'], [xbk])
                    mm(pw_[:, 0:TSEG], permb[:], xb_[:, 0:TSEG], True, True, [xbk, 'ident'], [pwk])
                    tt('dve', xf[:, 0:TSEG], pw_[:, 0:TSEG], sinT[:, 0:TSEG], ALU.mult, [pwk, 'rope'], [xfk])
                    if N > TSEG:
                        mm(pw_[:, 0:NS], permb[:], xb_[:, TSEG:N], True, True, [xbk, 'ident', xfk], [pwk])
                        tt('dve', xf[:, TSEG:N], pw_[:, 0:NS], sinT[:, TSEG:N], ALU.mult, [pwk, 'rope'], [xfk])
                    tt('pool', xw0[:, 0:N], xb_[:, 0:N], cosT[:, 0:N], ALU.mult, [xbk, 'rope'], ['xsw0'])
                    tt('dve', dst_fn(0, N), xf[:, 0:N], xw0[:, 0:N], ALU.add, [xfk, 'xsw0'], [wkey])
                return f

            for kt in range(2):
                cp('pool', kT[:, kt, 0:128], khalo[:, l, kt, :], ['khalo'], ['kT'])
            cp('pool', Vt[:, 0, :], vhalo[:, l, :], ['vhalo'], ['Vt'])
            for kt in range(2):
                proj(l, None, in_pieces(l, [(OFF_K + 128 * kt, 128)]), N,
                     evac_rope(lambda a, b, kt=kt: kT[:, kt, 128 + a:128 + b], 'kT'), hT, ['hT'])
            for kt in range(2):
                proj(l, None, in_pieces(l, [(OFF_V + 128 * kt, 128)]), N,
                     evac_store(lambda a, b: tB[0][:, a:b], 'tB0'), hT, ['hT'])
                nblk = NB + (1 if seg == 0 else 0)
                for b in range(nblk):
                    rows = 128 if b < NB else NS
                    tr(psT[0:rows, b * 128:b * 128 + 128], tB[0][:, b * 128:b * 128 + rows], identb[:], ['tB0'], ['psT'])
                cp('dve', Vt[:, 1:1 + NB, 128 * kt:128 * kt + 128], psT[:, 0:NB * 128].rearrange("q (a b) -> q a b", b=128), ['psT'], ['Vt'])
                if seg == 0:
                    cp('dve', Vt[0:NS, NB + 1, 128 * kt:128 * kt + 128], psT[0:NS, NB * 128:NB * 128 + 128], ['psT'], ['Vt'])
            for tq in range(8):
                proj(l, None, in_pieces(l, qcols(OFF_Q, tq)), N,
                     evac_rope(lambda a, b, tq=tq: mix[:, 4 + tq, a:b], 'mixb%d' % tq), hT, ['hT'])
            for kt in range(2):
                cp('pool', khalo[:, l, kt, :], kT[:, kt, TSEG:TSEG + 128], ['kT'], ['khalo'])
            cp('pool', vhalo[:, l, :], Vt[:, NB, :], ['Vt'], ['vhalo'])
            if seg == nseg - 1:
                for kt in range(2):
                    tr(psT[:, 0:128], kT[:, kt, TSEG:TSEG + 128], identb[:], ['kT'], ['psT'])
                    cp('dve', PTs[:, 128 * kt:128 * kt + 128], psT[:, 0:128], ['psT'], ['PTs0'])
                dma('pool', dap(nk_p, l * 128 * 256, [[256, 128], [1, 256]]), PTs[:, 0:256], ['PTs0'], ['out'])
                dma('pool', dap(nv_p, l * 128 * 256, [[256, 128], [1, 256]]), Vt[:, NB, :], ['Vt'], ['out'])
            if seg == 0:
                for kt in range(2):
                    tr(psT[0:NS, 0:128], kT[:, kt, 128 + TSEG:128 + N], identb[:], ['kT'], ['psT'])
                    cp('dve', PTs[0:NS, 128 * kt:128 * kt + 128], psT[0:NS, 0:128], ['psT'], ['PTs0'])
                dma('pool', dap(nk_s, l * NS * 256, [[256, NS], [1, 256]]), PTs[0:NS, 0:256], ['PTs0'], ['out'])
                dma('pool', dap(nv_s, l * NS * 256, [[256, NS], [1, 256]]), Vt[0:NS, NB + 1, :], ['Vt'], ['out'])
            for T in range(4):
                uvw = mix[:, 12 + T, 0:N].rearrange("q (j s) -> q j s", s=8)
                for t in range(8):
                    reg = psS[:, (t // 4) * 512 + (t % 4) * W:(t // 4) * 512 + (t % 4) * W + NJJ]
                    for s in range(t + 1):
                        mm(reg, BD[:, T, t - s, :], uvw[:, :, s], s == 0, False, ['tabA', 'mixc%d' % T], ['psS'])
                    for r4 in range(4):
                        pi_i = 4 * T + r4
                        for ri in range(2):
                            o_ = psS[32 * r4:32 * r4 + 32, (t // 4) * 512 + (t % 4) * W:(t // 4) * 512 + (t % 4) * W + NJJ]
                            mm(o_, W3[:, t, pi_i, ri, :], Sprev[:, pi_i, ri, 0:NJJ], False, (ri == 1),
                               ['tabW', 'Sprev%d' % pi_i], ['psS'], tile_position=(0, 32 * r4))
                yv = tA[2][:, 0:N].rearrange("q (j s) -> q s j", s=8)
                for hb in range(2):
                    pv = psS[:, hb * 512:hb * 512 + 4 * W].rearrange("q (t j) -> q t j", j=W)[:, :, 0:NJJ]
                    uv2 = mix[:, 12 + T, 0:N].rearrange("q (j s) -> q s j", s=8)[:, 4 * hb:4 * hb + 4, :]
                    stt(yv[:, 4 * hb:4 * hb + 4, :], uv2, dcol[:, l * 4 + T:l * 4 + T + 1], pv, ALU.mult, ALU.add,
                        ['psS', 'mixc%d' % T, 'dcol'], ['tA2'])
                y_ = tA[2][:, 0:N]; g1 = tA[3][:, 0:N]
                tt('pool', g1, y_, y_, ALU.mult, ['tA2'], ['tA3'])
                ts('dve', g1, g1, 0.044715, ALU.mult, ['tA3'], ['tA3'], s2=1.0, op1=ALU.add)
                tt('pool', g1, g1, y_, ALU.mult, ['tA3', 'tA2'], ['tA3'])
                act(g1, g1, AF.Sigmoid, ['tA3'], ['tA3'], scale=2.0 * math.sqrt(2.0 / math.pi))
                tt('dve', yb[:, T, 0:N], y_, g1, ALU.mult, ['tA2', 'tA3'], ['yb'])
            for T in range(4):
                for kc in range(4):
                    mm(ps0[:, 0:TSEG], glwb[:, kc, 128 * T:128 * T + 128], yb[:, kc, 0:TSEG], kc == 0, kc == 3, ['glwb', 'yb'], ['ps0'])
                    if N > TSEG:
                        mm(ps2[:, 0:NS], glwb[:, kc, 128 * T:128 * T + 128], yb[:, kc, TSEG:N], kc == 0, kc == 3, ['glwb', 'yb'], ['ps2'])
                g2 = tA[3]
                act(g2[:, 0:TSEG], ps0[:, 0:TSEG], AF.Sigmoid, ['ps0'], ['tA3'], bias=gbcol[:, l * 4 + T:l * 4 + T + 1])
                if N > TSEG:
                    act(g2[:, TSEG:N], ps2[:, 0:NS], AF.Sigmoid, ['ps2'], ['tA3'], bias=gbcol[:, l * 4 + T:l * 4 + T + 1])
                tt('dve', mix[:, 12 + T, 0:N], yb[:, T, 0:N], g2[:, 0:N], ALU.mult, ['yb', 'tA3'], ['mixc%d' % T])
            for T in range(4):
                proj(l, None, in_pieces(l, [(OFF_GC + 128 * T, 128)]), N, evac_gate(12 + T, 'mixc%d' % T), hT, ['hT'])

            units = [(bq, kt, half) for bq in range(NB) for kt in range(2) for half in range(2)]
            import os as _os
            if _os.environ.get('V1'):
                psT2 = [psT, psT]; psTk = ['psT', 'psT']
            else:
                psT2 = [psT, ps3[:].bitcast(BF16)]; psTk = ['psT', 'ps3']
            Obuf = [ps6, ps2]; Obk = ['ps6', 'ps2']

            def stage1(u, bq, kt, half):
                par = u % 2
                Sb = psS if par == 0 else ps01
                Sk = ['psS'] if par == 0 else ['ps0', 'ps1']
                msk = maskF if (seg == 0 and bq == 0) else maskA
                hs = slice(64 * half, 64 * half + 64)
                for i2 in range(2):
                    for i in (2 * i2, 2 * i2 + 1):
                        tq = 4 * kt + i
                        mm(Sb[:, i * 256:(i + 1) * 256], mix[hs, 4 + tq, 128 * bq:128 * bq + 128],
                           kT[hs, kt, 128 * bq:128 * bq + 256], i % 2 == 0, False, ['mixb%d' % tq, 'kT'], Sk)
                    mm(Sb[:, i2 * 512:(i2 + 1) * 512], identb[:], msk[:], False, True, ['masks', 'ident'], Sk)

            def smv(par, np_=128, nh=4):
                so = 32 * par
                return dict(mx=sm[0:np_, so + 0:so + nh], m=sm[0:np_, so + 4:so + 4 + nh], ng=sm[0:np_, so + 8:so + 8 + nh],
                            ssum=sm[0:np_, so + 12:so + 12 + nh], dd=sm[0:np_, so + 16:so + 16 + nh], es=sm[0:np_, so + 20:so + 20 + nh],
                            den=sm[0:np_, so + 24:so + 24 + nh], rd=sm[0:np_, so + 28:so + 28 + nh])

            def stageF(u, bq, kt, half):
                par = u % 2
                Sb = psS if par == 0 else ps01
                Sk = ['psS'] if par == 0 else ['ps0', 'ps1']
                v = smv(par); kA = 'smA%d' % par
                h0 = l * 16 + 8 * kt + 4 * half
                sinkcols = sinkbc[:, h0:h0 + 4]
                src3 = Sb[:, :].rearrange("q (a b) -> q a b", b=256)
                P.add('dve', lambda e, o=v['mx'], i=src3: e.tensor_reduce(out=o, in_=i, axis=AX.X, op=ALU.max), r=Sk, w=[kA])
                ts('dve', v['m'], v['mx'], 0.125, ALU.mult, [kA], [kA])
                tt('dve', v['m'], v['m'], sinkcols, ALU.max, [kA, 'sinkbc'], [kA])
                ts('dve', v['ng'], v['m'], -1.0, ALU.mult, [kA], [kA])
                tt('dve', v['dd'], sinkcols, v['m'], ALU.subtract, [kA, 'sinkbc'], [kA])

            def stageX(u, bq, kt, half):
                par = u % 2
                Sb = psS if par == 0 else ps01
                Sk = ['psS'] if par == 0 else ['ps0', 'ps1']
                v = smv(par); kA = 'smA%d' % par; kS = 'smS%d' % par
                src3 = Sb[:, :].rearrange("q (a b) -> q a b", b=256)
                for i in range(4):
                    act(Pf2[par][:, i, :], src3[:, i, :], AF.Exp, Sk + [kA], ['p%dPf' % par, 'glwb', kS], scale=0.125,
                        bias=v['ng'][:, i:i + 1], accum=v['ssum'][:, i:i + 1])
                act(v['es'], v['dd'], AF.Exp, [kA], [kS])

            def stageB(u, bq, kt, half):
                par = u % 2
                v = smv(par); kS = 'smS%d' % par; kR = 'smR%d' % par
                tt('dve', v['den'], v['ssum'], v['es'], ALU.add, [kS], [kR])
                P.add('dve', lambda e, o=v['rd'], i=v['den']: e.reciprocal(out=o, in_=i), r=[kR], w=[kR])
                tt('pool', Pb2[par], Pf2[par], v['rd'].unsqueeze(2).to_broadcast([128, 4, 256]), ALU.mult, ['p%dPf' % par, kR],
                   ['p%dPb' % par, 'glwb'])

            def stage2(u, bq, kt, half):
                par = u % 2
                hs = slice(64 * half, 64 * half + 64)
                pT = psT2[par]; pTk = psTk[par]; PT_ = PTs2[par]; PTk = 'PTs%d' % par
                Ob = Obuf[kt]; Ok = Obk[kt]
                for i in range(4):
                    for kb in range(2):
                        tr(pT[:, (2 * i + kb) * 128:(2 * i + kb) * 128 + 128], Pb2[par][:, i, kb * 128:kb * 128 + 128], identb[:],
                           ['p%dPb' % par], [pTk])
                cp('act', PT_[:], pT[:], [pTk], [PTk])
                for i in range(4):
                    for kb in range(2):
                        mm(Ob[hs, i * 128:i * 128 + 128], Vt[:, bq + kb, (2 * kt + half) * 64:(2 * kt + half) * 64 + 64],
                           PT_[:, (2 * i + kb) * 128:(2 * i + kb) * 128 + 128], kb == 0, kb == 1, ['Vt', PTk], [Ok],
                           tile_position=(0, 64 * half))
                if half == 1:
                    def ev(Ob=Ob, Ok=Ok, kt=kt, bq=bq):
                        for i in range(4):
                            cp('dve', mix[:, 4 + 4 * kt + i, 128 * bq:128 * bq + 128], Ob[:, i * 128:i * 128 + 128],
                               [Ok], ['mixb%d' % (4 * kt + i)])
                    pend.append(ev)
            nu = len(units)
            pend = []
            stage1(0, *units[0])
            for k_ in range(nu + 2):
                if k_ + 1 < nu:
                    stage1(k_ + 1, *units[k_ + 1])
                if k_ >= 2:
                    stageB(k_ - 2, *units[k_ - 2])
                if k_ < nu:
                    stageF(k_, *units[k_])
                    stageX(k_, *units[k_])
                for ev_ in pend:
                    ev_()
                del pend[:]
                if k_ >= 2:
                    stage2(k_ - 2, *units[k_ - 2])
            for ev_ in pend:
                ev_()
            del pend[:]
            if seg == 0:
                dma('pool', ckb, dap(ck, l * 4 * 128 * 256, [[256, 128], [128 * 256, 4], [1, 256]]), [], ['iost0'])
                dma('pool', cvb, dap(cv, l * 4 * 128 * 256, [[256, 128], [128 * 256, 4], [1, 256]]), [], ['iost0'])
                for b in range(4):
                    for kt in range(2):
                        tr(psT[:, (2 * b + kt) * 128:(2 * b + kt) * 128 + 128], ckb[:, b, kt * 128:kt * 128 + 128], identb[:], ['iost0'], ['psT'])
                cp('dve', ckT, psT[:].rearrange("q (b k w) -> q b k w", b=4, k=2), ['psT'], ['iost0'])
                for kt in range(2):
                    for half in range(2):
                        hs = slice(64 * half, 64 * half + 64)
                        for i in range(4):
                            tq = 4 * kt + i
                            qs = mix[hs, 4 + tq, TSEG:N]
                            for b in range(4):
                                mm(psS[0:NS, b * 128:b * 128 + 128], qs, ckT[hs, b, kt, :], b == 0, False, ['mixb%d' % tq, 'iost0'], ['psS'])
                            mm(psS[0:NS, 0:512], identb[0:NS, 0:NS], maskS[:, 0:512], False, True, ['masks', 'ident'], ['psS'])
                            mm(psS[0:NS, 512:512 + NS], qs, kT[hs, kt, 128 + TSEG:128 + N], True, False, ['mixb%d' % tq, 'kT'], ['psS'])
                            mm(psS[0:NS, 512:512 + NS], identb[0:NS, 0:NS], maskS[:, 512:544], False, True, ['masks', 'ident'], ['psS'])
                            h0 = l * 16 + 8 * kt + 4 * half + i
                            softmax(psS[0:NS, 0:544].rearrange("q (a b) -> q a b", b=544), NS, 1, 544, sinkbc[0:NS, h0:h0 + 1],
                                    PfS.rearrange("q (a b) -> q a b", b=544), PbS.rearrange("q (a b) -> q a b", b=544), ['psS'], 'p0')
                            for b in range(4):
                                tr(psT[:, b * 32:b * 32 + 32], PbS[:, b * 128:b * 128 + 128], identb[0:NS, 0:NS], ['p0Pb'], ['psT'])
                            tr(psT[0:NS, 128:160], PbS[:, 512:544], identb[0:NS, 0:NS], ['p0Pb'], ['psT'])
                            cp('act', PTS2[:, 0:4, :], psT[:, 0:128].rearrange("q (a b) -> q a b", b=32), ['psT'], ['PTS2'])
                            cp('act', PTS2[0:NS, 4, :], psT[0:NS, 128:160], ['psT'], ['PTS2'])
                            vc = (2 * kt + half) * 64
                            for b in range(4):
                                mm(ps6[hs, i * 32:i * 32 + 32], cvb[:, b, vc:vc + 64], PTS2[:, b, :], b == 0, False, ['iost0', 'PTS2'], ['ps6'],
                                   tile_position=(0, 64 * half))
                            mm(ps6[hs, i * 32:i * 32 + 32], Vt[0:NS, NB + 1, vc:vc + 64], PTS2[0:NS, 4, :], False, True, ['Vt', 'PTS2'], ['ps6'],
                               tile_position=(0, 64 * half))
                    for i in range(4):
                        cp('act' if i % 2 else 'dve', mix[:, 4 + 4 * kt + i, TSEG:N], ps6[:, i * 32:i * 32 + 32], ['ps6'], ['mixb%d' % (4 * kt + i)])
            for tq in range(8):
                proj(l, None, in_pieces(l, qcols(OFF_GB, tq)), N, evac_gate(4 + tq, 'mixb%d' % tq), hT, ['hT'])

            for T in range(4):
                proj(l, None, in_pieces(l, [(OFF_UA + 128 * T, 128)]), N,
                     evac_store(lambda a, b, T=T: mix[:, T, a:b], 'mixa%d' % T), hT, ['hT'])
            for T in range(4):
                proj(l, None, in_pieces(l, [(OFF_VA + 128 * T, 128)]), N,
                     evac_store(lambda a, b: tB[1][:, a:b], 'tB0'), hT, ['hT'])
                nblk = NB + (1 if seg == 0 else 0)
                for b in range(nblk):
                    rows = 128 if b < NB else NS
                    tr(psT[0:rows, b * 128:b * 128 + 128], tB[1][:, b * 128:b * 128 + rows], identb[:], ['tB0'], ['psT'])
                cp('dve', vatm[:, 0:NB, 128 * T:128 * T + 128], psT[:, 0:NB * 128].rearrange("q (a b) -> q a b", b=128), ['psT'], ['vatm'])
                if seg == 0:
                    cp('dve', vatm[0:NS, NB, 128 * T:128 * T + 128], psT[0:NS, NB * 128:NB * 128 + 128], ['psT'], ['vatm'])
            if seg == 0:
                dma('pool', dap(va_s, l * NS * 512, [[512, NS], [1, 512]]), vatm[0:NS, NB, :], ['vatm'], ['out'])
            for h in range(4):
                lh = l * 4 + h
                for b in range(NB):
                    mm(ps3[:, b * 128:b * 128 + 128], vatm[:, b, 128 * h:128 * h + 128], wsT[:, lh, :], b == 0, False, ['vatm', 'wsT'], ['ps3'])
                for b in range(NB):
                    P.add('pe', lambda e, b=b, lh=lh: e.matmul(ps3[:, b * 128:b * 128 + 128], lhsT=ones1[0:1, :],
                                                              rhs=bsrow[0:1, (lh % 4) * 128:(lh % 4) * 128 + 128], start=False, stop=(b == NB - 1)),
                          r=['bsrow', 'ident'], w=['ps3'])
                tt('dve', mix[:, h, 0:TSEG], mix[:, h, 0:TSEG], ps3[:, 0:TSEG], ALU.mult, ['ps3', 'mixa%d' % h], ['mixa%d' % h])
                if seg == 0:
                    mm(ps6[:, 0:NS], vatm[0:NS, NB, 128 * h:128 * h + 128], wsS[:, lh, :], True, False, ['vatm', 'wsS'], ['ps6'])
                    P.add('pe', lambda e, lh=lh: e.matmul(ps6[:, 0:NS], lhsT=ones1[0:1, :], rhs=bsS[0:1, lh % 4, :], start=False, stop=True),
                          r=['bsS', 'ident'], w=['ps6'])
                    tt('dve', mix[:, h, TSEG:N], mix[:, h, TSEG:N], ps6[:, 0:NS], ALU.mult, ['ps6', 'mixa%d' % h], ['mixa%d' % h])
            for T in range(4):
                proj(l, None, in_pieces(l, [(OFF_GA + 128 * T, 128)]), N, evac_gate(T, 'mixa%d' % T), hT, ['hT'])

            allmix = ['mixa%d' % i for i in range(4)] + ['mixb%d' % i for i in range(8)] + ['mixc%d' % i for i in range(4)]

            def evac_res(j):
                def f(pm, pmk):
                    tt('dve', xT[:, j, 0:TSEG], xT[:, j, 0:TSEG], pm[:, 0:TSEG], ALU.add, [pmk, 'xT'], ['xT'])
                    if N > TSEG:
                        tt('dve', xT[:, j, TSEG:N], xT[:, j, TSEG:N], ps2[:, 0:NS], ALU.add, ['ps2', 'xT'], ['xT'])
                return f
            for j in range(KD):
                proj(l, None, out_pieces(l, j), N, evac_res(j), mix, allmix)

        rmsnorm(N, depth, lambda k, g: stt(xT[:, k, 0:N], xT[:, k, 0:N], g, rstd[:, 0:N], ALU.mult, ALU.mult,
                                           ['xT', 'rstd', 'gcol'], ['xT']))
        oblocks = [(128, b * 128, dap(y_p, (seg * TSEG + b * 128) * D, [[D, 128], [1, D]])) for b in range(NB)]
        if seg == 0:
            oblocks.append((NS, TSEG, y_s.ap()))
        for bi, (rows, c0, dst) in enumerate(oblocks):
            io = iost[bi % 2]; iok = 'iost0'
            for k4 in range(4):
                for kk in range(4):
                    k = 4 * k4 + kk
                    tr(ps3[0:rows, kk * 128:kk * 128 + 128], xT[:, k, c0:c0 + rows], ident[:], ['xT'], ['ps3'])
                cp('dve' if k4 % 2 == 0 else 'act', io[0:rows, 512 * k4:512 * k4 + 512], ps3[0:rows, :], ['ps3'], [iok])
            dma('sp', dst, io[0:rows, :], [iok], ['out'])

    import os
    ks = os.environ.get('KSTOP')
    if ks:
        print('NOPS', len(P.ops)); print('LASTOPS', [(i, o['eng'], o['line']) for i, o in list(enumerate(P.ops))[max(0, int(ks) - 3):int(ks)]]); P.ops = P.ops[:int(ks)]
    P.emit()
    st.close()
    return nc


def _consts(seqlen):
    half = 8
    inv = (500000.0 ** (-np.arange(half, dtype=np.float32) * 2.0 / 16.0)).astype(np.float32)
    pos = np.concatenate([np.arange(seqlen, dtype=np.float32), np.arange(8, dtype=np.float32) + PAST] * 1)
    pos = np.concatenate([np.arange(seqlen, dtype=np.float32)] + [np.arange(8, dtype=np.float32) + np.float32(PAST)] * 4)
    ang = pos[None, :] * inv[:, None]
    cosv = np.cos(ang).astype(np.float32); sinv = np.sin(ang).astype(np.float32)
    C = np.ones((128, pos.shape[0]), np.float32); S = np.zeros((128, pos.shape[0]), np.float32)
    for h0 in (0, 64):
        C[h0:h0 + 8] = cosv; C[h0 + 8:h0 + 16] = cosv
        S[h0:h0 + 8] = -sinv; S[h0 + 8:h0 + 16] = sinv
    i = np.arange(128)[:, None]; j = np.arange(256)[None, :]
    diff = i + 128 - j
    valid = (diff >= 0) & (diff < 128)
    mA = np.where(valid, 0.0, NEGM).astype(np.float32)
    mF = mA.copy(); mF[:, 0:128] = NEGM
    mA2 = np.concatenate([mA, mA], 1); mF2 = np.concatenate([mF, mF], 1)
    mS = np.full((32, 544), NEGM, np.float32)
    for b in range(4):
        for t in range(8):
            q = 8 * b + t
            for jj in range(128):
                if jj > t:
                    mS[q, b * 128 + jj] = 0.0
            for s in range(t + 1):
                mS[q, 512 + 8 * b + s] = 0.0
    tril = np.tril(np.ones((128, 128), np.float32))
    maskm = np.zeros((128, 2), np.float32)
    for q in range(128):
        maskm[q, (q // 16) % 2] = 1.0
    bd = np.zeros((32, 32), np.float32)
    for b in range(4):
        for t in range(8):
            for s in range(t + 1):
                bd[8 * b + t, 8 * b + s] = 1.0
    perm = np.zeros((128, 128), np.float32)
    for m_ in range(128):
        mm_ = m_ % 64
        if mm_ < 8:
            perm[m_ + 8, m_] = 1.0
        elif mm_ < 16:
            perm[m_ - 8, m_] = 1.0
    return dict(c_perm=perm, c_ident=np.eye(128, dtype=np.float32), c_tril=tril, c_maskA=mA2, c_maskF=mF2, c_maskS=mS, c_maskm=maskm,
                c_bd=bd, c_cos=C, c_sin=S)


_NC_CACHE = {}


def kernel(x_prompt, x_sample, cache_swa_k, cache_swa_v, state_ssm_re, state_ssm_im,
           norm_g, final_norm_g, w_in, w_out, chunk_w_s, chunk_b_s, attn_sinks,
           ssm_a_re, ssm_a_im, ssm_log_dt, ssm_b_re, ssm_b_im, ssm_c_re, ssm_c_im,
           ssm_d, glu_w, glu_b):
    f = lambda a: np.ascontiguousarray(np.asarray(a, dtype=np.float32))
    depth = int(np.asarray(w_in).shape[0]); seqlen = int(np.asarray(x_prompt).shape[1])
    nseg = seqlen // TSEG
    key = (depth, nseg)
    if key not in _NC_CACHE:
        _NC_CACHE[key] = build_program(depth, nseg)
    nc = _NC_CACHE[key]
    cst = _consts(seqlen)
    shared = dict(
        ng=np.concatenate([f(norm_g).reshape(depth * 16, 128), f(final_norm_g).reshape(16, 128)], 0),
        w_in=f(w_in), w_out=f(w_out), cws=f(chunk_w_s), cbs=f(chunk_b_s).reshape(1, -1), sinks=f(attn_sinks).reshape(1, -1),
        a_re=f(ssm_a_re), a_im=f(ssm_a_im), ldt=f(ssm_log_dt), b_re=f(ssm_b_re), b_im=f(ssm_b_im),
        c_re=f(ssm_c_re).reshape(depth, 512, 64), c_im=f(ssm_c_im).reshape(depth, 512, 64),
        dsk=f(ssm_d).reshape(depth * 4, 128), glw=f(glu_w), glb=f(glu_b).reshape(depth * 4, 128), **cst)
    xpf, xsf = f(x_prompt), f(x_sample)
    ckf, cvf = f(cache_swa_k), f(cache_swa_v)
    srf, sif = f(state_ssm_re), f(state_ssm_im)
    nb = xpf.shape[0]
    in_maps = []
    for c in range(8):
        m = dict(shared)
        m["xp"] = xpf[c % nb]
        m["xs"] = xsf[4 * c:4 * c + 4].reshape(NS, D)
        m["ck"] = ckf[:, 4 * c:4 * c + 4].reshape(depth, 4, 128, 256)
        m["cv"] = cvf[:, 4 * c:4 * c + 4].reshape(depth, 4, 128, 256)
        m["sre"] = srf[:, 4 * c:4 * c + 4].reshape(depth, 4, 16, 128)
        m["sim"] = sif[:, 4 * c:4 * c + 4].reshape(depth, 4, 16, 128)
        in_maps.append(m)
    res = run_bass_kernel_spmd(nc, in_maps, core_ids=list(range(8))).results
    y_prompt = np.stack([res[b]["y_p"] for b in range(nb)], 0)
    y_sample = np.concatenate([res[c]["y_s"].reshape(4, 8, D) for c in range(8)], 0)
    nkp = np.stack([res[b]["nk_p"].reshape(depth, 128, 4, 64) for b in range(nb)], 1)
    nvp = np.stack([res[b]["nv_p"].reshape(depth, 128, 4, 64) for b in range(nb)], 1)
    nks = np.concatenate([res[c]["nk_s"].reshape(depth, 4, 8, 4, 64) for c in range(8)], 1)
    nvs = np.concatenate([res[c]["nv_s"].reshape(depth, 4, 8, 4, 64) for c in range(8)], 1)
    hrp = np.stack([res[b]["hr_p"].reshape(depth, 32, 64) for b in range(nb)], 1)
    hip = np.stack([res[b]["hi_p"].reshape(depth, 32, 64) for b in range(nb)], 1)
    hrs = np.concatenate([res[c]["hr_s"].reshape(depth, 4, 32, 64) for c in range(8)], 1)
    his = np.concatenate([res[c]["hi_s"].reshape(depth, 4, 32, 64) for c in range(8)], 1)
    vas = np.concatenate([res[c]["va_s"].reshape(depth, 4, 8, 512) for c in range(8)], 1)
    return tuple(np.ascontiguousarray(a, dtype=np.float32) for a in
                 (y_prompt, y_sample, nkp, nvp, nks, nvs, hrp, hip, hrs, his, vas))
```

```python
import math
import numpy as np
import ml_dtypes
import concourse.bass as bass
import concourse.mybir as mybir
from concourse.bass_utils import run_bass_kernel_spmd

F32 = mybir.dt.float32
BF16 = mybir.dt.bfloat16
ALU = mybir.AluOpType
AF = mybir.ActivationFunctionType
AX = mybir.AxisListType

D = 2048
KD = 16
DIN = 5120
DEPTH = 4
SEQ = 4096
TSEG = 512
NSEG = SEQ // TSEG
NB = TSEG // 128
NS = 32
NJ = TSEG // 8
PAST = 16384
OFF_UA, OFF_VA, OFF_GA, OFF_Q, OFF_K, OFF_V, OFF_GB, OFF_UC, OFF_GC = 0, 512, 1024, 1536, 2560, 2816, 3072, 4096, 4608
NEGM = -60000.0
TW1 = 0
TW3 = 4096
TBD = 8192
TER = 10240
TEI = TER + 16 * (NJ + 4)
TRHO = TEI + 16 * (NJ + 4)
TABW = TRHO + 16


def dap(t, off, dims):
    return bass.AP(tensor=t, offset=off, ap=[[s, c] for s, c in dims])


class Prog:
    def __init__(self, nc):
        self.nc = nc
        self.ops = []
        self.last_w = {}
        self.readers = {}

    def add(self, eng, fn, r=(), w=(), dma=False):
        deps = set()
        for k in r:
            if k in self.last_w:
                deps.add(self.last_w[k])
        for k in w:
            if k in self.last_w:
                deps.add(self.last_w[k])
            for x in self.readers.get(k, ()):
                deps.add(x)
        idx = len(self.ops)
        deps.discard(idx)
        import sys as _s
        fr = _s._getframe(1)
        ln = []
        while fr is not None and len(ln) < 3:
            ln.append(fr.f_lineno); fr = fr.f_back
        self.ops.append(dict(eng=eng, fn=fn, deps=sorted(deps), dma=dma, line=ln))
        for k in w:
            self.last_w[k] = idx
            self.readers[k] = []
        for k in r:
            if k not in w:
                self.readers.setdefault(k, []).append(idx)
        return idx

    def emit(self):
        nc = self.nc
        ops = self.ops
        engs = ['pe', 'act', 'dve', 'pool', 'sp']
        RR = {'sp': 8, 'pool': 4, 'act': 2, 'pe': 1, 'dve': 1}
        for o in ops:
            o['sig'] = False
        for i, o in enumerate(ops):
            for d in o['deps']:
                p = ops[d]
                if p['dma']:
                    continue
                if p['eng'] == 'pe' and o['eng'] == 'pe' and not o['dma']:
                    continue
                p['sig'] = True
        cnt = {e: 0 for e in engs}
        dcnt = {e: 0 for e in engs}
        for o in ops:
            e = o['eng']
            if o['dma']:
                n = dcnt[e]
                o['dslot'] = n % RR[e]
                o['dval'] = 16 * (n // RR[e] + 1)
                o['dprev'] = None
                dcnt[e] += 1
            elif o['sig']:
                cnt[e] += 1
                o['val'] = cnt[e]
        lastslot = {}
        for i, o in enumerate(ops):
            if o['dma']:
                key = (o['eng'], o['dslot'])
                o['dprev'] = lastslot.get(key)
                lastslot[key] = i
        import contextlib
        with contextlib.ExitStack() as st:
            sems = {e: st.enter_context(nc.semaphore("s_" + e)) for e in engs}
            dsems = {}
            for e in ['sp', 'pool', 'act']:
                for s in range(RR[e]):
                    dsems[(e, s)] = st.enter_context(nc.semaphore("d_%s%d" % (e, s)))
            block = st.enter_context(nc.Block())
            per = {e: [o for o in ops if o['eng'] == e] for e in engs}

            def run(e, engine):
                waited = {}

                def need(sem_key, sem, val):
                    if waited.get(sem_key, 0) >= val:
                        return
                    engine.wait_ge(sem, val)
                    waited[sem_key] = val

                for o in per[e]:
                    for d in o['deps']:
                        p = ops[d]
                        if p['dma']:
                            need(('d', p['eng'], p['dslot']), dsems[(p['eng'], p['dslot'])], p['dval'])
                        else:
                            if p['eng'] == 'pe' and e == 'pe' and not o['dma']:
                                continue
                            need(('c', p['eng']), sems[p['eng']], p['val'])
                    if o['dma'] and o['dprev'] is not None:
                        p = ops[o['dprev']]
                        need(('d', p['eng'], p['dslot']), dsems[(p['eng'], p['dslot'])], p['dval'])
                    ins = o['fn'](engine)
                    if o['dma']:
                        ins.then_inc(dsems[(e, o['dslot'])], 16)
                    elif o['sig']:
                        ins.then_inc(sems[e], 1)
                if e in ('sp', 'pool', 'act'):
                    tot = {}
                    for o in per[e]:
                        if o['dma']:
                            tot[o['dslot']] = o['dval']
                    for s, v in tot.items():
                        engine.wait_ge(dsems[(e, s)], v)

            @block.tensor
            def _(en):
                run('pe', en)

            @block.scalar
            def _(en):
                run('act', en)

            @block.vector
            def _(en):
                run('dve', en)

            @block.gpsimd
            def _(en):
                run('pool', en)

            @block.sync
            def _(en):
                run('sp', en)


def build_program(depth=DEPTH, nseg=NSEG, debug=False):
    nc = bass.Bass("TRN2", target_bir_lowering=False)
    P = Prog(nc)
    seqlen = nseg * TSEG

    def din(name, shape, dt=F32):
        return nc.dram_tensor(name, list(shape), dt, kind="ExternalInput")

    def dout(name, shape, dt=F32):
        return nc.dram_tensor(name, list(shape), dt, kind="ExternalOutput")

    xp = din("xp", [seqlen, D]); xs = din("xs", [NS, D])
    ck = din("ck", [depth, 4, 128, 256]); cv = din("cv", [depth, 4, 128, 256])
    sre = din("sre", [depth, 4, 16, 128]); sim = din("sim", [depth, 4, 16, 128])
    ng = din("ng", [depth * 16 + 16, 128])
    w_in = din("w_in", [depth, D, DIN]); w_out = din("w_out", [depth, D, D])
    cws = din("cws", [depth, 4, 128, 128]); cbs = din("cbs", [1, depth * 4 * 128])
    sinks = din("sinks", [1, depth * 16])
    a_re = din("a_re", [depth, 32, 64]); a_im = din("a_im", [depth, 32, 64]); ldt = din("ldt", [depth, 32])
    b_re = din("b_re", [depth, 32, 64, 16]); b_im = din("b_im", [depth, 32, 64, 16])
    c_re = din("c_re", [depth, 512, 64]); c_im = din("c_im", [depth, 512, 64])
    dsk = din("dsk", [depth * 4, 128]); glw = din("glw", [depth, 512, 512]); glb = din("glb", [depth * 4, 128])
    c_ident = din("c_ident", [128, 128]); c_tril = din("c_tril", [128, 128])
    c_maskA = din("c_maskA", [128, 512]); c_maskF = din("c_maskF", [128, 512]); c_maskS = din("c_maskS", [32, 544])
    c_maskm = din("c_maskm", [128, 2]); c_bd = din("c_bd", [32, 32])
    c_perm = din("c_perm", [128, 128]); c_cos = din("c_cos", [128, seqlen + NS]); c_sin = din("c_sin", [128, seqlen + NS])

    y_p = dout("y_p", [seqlen, D]); y_s = dout("y_s", [NS, D])
    nk_p = dout("nk_p", [depth, 128, 256]); nv_p = dout("nv_p", [depth, 128, 256])
    nk_s = dout("nk_s", [depth, NS, 256]); nv_s = dout("nv_s", [depth, NS, 256])
    hr_p = dout("hr_p", [depth, 16, 128]); hi_p = dout("hi_p", [depth, 16, 128])
    hr_s = dout("hr_s", [depth, 4, 16, 128]); hi_s = dout("hi_s", [depth, 4, 16, 128])
    va_s = dout("va_s", [depth, NS, 512])
    s5tab = nc.dram_tensor("s5tab", [depth, 128, TABW], F32)
    wcache = nc.dram_tensor("wcache", [depth * 56, 128, KD * 128], BF16)
    wcn_in = nc.dram_tensor("wcn_in", [depth, D, DIN], BF16)
    wcn_out = nc.dram_tensor("wcn_out", [depth, D, D], BF16)

    NMAX = TSEG + NS
    import contextlib
    st = contextlib.ExitStack()

    def sb(name, shape, dt=F32):
        return st.enter_context(nc.sbuf_tensor(name, list(shape), dt))

    def ps(name, shape, dt=F32):
        return st.enter_context(nc.psum_tensor(name, list(shape), dt))

    xT = sb("xT", [128, KD, NMAX]); hT = sb("hT", [128, KD, NMAX], BF16); mix = sb("mix", [128, KD, NMAX], BF16)
    kT = sb("kT", [128, 2, 128 + NMAX], BF16); Vt = sb("Vt", [128, NB + 2, 256], BF16)
    vatm = sb("vatm", [128, NB + 1, 512], BF16)
    khalo = sb("khalo", [128, depth, 2, 128], BF16); vhalo = sb("vhalo", [128, depth, 256], BF16)
    cosT = sb("cosT", [128, NMAX]); sinT = sb("sinT", [128, NMAX])
    NWB = 3
    wb = [sb("wb%d" % i, [128, KD, 128], BF16) for i in range(NWB)]
    iost0_ = sb("iost0", [128, D]); iost = [iost0_, iost0_]
    ident = sb("ident", [128, 128]); identb = sb("identb", [128, 128], BF16); onesb = sb("onesb", [128, 128], BF16)
    ones1 = sb("ones1", [1, 128]); epsc = sb("epsc", [128, 1])
    maskA = sb("maskA", [128, 512], BF16); maskF = sb("maskF", [128, 512], BF16); maskS = sb("maskS", [32, 544], BF16)
    gcol = sb("gcol", [128, depth * 16 + 16]); dcol = sb("dcol", [128, depth * 4]); gbcol = sb("gbcol", [128, depth * 4])
    sinkbc = sb("sinkbc", [128, depth * 16]); bsrow = sb("bsrow", [1, 512]); bsS = sb("bsS", [1, 4, 32])
    wsT = sb("wsT", [128, depth * 4, 128], BF16); wsS = sb("wsS", [32, depth * 4, 32], BF16)
    hprev = sb("hprev", [128, depth, 16, 2]); s_in = sb("s_in", [128, depth, 4, 2, 16])
    tabA = sb("tabA", [128, 4096 + 2048])
    tabE = sb("tabE", [128, TABW - TER])
    Sprev = sb("Sprev", [128, 16, 2, NJ + 4], BF16)
    sq = [sb("sq%d" % i, [128, NMAX], BF16) for i in range(2)]
    rstd = sb("rstd", [128, NMAX]); tA = [sb("tA%d" % i, [128, NMAX]) for i in range(4)]
    xsw0_ = sb("xsw0", [128, NMAX]); xsw = [xsw0_, xsw0_]
    tB0_ = sb("tB0", [128, NMAX], BF16); tB = [tB0_, tB0_]
    yb = sb("yb", [128, 4, NMAX], BF16)
    attb = sb("attb", [128, 4096], BF16); Pf2 = [attb[:, 2048 * i:2048 * i + 1024].rearrange("q (a b) -> q a b", b=256) for i in range(2)]; Pb2 = [attb[:, 2048 * i + 1024:2048 * i + 2048].rearrange("q (a b) -> q a b", b=256) for i in range(2)]; Pf = Pf2[0]; Pb = Pb2[0]; glwb = attb[:, 0:2048].rearrange("q (a b) -> q a b", b=512); PTs = sb("PTs", [128, 1024], BF16); PTs2 = [PTs, sb("PTsB", [128, 1024], BF16)]
    sm = sb("sm", [128, 64])
    iob = iost0_[:].bitcast(BF16)
    ckb = iob[:, 0:1024].rearrange("q (a b) -> q a b", b=256); cvb = iob[:, 1024:2048].rearrange("q (a b) -> q a b", b=256)
    ckT = iob[:, 2048:3072].rearrange("q (b k w) -> q b k w", b=4, k=2)
    PfS = Pf[0:32].rearrange("q a b -> q (a b)")[:, 0:544]; PbS = Pb[0:32].rearrange("q a b -> q (a b)")[:, 0:544]; PTS2 = sb("PTS2", [128, 5, 32], BF16)
    s5w = [sb("s5w%d" % i, [128, 4 * NJ]) for i in range(10)]
    outst = sb("outst", [128, 416])
    ps01 = ps("ps01", [128, 1024]); ps0 = ps01[:, 0:512]; ps1 = ps01[:, 512:1024]; ps2 = ps("ps2", [128, 512]); ps3 = ps("ps3", [128, 512])
    psS = ps("psS", [128, 1024]); ps6 = ps("ps6", [128, 512]); psT = ps("psT", [128, 1024], BF16)

    tabAb = tabA[:].bitcast(BF16)

    def E(name):
        return {'pe': 'pe', 'act': 'act', 'dve': 'dve', 'pool': 'pool', 'sp': 'sp'}[name]

    def dma(q, out, in_, r, w):
        return P.add(q, lambda e: e.dma_start(out=out, in_=in_, allow_slow_non_contiguous=True), r=r, w=w, dma=True)

    def mm(out, lhsT, rhs, start, stop, r, w, **kw):
        return P.add('pe', lambda e: e.matmul(out, lhsT=lhsT, rhs=rhs, start=start, stop=stop, **kw), r=r, w=w)

    def tr(out, in_, idn, r, w, **kw):
        return P.add('pe', lambda e: e.transpose(out, in_, idn, **kw), r=r + ['ident'], w=w)

    def act(out, in_, func, r, w, scale=1.0, bias=None, accum=None):
        def f(e):
            kw = dict(out=out, in_=in_, func=func, scale=scale)
            if bias is not None:
                kw['bias'] = bias
            if accum is not None:
                kw['accum_out'] = accum
            return e.activation(**kw)
        return P.add('act', f, r=r, w=w)

    def tt(eng, out, a, b, op, r, w):
        return P.add(eng, lambda e: e.tensor_tensor(out=out, in0=a, in1=b, op=op), r=r, w=w)

    def ts(eng, out, a, s1, op0, r, w, s2=None, op1=None):
        if op1 is None:
            return P.add(eng, lambda e: e.tensor_scalar(out=out, in0=a, scalar1=s1, scalar2=None, op0=op0), r=r, w=w)
        return P.add(eng, lambda e: e.tensor_scalar(out=out, in0=a, scalar1=s1, scalar2=s2, op0=op0, op1=op1), r=r, w=w)

    def stt(out, a, s, b, op0, op1, r, w):
        return P.add('dve', lambda e: e.scalar_tensor_tensor(out=out, in0=a, scalar=s, in1=b, op0=op0, op1=op1), r=r, w=w)

    def cp(eng, out, in_, r, w):
        if eng == 'act':
            return P.add('act', lambda e: e.copy(out=out, in_=in_), r=r, w=w)
        return P.add(eng, lambda e: e.tensor_copy(out=out, in_=in_), r=r, w=w)

    def memset(eng, ap, v, w):
        return P.add(eng, lambda e: e.memset(ap, v), r=[], w=w)

    for l in range(depth):
        for rc in range(4):
            dma('pool', dap(wcn_in, (l * D + rc * 512) * DIN, [[DIN, 512], [1, DIN]]),
                dap(w_in, (l * D + rc * 512) * DIN, [[DIN, 512], [1, DIN]]), [], ['wcn%d' % l])
        for rc in range(2):
            dma('pool', dap(wcn_out, (l * D + rc * 1024) * D, [[D, 1024], [1, D]]),
                dap(w_out, (l * D + rc * 1024) * D, [[D, 1024], [1, D]]), [], ['wcn%d' % l])
    dma('sp', ident[:], c_ident.ap(), [], ['ident'])
    dma('pool', identb[:], c_ident.ap(), [], ['ident'])
    dma('pool', maskA[:], c_maskA.ap(), [], ['masks']); dma('pool', maskF[:], c_maskF.ap(), [], ['masks'])
    dma('pool', maskS[:], c_maskS.ap(), [], ['masks'])
    memset('dve', onesb[:], 1.0, ['ident']); memset('dve', ones1[:], 1.0, ['ident']); memset('dve', epsc[:], 1e-5, ['ident'])
    xw0 = xsw[0]
    permb = sb("permb", [128, 128], BF16)
    dma('pool', permb[:], c_perm.ap(), [], ['ident'])
    memset('pool', kT[:], 0.0, ['kT']); memset('pool', Vt[:], 0.0, ['Vt'])
    memset('pool', khalo[:], 0.0, ['khalo']); memset('pool', vhalo[:], 0.0, ['vhalo']); memset('dve', hprev[:], 0.0, ['hprev'])
    dma('sp', sinkbc[:], dap(sinks, 0, [[0, 128], [1, depth * 16]]), [], ['sinkbc'])
    nrow = depth * 16 + 16
    dma('sp', iost[0][0:nrow, 0:128], ng.ap(), [], ['iost0'])
    tr(ps3[:, 0:nrow], iost[0][0:nrow, 0:128], ident[0:nrow, 0:nrow], ['iost0'], ['ps3'])
    cp('dve', gcol[:], ps3[:, 0:nrow], ['ps3'], ['gcol'])
    dma('sp', iost[0][0:depth * 4, 128:256], dsk.ap(), [], ['iost0'])
    tr(ps3[:, 0:depth * 4], iost[0][0:depth * 4, 128:256], ident[0:depth * 4, 0:depth * 4], ['iost0'], ['ps3'])
    cp('dve', dcol[:], ps3[:, 0:depth * 4], ['ps3'], ['dcol'])
    dma('sp', iost[0][0:depth * 4, 256:384], glb.ap(), [], ['iost0'])
    tr(ps3[:, 0:depth * 4], iost[0][0:depth * 4, 256:384], ident[0:depth * 4, 0:depth * 4], ['iost0'], ['ps3'])
    cp('dve', gbcol[:], ps3[:, 0:depth * 4], ['ps3'], ['gbcol'])
    dma('sp', iost[0][:, 1536:1664], c_tril.ap(), [], ['iost0'])
    for lh in range(depth * 4):
        dma('sp', iost[0][:, 512:640], dap(cws, lh * 128 * 128, [[128, 128], [1, 128]]), [], ['iost0'])
        tt('dve', iost[0][:, 640:768], iost[0][:, 512:640], iost[0][:, 1536:1664], ALU.mult, ['iost0', 'iost0'], ['iost0b'])
        tr(ps3[:, 0:128], iost[0][:, 640:768], ident[:], ['iost0b'], ['ps3'])
        cp('dve', wsT[:, lh, :], ps3[:, 0:128], ['ps3'], ['wsT'])
    dma('sp', iost[0][0:32, 1664:1696], c_bd.ap(), [], ['iost0'])
    for lh in range(depth * 4):
        for b in range(4):
            dma('sp', iost[0][8 * b:8 * b + 8, 1024:1056].rearrange("q (a s) -> q a s", s=8), dap(cws, lh * 128 * 128, [[128, 8], [0, 4], [1, 8]]), [], ['iost0'])
        tt('dve', iost[0][0:32, 1056:1088], iost[0][0:32, 1024:1056], iost[0][0:32, 1664:1696], ALU.mult, ['iost0', 'iost0'], ['iost0b'])
        tr(ps3[0:32, 0:32], iost[0][0:32, 1056:1088], ident[0:32, 0:32], ['iost0b'], ['ps3'])
        cp('dve', wsS[:, lh, :], ps3[0:32, 0:32], ['ps3'], ['wsS'])
    for l in range(depth):
        for b in range(4):
            for ri, src in enumerate((sre, sim)):
                dma('sp', iost[0][0:16, 0:128], dap(src, ((l * 4 + b) * 16) * 128, [[128, 16], [1, 128]]), [], ['iost0'])
                tr(ps3[:, 0:16], iost[0][0:16, 0:128], ident[0:16, 0:16], ['iost0'], ['ps3'])
                cp('dve', s_in[:, l, b, ri, :], ps3[:, 0:16], ['ps3'], ['s_in'])

    W = NJ + 4
    maskm = sb("maskm", [128, 2])
    dma('sp', maskm[:], c_maskm.ap(), [], ['maskm'])
    mixf = mix[:].rearrange("q a b -> q (a b)").bitcast(F32)
    xTf = xT[:].rearrange("q a b -> q (a b)")
    _o = [0]

    def carve(buf, n, lim):
        a = _o[0]; _o[0] += n
        assert _o[0] <= lim
        return buf[:, a:a + n]
    sc = [carve(mixf, 16, 4352) for i in range(72)]
    Bn = [carve(mixf, 256, 4352).rearrange("q (a b) -> q a b", b=16) for i in range(2)]
    Cn = [carve(mixf, 256, 4352).rearrange("q (a b) -> q a b", b=64) for i in range(2)]
    Cp = [carve(mixf, 512, 4352).rearrange("q (a b) -> q a b", b=32) for i in range(2)]
    Y3 = [carve(mixf, 512, 4352).rearrange("q (a b) -> q a b", b=32) for i in range(2)]
    _o[0] = 4 * 16 * W
    tmpY = [carve(xTf, 256, 8704).rearrange("q (a b) -> q a b", b=16) for i in range(4)]
    Cpad = carve(xTf, 128, 8704)
    pw_all = carve(xTf, 288, 8704).rearrange("q (k r a) -> q k r a", k=9, r=2)
    CpB_all = carve(xTf, 512, 8704).bitcast(BF16).rearrange("q (r a c) -> q r a c", r=2, a=16)
    W1st = tabAb[:, 0:8192]; BDst = tabAb[:, 8192:12288]; W3half = iost0_[:].bitcast(BF16)
    Est = tabE[:, 0:2 * 16 * W].rearrange("q (a b c) -> q a b c", a=2, b=16)
    Etmp = [xTf[:, i * 16 * W:(i + 1) * 16 * W].rearrange("q (a b) -> q a b", b=W) for i in range(4)]
    Y1all = hT[:].rearrange("q a b -> q (a b)")[:, 0:8192].rearrange("q (k r a c) -> q k r a c", k=8, r=2, a=16)
    rs = ['s5p', 'xT', 'hT', 'tabA', 'tabW', 'tabE', 'iost0'] + ['mixa%d' % i for i in range(4)] + ['mixb%d' % i for i in range(8)] + ['mixc%d' % i for i in range(4)]
    for l in range(depth):
        cnt_i = [0]

        def T_():
            cnt_i[0] += 1
            return sc[cnt_i[0] - 1]
        are, aim, dtl = T_(), T_(), T_()
        for m in range(2):
            dma('sp', are[64 * m:64 * m + 64, :], dap(a_re, l * 2048 + m * 64, [[1, 64], [128, 16]]), [], rs)
            dma('sp', aim[64 * m:64 * m + 64, :], dap(a_im, l * 2048 + m * 64, [[1, 64], [128, 16]]), [], rs)
            dma('sp', dtl[64 * m:64 * m + 64, :], dap(ldt, l * 32 + m, [[0, 64], [2, 16]]), [], rs)
            dma('sp', Bn[0][64 * m:64 * m + 64, :, :], dap(b_re, l * 32768 + m * 1024, [[16, 64], [2048, 16], [1, 16]]), [], rs)
            dma('sp', Bn[1][64 * m:64 * m + 64, :, :], dap(b_im, l * 32768 + m * 1024, [[16, 64], [2048, 16], [1, 16]]), [], rs)
        dma('sp', Cn[0][:], dap(c_re, l * 32768, [[64, 128], [8192, 4], [1, 64]]), [], rs)
        dma('sp', Cn[1][:], dap(c_im, l * 32768, [[64, 128], [8192, 4], [1, 64]]), [], rs)
        def poly(x, coef):
            t = T_()
            n = len(coef) - 1
            ts('dve', t[:], x, float(coef[n]), ALU.mult, rs, rs)
            for k in range(n - 1, 0, -1):
                stt(t[:], t[:], float(coef[k]), x, ALU.add, ALU.mult, rs, rs)
            ts('dve', t[:], t[:], float(coef[0]), ALU.add, rs, rs)
            return t

        def exp_poly(x, deg, nsq):
            xs_ = T_(); ts('dve', xs_[:], x, 1.0 / (2 ** nsq), ALU.mult, rs, rs)
            e = poly(xs_[:], [1.0 / math.factorial(k) for k in range(deg + 1)])
            for _ in range(nsq):
                tt('dve', e[:], e[:], e[:], ALU.mult, rs, rs)
            return e
        dt_ = exp_poly(dtl[:], 12, 3)
        ar = T_(); tt('dve', ar[:], are[:], dt_[:], ALU.mult, rs, rs)
        th = T_(); tt('dve', th[:], aim[:], dt_[:], ALU.mult, rs, rs)
        mag = exp_poly(ar[:], 7, 0)
        TWO_PI = 2.0 * math.pi
        MAGIC = 12582912.0
        u = T_(); ts('dve', u[:], th[:], 1.0 / TWO_PI, ALU.mult, rs, rs)
        n_ = T_(); ts('dve', n_[:], u[:], MAGIC, ALU.add, rs, rs)
        n2 = T_(); ts('dve', n2[:], n_[:], MAGIC, ALU.subtract, rs, rs)
        fr0 = T_(); tt('dve', fr0[:], u[:], n2[:], ALU.subtract, rs, rs)
        xq = T_(); ts('dve', xq[:], fr0[:], TWO_PI / 4.0, ALU.mult, rs, rs)
        x2 = T_(); tt('dve', x2[:], xq[:], xq[:], ALU.mult, rs, rs)
        ps_ = poly(x2[:], [(-1.0) ** k / math.factorial(2 * k + 1) for k in range(7)])
        sn = T_(); tt('dve', sn[:], ps_[:], xq[:], ALU.mult, rs, rs)
        cs = poly(x2[:], [(-1.0) ** k / math.factorial(2 * k) for k in range(8)])
        for _ in range(2):
            s2_ = T_(); tt('dve', s2_[:], sn[:], cs[:], ALU.mult, rs, rs)
            q2_ = T_(); tt('dve', q2_[:], sn[:], sn[:], ALU.mult, rs, rs)
            sn = T_(); ts('dve', sn[:], s2_[:], 2.0, ALU.mult, rs, rs)
            cs = T_(); ts('dve', cs[:], q2_[:], -2.0, ALU.mult, rs, rs, s2=1.0, op1=ALU.add)
        lr = T_(); tt('dve', lr[:], mag[:], cs[:], ALU.mult, rs, rs)
        li = T_(); tt('dve', li[:], mag[:], sn[:], ALU.mult, rs, rs)

        def cmul(ar_, ai_, br_, bi_, bc=None):
            t1, t2, t3, t4, orr, oi = T_(), T_(), T_(), T_(), T_(), T_()
            tt('dve', t1[:], ar_, br_, ALU.mult, rs, rs); tt('dve', t2[:], ai_, bi_, ALU.mult, rs, rs)
            tt('dve', orr[:], t1[:], t2[:], ALU.subtract, rs, rs)
            tt('dve', t3[:], ar_, bi_, ALU.mult, rs, rs); tt('dve', t4[:], ai_, br_, ALU.mult, rs, rs)
            tt('dve', oi[:], t3[:], t4[:], ALU.add, rs, rs)
            return orr, oi
        nr = T_(); ts('dve', nr[:], lr[:], -1.0, ALU.add, rs, rs)
        d1 = T_(); tt('dve', d1[:], are[:], are[:], ALU.mult, rs, rs)
        d2 = T_(); tt('dve', d2[:], aim[:], aim[:], ALU.mult, rs, rs)
        den = T_(); tt('dve', den[:], d1[:], d2[:], ALU.add, rs, rs)
        rden = T_(); P.add('dve', lambda e, o=rden, i=den: e.reciprocal(out=o[:], in_=i[:]), r=rs, w=rs)
        nai = T_(); ts('dve', nai[:], aim[:], -1.0, ALU.mult, rs, rs)
        f0r, f0i = cmul(nr[:], li[:], are[:], nai[:])
        fr_ = T_(); tt('dve', fr_[:], f0r[:], rden[:], ALU.mult, rs, rs)
        fi_ = T_(); tt('dve', fi_[:], f0i[:], rden[:], ALU.mult, rs, rs)
        for ri in range(2):
            for T in range(4):
                for m in range(2):
                    ts('dve', Cpad[:, 64 * m:64 * m + 64], Cn[ri][:, T, :], maskm[:, m:m + 1], ALU.mult, rs, rs)
                tr(ps3[:, 0:128], Cpad[:], ident[:], rs, ['ps3'])
                cp('dve', Cp[ri][:, 4 * T:4 * T + 4, :], ps3[:, 0:128].rearrange("q (a b) -> q a b", b=32), ['ps3'], rs)
        pw = pw_all
        memset('dve', pw[:, 0, 0, :], 1.0, rs); memset('dve', pw[:, 0, 1, :], 0.0, rs)
        cp('dve', pw[:, 1, 0, :], lr[:], rs, rs); cp('dve', pw[:, 1, 1, :], li[:], rs, rs)
        base_i = cnt_i[0]
        for k in range(2, 9):
            cnt_i[0] = base_i
            orr, oi = cmul(pw[:, k - 1, 0, :], pw[:, k - 1, 1, :], lr[:], li[:])
            cp('dve', pw[:, k, 0, :], orr[:], rs, rs); cp('dve', pw[:, k, 1, :], oi[:], rs, rs)
        W3h = W3half.rearrange("q (t a r c) -> q t a r c", t=4, a=16, r=2)
        for t in range(8):
            pr = pw[:, t + 1, 0, :].unsqueeze(2).to_broadcast([128, 16, 32])
            pi_ = pw[:, t + 1, 1, :].unsqueeze(2).to_broadcast([128, 16, 32])
            tt('dve', Y3[0][:], Cp[0][:], pr, ALU.mult, rs, rs); tt('dve', Y3[1][:], Cp[1][:], pi_, ALU.mult, rs, rs)
            tt('dve', W3h[:, t % 4, :, 0, :], Y3[0][:], Y3[1][:], ALU.subtract, rs, rs)
            tt('dve', Y3[0][:], Cp[0][:], pi_, ALU.mult, rs, rs); tt('dve', Y3[1][:], Cp[1][:], pr, ALU.mult, rs, rs)
            tt('dve', Y3[0][:], Y3[0][:], Y3[1][:], ALU.add, rs, rs)
            ts('dve', W3h[:, t % 4, :, 1, :], Y3[0][:], -1.0, ALU.mult, rs, rs)
            if t % 4 == 3:
                dma('sp', dap(s5tab, l * 128 * TABW + TW3 + (t // 4) * 2048, [[TABW, 128], [1, 2048]]), iost0_[:], rs, ['s5tab'] + rs)
        memset('dve', hT[:].rearrange("q a b -> q (a b)")[:, 0:8192], 0.0, rs)
        for k in range(8):
            cnt_i[0] = base_i
            Fr, Fi = cmul(pw[:, k, 0, :], pw[:, k, 1, :], fr_[:], fi_[:])
            Frb = Fr[:].unsqueeze(2).to_broadcast([128, 16, 16]); Fib = Fi[:].unsqueeze(2).to_broadcast([128, 16, 16])
            tt('dve', tmpY[0][:], Bn[0][:], Frb, ALU.mult, rs, rs); tt('dve', tmpY[1][:], Bn[1][:], Fib, ALU.mult, rs, rs)
            tt('dve', tmpY[2][:], Bn[1][:], Frb, ALU.mult, rs, rs); tt('dve', tmpY[3][:], Bn[0][:], Fib, ALU.mult, rs, rs)
            for m in range(2):
                sl = slice(64 * m, 64 * m + 64)
                tt('dve', Y1all[sl, k, 0, :, 16 * m:16 * m + 16], tmpY[0][sl], tmpY[1][sl], ALU.subtract, rs, rs)
                tt('dve', Y1all[sl, k, 1, :, 16 * m:16 * m + 16], tmpY[2][sl], tmpY[3][sl], ALU.add, rs, rs)
        W1v = W1st.rearrange("q (s t r c) -> q s t r c", s=8, t=4, r=2)
        for s in range(8):
            k = 7 - s
            for T in range(4):
                for ri in range(2):
                    for r4 in range(4):
                        tr(psT[32 * r4:32 * r4 + 32, 0:128], Y1all[:, k, ri, 4 * T + r4, :], identb[:], rs, ['psT'], tile_position=(0, 32 * r4))
                    cp('dve', W1v[:, s, T, ri, :], psT[:, 0:128], ['psT'], rs)
        memset('dve', BDst, 0.0, rs)
        BDv = BDst.rearrange("q (t a c) -> q t a c", t=4, a=8)
        CpB = CpB_all
        cp('dve', CpB[:, 0], Cp[0][:], rs, rs); ts('dve', CpB[:, 1], Cp[1][:], -1.0, ALU.mult, rs, rs)
        for T in range(4):
            for tau in range(8):
                for r4 in range(4):
                    pi_i = 4 * T + r4
                    for ri in range(2):
                        mm(ps3[32 * r4:32 * r4 + 32, tau * 32:tau * 32 + 32], Y1all[:, tau, ri, pi_i, :], CpB[:, ri, pi_i, :],
                           ri == 0, ri == 1, rs, ['ps3'], tile_position=(0, 32 * r4))
            for r4 in range(4):
                sl = slice(32 * r4, 32 * r4 + 32)
                cp('dve', BDv[sl, T, :, 32 * r4:32 * r4 + 32], ps3[sl, 0:256].rearrange("q (a b) -> q a b", b=32), ['ps3'], rs)
        cnt_i[0] = base_i + 6
        r2 = T_(); tt('dve', r2[:], mag[:], mag[:], ALU.mult, rs, rs)
        r4_ = T_(); tt('dve', r4_[:], r2[:], r2[:], ALU.mult, rs, rs)
        rho = T_(); tt('dve', rho[:], r4_[:], r4_[:], ALU.mult, rs, rs)
        ur, ui = cs, sn
        for _ in range(3):
            a2 = T_(); tt('dve', a2[:], ur[:], ur[:], ALU.mult, rs, rs)
            b2 = T_(); tt('dve', b2[:], ui[:], ui[:], ALU.mult, rs, rs)
            ab = T_(); tt('dve', ab[:], ur[:], ui[:], ALU.mult, rs, rs)
            ur = T_(); tt('dve', ur[:], a2[:], b2[:], ALU.subtract, rs, rs)
            ui = T_(); ts('dve', ui[:], ab[:], 2.0, ALU.mult, rs, rs)
        cp('dve', Est[:, 0, :, 0], ur[:], rs, rs)
        cp('dve', Est[:, 1, :, 0], ui[:], rs, rs)
        k = 1
        while k < NJ:
            n = min(k, NJ - k)
            br_ = Est[:, 0, :, k - 1:k].to_broadcast([128, 16, n]); bi_ = Est[:, 1, :, k - 1:k].to_broadcast([128, 16, n])
            xr = Est[:, 0, :, 0:n]; xi = Est[:, 1, :, 0:n]
            tt('dve', Etmp[0][:, :, 0:n], xr, br_, ALU.mult, rs, rs); tt('dve', Etmp[1][:, :, 0:n], xi, bi_, ALU.mult, rs, rs)
            tt('dve', Etmp[2][:, :, 0:n], xr, bi_, ALU.mult, rs, rs); tt('dve', Etmp[3][:, :, 0:n], xi, br_, ALU.mult, rs, rs)
            tt('dve', Est[:, 0, :, k:k + n], Etmp[0][:, :, 0:n], Etmp[1][:, :, 0:n], ALU.subtract, rs, rs)
            tt('dve', Est[:, 1, :, k:k + n], Etmp[2][:, :, 0:n], Etmp[3][:, :, 0:n], ALU.add, rs, rs)
            k += n
        for ri in range(2):
            cp('dve', Est[:, ri, :, NJ:NJ + 4], Est[:, ri, :, 0:1].to_broadcast([128, 16, 4]), rs, rs)
        cp('dve', tabE[:, 32 * W:32 * W + 16], rho[:], rs, rs)
        dma('sp', dap(s5tab, l * 128 * TABW + TW1, [[TABW, 128], [1, 4096]]), tabA[:, 0:4096], rs, ['s5tab'] + rs)
        dma('sp', dap(s5tab, l * 128 * TABW + TBD, [[TABW, 128], [1, 2048]]), tabA[:, 4096:6144], rs, ['s5tab'] + rs)
        dma('sp', dap(s5tab, l * 128 * TABW + TER, [[TABW, 128], [1, TABW - TER]]), tabE[:], rs, ['s5tab'] + rs)

    wbi = [0]
    blkc = [0]
    cur_seg = [0]

    def proj(l, wsrc, pieces_fn, N, evac, hsrc, rkeys):
        i = wbi[0] % NWB
        wbi[0] += 1
        w = wb[i]
        wk = 'wb%d' % i
        blk = l * 56 + blkc[0]
        blkc[0] += 1
        ckey = 'wc%d' % blk
        wflat = w[:].rearrange("q a b -> q (a b)")
        if cur_seg[0] == 0:
            pieces_fn(w, wk)
            if nseg > 1:
                dma('sp', dap(wcache, blk * 128 * KD * 128, [[KD * 128, 128], [1, KD * 128]]), wflat, [wk], [ckey])
        else:
            dma('sp', wflat, dap(wcache, blk * 128 * KD * 128, [[KD * 128, 128], [1, KD * 128]]), [ckey], [wk])
        pm = ps0 if (wbi[0] % 2 == 0) else ps1
        pmk = 'ps0' if (wbi[0] % 2 == 0) else 'ps1'
        for k in range(KD):
            mm(pm[:, 0:TSEG], w[:, k, :], hsrc[:, k, 0:TSEG], k == 0, k == KD - 1, [wk] + rkeys, [pmk])
            if N > TSEG:
                mm(ps2[:, 0:NS], w[:, k, :], hsrc[:, k, TSEG:N], k == 0, k == KD - 1, [wk] + rkeys, ['ps2'])
        evac(pm, pmk)

    def in_pieces(l, col_pieces):
        def f(w, wk):
            off = 0
            for c0, wd in col_pieces:
                dma('sp', w[:, :, off:off + wd], dap(wcn_in, l * D * DIN + c0, [[DIN, 128], [128 * DIN, KD], [1, wd]]), ['wcn%d' % l], [wk])
                off += wd
        return f

    def qcols(base, tq):
        kt, i = tq // 4, tq % 4
        return [(base + 64 * (8 * kt + i), 64), (base + 64 * (8 * kt + 4 + i), 64)]

    def out_pieces(l, j):
        def f(w, wk):
            base = l * D * D + j * 128
            dma('sp', w[:, 0:4, :], dap(wcn_out, base, [[D, 128], [128 * D, 4], [1, 128]]), ['wcn%d' % l], [wk])
            dma('sp', w[:, 12:16, :], dap(wcn_out, base + 1536 * D, [[D, 128], [128 * D, 4], [1, 128]]), ['wcn%d' % l], [wk])
            for half in range(2):
                for kt in range(2):
                    dma('sp', w[64 * half:64 * half + 64, 4 + 4 * kt:8 + 4 * kt, :],
                        dap(wcn_out, base + (512 + 512 * kt + 256 * half) * D, [[D, 64], [64 * D, 4], [1, 128]]), ['wcn%d' % l], [wk])
        return f

    def rmsnorm(N, gidx, out_fn):
        for k in range(KD):
            s = sq[k % 2]
            act(s[:, 0:N], xT[:, k, 0:N], AF.Square, ['xT'], ['sq%d' % (k % 2)])
            mm(ps0[:, 0:TSEG], onesb[:], s[:, 0:TSEG], k == 0, k == KD - 1, ['sq%d' % (k % 2)], ['ps0'])
            if N > TSEG:
                mm(ps2[:, 0:NS], onesb[:], s[:, TSEG:N], k == 0, k == KD - 1, ['sq%d' % (k % 2)], ['ps2'])
        act(rstd[:, 0:TSEG], ps0[:, 0:TSEG], AF.Sqrt, ['ps0'], ['rstd'], scale=1.0 / D, bias=epsc[:])
        if N > TSEG:
            act(rstd[:, TSEG:N], ps2[:, 0:NS], AF.Sqrt, ['ps2'], ['rstd'], scale=1.0 / D, bias=epsc[:])
        P.add('dve', lambda e: e.reciprocal(out=rstd[:, 0:N], in_=rstd[:, 0:N]), r=['rstd'], w=['rstd'])
        for k in range(KD):
            out_fn(k, gcol[:, gidx * 16 + k:gidx * 16 + k + 1])

    def softmax(src3, np_, nh, Wd, sinkcols, Pf_, Pb_, rk, tag, so=0):
        smk = 'sm%d' % so
        mx = sm[0:np_, so + 0:so + nh]; m_ = sm[0:np_, so + 4:so + 4 + nh]; ng_ = sm[0:np_, so + 8:so + 8 + nh]; ssum = sm[0:np_, so + 12:so + 12 + nh]
        dd = sm[0:np_, so + 16:so + 16 + nh]; es = sm[0:np_, so + 20:so + 20 + nh]; den = sm[0:np_, so + 24:so + 24 + nh]; rd = sm[0:np_, so + 28:so + 28 + nh]
        P.add('dve', lambda e: e.tensor_reduce(out=mx, in_=src3, axis=AX.X, op=ALU.max), r=rk, w=[smk])
        ts('dve', m_, mx, 0.125, ALU.mult, [smk], [smk])
        tt('dve', m_, m_, sinkcols, ALU.max, [smk, 'sinkbc'], [smk])
        ts('dve', ng_, m_, -1.0, ALU.mult, [smk], [smk])
        for i in range(nh):
            act(Pf_[:, i, :], src3[:, i, :], AF.Exp, rk + [smk], [tag + 'Pf', 'glwb', smk + '2%d' % i], scale=0.125, bias=ng_[:, i:i + 1],
                accum=ssum[:, i:i + 1])
        tt('dve', dd, sinkcols, m_, ALU.subtract, [smk, 'sinkbc'], [smk])
        act(es, dd, AF.Exp, [smk], [smk])
        tt('dve', den, ssum, es, ALU.add, [smk] + [smk + '2%d' % i for i in range(nh)], [smk])
        P.add('dve', lambda e: e.reciprocal(out=rd, in_=den), r=[smk], w=[smk])
        tt('pool', Pb_, Pf_, rd.unsqueeze(2).to_broadcast([np_, nh, Wd]), ALU.mult, [tag + 'Pf', smk], [tag + 'Pb', 'glwb'])

    evi = [0]
    for seg in range(nseg):
        N = TSEG + (NS if seg == 0 else 0)
        cur_seg[0] = seg
        dma('sp', cosT[:, 0:TSEG], dap(c_cos, seg * TSEG, [[seqlen + NS, 128], [1, TSEG]]), [], ['rope'])
        dma('sp', sinT[:, 0:TSEG], dap(c_sin, seg * TSEG, [[seqlen + NS, 128], [1, TSEG]]), [], ['rope'])
        if seg == 0:
            dma('sp', cosT[:, TSEG:N], dap(c_cos, seqlen, [[seqlen + NS, 128], [1, NS]]), [], ['rope'])
            dma('sp', sinT[:, TSEG:N], dap(c_sin, seqlen, [[seqlen + NS, 128], [1, NS]]), [], ['rope'])
        blocks = [(b, 128, dap(xp, (seg * TSEG + b * 128) * D, [[D, 128], [1, D]]), b * 128) for b in range(NB)]
        if seg == 0:
            blocks.append((NB, NS, xs.ap(), TSEG))
        for bi, (b, rows, src, c0) in enumerate(blocks):
            io = iost[bi % 2]; iok = 'iost0'
            dma('sp', io[0:rows, :], src, [], [iok])
            for k4 in range(4):
                for kk in range(4):
                    k = 4 * k4 + kk
                    tr(ps3[:, kk * 128:kk * 128 + rows], io[0:rows, k * 128:(k + 1) * 128], ident[0:rows, 0:rows], [iok], ['ps3'])
                cp('dve' if k4 % 2 == 0 else 'act', xT[:, 4 * k4:4 * k4 + 4, c0:c0 + rows],
                   ps3[:, :].rearrange("q (a b) -> q a b", b=128)[:, :, 0:rows], ['ps3'], ['xT'])

        for l in range(depth):
            blkc[0] = 0
            dma('sp', tabA[:, 0:4096], dap(s5tab, l * 128 * TABW + TW1, [[TABW, 128], [1, 4096]]), ['s5tab'], ['tabW'])
            dma('sp', tabA[:, 4096:6144], dap(s5tab, l * 128 * TABW + TBD, [[TABW, 128], [1, 2048]]), ['s5tab'], ['tabA'])
            dma('sp', tabE[:], dap(s5tab, l * 128 * TABW + TER, [[TABW, 128], [1, TABW - TER]]), ['s5tab'], ['tabE'])
            dma('sp', bsrow[:], dap(cbs, l * 512, [[0, 1], [1, 512]]), [], ['bsrow'])
            if seg == 0:
                for b in range(4):
                    cp('pool', bsS[:, :, 8 * b:8 * b + 8], bsrow[:].rearrange("o (g t) -> o g t", t=128)[:, :, 0:8], ['bsrow'], ['bsS'])
            dma('pool', glwb[:], dap(glw, l * 512 * 512, [[512, 128], [128 * 512, 4], [1, 512]]), [], ['glwb', 'p0Pf', 'p0Pb'])
            W1 = tabAb[:, 0:8192].rearrange("q (s t r c) -> q s t r c", s=8, t=4, r=2)
            W3 = tabAb[:, 0:8192].rearrange("q (t a r c) -> q t a r c", t=8, a=16, r=2)
            BD = tabAb[:, 8192:12288].rearrange("q (t a c) -> q t a c", t=4, a=8)
            Er = tabE[:, 0:16 * W].rearrange("q (a b) -> q a b", b=W)
            Ei = tabE[:, 16 * W:32 * W].rearrange("q (a b) -> q a b", b=W)
            rhoc = tabE[:, 32 * W:32 * W + 16]
            NJJ = N // 8
            rmsnorm(N, l, lambda k, g: stt(hT[:, k, 0:N], xT[:, k, 0:N], g, rstd[:, 0:N], ALU.mult, ALU.mult,
                                           ['xT', 'rstd', 'gcol'], ['hT']))

            def evac_store(dst_fn, wkey):
                def f(pm, pmk):
                    evi[0] += 1
                    eng = 'act' if evi[0] % 2 == 0 else 'dve'
                    cp(eng, dst_fn(0, TSEG), pm[:, 0:TSEG], [pmk], [wkey])
                    if N > TSEG:
                        cp(eng, dst_fn(TSEG, N), ps2[:, 0:NS], ['ps2'], [wkey])
                return f

            def evac_gate(tile, wkey):
                def f(pm, pmk):
                    evi[0] += 1
                    t = tA[evi[0] % 2]; tk = 'tA%d' % (evi[0] % 2)
                    act(t[:, 0:TSEG], pm[:, 0:TSEG], AF.Silu, [pmk], [tk])
                    if N > TSEG:
                        act(t[:, TSEG:N], ps2[:, 0:NS], AF.Silu, ['ps2'], [tk])
                    tt('dve', mix[:, tile, 0:N], mix[:, tile, 0:N], t[:, 0:N], ALU.mult, [tk, wkey], [wkey])
                return f

            for T in range(4):
                proj(l, None, in_pieces(l, [(OFF_UC + 128 * T, 128)]), N,
                     evac_store(lambda a, b, T=T: mix[:, 12 + T, a:b], 'mixc%d' % T), hT, ['hT'])
            if seg == 0:
                for pi_i in range(16):
                    for ri in range(2):
                        cp('pool', Sprev[:, pi_i, ri, NJ:NJ + 4], s_in[:, l, :, ri, pi_i], ['s_in'], ['Sprev%d' % pi_i])
            G = 4 if NJJ * 8 <= 512 else 2
            WP = 512 // (2 * G)
            Er4 = Er.rearrange("q (t r) j -> q t r j", r=4); Ei4 = Ei.rearrange("q (t r) j -> q t r j", r=4)
            Sp4 = Sprev[:].rearrange("q (t r) i j -> q t r i j", r=4)
            hp4 = hprev[:].rearrange("q l (t r) i -> q l t r i", r=4)
            gi = 0
            for r4 in range(4):
              for T0 in range(0, 4, G):
                gi += 1
                pd = ps3 if gi % 2 == 0 else ps6
                pdk = 'ps3' if gi % 2 == 0 else 'ps6'
                pis = [4 * (T0 + g) + r4 for g in range(G)]
                sks = ['Sprev%d' % p_ for p_ in pis]
                for g in range(G):
                    T = T0 + g
                    uview = mix[32 * r4:32 * r4 + 32, 12 + T, 0:N].rearrange("q (j s) -> q j s", s=8)
                    for ri in range(2):
                        c0 = (g * 2 + ri) * WP
                        for s_ in range(8):
                            mm(pd[:, c0:c0 + NJJ], W1[32 * r4:32 * r4 + 32, s_, T, ri, :], uview[:, :, s_], s_ == 0, s_ == 7,
                               ['tabW', 'mixc%d' % T], [pdk], tile_position=(32 * r4, 0))
                Dv = pd[:, 0:512].rearrange("q (g r j) -> q g r j", g=G, r=2)
                Dre = Dv[:, :, 0, 0:NJJ]; Dim = Dv[:, :, 1, 0:NJJ]
                er = Er4[:, T0:T0 + G, r4, 0:NJJ]; ei = Ei4[:, T0:T0 + G, r4, 0:NJJ]
                w_ = [x[:, 0:G * NJJ].rearrange("q (g j) -> q g j", g=G) for x in s5w]
                k5 = ['s5w']
                tt('dve', w_[0], Dre, er, ALU.mult, [pdk, 'tabE'], k5); tt('dve', w_[1], Dim, ei, ALU.mult, [pdk, 'tabE'], k5)
                tt('pool', w_[2], w_[0], w_[1], ALU.add, k5, k5)
                tt('dve', w_[0], Dim, er, ALU.mult, [pdk, 'tabE'] + k5, k5); tt('dve', w_[1], Dre, ei, ALU.mult, [pdk, 'tabE'], k5)
                tt('pool', w_[3], w_[0], w_[1], ALU.subtract, k5, k5)
                for g in range(G):
                    pi_i = pis[g]
                    rb = rhoc[:, pi_i:pi_i + 1]
                    for ri, (cc, qq) in enumerate(((w_[2], w_[4]), (w_[3], w_[5]))):
                        P.add('dve', lambda e, cc=cc[:, g, 0:NJ], qq=qq[:, g, 0:NJ], rbb=rb.to_broadcast([128, NJ]),
                              ini=hprev[:, l, pi_i, ri:ri + 1]:
                              e.tensor_tensor_scan(out=qq, data0=rbb, data1=cc, initial=ini,
                                                   op0=ALU.mult, op1=ALU.add), r=k5 + ['hprev', 'tabE'], w=k5)
                        if seg == 0:
                            stt(qq[:, g, NJ:NJJ], s_in[:, l, :, ri, pi_i], rb, cc[:, g, NJ:NJJ], ALU.mult, ALU.add, k5 + ['s_in', 'tabE'], k5)
                tt('dve', w_[6], w_[4], er, ALU.mult, k5 + ['tabE'], k5); tt('dve', w_[7], w_[5], ei, ALU.mult, k5 + ['tabE'], k5)
                tt('pool', w_[8], w_[6], w_[7], ALU.subtract, k5, k5)
                tt('dve', w_[6], w_[5], er, ALU.mult, k5 + ['tabE'], k5); tt('dve', w_[7], w_[4], ei, ALU.mult, k5 + ['tabE'], k5)
                tt('pool', w_[9], w_[6], w_[7], ALU.add, k5, k5)
                for ri, S_ in enumerate((w_[8], w_[9])):
                    cp('act', Sp4[:, T0:T0 + G, r4, ri, 0:1], hp4[:, l, T0:T0 + G, r4, ri:ri + 1], ['hprev'], sks)
                    cp('act', Sp4[:, T0:T0 + G, r4, ri, 1:NJ], S_[:, :, 0:NJ - 1], k5, sks)
                    cp('dve', hp4[:, l, T0:T0 + G, r4, ri:ri + 1], S_[:, :, NJ - 1:NJ], k5 + sks, ['hprev'])
                    if seg == 0:
                        for g in range(G):
                            cp('pool', outst[:, 0:128].rearrange("q (b r a) -> q b r a", b=4, r=2)[:, :, ri, pis[g]], S_[:, g, NJ:NJJ],
                               k5, ['outst_s'])
            if seg == 0:
                for b in range(4):
                    for ri, dst in enumerate((hr_s, hi_s)):
                        tr(ps3[0:16, 0:128], outst[:, (b * 2 + ri) * 16:(b * 2 + ri) * 16 + 16], ident[:], ['outst_s'], ['ps3'])
                        cp('dve', outst[0:16, 128:256], ps3[0:16, 0:128], ['ps3'], ['outst_t'])
                        dma('sp', dap(dst, ((l * 4 + b) * 16) * 128, [[128, 16], [1, 128]]), outst[0:16, 128:256], ['outst_t'], ['out'])
            if seg == nseg - 1:
                for ri, dst in enumerate((hr_p, hi_p)):
                    cp('dve', outst[:, 256 + 16 * ri:272 + 16 * ri], hprev[:, l, :, ri], ['hprev'], ['outst_h'])
                    tr(ps3[0:16, 0:128], outst[:, 256 + 16 * ri:272 + 16 * ri], ident[:], ['outst_h'], ['ps3'])
                    cp('dve', outst[0:16, 288:416], ps3[0:16, 0:128], ['ps3'], ['outst_t2'])
                    dma('sp', dap(dst, l * 2048, [[128, 16], [1, 128]]), outst[0:16, 288:416], ['outst_t2'], ['out'])
            dma('sp', tabA[:, 0:4096], dap(s5tab, l * 128 * TABW + TW3, [[TABW, 128], [1, 4096]]), ['s5tab'], ['tabW'])
            def evac_rope(dst_fn, wkey):
                def f(pm, pmk):
                    evi[0] += 1
                    i2 = evi[0] % 2
                    xf = tA[i2]; xfk = 'tA%d' % i2
                    xb_ = sq[i2]; xbk = 'sq%d' % i2
                    pw_ = ps3 if i2 == 0 else ps6
                    pwk = 'ps3' if i2 == 0 else 'ps6'
                    cp('act', xb_[:, 0:TSEG], pm[:, 0:TSEG], [pmk], [xbk])
                    if N > TSEG:
                        cp('act', xb_[:, TSEG:N], ps2[:, 0:NS], ['ps2'], [xbk])
                    mm(pw_[:, 0:TSEG], permb[:], xb_[:, 0:TSEG], True, True, [xbk, 'ident'], [pwk])
                    tt('dve', xf[:, 0:TSEG], pw_[:, 0:TSEG], sinT[:, 0:TSEG], ALU.mult, [pwk, 'rope'], [xfk])
                    if N > TSEG:
                        mm(pw_[:, 0:NS], permb[:], xb_[:, TSEG:N], True, True, [xbk, 'ident', xfk], [pwk])
                        tt('dve', xf[:, TSEG:N], pw_[:, 0:NS], sinT[:, TSEG:N], ALU.mult, [pwk, 'rope'], [xfk])
                    tt('pool', xw0[:, 0:N], xb_[:, 0:N], cosT[:, 0:N], ALU.mult, [xbk, 'rope'], ['xsw0'])
                    tt('dve', dst_fn(0, N), xf[:, 0:N], xw0[:, 0:N], ALU.add, [xfk, 'xsw0'], [wkey])
                return f

            for kt in range(2):
                cp('pool', kT[:, kt, 0:128], khalo[:, l, kt, :], ['khalo'], ['kT'])
            cp('pool', Vt[:, 0, :], vhalo[:, l, :], ['vhalo'], ['Vt'])
            for kt in range(2):
                proj(l, None, in_pieces(l, [(OFF_K + 128 * kt, 128)]), N,
                     evac_rope(lambda a, b, kt=kt: kT[:, kt, 128 + a:128 + b], 'kT'), hT, ['hT'])
            for kt in range(2):
                proj(l, None, in_pieces(l, [(OFF_V + 128 * kt, 128)]), N,
                     evac_store(lambda a, b: tB[0][:, a:b], 'tB0'), hT, ['hT'])
                nblk = NB + (1 if seg == 0 else 0)
                for b in range(nblk):
                    rows = 128 if b < NB else NS
                    tr(psT[0:rows, b * 128:b * 128 + 128], tB[0][:, b * 128:b * 128 + rows], identb[:], ['tB0'], ['psT'])
                cp('dve', Vt[:, 1:1 + NB, 128 * kt:128 * kt + 128], psT[:, 0:NB * 128].rearrange("q (a b) -> q a b", b=128), ['psT'], ['Vt'])
                if seg == 0:
                    cp('dve', Vt[0:NS, NB + 1, 128 * kt:128 * kt + 128], psT[0:NS, NB * 128:NB * 128 + 128], ['psT'], ['Vt'])
            for tq in range(8):
                proj(l, None, in_pieces(l, qcols(OFF_Q, tq)), N,
                     evac_rope(lambda a, b, tq=tq: mix[:, 4 + tq, a:b], 'mixb%d' % tq), hT, ['hT'])
            for kt in range(2):
                cp('pool', khalo[:, l, kt, :], kT[:, kt, TSEG:TSEG + 128], ['kT'], ['khalo'])
            cp('pool', vhalo[:, l, :], Vt[:, NB, :], ['Vt'], ['vhalo'])
            if seg == nseg - 1:
                for kt in range(2):
                    tr(psT[:, 0:128], kT[:, kt, TSEG:TSEG + 128], identb[:], ['kT'], ['psT'])
                    cp('dve', PTs[:, 128 * kt:128 * kt + 128], psT[:, 0:128], ['psT'], ['PTs0'])
                dma('pool', dap(nk_p, l * 128 * 256, [[256, 128], [1, 256]]), PTs[:, 0:256], ['PTs0'], ['out'])
                dma('pool', dap(nv_p, l * 128 * 256, [[256, 128], [1, 256]]), Vt[:, NB, :], ['Vt'], ['out'])
            if seg == 0:
                for kt in range(2):
                    tr(psT[0:NS, 0:128], kT[:, kt, 128 + TSEG:128 + N], identb[:], ['kT'], ['psT'])
                    cp('dve', PTs[0:NS, 128 * kt:128 * kt + 128], psT[0:NS, 0:128], ['psT'], ['PTs0'])
                dma('pool', dap(nk_s, l * NS * 256, [[256, NS], [1, 256]]), PTs[0:NS, 0:256], ['PTs0'], ['out'])
                dma('pool', dap(nv_s, l * NS * 256, [[256, NS], [1, 256]]), Vt[0:NS, NB + 1, :], ['Vt'], ['out'])
            for T in range(4):
                uvw = mix[:, 12 + T, 0:N].rearrange("q (j s) -> q j s", s=8)
                for t in range(8):
                    reg = psS[:, (t // 4) * 512 + (t % 4) * W:(t // 4) * 512 + (t % 4) * W + NJJ]
                    for s in range(t + 1):
                        mm(reg, BD[:, T, t - s, :], uvw[:, :, s], s == 0, False, ['tabA', 'mixc%d' % T], ['psS'])
                    for r4 in range(4):
                        pi_i = 4 * T + r4
                        for ri in range(2):
                            o_ = psS[32 * r4:32 * r4 + 32, (t // 4) * 512 + (t % 4) * W:(t // 4) * 512 + (t % 4) * W + NJJ]
                            mm(o_, W3[:, t, pi_i, ri, :], Sprev[:, pi_i, ri, 0:NJJ], False, (ri == 1),
                               ['tabW', 'Sprev%d' % pi_i], ['psS'], tile_position=(0, 32 * r4))
                yv = tA[2][:, 0:N].rearrange("q (j s) -> q s j", s=8)
                for hb in range(2):
                    pv = psS[:, hb * 512:hb * 512 + 4 * W].rearrange("q (t j) -> q t j", j=W)[:, :, 0:NJJ]
                    uv2 = mix[:, 12 + T, 0:N].rearrange("q (j s) -> q s j", s=8)[:, 4 * hb:4 * hb + 4, :]
                    stt(yv[:, 4 * hb:4 * hb + 4, :], uv2, dcol[:, l * 4 + T:l * 4 + T + 1], pv, ALU.mult, ALU.add,
                        ['psS', 'mixc%d' % T, 'dcol'], ['tA2'])
                y_ = tA[2][:, 0:N]; g1 = tA[3][:, 0:N]
                tt('pool', g1, y_, y_, ALU.mult, ['tA2'], ['tA3'])
                ts('dve', g1, g1, 0.044715, ALU.mult, ['tA3'], ['tA3'], s2=1.0, op1=ALU.add)
                tt('pool', g1, g1, y_, ALU.mult, ['tA3', 'tA2'], ['tA3'])
                act(g1, g1, AF.Sigmoid, ['tA3'], ['tA3'], scale=2.0 * math.sqrt(2.0 / math.pi))
                tt('dve', yb[:, T, 0:N], y_, g1, ALU.mult, ['tA2', 'tA3'], ['yb'])
            for T in range(4):
                for kc in range(4):
                    mm(ps0[:, 0:TSEG], glwb[:, kc, 128 * T:128 * T + 128], yb[:, kc, 0:TSEG], kc == 0, kc == 3, ['glwb', 'yb'], ['ps0'])
                    if N > TSEG:
                        mm(ps2[:, 0:NS], glwb[:, kc, 128 * T:128 * T + 128], yb[:, kc, TSEG:N], kc == 0, kc == 3, ['glwb', 'yb'], ['ps2'])
                g2 = tA[3]
                act(g2[:, 0:TSEG], ps0[:, 0:TSEG], AF.Sigmoid, ['ps0'], ['tA3'], bias=gbcol[:, l * 4 + T:l * 4 + T + 1])
                if N > TSEG:
                    act(g2[:, TSEG:N], ps2[:, 0:NS], AF.Sigmoid, ['ps2'], ['tA3'], bias=gbcol[:, l * 4 + T:l * 4 + T + 1])
                tt('dve', mix[:, 12 + T, 0:N], yb[:, T, 0:N], g2[:, 0:N], ALU.mult, ['yb', 'tA3'], ['mixc%d' % T])
            for T in range(4):
                proj(l, None, in_pieces(l, [(OFF_GC + 128 * T, 128)]), N, evac_gate(12 + T, 'mixc%d' % T), hT, ['hT'])

            units = [(bq, kt, half) for bq in range(NB) for kt in range(2) for half in range(2)]
            import os as _os
            if _os.environ.get('V1'):
                psT2 = [psT, psT]; psTk = ['psT', 'psT']
            else:
                psT2 = [psT, ps3[:].bitcast(BF16)]; psTk = ['psT', 'ps3']
            Obuf = [ps6, ps2]; Obk = ['ps6', 'ps2']

            def stage1(u, bq, kt, half):
                par = u % 2
                Sb = psS if par == 0 else ps01
                Sk = ['psS'] if par == 0 else ['ps0', 'ps1']
                msk = maskF if (seg == 0 and bq == 0) else maskA
                hs = slice(64 * half, 64 * half + 64)
                for i2 in range(2):
                    for i in (2 * i2, 2 * i2 + 1):
                        tq = 4 * kt + i
                        mm(Sb[:, i * 256:(i + 1) * 256], mix[hs, 4 + tq, 128 * bq:128 * bq + 128],
                           kT[hs, kt, 128 * bq:128 * bq + 256], i % 2 == 0, False, ['mixb%d' % tq, 'kT'], Sk)
                    mm(Sb[:, i2 * 512:(i2 + 1) * 512], identb[:], msk[:], False, True, ['masks', 'ident'], Sk)

            def smv(par, np_=128, nh=4):
                so = 32 * par
                return dict(mx=sm[0:np_, so + 0:so + nh], m=sm[0:np_, so + 4:so + 4 + nh], ng=sm[0:np_, so + 8:so + 8 + nh],
                            ssum=sm[0:np_, so + 12:so + 12 + nh], dd=sm[0:np_, so + 16:so + 16 + nh], es=sm[0:np_, so + 20:so + 20 + nh],
                            den=sm[0:np_, so + 24:so + 24 + nh], rd=sm[0:np_, so + 28:so + 28 + nh])

            def stageF(u, bq, kt, half):
                par = u % 2
                Sb = psS if par == 0 else ps01
                Sk = ['psS'] if par == 0 else ['ps0', 'ps1']
                v = smv(par); kA = 'smA%d' % par
                h0 = l * 16 + 8 * kt + 4 * half
                sinkcols = sinkbc[:, h0:h0 + 4]
                src3 = Sb[:, :].rearrange("q (a b) -> q a b", b=256)
                P.add('dve', lambda e, o=v['mx'], i=src3: e.tensor_reduce(out=o, in_=i, axis=AX.X, op=ALU.max), r=Sk, w=[kA])
                ts('dve', v['m'], v['mx'], 0.125, ALU.mult, [kA], [kA])
                tt('dve', v['m'], v['m'], sinkcols, ALU.max, [kA, 'sinkbc'], [kA])
                ts('dve', v['ng'], v['m'], -1.0, ALU.mult, [kA], [kA])
                tt('dve', v['dd'], sinkcols, v['m'], ALU.subtract, [kA, 'sinkbc'], [kA])

            def stageX(u, bq, kt, half):
                par = u % 2
                Sb = psS if par == 0 else ps01
                Sk = ['psS'] if par == 0 else ['ps0', 'ps1']
                v = smv(par); kA = 'smA%d' % par; kS = 'smS%d' % par
                src3 = Sb[:, :].rearrange("q (a b) -> q a b", b=256)
                for i in range(4):
                    act(Pf2[par][:, i, :], src3[:, i, :], AF.Exp, Sk + [kA], ['p%dPf' % par, 'glwb', kS], scale=0.125,
                        bias=v['ng'][:, i:i + 1], accum=v['ssum'][:, i:i + 1])
                act(v['es'], v['dd'], AF.Exp, [kA], [kS])

            def stageB(u, bq, kt, half):
                par = u % 2
                v = smv(par); kS = 'smS%d' % par; kR = 'smR%d' % par
                tt('dve', v['den'], v['ssum'], v['es'], ALU.add, [kS], [kR])
                P.add('dve', lambda e, o=v['rd'], i=v['den']: e.reciprocal(out=o, in_=i), r=[kR], w=[kR])
                tt('pool', Pb2[par], Pf2[par], v['rd'].unsqueeze(2).to_broadcast([128, 4, 256]), ALU.mult, ['p%dPf' % par, kR],
                   ['p%dPb' % par, 'glwb'])

            def stage2(u, bq, kt, half):
                par = u % 2
                hs = slice(64 * half, 64 * half + 64)
                pT = psT2[par]; pTk = psTk[par]; PT_ = PTs2[par]; PTk = 'PTs%d' % par
                Ob = Obuf[kt]; Ok = Obk[kt]
                for i in range(4):
                    for kb in range(2):
                        tr(pT[:, (2 * i + kb) * 128:(2 * i + kb) * 128 + 128], Pb2[par][:, i, kb * 128:kb * 128 + 128], identb[:],
                           ['p%dPb' % par], [pTk])
                cp('act', PT_[:], pT[:], [pTk], [PTk])
                for i in range(4):
                    for kb in range(2):
                        mm(Ob[hs, i * 128:i * 128 + 128], Vt[:, bq + kb, (2 * kt + half) * 64:(2 * kt + half) * 64 + 64],
                           PT_[:, (2 * i + kb) * 128:(2 * i + kb) * 128 + 128], kb == 0, kb == 1, ['Vt', PTk], [Ok],
                           tile_position=(0, 64 * half))
                if half == 1:
                    def ev(Ob=Ob, Ok=Ok, kt=kt, bq=bq):
                        for i in range(4):
                            cp('dve', mix[:, 4 + 4 * kt + i, 128 * bq:128 * bq + 128], Ob[:, i * 128:i * 128 + 128],
                               [Ok], ['mixb%d' % (4 * kt + i)])
                    pend.append(ev)
            nu = len(units)
            pend = []
            stage1(0, *units[0])
            for k_ in range(nu + 2):
                if k_ + 1 < nu:
                    stage1(k_ + 1, *units[k_ + 1])
                if k_ >= 2:
                    stageB(k_ - 2, *units[k_ - 2])
                if k_ < nu:
                    stageF(k_, *units[k_])
                    stageX(k_, *units[k_])
                for ev_ in pend:
                    ev_()
                del pend[:]
                if k_ >= 2:
                    stage2(k_ - 2, *units[k_ - 2])
            for ev_ in pend:
                ev_()
            del pend[:]
            if seg == 0:
                dma('pool', ckb, dap(ck, l * 4 * 128 * 256, [[256, 128], [128 * 256, 4], [1, 256]]), [], ['iost0'])
                dma('pool', cvb, dap(cv, l * 4 * 128 * 256, [[256, 128], [128 * 256, 4], [1, 256]]), [], ['iost0'])
                for b in range(4):
                    for kt in range(2):
                        tr(psT[:, (2 * b + kt) * 128:(2 * b + kt) * 128 + 128], ckb[:, b, kt * 128:kt * 128 + 128], identb[:], ['iost0'], ['psT'])
                cp('dve', ckT, psT[:].rearrange("q (b k w) -> q b k w", b=4, k=2), ['psT'], ['iost0'])
                for kt in range(2):
                    for half in range(2):
                        hs = slice(64 * half, 64 * half + 64)
                        for i in range(4):
                            tq = 4 * kt + i
                            qs = mix[hs, 4 + tq, TSEG:N]
                            for b in range(4):
                                mm(psS[0:NS, b * 128:b * 128 + 128], qs, ckT[hs, b, kt, :], b == 0, False, ['mixb%d' % tq, 'iost0'], ['psS'])
                            mm(psS[0:NS, 0:512], identb[0:NS, 0:NS], maskS[:, 0:512], False, True, ['masks', 'ident'], ['psS'])
                            mm(psS[0:NS, 512:512 + NS], qs, kT[hs, kt, 128 + TSEG:128 + N], True, False, ['mixb%d' % tq, 'kT'], ['psS'])
                            mm(psS[0:NS, 512:512 + NS], identb[0:NS, 0:NS], maskS[:, 512:544], False, True, ['masks', 'ident'], ['psS'])
                            h0 = l * 16 + 8 * kt + 4 * half + i
                            softmax(psS[0:NS, 0:544].rearrange("q (a b) -> q a b", b=544), NS, 1, 544, sinkbc[0:NS, h0:h0 + 1],
                                    PfS.rearrange("q (a b) -> q a b", b=544), PbS.rearrange("q (a b) -> q a b", b=544), ['psS'], 'p0')
                            for b in range(4):
                                tr(psT[:, b * 32:b * 32 + 32], PbS[:, b * 128:b * 128 + 128], identb[0:NS, 0:NS], ['p0Pb'], ['psT'])
                            tr(psT[0:NS, 128:160], PbS[:, 512:544], identb[0:NS, 0:NS], ['p0Pb'], ['psT'])
                            cp('act', PTS2[:, 0:4, :], psT[:, 0:128].rearrange("q (a b) -> q a b", b=32), ['psT'], ['PTS2'])
                            cp('act', PTS2[0:NS, 4, :], psT[0:NS, 128:160], ['psT'], ['PTS2'])
                            vc = (2 * kt + half) * 64
                            for b in range(4):
                                mm(ps6[hs, i * 32:i * 32 + 32], cvb[:, b, vc:vc + 64], PTS2[:, b, :], b == 0, False, ['iost0', 'PTS2'], ['ps6'],
                                   tile_position=(0, 64 * half))
                            mm(ps6[hs, i * 32:i * 32 + 32], Vt[0:NS, NB + 1, vc:vc + 64], PTS2[0:NS, 4, :], False, True, ['Vt', 'PTS2'], ['ps6'],
                               tile_position=(0, 64 * half))
                    for i in range(4):
                        cp('act' if i % 2 else 'dve', mix[:, 4 + 4 * kt + i, TSEG:N], ps6[:, i * 32:i * 32 + 32], ['ps6'], ['mixb%d' % (4 * kt + i)])
            for tq in range(8):
                proj(l, None, in_pieces(l, qcols(OFF_GB, tq)), N, evac_gate(4 + tq, 'mixb%d' % tq), hT, ['hT'])

            for T in range(4):
                proj(l, None, in_pieces(l, [(OFF_UA + 128 * T, 128)]), N,
                     evac_store(lambda a, b, T=T: mix[:, T, a:b], 'mixa%d' % T), hT, ['hT'])
            for T in range(4):
                proj(l, None, in_pieces(l, [(OFF_VA + 128 * T, 128)]), N,
                     evac_store(lambda a, b: tB[1][:, a:b], 'tB0'), hT, ['hT'])
                nblk = NB + (1 if seg == 0 else 0)
                for b in range(nblk):
                    rows = 128 if b < NB else NS
                    tr(psT[0:rows, b * 128:b * 128 + 128], tB[1][:, b * 128:b * 128 + rows], identb[:], ['tB0'], ['psT'])
                cp('dve', vatm[:, 0:NB, 128 * T:128 * T + 128], psT[:, 0:NB * 128].rearrange("q (a b) -> q a b", b=128), ['psT'], ['vatm'])
                if seg == 0:
                    cp('dve', vatm[0:NS, NB, 128 * T:128 * T + 128], psT[0:NS, NB * 128:NB * 128 + 128], ['psT'], ['vatm'])
            if seg == 0:
                dma('pool', dap(va_s, l * NS * 512, [[512, NS], [1, 512]]), vatm[0:NS, NB, :], ['vatm'], ['out'])
            for h in range(4):
                lh = l * 4 + h
                for b in range(NB):
                    mm(ps3[:, b * 128:b * 128 + 128], vatm[:, b, 128 * h:128 * h + 128], wsT[:, lh, :], b == 0, False, ['vatm', 'wsT'], ['ps3'])
                for b in range(NB):
                    P.add('pe', lambda e, b=b, lh=lh: e.matmul(ps3[:, b * 128:b * 128 + 128], lhsT=ones1[0:1, :],
                                                              rhs=bsrow[0:1, (lh % 4) * 128:(lh % 4) * 128 + 128], start=False, stop=(b == NB - 1)),
                          r=['bsrow', 'ident'], w=['ps3'])
                tt('dve', mix[:, h, 0:TSEG], mix[:, h, 0:TSEG], ps3[:, 0:TSEG], ALU.mult, ['ps3', 'mixa%d' % h], ['mixa%d' % h])
                if seg == 0:
                    mm(ps6[:, 0:NS], vatm[0:NS, NB, 128 * h:128 * h + 128], wsS[:, lh, :], True, False, ['vatm', 'wsS'], ['ps6'])
                    P.add('pe', lambda e, lh=lh: e.matmul(ps6[:, 0:NS], lhsT=ones1[0:1, :], rhs=bsS[0:1, lh % 4, :], start=False, stop=True),
                          r=['bsS', 'ident'], w=['ps6'])
                    tt('dve', mix[:, h, TSEG:N], mix[:, h, TSEG:N], ps6[:, 0:NS], ALU.mult, ['ps6', 'mixa%d' % h], ['mixa%d' % h])
            for T in range(4):
                proj(l, None, in_pieces(l, [(OFF_GA + 128 * T, 128)]), N, evac_gate(T, 'mixa%d' % T), hT, ['hT'])

            allmix = ['mixa%d' % i for i in range(4)] + ['mixb%d' % i for i in range(8)] + ['mixc%d' % i for i in range(4)]

            def evac_res(j):
                def f(pm, pmk):
                    tt('dve', xT[:, j, 0:TSEG], xT[:, j, 0:TSEG], pm[:, 0:TSEG], ALU.add, [pmk, 'xT'], ['xT'])
                    if N > TSEG:
                        tt('dve', xT[:, j, TSEG:N], xT[:, j, TSEG:N], ps2[:, 0:NS], ALU.add, ['ps2', 'xT'], ['xT'])
                return f
            for j in range(KD):
                proj(l, None, out_pieces(l, j), N, evac_res(j), mix, allmix)

        rmsnorm(N, depth, lambda k, g: stt(xT[:, k, 0:N], xT[:, k, 0:N], g, rstd[:, 0:N], ALU.mult, ALU.mult,
                                           ['xT', 'rstd', 'gcol'], ['xT']))
        oblocks = [(128, b * 128, dap(y_p, (seg * TSEG + b * 128) * D, [[D, 128], [1, D]])) for b in range(NB)]
        if seg == 0:
            oblocks.append((NS, TSEG, y_s.ap()))
        for bi, (rows, c0, dst) in enumerate(oblocks):
            io = iost[bi % 2]; iok = 'iost0'
            for k4 in range(4):
                for kk in range(4):
                    k = 4 * k4 + kk
                    tr(ps3[0:rows, kk * 128:kk * 128 + 128], xT[:, k, c0:c0 + rows], ident[:], ['xT'], ['ps3'])
                cp('dve' if k4 % 2 == 0 else 'act', io[0:rows, 512 * k4:512 * k4 + 512], ps3[0:rows, :], ['ps3'], [iok])
            dma('sp', dst, io[0:rows, :], [iok], ['out'])

    import os
    ks = os.environ.get('KSTOP')
    if ks:
        print('NOPS', len(P.ops)); print('LASTOPS', [(i, o['eng'], o['line']) for i, o in list(enumerate(P.ops))[max(0, int(ks) - 3):int(ks)]]); P.ops = P.ops[:int(ks)]
    P.emit()
    st.close()
    return nc


def _consts(seqlen):
    half = 8
    inv = (500000.0 ** (-np.arange(half, dtype=np.float32) * 2.0 / 16.0)).astype(np.float32)
    pos = np.concatenate([np.arange(seqlen, dtype=np.float32), np.arange(8, dtype=np.float32) + PAST] * 1)
    pos = np.concatenate([np.arange(seqlen, dtype=np.float32)] + [np.arange(8, dtype=np.float32) + np.float32(PAST)] * 4)
    ang = pos[None, :] * inv[:, None]
    cosv = np.cos(ang).astype(np.float32); sinv = np.sin(ang).astype(np.float32)
    C = np.ones((128, pos.shape[0]), np.float32); S = np.zeros((128, pos.shape[0]), np.float32)
    for h0 in (0, 64):
        C[h0:h0 + 8] = cosv; C[h0 + 8:h0 + 16] = cosv
        S[h0:h0 + 8] = -sinv; S[h0 + 8:h0 + 16] = sinv
    i = np.arange(128)[:, None]; j = np.arange(256)[None, :]
    diff = i + 128 - j
    valid = (diff >= 0) & (diff < 128)
    mA = np.where(valid, 0.0, NEGM).astype(np.float32)
    mF = mA.copy(); mF[:, 0:128] = NEGM
    mA2 = np.concatenate([mA, mA], 1); mF2 = np.concatenate([mF, mF], 1)
    mS = np.full((32, 544), NEGM, np.float32)
    for b in range(4):
        for t in range(8):
            q = 8 * b + t
            for jj in range(128):
                if jj > t:
                    mS[q, b * 128 + jj] = 0.0
            for s in range(t + 1):
                mS[q, 512 + 8 * b + s] = 0.0
    tril = np.tril(np.ones((128, 128), np.float32))
    maskm = np.zeros((128, 2), np.float32)
    for q in range(128):
        maskm[q, (q // 16) % 2] = 1.0
    bd = np.zeros((32, 32), np.float32)
    for b in range(4):
        for t in range(8):
            for s in range(t + 1):
                bd[8 * b + t, 8 * b + s] = 1.0
    perm = np.zeros((128, 128), np.float32)
    for m_ in range(128):
        mm_ = m_ % 64
        if mm_ < 8:
            perm[m_ + 8, m_] = 1.0
        elif mm_ < 16:
            perm[m_ - 8, m_] = 1.0
    return dict(c_perm=perm, c_ident=np.eye(128, dtype=np.float32), c_tril=tril, c_maskA=mA2, c_maskF=mF2, c_maskS=mS, c_maskm=maskm,
                c_bd=bd, c_cos=C, c_sin=S)


_NC_CACHE = {}


def kernel(x_prompt, x_sample, cache_swa_k, cache_swa_v, state_ssm_re, state_ssm_im,
           norm_g, final_norm_g, w_in, w_out, chunk_w_s, chunk_b_s, attn_sinks,
           ssm_a_re, ssm_a_im, ssm_log_dt, ssm_b_re, ssm_b_im, ssm_c_re, ssm_c_im,
           ssm_d, glu_w, glu_b):
    f = lambda a: np.ascontiguousarray(np.asarray(a, dtype=np.float32))
    depth = int(np.asarray(w_in).shape[0]); seqlen = int(np.asarray(x_prompt).shape[1])
    nseg = seqlen // TSEG
    key = (depth, nseg)
    if key not in _NC_CACHE:
        _NC_CACHE[key] = build_program(depth, nseg)
    nc = _NC_CACHE[key]
    cst = _consts(seqlen)
    shared = dict(
        ng=np.concatenate([f(norm_g).reshape(depth * 16, 128), f(final_norm_g).reshape(16, 128)], 0),
        w_in=f(w_in), w_out=f(w_out), cws=f(chunk_w_s), cbs=f(chunk_b_s).reshape(1, -1), sinks=f(attn_sinks).reshape(1, -1),
        a_re=f(ssm_a_re), a_im=f(ssm_a_im), ldt=f(ssm_log_dt), b_re=f(ssm_b_re), b_im=f(ssm_b_im),
        c_re=f(ssm_c_re).reshape(depth, 512, 64), c_im=f(ssm_c_im).reshape(depth, 512, 64),
        dsk=f(ssm_d).reshape(depth * 4, 128), glw=f(glu_w), glb=f(glu_b).reshape(depth * 4, 128), **cst)
    xpf, xsf = f(x_prompt), f(x_sample)
    ckf, cvf = f(cache_swa_k), f(cache_swa_v)
    srf, sif = f(state_ssm_re), f(state_ssm_im)
    nb = xpf.shape[0]
    in_maps = []
    for c in range(8):
        m = dict(shared)
        m["xp"] = xpf[c % nb]
        m["xs"] = xsf[4 * c:4 * c + 4].reshape(NS, D)
        m["ck"] = ckf[:, 4 * c:4 * c + 4].reshape(depth, 4, 128, 256)
        m["cv"] = cvf[:, 4 * c:4 * c + 4].reshape(depth, 4, 128, 256)
        m["sre"] = srf[:, 4 * c:4 * c + 4].reshape(depth, 4, 16, 128)
        m["sim"] = sif[:, 4 * c:4 * c + 4].reshape(depth, 4, 16, 128)
        in_maps.append(m)
    res = run_bass_kernel_spmd(nc, in_maps, core_ids=list(range(8))).results
    y_prompt = np.stack([res[b]["y_p"] for b in range(nb)], 0)
    y_sample = np.concatenate([res[c]["y_s"].reshape(4, 8, D) for c in range(8)], 0)
    nkp = np.stack([res[b]["nk_p"].reshape(depth, 128, 4, 64) for b in range(nb)], 1)
    nvp = np.stack([res[b]["nv_p"].reshape(depth, 128, 4, 64) for b in range(nb)], 1)
    nks = np.concatenate([res[c]["nk_s"].reshape(depth, 4, 8, 4, 64) for c in range(8)], 1)
    nvs = np.concatenate([res[c]["nv_s"].reshape(depth, 4, 8, 4, 64) for c in range(8)], 1)
    hrp = np.stack([res[b]["hr_p"].reshape(depth, 32, 64) for b in range(nb)], 1)
    hip = np.stack([res[b]["hi_p"].reshape(depth, 32, 64) for b in range(nb)], 1)
    hrs = np.concatenate([res[c]["hr_s"].reshape(depth, 4, 32, 64) for c in range(8)], 1)
    his = np.concatenate([res[c]["hi_s"].reshape(depth, 4, 32, 64) for c in range(8)], 1)
    vas = np.concatenate([res[c]["va_s"].reshape(depth, 4, 8, 512) for c in range(8)], 1)
    return tuple(np.ascontiguousarray(a, dtype=np.float32) for a in
                 (y_prompt, y_sample, nkp, nvp, nks, nvs, hrp, hip, hrs, his, vas))
```
